# Optimizing a Trainium2 kernel written in Bass

```python
import math
import jax, jax.numpy as jnp
from jax import lax
import numpy as np

D_MODEL = 2048
BATCH = 4
SEQ = 4096
DEPTH = 4

CHUNK = 64
MIX_W = D_MODEL // 2
N_BRANCH = 3
LRU_W = MIX_W
LRU_BLOCKS = 16
LRU_BW = LRU_W // LRU_BLOCKS
CONV_W = 4
LRU_C = 8.0
ATT_HEADS = 8
ATT_HD = MIX_W // ATT_HEADS
ATT_LEFT_CHUNKS = 8
ATT_BAND = (ATT_LEFT_CHUNKS + 1) * CHUNK
MAX_REL = 128
N_REL = 2 * MAX_REL + 1
SSM_W = MIX_W
SSM_GROUP = 16
SSM_G = SSM_W // SSM_GROUP
SSM_P = 64
FFN_HIDDEN = -(-8 * D_MODEL // (3 * 256)) * 256
IN_SPLITS = [LRU_W, LRU_W, MIX_W, MIX_W, MIX_W, SSM_W]
IN_W = sum(IN_SPLITS) + N_BRANCH * D_MODEL
NORM_EPS = 1e-6
MASK_VALUE = -1e30

kernel_name = "hybrid_gated_lru_chunkattn_s5_encoder"


def rmsnorm(x, g):
    xf = x.astype(jnp.float32)
    xf = xf * lax.rsqrt(jnp.mean(xf * xf, axis=-1, keepdims=True) + NORM_EPS)
    return xf.astype(x.dtype) * g


def causal_dwconv(x, w, b):
    s = x.shape[1]
    xp = jnp.pad(x, ((0, 0), (CONV_W - 1, 0), (0, 0)))
    y = b
    for k in range(CONV_W):
        y = y + xp[:, k:k + s] * w[k]
    return y


def _lin_op(e1, e2):
    a1, b1 = e1
    a2, b2 = e2
    return a1 * a2, a2 * b1 + b2


def _complex_lin_op(e1, e2):
    ar1, ai1, br1, bi1 = e1
    ar2, ai2, br2, bi2 = e2
    return (ar2 * ar1 - ai2 * ai1,
            ar2 * ai1 + ai2 * ar1,
            ar2 * br1 - ai2 * bi1 + br2,
            ar2 * bi1 + ai2 * br1 + bi2)


def rg_lru_branch(xin, gate_in, conv_w, conv_b, wa, ba, wx, bx, lam):
    b_, s_, _ = xin.shape
    xc = causal_dwconv(xin, conv_w, conv_b).astype(jnp.float32)
    xb = xc.reshape(b_, s_, LRU_BLOCKS, LRU_BW)
    r = jax.nn.sigmoid(jnp.einsum("bsnk,nkj->bsnj", xb, wa.astype(jnp.float32)).reshape(b_, s_, LRU_W) + ba)
    i = jax.nn.sigmoid(jnp.einsum("bsnk,nkj->bsnj", xb, wx.astype(jnp.float32)).reshape(b_, s_, LRU_W) + bx)
    log_a = -LRU_C * r * jax.nn.softplus(-lam.astype(jnp.float32))
    a = jnp.exp(log_a)
    inp = jnp.sqrt(-jnp.expm1(2.0 * log_a)) * (i * xc)
    h = lax.associative_scan(_lin_op, (a, inp), axis=1)[1]
    return (h * jax.nn.gelu(gate_in.astype(jnp.float32))).astype(xin.dtype)


def chunk_attention_branch(q, k, v, rel_bias):
    b_, s_, _ = q.shape
    n_c = s_ // CHUNK
    qc = q.reshape(b_, n_c, CHUNK, ATT_HEADS, ATT_HD)
    pad = ((0, 0), (ATT_LEFT_CHUNKS, 0), (0, 0), (0, 0), (0, 0))
    kp = jnp.pad(k.reshape(b_, n_c, CHUNK, ATT_HEADS, ATT_HD), pad)
    vp = jnp.pad(v.reshape(b_, n_c, CHUNK, ATT_HEADS, ATT_HD), pad)
    band_idx = jnp.arange(n_c)[:, None] + jnp.arange(ATT_LEFT_CHUNKS + 1)[None, :]
    kb = kp[:, band_idx].reshape(b_, n_c, ATT_BAND, ATT_HEADS, ATT_HD)
    vb = vp[:, band_idx].reshape(b_, n_c, ATT_BAND, ATT_HEADS, ATT_HD)
    scores = jnp.einsum("bcqhd,bckhd->bchqk", qc, kb).astype(jnp.float32) * (ATT_HD ** -0.5)
    q_pos = ATT_LEFT_CHUNKS * CHUNK + jnp.arange(CHUNK)
    k_pos = jnp.arange(ATT_BAND)
    rel = jnp.clip(q_pos[:, None] - k_pos[None, :], -MAX_REL, MAX_REL) + MAX_REL
    bias = rel_bias.astype(jnp.float32)[:, rel]
    key_abs = (jnp.arange(n_c)[:, None] - ATT_LEFT_CHUNKS) * CHUNK + k_pos[None, :]
    valid = key_abs >= 0
    scores = jnp.where(valid[None, :, None, None, :], scores + bias[None, None], MASK_VALUE)
    p = jax.nn.softmax(scores, axis=-1).astype(v.dtype)
    o = jnp.einsum("bchqk,bckhd->bcqhd", p, vb)
    return o.reshape(b_, s_, MIX_W)


def s5_branch(u, a_re, a_im, b_re, b_im, c_re, c_im, d, log_step):
    b_, s_, _ = u.shape
    uf = u.astype(jnp.float32)
    ug = uf.reshape(b_, s_, SSM_G, SSM_GROUP)
    a_re = a_re.astype(jnp.float32)
    a_im = a_im.astype(jnp.float32)
    step = jnp.exp(log_step.astype(jnp.float32))[:, None]
    mag = jnp.exp(a_re * step)
    ang = a_im * step
    lb_re = mag * jnp.cos(ang)
    lb_im = mag * jnp.sin(ang)
    den = a_re * a_re + a_im * a_im
    nr = lb_re - 1.0
    coef_re = (nr * a_re + lb_im * a_im) / den
    coef_im = (lb_im * a_re - nr * a_im) / den
    b_re = b_re.astype(jnp.float32)
    b_im = b_im.astype(jnp.float32)
    bb_re = coef_re[..., None] * b_re - coef_im[..., None] * b_im
    bb_im = coef_re[..., None] * b_im + coef_im[..., None] * b_re
    bu_re = jnp.einsum("bsgh,gph->bsgp", ug, bb_re)
    bu_im = jnp.einsum("bsgh,gph->bsgp", ug, bb_im)
    ar = jnp.broadcast_to(lb_re, bu_re.shape)
    ai = jnp.broadcast_to(lb_im, bu_re.shape)
    _, _, xs_re, xs_im = lax.associative_scan(_complex_lin_op, (ar, ai, bu_re, bu_im), axis=1)
    y = (jnp.einsum("bsgp,ghp->bsgh", xs_re, c_re.astype(jnp.float32))
         - jnp.einsum("bsgp,ghp->bsgh", xs_im, c_im.astype(jnp.float32)))
    y = y.reshape(b_, s_, SSM_W) + d.astype(jnp.float32) * uf
    return y.astype(u.dtype)


def setup_inputs(seed: int = 0) -> dict:
    key = jax.random.key(seed)
    ks = jax.random.split(key, 32)
    f32 = jnp.float32

    def nrm(k, shape, scale):
        return jax.random.normal(k, shape, f32) * scale

    u_lam = jax.random.uniform(ks[8], (DEPTH, LRU_W), f32, 0.9, 0.999)
    p_lam = u_lam ** (1.0 / LRU_C)
    lru_lambda = jnp.log(p_lam) - jnp.log1p(-p_lam)
    n_idx = jnp.arange(SSM_P, dtype=f32)
    return {
        "x": nrm(ks[0], (BATCH, SEQ, D_MODEL), 1.0),
        "norm_mix_g": 1.0 + nrm(ks[1], (DEPTH, D_MODEL), 0.01),
        "w_in": nrm(ks[2], (DEPTH, D_MODEL, IN_W), D_MODEL ** -0.5),
        "gate_bias": nrm(ks[3], (DEPTH, N_BRANCH, D_MODEL), 0.01),
        "lru_conv_w": nrm(ks[4], (DEPTH, CONV_W, LRU_W), CONV_W ** -0.5),
        "lru_conv_b": nrm(ks[5], (DEPTH, LRU_W), 0.01),
        "lru_wa": nrm(ks[6], (DEPTH, LRU_BLOCKS, LRU_BW, LRU_BW), LRU_BW ** -0.5),
        "lru_ba": nrm(ks[7], (DEPTH, LRU_W), 0.01),
        "lru_wx": nrm(ks[9], (DEPTH, LRU_BLOCKS, LRU_BW, LRU_BW), LRU_BW ** -0.5),
        "lru_bx": nrm(ks[10], (DEPTH, LRU_W), 0.01),
        "lru_lambda": lru_lambda,
        "attn_rel_bias": nrm(ks[11], (DEPTH, ATT_HEADS, N_REL), 0.1),
        "ssm_a_re": -0.5 + nrm(ks[12], (DEPTH, SSM_G, SSM_P), 0.01),
        "ssm_a_im": math.pi * n_idx + nrm(ks[13], (DEPTH, SSM_G, SSM_P), 0.01),
        "ssm_b_re": nrm(ks[14], (DEPTH, SSM_G, SSM_P, SSM_GROUP), (2 * SSM_GROUP) ** -0.5),
        "ssm_b_im": nrm(ks[15], (DEPTH, SSM_G, SSM_P, SSM_GROUP), (2 * SSM_GROUP) ** -0.5),
        "ssm_c_re": nrm(ks[16], (DEPTH, SSM_G, SSM_GROUP, SSM_P), (2 * SSM_P) ** -0.5),
        "ssm_c_im": nrm(ks[17], (DEPTH, SSM_G, SSM_GROUP, SSM_P), (2 * SSM_P) ** -0.5),
        "ssm_d": nrm(ks[18], (DEPTH, SSM_W), 1.0),
        "ssm_log_step": jax.random.uniform(ks[19], (DEPTH, SSM_G), f32, math.log(1e-3), math.log(1e-1)),
        "ssm_w_glu": nrm(ks[20], (DEPTH, SSM_W, D_MODEL), SSM_W ** -0.5),
        "w_branch": nrm(ks[21], (DEPTH, N_BRANCH, MIX_W, D_MODEL), MIX_W ** -0.5),
        "w_out": nrm(ks[22], (DEPTH, D_MODEL, D_MODEL), D_MODEL ** -0.5),
        "norm_ffn_g": 1.0 + nrm(ks[23], (DEPTH, D_MODEL), 0.01),
        "w_ffn_gate": nrm(ks[24], (DEPTH, D_MODEL, FFN_HIDDEN), D_MODEL ** -0.5),
        "w_ffn_up": nrm(ks[25], (DEPTH, D_MODEL, FFN_HIDDEN), D_MODEL ** -0.5),
        "w_ffn_down": nrm(ks[26], (DEPTH, FFN_HIDDEN, D_MODEL), FFN_HIDDEN ** -0.5),
        "norm_final_g": 1.0 + nrm(ks[27], (D_MODEL,), 0.01),
    }


def reference(x, norm_mix_g, w_in, gate_bias, lru_conv_w, lru_conv_b, lru_wa, lru_ba, lru_wx, lru_bx,
              lru_lambda, attn_rel_bias, ssm_a_re, ssm_a_im, ssm_b_re, ssm_b_im, ssm_c_re, ssm_c_im,
              ssm_d, ssm_log_step, ssm_w_glu, w_branch, w_out, norm_ffn_g, w_ffn_gate, w_ffn_up,
              w_ffn_down, norm_final_g):
    b_, s_, _ = x.shape
    split_pts = [int(p) for p in np.cumsum(IN_SPLITS)]
    for l in range(DEPTH):
        h = rmsnorm(x, norm_mix_g[l])
        proj = h @ w_in[l]
        lru_x, lru_gate, q, k, v, ssm_u, gates = jnp.split(proj, split_pts, axis=-1)
        y_a = rg_lru_branch(lru_x, lru_gate, lru_conv_w[l], lru_conv_b[l], lru_wa[l], lru_ba[l],
                            lru_wx[l], lru_bx[l], lru_lambda[l])
        y_b = chunk_attention_branch(q, k, v, attn_rel_bias[l])
        y_c = jax.nn.gelu(s5_branch(ssm_u, ssm_a_re[l], ssm_a_im[l], ssm_b_re[l], ssm_b_im[l],
                                    ssm_c_re[l], ssm_c_im[l], ssm_d[l], ssm_log_step[l]))
        br_a = y_a @ w_branch[l, 0]
        br_b = y_b @ w_branch[l, 1]
        br_c = (y_c @ w_branch[l, 2]) * jax.nn.sigmoid(y_c @ ssm_w_glu[l])
        g = jax.nn.sigmoid(gates.reshape(b_, s_, N_BRANCH, D_MODEL) + gate_bias[l])
        merged = g[:, :, 0] * br_a + g[:, :, 1] * br_b + g[:, :, 2] * br_c
        x = x + merged @ w_out[l]
        h = rmsnorm(x, norm_ffn_g[l])
        x = x + (jax.nn.silu(h @ w_ffn_gate[l]) * (h @ w_ffn_up[l])) @ w_ffn_down[l]
    return rmsnorm(x, norm_final_g)
```

```python
import math
from contextlib import ExitStack
import numpy as np
import concourse.bass as bass
import concourse.mybir as mybir
from concourse.bass_utils import run_bass_kernel_spmd

F32 = mybir.dt.float32
BF16 = mybir.dt.bfloat16
AF = mybir.ActivationFunctionType
ALU = mybir.AluOpType

D = 2048
NK = 16
MIXW = 1024
FH = 5632
NJ = 44
TWO_PI = float(2 * math.pi)
MAGIC = 12582912.0


class Buf:
    __slots__ = ("name", "w", "r", "dsem")

    def __init__(self, name):
        self.name = name
        self.w = None
        self.r = {}
        self.dsem = None


class Eng:
    def __init__(self, name, sem):
        self.name = name
        self.sem = sem
        self.cnt = 0
        self.waited = {}
        self.prog = []


class KB:
    def __init__(self, nc):
        self.nc = nc
        self.stack = ExitStack()
        self.sems = {}
        self.semcnt = {}
        self.free_dsems = []
        self.dirty = {}
        self.eng = {}
        for name in ("pe", "act", "dve", "pool", "sp"):
            self.eng[name] = Eng(name, self.newsem("e_" + name))
        self.uid = 0

    def newsem(self, name):
        h = self.stack.enter_context(self.nc.semaphore(name))
        key = len(self.sems)
        self.sems[key] = h
        self.semcnt[key] = 0
        return key

    def buf(self, name="b"):
        self.uid += 1
        return Buf(f"{name}_{self.uid}")

    def _deps(self, reads, writes):
        deps = {}
        for b in reads:
            if b.w is not None and deps.get(b.w[0], 0) < b.w[1]:
                deps[b.w[0]] = b.w[1]
        for b in writes:
            if b.w is not None and deps.get(b.w[0], 0) < b.w[1]:
                deps[b.w[0]] = b.w[1]
            for k, v in b.r.items():
                if deps.get(k, 0) < v:
                    deps[k] = v
        return deps

    def _emit_waits(self, e, deps, skip=None):
        for k, v in deps.items():
            if k == skip:
                continue
            if e.waited.get(k, 0) < v:
                e.prog.append(("w", k, v))
                e.waited[k] = v

    def _update(self, tok, reads, writes):
        k, v = tok
        for b in reads:
            if b.r.get(k, 0) < v:
                b.r[k] = v
        for b in writes:
            b.w = tok
            b.r = {}

    def op(self, en, fns, reads=(), writes=()):
        e = self.eng[en]
        if callable(fns):
            fns = [fns]
        deps = self._deps(reads, writes)
        self._emit_waits(e, deps, skip=e.sem if en == "pe" else None)
        for f in fns[:-1]:
            e.prog.append(("i", f, None, 0))
        e.cnt += 1
        e.prog.append(("i", fns[-1], e.sem, 1))
        tok = (e.sem, e.cnt)
        self._update(tok, reads, writes)
        return tok

    def dma(self, en, fns, reads=(), writes=(), owner=None):
        e = self.eng[en]
        if callable(fns):
            fns = [fns]
        if owner.dsem is None:
            owner.dsem = self.free_dsems.pop() if self.free_dsems else self.newsem("d%d" % len(self.sems))
        k = owner.dsem
        deps = self._deps(reads, writes)
        self._emit_waits(e, deps)
        for f in fns:
            self.semcnt[k] += 16
            e.prog.append(("i", f, k, 16))
        tok = (k, self.semcnt[k])
        self.dirty[k] = self.semcnt[k]
        self._update(tok, reads, writes)
        return tok

    def barrier(self):
        toks = dict(self.dirty)
        for e in self.eng.values():
            if e.cnt > 0:
                toks[e.sem] = e.cnt
        for e in self.eng.values():
            self._emit_waits(e, toks)
        self.dirty = {}

    def release(self, bufs):
        for b in bufs:
            if b.dsem is not None:
                self.free_dsems.append(b.dsem)
                b.dsem = None

    def finalize(self):
        nc = self.nc
        self.barrier()
        engs, sems = self.eng, self.sems

        def replay(e, h):
            for it in e.prog:
                if it[0] == "w":
                    h.wait_ge(sems[it[1]], it[2])
                else:
                    inst = it[1](h)
                    if it[2] is not None:
                        inst.then_inc(sems[it[2]], it[3])

        with nc.Block() as block:
            @block.tensor
            def _(h):
                replay(engs["pe"], h)

            @block.scalar
            def _(h):
                replay(engs["act"], h)

            @block.vector
            def _(h):
                replay(engs["dve"], h)

            @block.gpsimd
            def _(h):
                replay(engs["pool"], h)

            @block.sync
            def _(h):
                replay(engs["sp"], h)
        self.stack.close()


class Phase:
    def __init__(self, kb):
        self.kb = kb
        self.st = ExitStack()
        self.bufs = []

    def tile(self, name, shape, dtype):
        kb = self.kb
        kb.uid += 1
        t = self.st.enter_context(kb.nc.sbuf_tensor(f"{name}_{kb.uid}", list(shape), dtype))
        b = kb.buf(name)
        self.bufs.append(b)
        return t, b

    def ring(self, name, n, shape, dtype):
        return Ring([self.tile(name, shape, dtype) for _ in range(n)])

    def close(self):
        self.kb.barrier()
        self.kb.release(self.bufs)
        self.st.close()


class Ring:
    def __init__(self, items):
        self.items = items
        self.i = 0

    def next(self):
        it = self.items[self.i % len(self.items)]
        self.i += 1
        return it


def col_layout(L):
    segs = [("gmix", L * 16), ("gffn", L * 16), ("gfin", 16), ("gbias", L * 48), ("convw", L * 32),
            ("convb", L * 8), ("ba", L * 8), ("bx", L * 8), ("lam", L * 8), ("ssmd", L * 8),
            ("acre", L * 32), ("acim", L * 32), ("lsc", L * 32), ("iota", 1)]
    off, o = {}, 0
    for n, w in segs:
        off[n] = o
        o += w
    return off, o


def build_program(L, S, ST, dbg=False, has_prev=False, skip=()):
    nc = bass.Bass("TRN2", target_bir_lowering=False)
    kb = KB(nc)
    gst = kb.stack
    nST = S // ST
    nTT = ST // 512
    COFF, NCOL = col_layout(L)
    okind = "ExternalOutput" if dbg else "Internal"

    def din(name, shape, dt=F32):
        return nc.dram_tensor(name, list(shape), dt, kind="ExternalInput").ap()

    def dscr(name, shape, dt):
        return nc.dram_tensor(name, list(shape), dt, kind=okind).ap()

    xT = din("xT", [D, S])
    w_in_t = din("w_in_t", [L, 96, 128, 16, 128])
    wbr_t = din("wbr_t", [L, 16, 128, 4, 8, 128])
    wout_t = din("wout_t", [L, 16, 128, 16, 128])
    wgu_t = din("wgu_t", [L, NJ, 128, 2, 16, 128])
    wdn_t = din("wdn_t", [L, 16, 128, NJ, 128])
    colp_d = din("colp", [128, NCOL])
    rowp_d = din("rowp", [L, 3, 4096])
    lruw_d = din("lruw", [L, 128, 8, 2, 128])
    abias_d = din("abias", [L, 8, 128, 640])
    bbd_d = din("bbd", [L, 128, 2, 4096])
    ct_d = din("ctd", [L, 128, 8192])
    cst_d = din("cst", [128, 386])
    eps_d = din("epsd", [128, 1])
    outT = nc.dram_tensor("outT", [D, S], F32, kind="ExternalOutput").ap()

    xr = dscr("xr", [D, S], F32)
    lxT = dscr("lxT", [MIXW, S], BF16)
    lgT = dscr("lgT", [MIXW, ST], BF16)
    qT = dscr("qT", [MIXW, ST], BF16)
    kT = dscr("kT", [MIXW, S], BF16)
    vS = dscr("vS", [S, MIXW], BF16)
    uT = dscr("uT", [MIXW, ST], BF16)
    gT = dscr("gT", [3 * D, ST], BF16)
    yaT = dscr("yaT", [MIXW, ST], BF16)
    ybT = dscr("ybT", [MIXW, ST], BF16)
    ycT = dscr("ycT", [MIXW, ST], BF16)
    mT = dscr("mT", [D, ST], BF16)
    actT = dscr("actT", [FH, ST], BF16)

    def gtile(name, shape, dt):
        t = gst.enter_context(nc.sbuf_tensor(name, list(shape), dt))
        return t, kb.buf(name)

    colp, b_colp = gtile("colp_s", [128, NCOL], F32)
    nsp8, b_nsp8 = gtile("nsp8", [128, L * 8], F32)
    ones_f, b_onesf = gtile("ones_f", [128, 128], F32)
    ones_b, b_onesb = gtile("ones_b", [128, 128], BF16)
    tri_b, b_trib = gtile("tri_b", [128, 128], BF16)
    iorow, b_iorow = gtile("iorow", [128, 129], F32)
    lru_h, b_lruh = gtile("lru_h", [128, 8], F32)
    s5ca, b_s5ca = gtile("s5ca", [128, 32], F32)
    s5cb, b_s5cb = gtile("s5cb", [128, 32], F32)
    epsc, b_epsc = gtile("epsc", [128, 1], F32)
    psum = gst.enter_context(nc.psum_tensor("psum", [128, 4096], F32))
    pb = [kb.buf(f"ps{i}") for i in range(8)]

    def bank(i):
        return psum[:, i * 512:(i + 1) * 512]

    def col(name, idx, n=1):
        o = COFF[name] + idx
        return colp[:, o:o + n]

    def I(name, **kw):
        return lambda h: getattr(h, name)(**kw)

    def MM(out, lhsT, rhs, start=True, stop=True):
        return I("matmul", out=out, lhsT=lhsT, rhs=rhs, start=start, stop=stop)

    def DMA(out, in_):
        return I("dma_start", out=out, in_=in_)

    def V(name, r, w, **kw):
        return kb.op("dve", I(name, **kw), reads=r, writes=w)

    def A(r, w, out, in_, func, **kw):
        return kb.op("act", I("activation", out=out, in_=in_, func=func, **kw), reads=r, writes=w)

    def TT(r, w, out, in0, in1, op):
        return V("tensor_tensor", r, w, out=out, in0=in0, in1=in1, op=op)

    def TS(r, w, out, in0, s1, s2, op0, op1=None):
        if op1 is None:
            return V("tensor_scalar", r, w, out=out, in0=in0, scalar1=s1, scalar2=None, op0=op0)
        return V("tensor_scalar", r, w, out=out, in0=in0, scalar1=s1, scalar2=s2, op0=op0, op1=op1)

    def STT(r, w, out, in0, scalar, in1, op0, op1):
        return V("scalar_tensor_tensor", r, w, out=out, in0=in0, scalar=scalar, in1=in1, op0=op0, op1=op1)

    kb.dma("sp", DMA(colp[:], colp_d), writes=[b_colp], owner=b_colp)
    kb.dma("sp", DMA(ones_f[:], cst_d[:, 128:256]), writes=[b_onesf], owner=b_onesf)
    kb.dma("sp", DMA(iorow[:], cst_d[:, 256:385]), writes=[b_iorow], owner=b_iorow)
    kb.dma("pool", DMA(tri_b[:], cst_d[:, 0:128]), writes=[b_trib], owner=b_trib)
    kb.dma("pool", DMA(ones_b[:], cst_d[:, 128:256]), writes=[b_onesb], owner=b_onesb)
    kb.dma("sp", DMA(epsc[:], eps_d), writes=[b_epsc], owner=b_epsc)
    A([b_colp], [b_nsp8], nsp8[:], col("lam", 0, L * 8), AF.Exp, scale=-1.0)
    A([b_nsp8], [b_nsp8], nsp8[:], nsp8[:], AF.Ln, bias=1.0, scale=1.0)
    TS([b_nsp8], [b_nsp8], nsp8[:], nsp8[:], -8.0, None, ALU.mult)
    b_x0 = kb.buf("x0")
    kb.dma("sp", [DMA(xr[c * 128:(c + 1) * 128, :], xT[c * 128:(c + 1) * 128, :]) for c in range(16)], writes=[b_x0], owner=b_x0)
    kb.barrier()

    xr_v = xr.rearrange("(k p) t -> p k t", p=128)

    def rms_norm_tile(xin, b_xin, tg, sq_ring, rs_ring, pbi):
        kb.dma("sp", DMA(xin[:], xr_v[:, :, tg:tg + 512]), writes=[b_xin], owner=b_xin)
        for k in range(NK):
            sq, b_sq = sq_ring.next()
            A([b_xin], [b_sq], sq[:], xin[:, k, :], AF.Square)
            kb.op("pe", MM(bank(pbi), ones_f[:], sq[:], start=(k == 0), stop=(k == NK - 1)), reads=[b_sq, b_onesf], writes=[pb[pbi]])
        rs, b_rs = rs_ring.next()
        A([pb[pbi], b_epsc], [b_rs], rs[:], bank(pbi), AF.Sqrt, scale=1.0 / D, bias=epsc[:])
        V("reciprocal", [b_rs], [b_rs], out=rs[:], in_=rs[:])
        return rs, b_rs

    def rms_norm_to(hT, b_hT, gname, l, t0, xin, b_xin, sq_ring, rs_ring, pbi):
        for tt in range(nTT):
            rs, b_rs = rms_norm_tile(xin, b_xin, t0 + tt * 512, sq_ring, rs_ring, pbi)
            for k in range(NK):
                STT([b_xin, b_rs, b_colp], [b_hT], hT[:, k, tt * 512:(tt + 1) * 512], xin[:, k, :], col(gname, l * 16 + k), rs[:], ALU.mult, ALU.mult)

    evac_flip = [0]

    def evac_copy(out_ap, in_ap, reads, writes):
        evac_flip[0] ^= 1
        if evac_flip[0]:
            A(reads, writes, out_ap, in_ap, AF.Copy)
        else:
            V("tensor_copy", reads, writes, out=out_ap, in_=in_ap)

    def proj_residual(l, t0, src_dram, nk, w_dram_l, chunk_tt):
        srcv = src_dram.rearrange("(k p) t -> p k t", p=128)
        splits = [(a, min(a + 16, nk)) for a in range(0, nk, 16)]
        for c0 in range(0, nTT, chunk_tt):
            ph = Phase(kb)
            ncols = chunk_tt * 512
            sm, b_sm = ph.tile("sm", [128, nk, ncols], BF16)
            kb.dma("sp", [DMA(sm[:, a:b, :], srcv[:, a:b, c0 * 512:c0 * 512 + ncols]) for a, b in splits], writes=[b_sm], owner=b_sm)
            wo_ring = ph.ring("wo", 2, [128, nk, 128], BF16)
            xs_ring = ph.ring("xs", 3, [128, 512], F32)
            it = 0
            for f in range(16):
                wo, b_wo = wo_ring.next()
                kb.dma("pool", [DMA(wo[:, a:b, :], w_dram_l[f, :, a:b, :]) for a, b in splits], writes=[b_wo], owner=b_wo)
                for tt in range(chunk_tt):
                    bi = it % 6
                    it += 1
                    tg = t0 + (c0 + tt) * 512
                    xs, b_xs = xs_ring.next()
                    kb.dma("sp", DMA(xs[:], xr[f * 128:(f + 1) * 128, tg:tg + 512]), writes=[b_xs], owner=b_xs)
                    kb.op("pe", [MM(bank(bi), wo[:, k, :], sm[:, k, tt * 512:(tt + 1) * 512], start=(k == 0), stop=(k == nk - 1)) for k in range(nk)],
                          reads=[b_wo, b_sm], writes=[pb[bi]])
                    TT([pb[bi], b_xs], [b_xs], xs[:], bank(bi), xs[:], ALU.add)
                    kb.dma("sp", DMA(xr[f * 128:(f + 1) * 128, tg:tg + 512], xs[:]), reads=[b_xs], owner=b_xs)
            ph.close()

    def v3(ap):
        return ap.rearrange("p (b j) -> p b j", b=4)

    for l in range(L):
        for s in range(nST):
            t0 = s * ST
            first = (s == 0) and not has_prev
            if "p1" not in skip:
                ph = Phase(kb)
                hT, b_hT = ph.tile("hT", [128, NK, ST], BF16)
                xin, b_xin = ph.tile("xin", [128, NK, 512], F32)
                sq_ring = ph.ring("sq", 2, [128, 512], F32)
                rs_ring = ph.ring("rs", 2, [128, 512], F32)
                wt_ring = ph.ring("wt", 3, [128, NK, 128], BF16)
                stg_ring = ph.ring("stg", 3, [128, ST], BF16)
                wv_ring = ph.ring("wv", 1, [128, NK, 512], BF16)
                sv_ring = ph.ring("sv", 3, [128, 512], BF16)
                rms_norm_to(hT, b_hT, "gmix", l, t0, xin, b_xin, sq_ring, rs_ring, 7)
                pbr = 0
                for m in list(range(0, 32)) + list(range(40, 96)):
                    wt, b_wt = wt_ring.next()
                    kb.dma("pool", DMA(wt[:], w_in_t[l, m]), writes=[b_wt], owner=b_wt)
                    stg, b_stg = stg_ring.next()
                    for tt in range(nTT):
                        bi = pbr % 6
                        pbr += 1
                        kb.op("pe", [MM(bank(bi), wt[:, k, :], hT[:, k, tt * 512:(tt + 1) * 512], start=(k == 0), stop=(k == NK - 1)) for k in range(NK)],
                              reads=[b_wt, b_hT], writes=[pb[bi]])
                        o_ap = stg[:, tt * 512:(tt + 1) * 512]
                        if m < 48:
                            evac_copy(o_ap, bank(bi), [pb[bi]], [b_stg])
                        else:
                            A([pb[bi], b_colp], [b_stg], o_ap, bank(bi), AF.Sigmoid, bias=col("gbias", l * 48 + (m - 48)), scale=1.0)
                    if m < 8:
                        dst = lxT[m * 128:(m + 1) * 128, t0:t0 + ST]
                    elif m < 16:
                        dst = lgT[(m - 8) * 128:(m - 7) * 128, :]
                    elif m < 24:
                        dst = qT[(m - 16) * 128:(m - 15) * 128, :]
                    elif m < 32:
                        dst = kT[(m - 24) * 128:(m - 23) * 128, t0:t0 + ST]
                    elif m < 48:
                        dst = uT[(m - 40) * 128:(m - 39) * 128, :]
                    else:
                        dst = gT[(m - 48) * 128:(m - 47) * 128, :]
                    kb.dma("sp", DMA(dst, stg[:]), reads=[b_stg], owner=b_stg)
                for cb in range(2):
                    wv, b_wv = wv_ring.next()
                    kb.dma("pool", [DMA(wv[:, :, mm * 128:(mm + 1) * 128], w_in_t[l, 32 + 4 * cb + mm]) for mm in range(4)], writes=[b_wv], owner=b_wv)
                    for j in range(ST // 128):
                        bi = pbr % 6
                        pbr += 1
                        kb.op("pe", [MM(bank(bi), hT[:, k, j * 128:(j + 1) * 128], wv[:, k, :], start=(k == 0), stop=(k == NK - 1)) for k in range(NK)],
                              reads=[b_wv, b_hT], writes=[pb[bi]])
                        sv, b_sv = sv_ring.next()
                        evac_copy(sv[:], bank(bi), [pb[bi]], [b_sv])
                        kb.dma("sp", DMA(vS[t0 + j * 128:t0 + (j + 1) * 128, cb * 512:(cb + 1) * 512], sv[:]), reads=[b_sv], owner=b_sv)
                ph.close()

            if "p2a" not in skip:
                ph = Phase(kb)
                bd, b_bd = ph.tile("bd", [128, 8, 2, 128], BF16)
                kb.dma("pool", DMA(bd[:], lruw_d[l]), writes=[b_bd], owner=b_bd)
                lx_ring = ph.ring("lx", 2, [128, 515], BF16)
                lg_ring = ph.ring("lg", 2, [128, 512], BF16)
                ya_ring = ph.ring("ya", 2, [128, 512], BF16)
                hs_ring = ph.ring("hs", 2, [128, 512], F32)
                xc, b_xc = ph.tile("xc", [128, 512], F32)
                rr, b_rr = ph.tile("rr", [128, 512], F32)
                ii, b_ii = ph.tile("ii", [128, 512], F32)
                aa, b_aa = ph.tile("aa", [128, 512], F32)
                a2, b_a2 = ph.tile("a2", [128, 512], F32)
                gg, b_gg = ph.tile("gg", [128, 512], F32)
                xcb, b_xcb = ph.tile("xcb", [128, 512], BF16)
                if first:
                    V("memset", [], [b_lruh], ap=lru_h[:], constant=0.0)
                for ct in range(8):
                    r0, r1 = ct * 128, (ct + 1) * 128
                    for tt in range(nTT):
                        tg = t0 + tt * 512
                        lx, b_lx = lx_ring.next()
                        lg, b_lg = lg_ring.next()
                        if tg == 0 and not has_prev:
                            V("memset", [], [b_lx], ap=lx[:, 0:3], constant=0.0)
                            kb.dma("sp", DMA(lx[:, 3:515], lxT[r0:r1, 0:512]), writes=[b_lx], owner=b_lx)
                        else:
                            kb.dma("sp", DMA(lx[:], lxT[r0:r1, tg - 3:tg + 512]), writes=[b_lx], owner=b_lx)
                        kb.dma("sp", DMA(lg[:], lgT[r0:r1, tt * 512:(tt + 1) * 512]), writes=[b_lg], owner=b_lg)
                        cwi = (l * 8 + ct) * 4
                        TS([b_lx, b_colp], [b_xc], xc[:], lx[:, 3:515], col("convw", cwi + 3), col("convb", l * 8 + ct), ALU.mult, ALU.add)
                        for k in range(3):
                            STT([b_lx, b_xc, b_colp], [b_xc], xc[:], lx[:, k:k + 512], col("convw", cwi + k), xc[:], ALU.mult, ALU.add)
                        A([b_xc], [b_xcb], xcb[:], xc[:], AF.Copy)
                        kb.op("pe", MM(bank(0), bd[:, ct, 0, :], xcb[:]), reads=[b_bd, b_xcb], writes=[pb[0]])
                        kb.op("pe", MM(bank(1), bd[:, ct, 1, :], xcb[:]), reads=[b_bd, b_xcb], writes=[pb[1]])
                        A([pb[0], b_colp], [b_rr], rr[:], bank(0), AF.Sigmoid, bias=col("ba", l * 8 + ct), scale=1.0)
                        A([pb[1], b_colp], [b_ii], ii[:], bank(1), AF.Sigmoid, bias=col("bx", l * 8 + ct), scale=1.0)
                        A([b_rr, b_nsp8], [b_aa], aa[:], rr[:], AF.Exp, scale=nsp8[:, l * 8 + ct:l * 8 + ct + 1])
                        TT([b_aa], [b_a2], a2[:], aa[:], aa[:], ALU.mult)
                        A([b_a2], [b_a2], a2[:], a2[:], AF.Sqrt, scale=-1.0, bias=1.0)
                        TT([b_ii, b_xc], [b_ii], ii[:], ii[:], xc[:], ALU.mult)
                        TT([b_ii, b_a2], [b_ii], ii[:], ii[:], a2[:], ALU.mult)
                        hs, b_hs = hs_ring.next()
                        V("tensor_tensor_scan", [b_aa, b_ii, b_lruh], [b_hs], out=hs[:], data0=aa[:], data1=ii[:], initial=lru_h[:, ct:ct + 1], op0=ALU.mult, op1=ALU.add)
                        V("tensor_copy", [b_hs], [b_lruh], out=lru_h[:, ct:ct + 1], in_=hs[:, 511:512])
                        A([b_lg], [b_gg], gg[:], lg[:], AF.Gelu_apprx_tanh)
                        ya, b_ya = ya_ring.next()
                        TT([b_hs, b_gg], [b_ya], ya[:], hs[:], gg[:], ALU.mult)
                        kb.dma("sp", DMA(yaT[r0:r1, tt * 512:(tt + 1) * 512], ya[:]), reads=[b_ya], owner=b_ya)
                ph.close()

            if "p2b" not in skip:
                ph = Phase(kb)
                NKT = (ST + 512) // 128
                qh_ring = ph.ring("qh", 2, [128, ST], BF16)
                kh_ring = ph.ring("kh", 2, [128, ST + 512], BF16)
                vh_ring = ph.ring("vh", 2, [128, NKT, 128], BF16)
                bh_ring = ph.ring("bh", 2, [128, 640], F32)
                yb_ring = ph.ring("yb", 2, [128, ST], BF16)
                sf_ring = ph.ring("sf", 2, [128, 640], F32)
                pt_ring = ph.ring("pt", 2, [128, 640], BF16)
                rc_ring = ph.ring("rc", 2, [128, 128], F32)
                it = 0
                for hd in range(8):
                    r0, r1 = hd * 128, (hd + 1) * 128
                    qh, b_qh = qh_ring.next()
                    kh, b_kh = kh_ring.next()
                    vh, b_vh = vh_ring.next()
                    bh, b_bh = bh_ring.next()
                    yb, b_yb = yb_ring.next()
                    kb.dma("sp", DMA(qh[:], qT[r0:r1, :]), writes=[b_qh], owner=b_qh)
                    kb.dma("sp", DMA(bh[:], abias_d[l, hd]), writes=[b_bh], owner=b_bh)
                    if first:
                        kb.dma("sp", DMA(kh[:, 512:], kT[r0:r1, 0:ST]), writes=[b_kh], owner=b_kh)
                        kb.dma("sp", DMA(vh[:, 4:, :], vS[0:ST, r0:r1].rearrange("(n p) d -> p n d", p=128)), writes=[b_vh], owner=b_vh)
                    else:
                        kb.dma("sp", DMA(kh[:], kT[r0:r1, t0 - 512:t0 + ST]), writes=[b_kh], owner=b_kh)
                        kb.dma("sp", DMA(vh[:], vS[t0 - 512:t0 + ST, r0:r1].rearrange("(n p) d -> p n d", p=128)), writes=[b_vh], owner=b_vh)
                    for m in range(ST // 128):
                        kts = [kt for kt in range(5) if (not first) or (m - 4 + kt) >= 0]
                        lo, hi = kts[0] * 128, 640
                        sb = (it % 2) * 2
                        po_b = 4 + (it % 2)
                        ps_b = 6 + (it % 2)
                        it += 1
                        sc = psum[:, sb * 512:sb * 512 + 640]
                        kb.op("pe", [MM(sc[:, kt * 128:(kt + 1) * 128], kh[:, (m + kt) * 128:(m + kt + 1) * 128], qh[:, m * 128:(m + 1) * 128]) for kt in kts],
                              reads=[b_kh, b_qh], writes=[pb[sb], pb[sb + 1]])
                        sf, b_sf = sf_ring.next()
                        pt, b_pt = pt_ring.next()
                        STT([pb[sb], pb[sb + 1], b_bh], [b_sf], sf[:, lo:hi], sc[:, lo:hi], float(128 ** -0.5), bh[:, lo:hi], ALU.mult, ALU.add)
                        A([b_sf], [b_pt], pt[:, lo:hi], sf[:, lo:hi], AF.Exp)
                        n = len(kts)
                        fns = [MM(bank(po_b)[:, 0:128], vh[:, m + kt, :], pt[:, kt * 128:(kt + 1) * 128], start=(i_ == 0), stop=(i_ == n - 1)) for i_, kt in enumerate(kts)]
                        fns += [MM(bank(ps_b)[:, 0:128], ones_b[:], pt[:, kt * 128:(kt + 1) * 128], start=(i_ == 0), stop=(i_ == n - 1)) for i_, kt in enumerate(kts)]
                        kb.op("pe", fns, reads=[b_vh, b_pt, b_onesb], writes=[pb[po_b], pb[ps_b]])
                        rc, b_rc = rc_ring.next()
                        V("reciprocal", [pb[ps_b]], [b_rc], out=rc[:], in_=bank(ps_b)[:, 0:128])
                        TT([pb[po_b], b_rc], [b_yb], yb[:, m * 128:(m + 1) * 128], bank(po_b)[:, 0:128], rc[:], ALU.mult)
                    kb.dma("sp", DMA(ybT[r0:r1, :], yb[:]), reads=[b_yb], owner=b_yb)
                ph.close()

            if "p2c" not in skip:
                ph = Phase(kb)
                W = 4096
                NF = 32 * 129
                E_re, b_Ere = ph.tile("E_re", [128, W], F32)
                E_im, b_Eim = ph.tile("E_im", [128, W], F32)
                F_re, b_Fre = ph.tile("F_re", [128, 32, 129], F32)
                F_im, b_Fim = ph.tile("F_im", [128, 32, 129], F32)
                bbr, b_bbr = ph.tile("bbr", [128, W], BF16)
                bbi, b_bbi = ph.tile("bbi", [128, W], BF16)
                Ffr = F_re[:].rearrange("p b j -> p (b j)")
                Ffi = F_im[:].rearrange("p b j -> p (b j)")
                pp = Phase(kb)
                ar, b_ar = pp.tile("ar", [128, W], F32)
                ai, b_ai = pp.tile("ai", [128, W], F32)
                sr, b_sr = pp.tile("sr", [128, W], F32)
                t1f, b_t1 = pp.tile("t1", [128, NF], F32)
                t2f, b_t2 = pp.tile("t2", [128, NF], F32)
                t3f, b_t3 = pp.tile("t3", [128, NF], F32)
                t1, t2, t3 = t1f[:, 0:W], t2f[:, 0:W], t3f[:, 0:W]
                bdr, b_bdr = Ffr[:, 0:W], b_Fre
                bdi, b_bdi = Ffi[:, 0:W], b_Fim
                kb.dma("sp", DMA(ar[:], rowp_d[l, 0:1, :].to_broadcast([128, W])), writes=[b_ar], owner=b_ar)
                kb.dma("sp", DMA(ai[:], rowp_d[l, 1:2, :].to_broadcast([128, W])), writes=[b_ai], owner=b_ai)
                kb.dma("sp", DMA(sr[:], rowp_d[l, 2:3, :].to_broadcast([128, W])), writes=[b_sr], owner=b_sr)
                kb.dma("sp", DMA(bdr, bbd_d[l, :, 0, :]), writes=[b_bdr], owner=b_bdr)
                kb.dma("sp", DMA(bdi, bbd_d[l, :, 1, :]), writes=[b_bdi], owner=b_bdi)

                def sincos(dst_sin, b_ds, dst_cos, b_dc, ycyc, b_y, tmp, b_tmp):
                    TS([b_y], [b_tmp], tmp, ycyc, MAGIC, MAGIC, ALU.add, ALU.subtract)
                    TT([b_y, b_tmp], [b_tmp], tmp, ycyc, tmp, ALU.subtract)
                    A([b_tmp], [b_ds], dst_sin, tmp, AF.Sin, scale=TWO_PI)
                    TS([b_y], [b_y], ycyc, ycyc, 0.25, None, ALU.add)
                    TS([b_y], [b_tmp], tmp, ycyc, MAGIC, MAGIC, ALU.add, ALU.subtract)
                    TT([b_y, b_tmp], [b_tmp], tmp, ycyc, tmp, ALU.subtract)
                    A([b_tmp], [b_dc], dst_cos, tmp, AF.Sin, scale=TWO_PI)

                MUL, ADD, SUB = ALU.mult, ALU.add, ALU.subtract
                A([b_sr], [b_sr], sr[:], sr[:], AF.Exp)
                TT([b_ar, b_sr], [b_t1], t1, ar[:], sr[:], MUL)
                A([b_t1], [b_t1], t1, t1, AF.Exp)
                TT([b_ai, b_sr], [b_t2], t2, ai[:], sr[:], MUL)
                TS([b_t2], [b_t2], t2, t2, 1.0 / TWO_PI, None, MUL)
                sincos(E_im[:], b_Eim, E_re[:], b_Ere, t2, b_t2, t3, b_t3)
                TT([b_Ere, b_t1], [b_Ere], E_re[:], E_re[:], t1, MUL)
                TT([b_Eim, b_t1], [b_Eim], E_im[:], E_im[:], t1, MUL)
                TS([b_Ere], [b_Ere], E_re[:], E_re[:], -1.0, None, ADD)
                TT([b_ar], [b_t1], t1, ar[:], ar[:], MUL)
                TT([b_ai], [b_t2], t2, ai[:], ai[:], MUL)
                TT([b_t1, b_t2], [b_t1], t1, t1, t2, ADD)
                V("reciprocal", [b_t1], [b_t1], out=t1, in_=t1)
                TT([b_Ere, b_ar], [b_t2], t2, E_re[:], ar[:], MUL)
                TT([b_Eim, b_ai], [b_t3], t3, E_im[:], ai[:], MUL)
                TT([b_t2, b_t3], [b_t2], t2, t2, t3, ADD)
                TT([b_t2, b_t1], [b_t2], t2, t2, t1, MUL)
                TT([b_Eim, b_ar], [b_t3], t3, E_im[:], ar[:], MUL)
                TT([b_Ere, b_ai], [b_Eim], E_im[:], E_re[:], ai[:], MUL)
                TT([b_t3, b_Eim], [b_t3], t3, t3, E_im[:], SUB)
                TT([b_t3, b_t1], [b_t3], t3, t3, t1, MUL)
                TT([b_t2, b_bdr], [b_Ere], E_re[:], t2, bdr, MUL)
                TT([b_t3, b_bdi], [b_Eim], E_im[:], t3, bdi, MUL)
                TT([b_Ere, b_Eim], [b_bbr], bbr[:], E_re[:], E_im[:], SUB)
                TT([b_t2, b_bdi], [b_Ere], E_re[:], t2, bdi, MUL)
                TT([b_t3, b_bdr], [b_Eim], E_im[:], t3, bdr, MUL)
                TT([b_Ere, b_Eim], [b_bbi], bbi[:], E_re[:], E_im[:], ADD)
                iocol = col("iota", 0)
                TT([b_ar, b_sr], [b_t1], t1, ar[:], sr[:], MUL)
                TS([b_t1, b_colp], [b_t1], t1, t1, iocol, -1.0, MUL, MUL)
                A([b_t1], [b_t1], t1, t1, AF.Exp)
                TT([b_ai, b_sr], [b_t2], t2, ai[:], sr[:], MUL)
                TS([b_t2, b_colp], [b_t2], t2, t2, iocol, -1.0 / TWO_PI, MUL, MUL)
                sincos(E_im[:], b_Eim, E_re[:], b_Ere, t2, b_t2, t3, b_t3)
                TT([b_Ere, b_t1], [b_Ere], E_re[:], E_re[:], t1, MUL)
                TT([b_Eim, b_t1], [b_Eim], E_im[:], E_im[:], t1, MUL)
                stc, b_stc = pp.tile("stc", [128, 32], F32)
                alc, b_alc = pp.tile("alc", [128, 32], F32)
                thc, b_thc = pp.tile("thc", [128, 32], F32)
                A([b_colp], [b_stc], stc[:], col("lsc", l * 32, 32), AF.Exp)
                TT([b_colp, b_stc], [b_alc], alc[:], col("acre", l * 32, 32), stc[:], MUL)
                TT([b_colp, b_stc], [b_thc], thc[:], col("acim", l * 32, 32), stc[:], MUL)
                TS([b_thc], [b_thc], thc[:], thc[:], 1.0 / TWO_PI, None, MUL)
                g1 = t1f[:].rearrange("p (b j) -> p b j", b=32)
                g2 = t2f[:].rearrange("p (b j) -> p b j", b=32)
                io_b = iorow[:].unsqueeze(1).to_broadcast([128, 32, 129])
                TT([b_iorow, b_alc], [b_t1], g1, io_b, alc[:].unsqueeze(2).to_broadcast([128, 32, 129]), MUL)
                A([b_t1], [b_t1], t1f[:], t1f[:], AF.Exp)
                TT([b_iorow, b_thc], [b_t2], g2, io_b, thc[:].unsqueeze(2).to_broadcast([128, 32, 129]), MUL)
                sincos(Ffi, b_Fim, Ffr, b_Fre, t2f[:], b_t2, t3f[:], b_t3)
                TT([b_Fre, b_t1], [b_Fre], Ffr, Ffr, t1f[:], MUL)
                TT([b_Fim, b_t1], [b_Fim], Ffi, Ffi, t1f[:], MUL)
                pp.close()
                cmat, b_cmat = ph.tile("cmat", [128, 8192], BF16)
                kb.dma("pool", [DMA(cmat[:, q * 2048:(q + 1) * 2048], ct_d[l, :, q * 2048:(q + 1) * 2048]) for q in range(4)], writes=[b_cmat], owner=b_cmat)
                u_all, b_u = ph.tile("u_all", [128, 8, ST], BF16)
                kb.dma("sp", DMA(u_all[:], uT.rearrange("(c p) t -> p c t", p=128)), writes=[b_u], owner=b_u)
                w_ring = ph.ring("wre", 2, [128, 2, 512], BF16)
                x_ring = ph.ring("xre", 2, [128, 2, 512], BF16)
                m_ring = ph.ring("mm", 2, [128, 2, 512], F32)
                cp_ring = ph.ring("cp", 2, [128, 2, 512], F32)
                yc_ring = ph.ring("yc", 4, [128, 128], BF16)
                ty_ring = ph.ring("ty", 2, [128, 128], F32)
                tn_ring = ph.ring("tn", 2, [128, 4, 4], F32)
                if first:
                    V("memset", [], [b_s5ca], ap=s5ca[:], constant=0.0)
                    V("memset", [], [b_s5cb], ap=s5cb[:], constant=0.0)
                it = 0
                for j in range(ST // 128):
                    for ct in range(8):
                        par = it % 2
                        it += 1
                        bu_r, bu_i = par * 2, par * 2 + 1
                        cs_r, cs_i = 4, 5
                        y_b = 6 + par
                        c0, c1 = ct * 512, (ct + 1) * 512
                        ul = u_all[:, ct, j * 128:(j + 1) * 128]
                        kb.op("pe", MM(bank(bu_r), ul, bbr[:, c0:c1]), reads=[b_u, b_bbr], writes=[pb[bu_r]])
                        kb.op("pe", MM(bank(bu_i), ul, bbi[:, c0:c1]), reads=[b_u, b_bbi], writes=[pb[bu_i]])
                        mm, b_mm = m_ring.next()
                        ww, b_ww = w_ring.next()
                        TT([pb[bu_r], b_Ere], [b_mm], mm[:, 0, :], bank(bu_r), E_re[:, c0:c1], MUL)
                        TT([pb[bu_i], b_Eim], [b_mm], mm[:, 1, :], bank(bu_i), E_im[:, c0:c1], MUL)
                        TT([b_mm], [b_ww], ww[:, 0, :], mm[:, 0, :], mm[:, 1, :], SUB)
                        TT([pb[bu_i], b_Ere], [b_mm], mm[:, 0, :], bank(bu_i), E_re[:, c0:c1], MUL)
                        TT([pb[bu_r], b_Eim], [b_mm], mm[:, 1, :], bank(bu_r), E_im[:, c0:c1], MUL)
                        TT([b_mm], [b_ww], ww[:, 1, :], mm[:, 0, :], mm[:, 1, :], ADD)
                        fns = []
                        for ri, bk in ((0, cs_r), (1, cs_i)):
                            for blk in range(4):
                                fns.append(MM(bank(bk)[:, blk * 128:(blk + 1) * 128], ww[:, ri, blk * 128:(blk + 1) * 128], tri_b[:]))
                        kb.op("pe", fns, reads=[b_ww, b_trib], writes=[pb[cs_r], pb[cs_i]])
                        cp, b_cp = cp_ring.next()
                        s0, s1 = ct * 4, ct * 4 + 4
                        TT([pb[cs_r], b_s5ca], [b_cp], v3(cp[:, 0, :]), v3(bank(cs_r)), s5ca[:, s0:s1].unsqueeze(2).to_broadcast([128, 4, 128]), ADD)
                        TT([pb[cs_i], b_s5cb], [b_cp], v3(cp[:, 1, :]), v3(bank(cs_i)), s5cb[:, s0:s1].unsqueeze(2).to_broadcast([128, 4, 128]), ADD)
                        Fr = F_re[:, s0:s1, 0:128]
                        Fi = F_im[:, s0:s1, 0:128]
                        mm2, b_mm2 = m_ring.next()
                        xx, b_xx = x_ring.next()
                        TT([b_cp, b_Fre], [b_mm2], v3(mm2[:, 0, :]), v3(cp[:, 0, :]), Fr, MUL)
                        TT([b_cp, b_Fim], [b_mm2], v3(mm2[:, 1, :]), v3(cp[:, 1, :]), Fi, MUL)
                        TT([b_mm2], [b_xx], xx[:, 0, :], mm2[:, 0, :], mm2[:, 1, :], SUB)
                        TT([b_cp, b_Fre], [b_mm2], v3(mm2[:, 0, :]), v3(cp[:, 1, :]), Fr, MUL)
                        TT([b_cp, b_Fim], [b_mm2], v3(mm2[:, 1, :]), v3(cp[:, 0, :]), Fi, MUL)
                        STT([b_mm2], [b_xx], xx[:, 1, :], mm2[:, 0, :], -1.0, mm2[:, 1, :], MUL, SUB)
                        tn, b_tn = tn_ring.next()
                        cr = v3(cp[:, 0, :])[:, :, 127]
                        ci = v3(cp[:, 1, :])[:, :, 127]
                        Gr = F_re[:, s0:s1, 128]
                        Gi = F_im[:, s0:s1, 128]
                        TT([b_cp, b_Fre], [b_tn], tn[:, 0, :], cr, Gr, MUL)
                        TT([b_cp, b_Fim], [b_tn], tn[:, 1, :], ci, Gi, MUL)
                        TT([b_cp, b_Fre], [b_tn], tn[:, 2, :], ci, Gr, MUL)
                        TT([b_cp, b_Fim], [b_tn], tn[:, 3, :], cr, Gi, MUL)
                        TT([b_tn], [b_s5ca], s5ca[:, s0:s1], tn[:, 0, :], tn[:, 1, :], SUB)
                        TT([b_tn], [b_s5cb], s5cb[:, s0:s1], tn[:, 2, :], tn[:, 3, :], ADD)
                        fns = []
                        for blk in range(4):
                            for ri in range(2):
                                o = ((ct * 4 + blk) * 2 + ri) * 128
                                fns.append(MM(bank(y_b)[:, 0:128], cmat[:, o:o + 128], xx[:, ri, blk * 128:(blk + 1) * 128],
                                              start=(blk == 0 and ri == 0), stop=(blk == 3 and ri == 1)))
                        kb.op("pe", fns, reads=[b_cmat, b_xx], writes=[pb[y_b]])
                        ty, b_ty = ty_ring.next()
                        yc, b_yc = yc_ring.next()
                        STT([b_u, pb[y_b], b_colp], [b_ty], ty[:], ul, col("ssmd", l * 8 + ct), bank(y_b)[:, 0:128], MUL, ADD)
                        A([b_ty], [b_yc], yc[:], ty[:], AF.Gelu_apprx_tanh)
                        kb.dma("sp", DMA(ycT[ct * 128:(ct + 1) * 128, j * 128:(j + 1) * 128], yc[:]), reads=[b_yc], owner=b_yc)
                ph.close()

            if "p3" not in skip:
                ph = Phase(kb)
                ys = []
                for nm, src in (("ya", yaT), ("yb", ybT), ("yc", ycT)):
                    t, b = ph.tile(nm, [128, 8, ST], BF16)
                    kb.dma("sp", DMA(t[:], src.rearrange("(c p) t -> p c t", p=128)), writes=[b], owner=b)
                    ys.append((t, b))
                ys.append(ys[2])
                wb_ring = ph.ring("wb", 2, [128, 4, 8, 128], BF16)
                g_ring = ph.ring("gg", 2, [128, 3, 512], BF16)
                ms_ring = ph.ring("ms", 2, [128, ST], BF16)
                tq_ring = ph.ring("tq", 2, [128, 4, 512], F32)
                gT_v = gT.rearrange("(b f p) t -> f p b t", b=3, f=16)
                it = 0
                for f in range(16):
                    wb, b_wb = wb_ring.next()
                    kb.dma("pool", [DMA(wb[:, q, :, :], wbr_t[l, f, :, q, :, :]) for q in range(4)], writes=[b_wb], owner=b_wb)
                    ms, b_ms = ms_ring.next()
                    for tt in range(nTT):
                        ts0, ts1 = tt * 512, (tt + 1) * 512
                        g3, b_g3 = g_ring.next()
                        kb.dma("sp", DMA(g3[:], gT_v[f][:, :, ts0:ts1]), writes=[b_g3], owner=b_g3)
                        b0 = (it % 2) * 4
                        it += 1
                        for q in range(4):
                            yt, b_yt = ys[q]
                            kb.op("pe", [MM(bank(b0 + q), wb[:, q, k, :], yt[:, k, ts0:ts1], start=(k == 0), stop=(k == 7)) for k in range(8)],
                                  reads=[b_wb, b_yt], writes=[pb[b0 + q]])
                        tq, b_tq = tq_ring.next()
                        A([pb[b0 + 3]], [b_tq], tq[:, 3, :], bank(b0 + 3), AF.Sigmoid)
                        TT([pb[b0], b_g3], [b_tq], tq[:, 0, :], bank(b0 + 0), g3[:, 0, :], ALU.mult)
                        TT([pb[b0 + 1], b_g3], [b_tq], tq[:, 1, :], bank(b0 + 1), g3[:, 1, :], ALU.mult)
                        TT([pb[b0 + 2], b_tq], [b_tq], tq[:, 2, :], bank(b0 + 2), tq[:, 3, :], ALU.mult)
                        TT([b_tq, b_g3], [b_tq], tq[:, 2, :], tq[:, 2, :], g3[:, 2, :], ALU.mult)
                        TT([b_tq], [b_tq], tq[:, 0, :], tq[:, 0, :], tq[:, 1, :], ALU.add)
                        TT([b_tq], [b_ms], ms[:, ts0:ts1], tq[:, 0, :], tq[:, 2, :], ALU.add)
                    kb.dma("sp", DMA(mT[f * 128:(f + 1) * 128, :], ms[:]), reads=[b_ms], owner=b_ms)
                ph.close()
                proj_residual(l, t0, mT, 16, wout_t[l], nTT)

            if "p4" not in skip:
                ph = Phase(kb)
                hT, b_hT = ph.tile("hT", [128, NK, ST], BF16)
                xin, b_xin = ph.tile("xin", [128, NK, 512], F32)
                sq_ring = ph.ring("sq", 2, [128, 512], F32)
                rs_ring = ph.ring("rs", 2, [128, 512], F32)
                rms_norm_to(hT, b_hT, "gffn", l, t0, xin, b_xin, sq_ring, rs_ring, 7)
                wg_ring = ph.ring("wg", 2, [128, 2, NK, 128], BF16)
                as_ring = ph.ring("as", 2, [128, ST], BF16)
                sl_ring = ph.ring("sl", 2, [128, 512], F32)
                it = 0
                for jj in range(NJ):
                    wg, b_wg = wg_ring.next()
                    kb.dma("pool", [DMA(wg[:, q, :, :], wgu_t[l, jj, :, q, :, :]) for q in range(2)], writes=[b_wg], owner=b_wg)
                    ast, b_ast = as_ring.next()
                    for tt in range(nTT):
                        ts0, ts1 = tt * 512, (tt + 1) * 512
                        bg = (it % 3) * 2
                        it += 1
                        for q in range(2):
                            kb.op("pe", [MM(bank(bg + q), wg[:, q, k, :], hT[:, k, ts0:ts1], start=(k == 0), stop=(k == NK - 1)) for k in range(NK)],
                                  reads=[b_wg, b_hT], writes=[pb[bg + q]])
                        sl, b_sl = sl_ring.next()
                        A([pb[bg]], [b_sl], sl[:], bank(bg), AF.Silu)
                        TT([pb[bg + 1], b_sl], [b_ast], ast[:, ts0:ts1], bank(bg + 1), sl[:], ALU.mult)
                    kb.dma("sp", DMA(actT[jj * 128:(jj + 1) * 128, :], ast[:]), reads=[b_ast], owner=b_ast)
                ph.close()
                proj_residual(l, t0, actT, NJ, wdn_t[l], min(2, nTT))

    ph = Phase(kb)
    xin_ring = ph.ring("xin", 2, [128, NK, 512], F32)
    sq_ring = ph.ring("sq", 2, [128, 512], F32)
    rs_ring = ph.ring("rs", 2, [128, 512], F32)
    outv = outT.rearrange("(k p) t -> p k t", p=128)
    for tt in range(S // 512):
        tg = tt * 512
        xin, b_xin = xin_ring.next()
        rs, b_rs = rms_norm_tile(xin, b_xin, tg, sq_ring, rs_ring, 7)
        for k in range(NK):
            STT([b_xin, b_rs, b_colp], [b_xin], xin[:, k, :], xin[:, k, :], col("gfin", k), rs[:], ALU.mult, ALU.mult)
        kb.dma("sp", DMA(outv[:, :, tg:tg + 512], xin[:]), reads=[b_xin], owner=b_xin)
    ph.close()
    kb.finalize()
    return nc


def prep_weights(inp, L):
    f = np.float32
    out = {}
    w_in = inp["w_in"][:L]
    out["w_in_t"] = np.ascontiguousarray(w_in.reshape(L, 16, 128, 96, 128).transpose(0, 3, 2, 1, 4))
    wb = np.concatenate([inp["w_branch"][:L], inp["ssm_w_glu"][:L][:, None]], axis=1)
    out["wbr_t"] = np.ascontiguousarray(wb.reshape(L, 4, 8, 128, 16, 128).transpose(0, 4, 3, 1, 2, 5))
    out["wout_t"] = np.ascontiguousarray(inp["w_out"][:L].reshape(L, 16, 128, 16, 128).transpose(0, 3, 2, 1, 4))
    wgu = np.stack([inp["w_ffn_gate"][:L], inp["w_ffn_up"][:L]], axis=1)
    out["wgu_t"] = np.ascontiguousarray(wgu.reshape(L, 2, 16, 128, NJ, 128).transpose(0, 4, 3, 1, 2, 5))
    out["wdn_t"] = np.ascontiguousarray(inp["w_ffn_down"][:L].reshape(L, NJ, 128, 16, 128).transpose(0, 3, 2, 1, 4))
    COFF, NCOL = col_layout(L)
    colp = np.zeros((128, NCOL), f)

    def put(name, arr):
        colp[:, COFF[name]:COFF[name] + arr.shape[1]] = arr
    put("gmix", inp["norm_mix_g"][:L].reshape(L, 16, 128).transpose(2, 0, 1).reshape(128, -1))
    put("gffn", inp["norm_ffn_g"][:L].reshape(L, 16, 128).transpose(2, 0, 1).reshape(128, -1))
    put("gfin", inp["norm_final_g"].reshape(16, 128).T)
    put("gbias", inp["gate_bias"][:L].reshape(L, 3, 16, 128).transpose(3, 0, 1, 2).reshape(128, -1))
    put("convw", inp["lru_conv_w"][:L].reshape(L, 4, 8, 128).transpose(3, 0, 2, 1).reshape(128, -1))
    for nm, key in (("convb", "lru_conv_b"), ("ba", "lru_ba"), ("bx", "lru_bx"), ("lam", "lru_lambda"), ("ssmd", "ssm_d")):
        put(nm, inp[key][:L].reshape(L, 8, 128).transpose(2, 0, 1).reshape(128, -1))
    for nm, arr in (("acre", inp["ssm_a_re"][:L]), ("acim", inp["ssm_a_im"][:L]),
                    ("lsc", np.repeat(inp["ssm_log_step"][:L][:, :, None], 64, axis=2))):
        put(nm, arr.reshape(L, 32, 2, 64).transpose(2, 3, 0, 1).reshape(128, -1))
    colp[:, COFF["iota"]] = np.arange(128, dtype=f)
    out["colp"] = colp
    rowp = np.stack([inp["ssm_a_re"][:L].reshape(L, 4096), inp["ssm_a_im"][:L].reshape(L, 4096),
                     np.repeat(inp["ssm_log_step"][:L][:, :, None], 64, axis=2).reshape(L, 4096)], axis=1)
    out["rowp"] = np.ascontiguousarray(rowp.astype(f))
    lruw = np.zeros((L, 128, 8, 2, 128), f)
    for wi, key in enumerate(("lru_wa", "lru_wx")):
        w = inp[key][:L].reshape(L, 8, 2, 64, 64)
        for nl in range(2):
            lruw[:, nl * 64:(nl + 1) * 64, :, wi, nl * 64:(nl + 1) * 64] = w[:, :, nl].transpose(0, 2, 1, 3)
    out["lruw"] = lruw
    kk = np.arange(128)[:, None, None]
    kt = np.arange(5)[None, :, None]
    qq = np.arange(128)[None, None, :]
    dist = (4 - kt) * 128 + qq - kk
    rel = np.clip(dist, -128, 128) + 128
    cdiff = 8 - 2 * kt + qq // 64 - kk // 64
    valid = (cdiff >= 0) & (cdiff <= 8)
    ab = inp["attn_rel_bias"][:L][:, :, rel]
    ab = np.where(valid[None, None], ab, f(-30000.0)).astype(f)
    out["abias"] = np.ascontiguousarray(ab.reshape(L, 8, 128, 640))
    bbd = np.zeros((L, 128, 2, 8, 8, 64), f)
    for ri, key in enumerate(("ssm_b_re", "ssm_b_im")):
        B = inp[key][:L].reshape(L, 8, 8, 64, 16)
        for gl in range(8):
            bbd[:, gl * 16:(gl + 1) * 16, ri, :, gl, :] = B[:, :, gl].transpose(0, 3, 1, 2)
    out["bbd"] = np.ascontiguousarray(bbd.reshape(L, 128, 2, 4096))
    ctd = np.zeros((L, 128, 8, 4, 2, 128), f)
    for ri, key in enumerate(("ssm_c_re", "ssm_c_im")):
        C = inp[key][:L].reshape(L, 8, 4, 2, 16, 64)
        for blk in range(4):
            for gl2 in range(2):
                c0 = (2 * blk + gl2) * 16
                ctd[:, gl2 * 64:(gl2 + 1) * 64, :, blk, ri, c0:c0 + 16] = C[:, :, blk, gl2].transpose(0, 3, 1, 2)
    out["ctd"] = np.ascontiguousarray(ctd.reshape(L, 128, 8192))
    cst = np.zeros((128, 386), f)
    cst[:, 0:128] = np.triu(np.ones((128, 128), f))
    cst[:, 128:256] = 1.0
    cst[:, 256:385] = np.arange(129, dtype=f)[None, :]
    cst[:, 385] = 1e-6
    out["cst"] = cst
    out["epsd"] = np.full((128, 1), 1e-6, f)
    return out


_CACHE = {}


def run_model(inputs, L, S_core, ST, n_cores, dbg=False, skip=()):
    x = np.asarray(inputs["x"], np.float32)
    B, S, _ = x.shape
    assert S == S_core and B <= n_cores
    key = (L, S_core, ST, dbg, tuple(skip))
    if key not in _CACHE:
        _CACHE[key] = build_program(L, S_core, ST, dbg=dbg, skip=skip)
    nc = _CACHE[key]
    wts = prep_weights({k: np.asarray(v, np.float32) for k, v in inputs.items() if k != "x"}, L)
    in_maps = []
    for c in range(n_cores):
        b = c % B
        m = dict(wts)
        m["xT"] = np.ascontiguousarray(x[b].T)
        in_maps.append(m)
    res = run_bass_kernel_spmd(nc, in_maps, core_ids=list(range(n_cores)))
    out = np.stack([np.ascontiguousarray(res.results[b]["outT"].T) for b in range(B)], axis=0)
    return out.astype(np.float32), res


def kernel(**inputs):
    out, _ = run_model(inputs, 4, 4096, 2048, 4)
    return out
```

```python
import math
from contextlib import ExitStack
import numpy as np
import concourse.bass as bass
import concourse.mybir as mybir
from concourse.bass_utils import run_bass_kernel_spmd

F32 = mybir.dt.float32
BF16 = mybir.dt.bfloat16
AF = mybir.ActivationFunctionType
ALU = mybir.AluOpType

D = 2048
NK = 16
MIXW = 1024
FH = 5632
NJ = 44
TWO_PI = float(2 * math.pi)
MAGIC = 12582912.0


class Buf:
    __slots__ = ("name", "w", "r", "dsem")

    def __init__(self, name):
        self.name = name
        self.w = None
        self.r = {}
        self.dsem = None


class Eng:
    def __init__(self, name, sem):
        self.name = name
        self.sem = sem
        self.cnt = 0
        self.waited = {}
        self.prog = []


class KB:
    def __init__(self, nc):
        self.nc = nc
        self.stack = ExitStack()
        self.sems = {}
        self.semcnt = {}
        self.free_dsems = []
        self.dirty = {}
        self.eng = {}
        for name in ("pe", "act", "dve", "pool", "sp"):
            self.eng[name] = Eng(name, self.newsem("e_" + name))
        self.uid = 0

    def newsem(self, name):
        h = self.stack.enter_context(self.nc.semaphore(name))
        key = len(self.sems)
        self.sems[key] = h
        self.semcnt[key] = 0
        return key

    def buf(self, name="b"):
        self.uid += 1
        return Buf(f"{name}_{self.uid}")

    def _deps(self, reads, writes):
        deps = {}
        for b in reads:
            if b.w is not None and deps.get(b.w[0], 0) < b.w[1]:
                deps[b.w[0]] = b.w[1]
        for b in writes:
            if b.w is not None and deps.get(b.w[0], 0) < b.w[1]:
                deps[b.w[0]] = b.w[1]
            for k, v in b.r.items():
                if deps.get(k, 0) < v:
                    deps[k] = v
        return deps

    def _emit_waits(self, e, deps, skip=None):
        for k, v in deps.items():
            if k == skip:
                continue
            if e.waited.get(k, 0) < v:
                e.prog.append(("w", k, v))
                e.waited[k] = v

    def _update(self, tok, reads, writes):
        k, v = tok
        for b in reads:
            if b.r.get(k, 0) < v:
                b.r[k] = v
        for b in writes:
            b.w = tok
            b.r = {}

    def op(self, en, fns, reads=(), writes=()):
        e = self.eng[en]
        if callable(fns):
            fns = [fns]
        deps = self._deps(reads, writes)
        self._emit_waits(e, deps, skip=e.sem if en == "pe" else None)
        for f in fns[:-1]:
            e.prog.append(("i", f, None, 0))
        e.cnt += 1
        e.prog.append(("i", fns[-1], e.sem, 1))
        tok = (e.sem, e.cnt)
        self._update(tok, reads, writes)
        return tok

    def dma(self, en, fns, reads=(), writes=(), owner=None):
        e = self.eng[en]
        if callable(fns):
            fns = [fns]
        if owner.dsem is None:
            owner.dsem = self.free_dsems.pop() if self.free_dsems else self.newsem("d%d" % len(self.sems))
        k = owner.dsem
        deps = self._deps(reads, writes)
        self._emit_waits(e, deps)
        for f in fns:
            self.semcnt[k] += 16
            e.prog.append(("i", f, k, 16))
        tok = (k, self.semcnt[k])
        self.dirty[k] = self.semcnt[k]
        self._update(tok, reads, writes)
        return tok

    def barrier(self):
        toks = dict(self.dirty)
        for e in self.eng.values():
            if e.cnt > 0:
                toks[e.sem] = e.cnt
        for e in self.eng.values():
            self._emit_waits(e, toks)
        self.dirty = {}

    def release(self, bufs):
        for b in bufs:
            if b.dsem is not None:
                self.free_dsems.append(b.dsem)
                b.dsem = None

    def finalize(self):
        nc = self.nc
        self.barrier()
        engs, sems = self.eng, self.sems

        def replay(e, h):
            for it in e.prog:
                if it[0] == "w":
                    h.wait_ge(sems[it[1]], it[2])
                else:
                    inst = it[1](h)
                    if it[2] is not None:
                        inst.then_inc(sems[it[2]], it[3])

        with nc.Block() as block:
            @block.tensor
            def _(h):
                replay(engs["pe"], h)

            @block.scalar
            def _(h):
                replay(engs["act"], h)

            @block.vector
            def _(h):
                replay(engs["dve"], h)

            @block.gpsimd
            def _(h):
                replay(engs["pool"], h)

            @block.sync
            def _(h):
                replay(engs["sp"], h)
        self.stack.close()


class Phase:
    def __init__(self, kb):
        self.kb = kb
        self.st = ExitStack()
        self.bufs = []

    def tile(self, name, shape, dtype):
        kb = self.kb
        kb.uid += 1
        t = self.st.enter_context(kb.nc.sbuf_tensor(f"{name}_{kb.uid}", list(shape), dtype))
        b = kb.buf(name)
        self.bufs.append(b)
        return t, b

    def ring(self, name, n, shape, dtype):
        return Ring([self.tile(name, shape, dtype) for _ in range(n)])

    def close(self):
        self.kb.barrier()
        self.kb.release(self.bufs)
        self.st.close()


class Ring:
    def __init__(self, items):
        self.items = items
        self.i = 0

    def next(self):
        it = self.items[self.i % len(self.items)]
        self.i += 1
        return it


def col_layout(L):
    segs = [("gmix", L * 16), ("gffn", L * 16), ("gfin", 16), ("gbias", L * 48), ("convw", L * 32),
            ("convb", L * 8), ("ba", L * 8), ("bx", L * 8), ("lam", L * 8), ("ssmd", L * 8),
            ("acre", L * 32), ("acim", L * 32), ("lsc", L * 32), ("iota", 1)]
    off, o = {}, 0
    for n, w in segs:
        off[n] = o
        o += w
    return off, o


def build_program(L, S, ST, dbg=False, has_prev=False, skip=()):
    nc = bass.Bass("TRN2", target_bir_lowering=False)
    kb = KB(nc)
    gst = kb.stack
    nST = S // ST
    nTT = ST // 512
    COFF, NCOL = col_layout(L)
    okind = "ExternalOutput" if dbg else "Internal"

    def din(name, shape, dt=F32):
        return nc.dram_tensor(name, list(shape), dt, kind="ExternalInput").ap()

    def dscr(name, shape, dt):
        return nc.dram_tensor(name, list(shape), dt, kind=okind).ap()

    xT = din("xT", [D, S])
    w_in_t = din("w_in_t", [L, 96, 128, 16, 128])
    wbr_t = din("wbr_t", [L, 16, 128, 4, 8, 128])
    wout_t = din("wout_t", [L, 16, 128, 16, 128])
    wgu_t = din("wgu_t", [L, NJ, 128, 2, 16, 128])
    wdn_t = din("wdn_t", [L, 16, 128, NJ, 128])
    colp_d = din("colp", [128, NCOL])
    rowp_d = din("rowp", [L, 3, 4096])
    lruw_d = din("lruw", [L, 128, 8, 2, 128])
    abias_d = din("abias", [L, 8, 128, 640])
    bbd_d = din("bbd", [L, 128, 2, 4096])
    ct_d = din("ctd", [L, 128, 8192])
    cst_d = din("cst", [128, 386])
    eps_d = din("epsd", [128, 1])
    outT = nc.dram_tensor("outT", [D, S], F32, kind="ExternalOutput").ap()

    xr = dscr("xr", [D, S], F32)
    lxT = dscr("lxT", [MIXW, S], BF16)
    lgT = dscr("lgT", [MIXW, ST], BF16)
    qT = dscr("qT", [MIXW, ST], BF16)
    kT = dscr("kT", [MIXW, S], BF16)
    vS = dscr("vS", [S, MIXW], BF16)
    uT = dscr("uT", [MIXW, ST], BF16)
    gT = dscr("gT", [3 * D, ST], BF16)
    yaT = dscr("yaT", [MIXW, ST], BF16)
    ybT = dscr("ybT", [MIXW, ST], BF16)
    ycT = dscr("ycT", [MIXW, ST], BF16)
    mT = dscr("mT", [D, ST], BF16)
    actT = dscr("actT", [FH, ST], BF16)

    def gtile(name, shape, dt):
        t = gst.enter_context(nc.sbuf_tensor(name, list(shape), dt))
        return t, kb.buf(name)

    colp, b_colp = gtile("colp_s", [128, NCOL], F32)
    nsp8, b_nsp8 = gtile("nsp8", [128, L * 8], F32)
    ones_f, b_onesf = gtile("ones_f", [128, 128], F32)
    ones_b, b_onesb = gtile("ones_b", [128, 128], BF16)
    tri_b, b_trib = gtile("tri_b", [128, 128], BF16)
    ntri_b, b_ntrib = gtile("ntri_b", [128, 128], BF16)
    iorow, b_iorow = gtile("iorow", [128, 129], F32)
    lru_h, b_lruh = gtile("lru_h", [128, 8], F32)
    s5ca, b_s5ca = gtile("s5ca", [128, 32], F32)
    s5cb, b_s5cb = gtile("s5cb", [128, 32], F32)
    epsc, b_epsc = gtile("epsc", [128, 1], F32)
    psum = gst.enter_context(nc.psum_tensor("psum", [128, 4096], F32))
    pb = [kb.buf(f"ps{i}") for i in range(8)]

    def bank(i):
        return psum[:, i * 512:(i + 1) * 512]

    def col(name, idx, n=1):
        o = COFF[name] + idx
        return colp[:, o:o + n]

    def I(name, **kw):
        return lambda h: getattr(h, name)(**kw)

    def MM(out, lhsT, rhs, start=True, stop=True):
        return I("matmul", out=out, lhsT=lhsT, rhs=rhs, start=start, stop=stop)

    def DMA(out, in_):
        return I("dma_start", out=out, in_=in_)

    def V(name, r, w, **kw):
        return kb.op("dve", I(name, **kw), reads=r, writes=w)

    def A(r, w, out, in_, func, **kw):
        return kb.op("act", I("activation", out=out, in_=in_, func=func, **kw), reads=r, writes=w)

    def TT(r, w, out, in0, in1, op):
        return V("tensor_tensor", r, w, out=out, in0=in0, in1=in1, op=op)

    def TS(r, w, out, in0, s1, s2, op0, op1=None):
        if op1 is None:
            return V("tensor_scalar", r, w, out=out, in0=in0, scalar1=s1, scalar2=None, op0=op0)
        return V("tensor_scalar", r, w, out=out, in0=in0, scalar1=s1, scalar2=s2, op0=op0, op1=op1)

    def STT(r, w, out, in0, scalar, in1, op0, op1):
        return V("scalar_tensor_tensor", r, w, out=out, in0=in0, scalar=scalar, in1=in1, op0=op0, op1=op1)

    kb.dma("sp", DMA(colp[:], colp_d), writes=[b_colp], owner=b_colp)
    kb.dma("sp", DMA(ones_f[:], cst_d[:, 128:256]), writes=[b_onesf], owner=b_onesf)
    kb.dma("sp", DMA(iorow[:], cst_d[:, 256:385]), writes=[b_iorow], owner=b_iorow)
    kb.dma("pool", DMA(tri_b[:], cst_d[:, 0:128]), writes=[b_trib], owner=b_trib)
    kb.dma("pool", DMA(ones_b[:], cst_d[:, 128:256]), writes=[b_onesb], owner=b_onesb)
    TS([b_trib], [b_ntrib], ntri_b[:], tri_b[:], -1.0, None, ALU.mult)
    kb.dma("sp", DMA(epsc[:], eps_d), writes=[b_epsc], owner=b_epsc)
    A([b_colp], [b_nsp8], nsp8[:], col("lam", 0, L * 8), AF.Exp, scale=-1.0)
    A([b_nsp8], [b_nsp8], nsp8[:], nsp8[:], AF.Ln, bias=1.0, scale=1.0)
    TS([b_nsp8], [b_nsp8], nsp8[:], nsp8[:], -8.0, None, ALU.mult)
    b_x0 = kb.buf("x0")
    kb.dma("sp", [DMA(xr[c * 128:(c + 1) * 128, :], xT[c * 128:(c + 1) * 128, :]) for c in range(16)], writes=[b_x0], owner=b_x0)
    kb.barrier()

    xr_v = xr.rearrange("(k p) t -> p k t", p=128)

    def rms_norm_tile(xin, b_xin, tg, sq_ring, rs_ring, pbi):
        kb.dma("sp", DMA(xin[:], xr_v[:, :, tg:tg + 512]), writes=[b_xin], owner=b_xin)
        for k in range(NK):
            sq, b_sq = sq_ring.next()
            A([b_xin], [b_sq], sq[:], xin[:, k, :], AF.Square)
            kb.op("pe", MM(bank(pbi), ones_f[:], sq[:], start=(k == 0), stop=(k == NK - 1)), reads=[b_sq, b_onesf], writes=[pb[pbi]])
        rs, b_rs = rs_ring.next()
        A([pb[pbi], b_epsc], [b_rs], rs[:], bank(pbi), AF.Sqrt, scale=1.0 / D, bias=epsc[:])
        V("reciprocal", [b_rs], [b_rs], out=rs[:], in_=rs[:])
        return rs, b_rs

    def rms_norm_to(hT, b_hT, gname, l, t0, xin, b_xin, sq_ring, rs_ring, pbi):
        for tt in range(nTT):
            rs, b_rs = rms_norm_tile(xin, b_xin, t0 + tt * 512, sq_ring, rs_ring, pbi)
            for k in range(NK):
                STT([b_xin, b_rs, b_colp], [b_hT], hT[:, k, tt * 512:(tt + 1) * 512], xin[:, k, :], col(gname, l * 16 + k), rs[:], ALU.mult, ALU.mult)

    evac_flip = [0]

    def evac_copy(out_ap, in_ap, reads, writes):
        evac_flip[0] ^= 1
        if evac_flip[0]:
            A(reads, writes, out_ap, in_ap, AF.Copy)
        else:
            V("tensor_copy", reads, writes, out=out_ap, in_=in_ap)

    def proj_residual(l, t0, src_dram, nk, w_dram_l, chunk_tt):
        srcv = src_dram.rearrange("(k p) t -> p k t", p=128)
        splits = [(a, min(a + 16, nk)) for a in range(0, nk, 16)]
        for c0 in range(0, nTT, chunk_tt):
            ph = Phase(kb)
            ncols = chunk_tt * 512
            sm, b_sm = ph.tile("sm", [128, nk, ncols], BF16)
            kb.dma("sp", [DMA(sm[:, a:b, :], srcv[:, a:b, c0 * 512:c0 * 512 + ncols]) for a, b in splits], writes=[b_sm], owner=b_sm)
            wo_ring = ph.ring("wo", 2, [128, nk, 128], BF16)
            xs_ring = ph.ring("xs", 3, [128, 512], F32)
            it = 0
            for f in range(16):
                wo, b_wo = wo_ring.next()
                kb.dma("pool", [DMA(wo[:, a:b, :], w_dram_l[f, :, a:b, :]) for a, b in splits], writes=[b_wo], owner=b_wo)
                for tt in range(chunk_tt):
                    bi = it % 6
                    it += 1
                    tg = t0 + (c0 + tt) * 512
                    xs, b_xs = xs_ring.next()
                    kb.dma("sp", DMA(xs[:], xr[f * 128:(f + 1) * 128, tg:tg + 512]), writes=[b_xs], owner=b_xs)
                    kb.op("pe", [MM(bank(bi), wo[:, k, :], sm[:, k, tt * 512:(tt + 1) * 512], start=(k == 0), stop=(k == nk - 1)) for k in range(nk)],
                          reads=[b_wo, b_sm], writes=[pb[bi]])
                    TT([pb[bi], b_xs], [b_xs], xs[:], bank(bi), xs[:], ALU.add)
                    kb.dma("sp", DMA(xr[f * 128:(f + 1) * 128, tg:tg + 512], xs[:]), reads=[b_xs], owner=b_xs)
            ph.close()

    def v3(ap):
        return ap.rearrange("p (b j) -> p b j", b=4)

    for l in range(L):
        for s in range(nST):
            t0 = s * ST
            first = (s == 0) and not has_prev
            if "p1" not in skip:
                ph = Phase(kb)
                hT, b_hT = ph.tile("hT", [128, NK, ST], BF16)
                xin, b_xin = ph.tile("xin", [128, NK, 512], F32)
                sq_ring = ph.ring("sq", 2, [128, 512], F32)
                rs_ring = ph.ring("rs", 2, [128, 512], F32)
                wt_ring = ph.ring("wt", 3, [128, NK, 128], BF16)
                stg_ring = ph.ring("stg", 3, [128, ST], BF16)
                wv_ring = ph.ring("wv", 1, [128, NK, 512], BF16)
                sv_ring = ph.ring("sv", 3, [128, 512], BF16)
                rms_norm_to(hT, b_hT, "gmix", l, t0, xin, b_xin, sq_ring, rs_ring, 7)
                pbr = 0
                for m in list(range(0, 32)) + list(range(40, 96)):
                    wt, b_wt = wt_ring.next()
                    kb.dma("pool", DMA(wt[:], w_in_t[l, m]), writes=[b_wt], owner=b_wt)
                    stg, b_stg = stg_ring.next()
                    for tt in range(nTT):
                        bi = pbr % 6
                        pbr += 1
                        kb.op("pe", [MM(bank(bi), wt[:, k, :], hT[:, k, tt * 512:(tt + 1) * 512], start=(k == 0), stop=(k == NK - 1)) for k in range(NK)],
                              reads=[b_wt, b_hT], writes=[pb[bi]])
                        o_ap = stg[:, tt * 512:(tt + 1) * 512]
                        if m < 48:
                            evac_copy(o_ap, bank(bi), [pb[bi]], [b_stg])
                        else:
                            A([pb[bi], b_colp], [b_stg], o_ap, bank(bi), AF.Sigmoid, bias=col("gbias", l * 48 + (m - 48)), scale=1.0)
                    if m < 8:
                        dst = lxT[m * 128:(m + 1) * 128, t0:t0 + ST]
                    elif m < 16:
                        dst = lgT[(m - 8) * 128:(m - 7) * 128, :]
                    elif m < 24:
                        dst = qT[(m - 16) * 128:(m - 15) * 128, :]
                    elif m < 32:
                        dst = kT[(m - 24) * 128:(m - 23) * 128, t0:t0 + ST]
                    elif m < 48:
                        dst = uT[(m - 40) * 128:(m - 39) * 128, :]
                    else:
                        dst = gT[(m - 48) * 128:(m - 47) * 128, :]
                    kb.dma("sp", DMA(dst, stg[:]), reads=[b_stg], owner=b_stg)
                for cb in range(2):
                    wv, b_wv = wv_ring.next()
                    kb.dma("pool", [DMA(wv[:, :, mm * 128:(mm + 1) * 128], w_in_t[l, 32 + 4 * cb + mm]) for mm in range(4)], writes=[b_wv], owner=b_wv)
                    for j in range(ST // 128):
                        bi = pbr % 6
                        pbr += 1
                        kb.op("pe", [MM(bank(bi), hT[:, k, j * 128:(j + 1) * 128], wv[:, k, :], start=(k == 0), stop=(k == NK - 1)) for k in range(NK)],
                              reads=[b_wv, b_hT], writes=[pb[bi]])
                        sv, b_sv = sv_ring.next()
                        evac_copy(sv[:], bank(bi), [pb[bi]], [b_sv])
                        kb.dma("sp", DMA(vS[t0 + j * 128:t0 + (j + 1) * 128, cb * 512:(cb + 1) * 512], sv[:]), reads=[b_sv], owner=b_sv)
                ph.close()

            if "p2a" not in skip:
                ph = Phase(kb)
                bd, b_bd = ph.tile("bd", [128, 8, 2, 128], BF16)
                kb.dma("pool", DMA(bd[:], lruw_d[l]), writes=[b_bd], owner=b_bd)
                lx_ring = ph.ring("lx", 2, [128, 515], BF16)
                lg_ring = ph.ring("lg", 2, [128, 512], BF16)
                ya_ring = ph.ring("ya", 2, [128, 512], BF16)
                hs_ring = ph.ring("hs", 2, [128, 512], F32)
                xc, b_xc = ph.tile("xc", [128, 512], F32)
                rr, b_rr = ph.tile("rr", [128, 512], F32)
                ii, b_ii = ph.tile("ii", [128, 512], F32)
                aa, b_aa = ph.tile("aa", [128, 512], F32)
                a2, b_a2 = ph.tile("a2", [128, 512], F32)
                gg, b_gg = ph.tile("gg", [128, 512], F32)
                xcb, b_xcb = ph.tile("xcb", [128, 512], BF16)
                if first:
                    V("memset", [], [b_lruh], ap=lru_h[:], constant=0.0)
                for ct in range(8):
                    r0, r1 = ct * 128, (ct + 1) * 128
                    for tt in range(nTT):
                        tg = t0 + tt * 512
                        lx, b_lx = lx_ring.next()
                        lg, b_lg = lg_ring.next()
                        if tg == 0 and not has_prev:
                            V("memset", [], [b_lx], ap=lx[:, 0:3], constant=0.0)
                            kb.dma("sp", DMA(lx[:, 3:515], lxT[r0:r1, 0:512]), writes=[b_lx], owner=b_lx)
                        else:
                            kb.dma("sp", DMA(lx[:], lxT[r0:r1, tg - 3:tg + 512]), writes=[b_lx], owner=b_lx)
                        kb.dma("sp", DMA(lg[:], lgT[r0:r1, tt * 512:(tt + 1) * 512]), writes=[b_lg], owner=b_lg)
                        cwi = (l * 8 + ct) * 4
                        TS([b_lx, b_colp], [b_xc], xc[:], lx[:, 3:515], col("convw", cwi + 3), col("convb", l * 8 + ct), ALU.mult, ALU.add)
                        for k in range(3):
                            STT([b_lx, b_xc, b_colp], [b_xc], xc[:], lx[:, k:k + 512], col("convw", cwi + k), xc[:], ALU.mult, ALU.add)
                        A([b_xc], [b_xcb], xcb[:], xc[:], AF.Copy)
                        kb.op("pe", MM(bank(0), bd[:, ct, 0, :], xcb[:]), reads=[b_bd, b_xcb], writes=[pb[0]])
                        kb.op("pe", MM(bank(1), bd[:, ct, 1, :], xcb[:]), reads=[b_bd, b_xcb], writes=[pb[1]])
                        A([pb[0], b_colp], [b_rr], rr[:], bank(0), AF.Sigmoid, bias=col("ba", l * 8 + ct), scale=1.0)
                        A([pb[1], b_colp], [b_ii], ii[:], bank(1), AF.Sigmoid, bias=col("bx", l * 8 + ct), scale=1.0)
                        A([b_rr, b_nsp8], [b_aa], aa[:], rr[:], AF.Exp, scale=nsp8[:, l * 8 + ct:l * 8 + ct + 1])
                        TT([b_aa], [b_a2], a2[:], aa[:], aa[:], ALU.mult)
                        A([b_a2], [b_a2], a2[:], a2[:], AF.Sqrt, scale=-1.0, bias=1.0)
                        TT([b_ii, b_xc], [b_ii], ii[:], ii[:], xc[:], ALU.mult)
                        TT([b_ii, b_a2], [b_ii], ii[:], ii[:], a2[:], ALU.mult)
                        hs, b_hs = hs_ring.next()
                        V("tensor_tensor_scan", [b_aa, b_ii, b_lruh], [b_hs], out=hs[:], data0=aa[:], data1=ii[:], initial=lru_h[:, ct:ct + 1], op0=ALU.mult, op1=ALU.add)
                        V("tensor_copy", [b_hs], [b_lruh], out=lru_h[:, ct:ct + 1], in_=hs[:, 511:512])
                        A([b_lg], [b_gg], gg[:], lg[:], AF.Gelu_apprx_tanh)
                        ya, b_ya = ya_ring.next()
                        TT([b_hs, b_gg], [b_ya], ya[:], hs[:], gg[:], ALU.mult)
                        kb.dma("sp", DMA(yaT[r0:r1, tt * 512:(tt + 1) * 512], ya[:]), reads=[b_ya], owner=b_ya)
                ph.close()

            if "p2b" not in skip:
                ph = Phase(kb)
                NKT = (ST + 512) // 128
                qh_ring = ph.ring("qh", 2, [128, ST], BF16)
                kh_ring = ph.ring("kh", 2, [128, ST + 512], BF16)
                vh_ring = ph.ring("vh", 2, [128, NKT, 128], BF16)
                bh_ring = ph.ring("bh", 2, [128, 640], F32)
                yb_ring = ph.ring("yb", 2, [128, ST], BF16)
                sf_ring = ph.ring("sf", 2, [128, 640], F32)
                pt_ring = ph.ring("pt", 2, [128, 640], BF16)
                rc_ring = ph.ring("rc", 2, [128, 128], F32)
                it = 0
                for hd in range(8):
                    r0, r1 = hd * 128, (hd + 1) * 128
                    qh, b_qh = qh_ring.next()
                    kh, b_kh = kh_ring.next()
                    vh, b_vh = vh_ring.next()
                    bh, b_bh = bh_ring.next()
                    yb, b_yb = yb_ring.next()
                    kb.dma("sp", DMA(qh[:], qT[r0:r1, :]), writes=[b_qh], owner=b_qh)
                    kb.dma("sp", DMA(bh[:], abias_d[l, hd]), writes=[b_bh], owner=b_bh)
                    if first:
                        kb.dma("sp", DMA(kh[:, 512:], kT[r0:r1, 0:ST]), writes=[b_kh], owner=b_kh)
                        kb.dma("sp", DMA(vh[:, 4:, :], vS[0:ST, r0:r1].rearrange("(n p) d -> p n d", p=128)), writes=[b_vh], owner=b_vh)
                    else:
                        kb.dma("sp", DMA(kh[:], kT[r0:r1, t0 - 512:t0 + ST]), writes=[b_kh], owner=b_kh)
                        kb.dma("sp", DMA(vh[:], vS[t0 - 512:t0 + ST, r0:r1].rearrange("(n p) d -> p n d", p=128)), writes=[b_vh], owner=b_vh)
                    for m in range(ST // 128):
                        kts = [kt for kt in range(5) if (not first) or (m - 4 + kt) >= 0]
                        lo, hi = kts[0] * 128, 640
                        sb = (it % 2) * 2
                        po_b = 4 + (it % 2)
                        ps_b = 6 + (it % 2)
                        it += 1
                        sc = psum[:, sb * 512:sb * 512 + 640]
                        kb.op("pe", [MM(sc[:, kt * 128:(kt + 1) * 128], kh[:, (m + kt) * 128:(m + kt + 1) * 128], qh[:, m * 128:(m + 1) * 128]) for kt in kts],
                              reads=[b_kh, b_qh], writes=[pb[sb], pb[sb + 1]])
                        sf, b_sf = sf_ring.next()
                        pt, b_pt = pt_ring.next()
                        STT([pb[sb], pb[sb + 1], b_bh], [b_sf], sf[:, lo:hi], sc[:, lo:hi], float(128 ** -0.5), bh[:, lo:hi], ALU.mult, ALU.add)
                        A([b_sf], [b_pt], pt[:, lo:hi], sf[:, lo:hi], AF.Exp)
                        n = len(kts)
                        fns = [MM(bank(po_b)[:, 0:128], vh[:, m + kt, :], pt[:, kt * 128:(kt + 1) * 128], start=(i_ == 0), stop=(i_ == n - 1)) for i_, kt in enumerate(kts)]
                        fns += [MM(bank(ps_b)[:, 0:128], ones_b[:], pt[:, kt * 128:(kt + 1) * 128], start=(i_ == 0), stop=(i_ == n - 1)) for i_, kt in enumerate(kts)]
                        kb.op("pe", fns, reads=[b_vh, b_pt, b_onesb], writes=[pb[po_b], pb[ps_b]])
                        rc, b_rc = rc_ring.next()
                        V("reciprocal", [pb[ps_b]], [b_rc], out=rc[:], in_=bank(ps_b)[:, 0:128])
                        TT([pb[po_b], b_rc], [b_yb], yb[:, m * 128:(m + 1) * 128], bank(po_b)[:, 0:128], rc[:], ALU.mult)
                    kb.dma("sp", DMA(ybT[r0:r1, :], yb[:]), reads=[b_yb], owner=b_yb)
                ph.close()

            if "p2c" not in skip:
                ph = Phase(kb)
                W = 4096
                NF = 32 * 129
                E_re, b_Ere = ph.tile("E_re", [128, W], F32)
                E_im, b_Eim = ph.tile("E_im", [128, W], F32)
                F_re, b_Fre = ph.tile("F_re", [128, 32, 129], F32)
                F_im, b_Fim = ph.tile("F_im", [128, 32, 129], F32)
                bbr, b_bbr = ph.tile("bbr", [128, W], BF16)
                bbi, b_bbi = ph.tile("bbi", [128, W], BF16)
                Ffr = F_re[:].rearrange("p b j -> p (b j)")
                Ffi = F_im[:].rearrange("p b j -> p (b j)")
                pp = Phase(kb)
                ar, b_ar = pp.tile("ar", [128, W], F32)
                ai, b_ai = pp.tile("ai", [128, W], F32)
                sr, b_sr = pp.tile("sr", [128, W], F32)
                t1f, b_t1 = pp.tile("t1", [128, NF], F32)
                t2f, b_t2 = pp.tile("t2", [128, NF], F32)
                t3f, b_t3 = pp.tile("t3", [128, NF], F32)
                t1, t2, t3 = t1f[:, 0:W], t2f[:, 0:W], t3f[:, 0:W]
                bdr, b_bdr = Ffr[:, 0:W], b_Fre
                bdi, b_bdi = Ffi[:, 0:W], b_Fim
                kb.dma("sp", DMA(ar[:], rowp_d[l, 0:1, :].to_broadcast([128, W])), writes=[b_ar], owner=b_ar)
                kb.dma("sp", DMA(ai[:], rowp_d[l, 1:2, :].to_broadcast([128, W])), writes=[b_ai], owner=b_ai)
                kb.dma("sp", DMA(sr[:], rowp_d[l, 2:3, :].to_broadcast([128, W])), writes=[b_sr], owner=b_sr)
                kb.dma("sp", DMA(bdr, bbd_d[l, :, 0, :]), writes=[b_bdr], owner=b_bdr)
                kb.dma("sp", DMA(bdi, bbd_d[l, :, 1, :]), writes=[b_bdi], owner=b_bdi)

                def sincos(dst_sin, b_ds, dst_cos, b_dc, ycyc, b_y, tmp, b_tmp):
                    TS([b_y], [b_tmp], tmp, ycyc, MAGIC, MAGIC, ALU.add, ALU.subtract)
                    TT([b_y, b_tmp], [b_tmp], tmp, ycyc, tmp, ALU.subtract)
                    A([b_tmp], [b_ds], dst_sin, tmp, AF.Sin, scale=TWO_PI)
                    TS([b_y], [b_y], ycyc, ycyc, 0.25, None, ALU.add)
                    TS([b_y], [b_tmp], tmp, ycyc, MAGIC, MAGIC, ALU.add, ALU.subtract)
                    TT([b_y, b_tmp], [b_tmp], tmp, ycyc, tmp, ALU.subtract)
                    A([b_tmp], [b_dc], dst_cos, tmp, AF.Sin, scale=TWO_PI)

                MUL, ADD, SUB = ALU.mult, ALU.add, ALU.subtract
                A([b_sr], [b_sr], sr[:], sr[:], AF.Exp)
                TT([b_ar, b_sr], [b_t1], t1, ar[:], sr[:], MUL)
                A([b_t1], [b_t1], t1, t1, AF.Exp)
                TT([b_ai, b_sr], [b_t2], t2, ai[:], sr[:], MUL)
                TS([b_t2], [b_t2], t2, t2, 1.0 / TWO_PI, None, MUL)
                sincos(E_im[:], b_Eim, E_re[:], b_Ere, t2, b_t2, t3, b_t3)
                TT([b_Ere, b_t1], [b_Ere], E_re[:], E_re[:], t1, MUL)
                TT([b_Eim, b_t1], [b_Eim], E_im[:], E_im[:], t1, MUL)
                TS([b_Ere], [b_Ere], E_re[:], E_re[:], -1.0, None, ADD)
                TT([b_ar], [b_t1], t1, ar[:], ar[:], MUL)
                TT([b_ai], [b_t2], t2, ai[:], ai[:], MUL)
                TT([b_t1, b_t2], [b_t1], t1, t1, t2, ADD)
                V("reciprocal", [b_t1], [b_t1], out=t1, in_=t1)
                TT([b_Ere, b_ar], [b_t2], t2, E_re[:], ar[:], MUL)
                TT([b_Eim, b_ai], [b_t3], t3, E_im[:], ai[:], MUL)
                TT([b_t2, b_t3], [b_t2], t2, t2, t3, ADD)
                TT([b_t2, b_t1], [b_t2], t2, t2, t1, MUL)
                TT([b_Eim, b_ar], [b_t3], t3, E_im[:], ar[:], MUL)
                TT([b_Ere, b_ai], [b_Eim], E_im[:], E_re[:], ai[:], MUL)
                TT([b_t3, b_Eim], [b_t3], t3, t3, E_im[:], SUB)
                TT([b_t3, b_t1], [b_t3], t3, t3, t1, MUL)
                TT([b_t2, b_bdr], [b_Ere], E_re[:], t2, bdr, MUL)
                TT([b_t3, b_bdi], [b_Eim], E_im[:], t3, bdi, MUL)
                TT([b_Ere, b_Eim], [b_bbr], bbr[:], E_re[:], E_im[:], SUB)
                TT([b_t2, b_bdi], [b_Ere], E_re[:], t2, bdi, MUL)
                TT([b_t3, b_bdr], [b_Eim], E_im[:], t3, bdr, MUL)
                TT([b_Ere, b_Eim], [b_bbi], bbi[:], E_re[:], E_im[:], ADD)
                iocol = col("iota", 0)
                TT([b_ar, b_sr], [b_t1], t1, ar[:], sr[:], MUL)
                TS([b_t1, b_colp], [b_t1], t1, t1, iocol, -1.0, MUL, MUL)
                A([b_t1], [b_t1], t1, t1, AF.Exp)
                TT([b_ai, b_sr], [b_t2], t2, ai[:], sr[:], MUL)
                TS([b_t2, b_colp], [b_t2], t2, t2, iocol, -1.0 / TWO_PI, MUL, MUL)
                sincos(E_im[:], b_Eim, E_re[:], b_Ere, t2, b_t2, t3, b_t3)
                TT([b_Ere, b_t1], [b_Ere], E_re[:], E_re[:], t1, MUL)
                TT([b_Eim, b_t1], [b_Eim], E_im[:], E_im[:], t1, MUL)
                stc, b_stc = pp.tile("stc", [128, 32], F32)
                alc, b_alc = pp.tile("alc", [128, 32], F32)
                thc, b_thc = pp.tile("thc", [128, 32], F32)
                A([b_colp], [b_stc], stc[:], col("lsc", l * 32, 32), AF.Exp)
                TT([b_colp, b_stc], [b_alc], alc[:], col("acre", l * 32, 32), stc[:], MUL)
                TT([b_colp, b_stc], [b_thc], thc[:], col("acim", l * 32, 32), stc[:], MUL)
                TS([b_thc], [b_thc], thc[:], thc[:], 1.0 / TWO_PI, None, MUL)
                g1 = t1f[:].rearrange("p (b j) -> p b j", b=32)
                g2 = t2f[:].rearrange("p (b j) -> p b j", b=32)
                io_b = iorow[:].unsqueeze(1).to_broadcast([128, 32, 129])
                TT([b_iorow, b_alc], [b_t1], g1, io_b, alc[:].unsqueeze(2).to_broadcast([128, 32, 129]), MUL)
                A([b_t1], [b_t1], t1f[:], t1f[:], AF.Exp)
                TT([b_iorow, b_thc], [b_t2], g2, io_b, thc[:].unsqueeze(2).to_broadcast([128, 32, 129]), MUL)
                sincos(Ffi, b_Fim, Ffr, b_Fre, t2f[:], b_t2, t3f[:], b_t3)
                TT([b_Fre, b_t1], [b_Fre], Ffr, Ffr, t1f[:], MUL)
                TT([b_Fim, b_t1], [b_Fim], Ffi, Ffi, t1f[:], MUL)
                pp.close()
                cmat, b_cmat = ph.tile("cmat", [128, 8192], BF16)
                kb.dma("pool", [DMA(cmat[:, q * 2048:(q + 1) * 2048], ct_d[l, :, q * 2048:(q + 1) * 2048]) for q in range(4)], writes=[b_cmat], owner=b_cmat)
                ncmat, b_ncmat = ph.tile("ncmat", [128, 8192], BF16)
                TS([b_cmat], [b_ncmat], ncmat[:], cmat[:], -1.0, None, MUL)
                u_all, b_u = ph.tile("u_all", [128, 8, ST], BF16)
                kb.dma("sp", DMA(u_all[:], uT.rearrange("(c p) t -> p c t", p=128)), writes=[b_u], owner=b_u)
                p_ring = ph.ring("pp4", 2, [128, 4, 512], BF16)
                q_ring = ph.ring("qq4", 2, [128, 4, 512], BF16)
                cp_ring = ph.ring("cp", 2, [128, 2, 512], F32)
                yc_ring = ph.ring("yc", 4, [128, 128], BF16)
                ty_ring = ph.ring("ty", 2, [128, 128], F32)
                tn_ring = ph.ring("tn", 2, [128, 4, 4], F32)
                if first:
                    V("memset", [], [b_s5ca], ap=s5ca[:], constant=0.0)
                    V("memset", [], [b_s5cb], ap=s5cb[:], constant=0.0)
                units = [(j, ct) for j in range(ST // 128) for ct in range(8)]

                def bu_banks(ui):
                    base = (ui % 2) * 4
                    return base, base + 1, base + 2, base + 3

                def emit_bu(ui):
                    j, ct = units[ui]
                    br, bi_, _, _ = bu_banks(ui)
                    ul = u_all[:, ct, j * 128:(j + 1) * 128]
                    c0, c1 = ct * 512, (ct + 1) * 512
                    kb.op("pe", MM(bank(br), ul, bbr[:, c0:c1]), reads=[b_u, b_bbr], writes=[pb[br]])
                    kb.op("pe", MM(bank(bi_), ul, bbi[:, c0:c1]), reads=[b_u, b_bbi], writes=[pb[bi_]])

                def st_eprod(ui, stt):
                    j, ct = units[ui]
                    br, bi_, cr_, ci_ = bu_banks(ui)
                    c0, c1 = ct * 512, (ct + 1) * 512
                    pp4, b_p = p_ring.next()
                    stt["p"] = (pp4, b_p)
                    TT([pb[br], b_Ere], [b_p], pp4[:, 0, :], bank(br), E_re[:, c0:c1], MUL)
                    TT([pb[bi_], b_Eim], [b_p], pp4[:, 1, :], bank(bi_), E_im[:, c0:c1], MUL)
                    TT([pb[bi_], b_Ere], [b_p], pp4[:, 2, :], bank(bi_), E_re[:, c0:c1], MUL)
                    TT([pb[br], b_Eim], [b_p], pp4[:, 3, :], bank(br), E_im[:, c0:c1], MUL)
                    fns = []
                    for blk in range(4):
                        bs = slice(blk * 128, (blk + 1) * 128)
                        fns.append(MM(bank(cr_)[:, bs], pp4[:, 0, bs], tri_b[:], start=True, stop=False))
                        fns.append(MM(bank(cr_)[:, bs], pp4[:, 1, bs], ntri_b[:], start=False, stop=True))
                        fns.append(MM(bank(ci_)[:, bs], pp4[:, 2, bs], tri_b[:], start=True, stop=False))
                        fns.append(MM(bank(ci_)[:, bs], pp4[:, 3, bs], tri_b[:], start=False, stop=True))
                    kb.op("pe", fns, reads=[b_p, b_trib, b_ntrib], writes=[pb[cr_], pb[ci_]])

                def st_xprod(ui, stt):
                    j, ct = units[ui]
                    br, bi_, cr_, ci_ = bu_banks(ui)
                    cp, b_cp = cp_ring.next()
                    s0, s1 = ct * 4, ct * 4 + 4
                    TT([pb[cr_], b_s5ca], [b_cp], v3(cp[:, 0, :]), v3(bank(cr_)), s5ca[:, s0:s1].unsqueeze(2).to_broadcast([128, 4, 128]), ADD)
                    TT([pb[ci_], b_s5cb], [b_cp], v3(cp[:, 1, :]), v3(bank(ci_)), s5cb[:, s0:s1].unsqueeze(2).to_broadcast([128, 4, 128]), ADD)
                    Fr = F_re[:, s0:s1, 0:128]
                    Fi = F_im[:, s0:s1, 0:128]
                    qq4, b_q = q_ring.next()
                    TT([b_cp, b_Fre], [b_q], v3(qq4[:, 0, :]), v3(cp[:, 0, :]), Fr, MUL)
                    TT([b_cp, b_Fim], [b_q], v3(qq4[:, 1, :]), v3(cp[:, 1, :]), Fi, MUL)
                    TT([b_cp, b_Fre], [b_q], v3(qq4[:, 2, :]), v3(cp[:, 1, :]), Fr, MUL)
                    TT([b_cp, b_Fim], [b_q], v3(qq4[:, 3, :]), v3(cp[:, 0, :]), Fi, MUL)
                    tn, b_tn = tn_ring.next()
                    cr = v3(cp[:, 0, :])[:, :, 127]
                    ci = v3(cp[:, 1, :])[:, :, 127]
                    Gr = F_re[:, s0:s1, 128]
                    Gi = F_im[:, s0:s1, 128]
                    TT([b_cp, b_Fre], [b_tn], tn[:, 0, :], cr, Gr, MUL)
                    TT([b_cp, b_Fim], [b_tn], tn[:, 1, :], ci, Gi, MUL)
                    TT([b_cp, b_Fre], [b_tn], tn[:, 2, :], ci, Gr, MUL)
                    TT([b_cp, b_Fim], [b_tn], tn[:, 3, :], cr, Gi, MUL)
                    TT([b_tn], [b_s5ca], s5ca[:, s0:s1], tn[:, 0, :], tn[:, 1, :], SUB)
                    TT([b_tn], [b_s5cb], s5cb[:, s0:s1], tn[:, 2, :], tn[:, 3, :], ADD)
                    fns = []
                    for blk in range(4):
                        bs = slice(blk * 128, (blk + 1) * 128)
                        o_re = ((ct * 4 + blk) * 2 + 0) * 128
                        o_im = ((ct * 4 + blk) * 2 + 1) * 128
                        ops = ((cmat, o_re, 0), (ncmat, o_re, 1), (ncmat, o_im, 2), (ncmat, o_im, 3))
                        for oi, (cm, o, qi) in enumerate(ops):
                            fns.append(MM(bank(cr_)[:, 0:128], cm[:, o:o + 128], qq4[:, qi, bs], start=(blk == 0 and oi == 0), stop=(blk == 3 and oi == 3)))
                    kb.op("pe", fns, reads=[b_cmat, b_ncmat, b_q], writes=[pb[cr_]])

                def st_out(ui, stt):
                    j, ct = units[ui]
                    br, bi_, cr_, ci_ = bu_banks(ui)
                    ul = u_all[:, ct, j * 128:(j + 1) * 128]
                    ty, b_ty = ty_ring.next()
                    yc, b_yc = yc_ring.next()
                    STT([b_u, pb[cr_], b_colp], [b_ty], ty[:], ul, col("ssmd", l * 8 + ct), bank(cr_)[:, 0:128], MUL, ADD)
                    A([b_ty], [b_yc], yc[:], ty[:], AF.Gelu_apprx_tanh)
                    kb.dma("sp", DMA(ycT[ct * 128:(ct + 1) * 128, j * 128:(j + 1) * 128], yc[:]), reads=[b_yc], owner=b_yc)

                nU = len(units)
                emit_bu(0)
                emit_bu(1)
                for pi in range(0, nU, 2):
                    a, b = pi, pi + 1
                    sa, sb_ = {}, {}
                    st_eprod(a, sa)
                    st_eprod(b, sb_)
                    st_xprod(a, sa)
                    if a + 2 < nU:
                        emit_bu(a + 2)
                    st_xprod(b, sb_)
                    if b + 2 < nU:
                        emit_bu(b + 2)
                    st_out(a, sa)
                    st_out(b, sb_)
                ph.close()


            if "p3" not in skip:
                ph = Phase(kb)
                ys = []
                for nm, src in (("ya", yaT), ("yb", ybT), ("yc", ycT)):
                    t, b = ph.tile(nm, [128, 8, ST], BF16)
                    kb.dma("sp", DMA(t[:], src.rearrange("(c p) t -> p c t", p=128)), writes=[b], owner=b)
                    ys.append((t, b))
                ys.append(ys[2])
                wb_ring = ph.ring("wb", 2, [128, 4, 8, 128], BF16)
                g_ring = ph.ring("gg", 2, [128, 3, 512], BF16)
                ms_ring = ph.ring("ms", 2, [128, ST], BF16)
                tq_ring = ph.ring("tq", 2, [128, 4, 512], F32)
                gT_v = gT.rearrange("(b f p) t -> f p b t", b=3, f=16)
                it = 0
                for f in range(16):
                    wb, b_wb = wb_ring.next()
                    kb.dma("pool", [DMA(wb[:, q, :, :], wbr_t[l, f, :, q, :, :]) for q in range(4)], writes=[b_wb], owner=b_wb)
                    ms, b_ms = ms_ring.next()
                    for tt in range(nTT):
                        ts0, ts1 = tt * 512, (tt + 1) * 512
                        g3, b_g3 = g_ring.next()
                        kb.dma("sp", DMA(g3[:], gT_v[f][:, :, ts0:ts1]), writes=[b_g3], owner=b_g3)
                        b0 = (it % 2) * 4
                        it += 1
                        for q in range(4):
                            yt, b_yt = ys[q]
                            kb.op("pe", [MM(bank(b0 + q), wb[:, q, k, :], yt[:, k, ts0:ts1], start=(k == 0), stop=(k == 7)) for k in range(8)],
                                  reads=[b_wb, b_yt], writes=[pb[b0 + q]])
                        tq, b_tq = tq_ring.next()
                        A([pb[b0 + 3]], [b_tq], tq[:, 3, :], bank(b0 + 3), AF.Sigmoid)
                        TT([pb[b0], b_g3], [b_tq], tq[:, 0, :], bank(b0 + 0), g3[:, 0, :], ALU.mult)
                        TT([pb[b0 + 1], b_g3], [b_tq], tq[:, 1, :], bank(b0 + 1), g3[:, 1, :], ALU.mult)
                        TT([pb[b0 + 2], b_tq], [b_tq], tq[:, 2, :], bank(b0 + 2), tq[:, 3, :], ALU.mult)
                        TT([b_tq, b_g3], [b_tq], tq[:, 2, :], tq[:, 2, :], g3[:, 2, :], ALU.mult)
                        TT([b_tq], [b_tq], tq[:, 0, :], tq[:, 0, :], tq[:, 1, :], ALU.add)
                        TT([b_tq], [b_ms], ms[:, ts0:ts1], tq[:, 0, :], tq[:, 2, :], ALU.add)
                    kb.dma("sp", DMA(mT[f * 128:(f + 1) * 128, :], ms[:]), reads=[b_ms], owner=b_ms)
                ph.close()
                proj_residual(l, t0, mT, 16, wout_t[l], nTT)

            if "p4" not in skip:
                ph = Phase(kb)
                hT, b_hT = ph.tile("hT", [128, NK, ST], BF16)
                xin, b_xin = ph.tile("xin", [128, NK, 512], F32)
                sq_ring = ph.ring("sq", 2, [128, 512], F32)
                rs_ring = ph.ring("rs", 2, [128, 512], F32)
                rms_norm_to(hT, b_hT, "gffn", l, t0, xin, b_xin, sq_ring, rs_ring, 7)
                wg_ring = ph.ring("wg", 2, [128, 2, NK, 128], BF16)
                as_ring = ph.ring("as", 2, [128, ST], BF16)
                sl_ring = ph.ring("sl", 2, [128, 512], F32)
                it = 0
                for jj in range(NJ):
                    wg, b_wg = wg_ring.next()
                    kb.dma("pool", [DMA(wg[:, q, :, :], wgu_t[l, jj, :, q, :, :]) for q in range(2)], writes=[b_wg], owner=b_wg)
                    ast, b_ast = as_ring.next()
                    for tt in range(nTT):
                        ts0, ts1 = tt * 512, (tt + 1) * 512
                        bg = (it % 3) * 2
                        it += 1
                        for q in range(2):
                            kb.op("pe", [MM(bank(bg + q), wg[:, q, k, :], hT[:, k, ts0:ts1], start=(k == 0), stop=(k == NK - 1)) for k in range(NK)],
                                  reads=[b_wg, b_hT], writes=[pb[bg + q]])
                        sl, b_sl = sl_ring.next()
                        A([pb[bg]], [b_sl], sl[:], bank(bg), AF.Silu)
                        TT([pb[bg + 1], b_sl], [b_ast], ast[:, ts0:ts1], bank(bg + 1), sl[:], ALU.mult)
                    kb.dma("sp", DMA(actT[jj * 128:(jj + 1) * 128, :], ast[:]), reads=[b_ast], owner=b_ast)
                ph.close()
                proj_residual(l, t0, actT, NJ, wdn_t[l], min(2, nTT))

    ph = Phase(kb)
    xin_ring = ph.ring("xin", 2, [128, NK, 512], F32)
    sq_ring = ph.ring("sq", 2, [128, 512], F32)
    rs_ring = ph.ring("rs", 2, [128, 512], F32)
    outv = outT.rearrange("(k p) t -> p k t", p=128)
    for tt in range(S // 512):
        tg = tt * 512
        xin, b_xin = xin_ring.next()
        rs, b_rs = rms_norm_tile(xin, b_xin, tg, sq_ring, rs_ring, 7)
        for k in range(NK):
            STT([b_xin, b_rs, b_colp], [b_xin], xin[:, k, :], xin[:, k, :], col("gfin", k), rs[:], ALU.mult, ALU.mult)
        kb.dma("sp", DMA(outv[:, :, tg:tg + 512], xin[:]), reads=[b_xin], owner=b_xin)
    ph.close()
    kb.finalize()
    return nc


def prep_weights(inp, L):
    f = np.float32
    out = {}
    w_in = inp["w_in"][:L]
    out["w_in_t"] = np.ascontiguousarray(w_in.reshape(L, 16, 128, 96, 128).transpose(0, 3, 2, 1, 4))
    wb = np.concatenate([inp["w_branch"][:L], inp["ssm_w_glu"][:L][:, None]], axis=1)
    out["wbr_t"] = np.ascontiguousarray(wb.reshape(L, 4, 8, 128, 16, 128).transpose(0, 4, 3, 1, 2, 5))
    out["wout_t"] = np.ascontiguousarray(inp["w_out"][:L].reshape(L, 16, 128, 16, 128).transpose(0, 3, 2, 1, 4))
    wgu = np.stack([inp["w_ffn_gate"][:L], inp["w_ffn_up"][:L]], axis=1)
    out["wgu_t"] = np.ascontiguousarray(wgu.reshape(L, 2, 16, 128, NJ, 128).transpose(0, 4, 3, 1, 2, 5))
    out["wdn_t"] = np.ascontiguousarray(inp["w_ffn_down"][:L].reshape(L, NJ, 128, 16, 128).transpose(0, 3, 2, 1, 4))
    COFF, NCOL = col_layout(L)
    colp = np.zeros((128, NCOL), f)

    def put(name, arr):
        colp[:, COFF[name]:COFF[name] + arr.shape[1]] = arr
    put("gmix", inp["norm_mix_g"][:L].reshape(L, 16, 128).transpose(2, 0, 1).reshape(128, -1))
    put("gffn", inp["norm_ffn_g"][:L].reshape(L, 16, 128).transpose(2, 0, 1).reshape(128, -1))
    put("gfin", inp["norm_final_g"].reshape(16, 128).T)
    put("gbias", inp["gate_bias"][:L].reshape(L, 3, 16, 128).transpose(3, 0, 1, 2).reshape(128, -1))
    put("convw", inp["lru_conv_w"][:L].reshape(L, 4, 8, 128).transpose(3, 0, 2, 1).reshape(128, -1))
    for nm, key in (("convb", "lru_conv_b"), ("ba", "lru_ba"), ("bx", "lru_bx"), ("lam", "lru_lambda"), ("ssmd", "ssm_d")):
        put(nm, inp[key][:L].reshape(L, 8, 128).transpose(2, 0, 1).reshape(128, -1))
    for nm, arr in (("acre", inp["ssm_a_re"][:L]), ("acim", inp["ssm_a_im"][:L]),
                    ("lsc", np.repeat(inp["ssm_log_step"][:L][:, :, None], 64, axis=2))):
        put(nm, arr.reshape(L, 32, 2, 64).transpose(2, 3, 0, 1).reshape(128, -1))
    colp[:, COFF["iota"]] = np.arange(128, dtype=f)
    out["colp"] = colp
    rowp = np.stack([inp["ssm_a_re"][:L].reshape(L, 4096), inp["ssm_a_im"][:L].reshape(L, 4096),
                     np.repeat(inp["ssm_log_step"][:L][:, :, None], 64, axis=2).reshape(L, 4096)], axis=1)
    out["rowp"] = np.ascontiguousarray(rowp.astype(f))
    lruw = np.zeros((L, 128, 8, 2, 128), f)
    for wi, key in enumerate(("lru_wa", "lru_wx")):
        w = inp[key][:L].reshape(L, 8, 2, 64, 64)
        for nl in range(2):
            lruw[:, nl * 64:(nl + 1) * 64, :, wi, nl * 64:(nl + 1) * 64] = w[:, :, nl].transpose(0, 2, 1, 3)
    out["lruw"] = lruw
    kk = np.arange(128)[:, None, None]
    kt = np.arange(5)[None, :, None]
    qq = np.arange(128)[None, None, :]
    dist = (4 - kt) * 128 + qq - kk
    rel = np.clip(dist, -128, 128) + 128
    cdiff = 8 - 2 * kt + qq // 64 - kk // 64
    valid = (cdiff >= 0) & (cdiff <= 8)
    ab = inp["attn_rel_bias"][:L][:, :, rel]
    ab = np.where(valid[None, None], ab, f(-30000.0)).astype(f)
    out["abias"] = np.ascontiguousarray(ab.reshape(L, 8, 128, 640))
    bbd = np.zeros((L, 128, 2, 8, 8, 64), f)
    for ri, key in enumerate(("ssm_b_re", "ssm_b_im")):
        B = inp[key][:L].reshape(L, 8, 8, 64, 16)
        for gl in range(8):
            bbd[:, gl * 16:(gl + 1) * 16, ri, :, gl, :] = B[:, :, gl].transpose(0, 3, 1, 2)
    out["bbd"] = np.ascontiguousarray(bbd.reshape(L, 128, 2, 4096))
    ctd = np.zeros((L, 128, 8, 4, 2, 128), f)
    for ri, key in enumerate(("ssm_c_re", "ssm_c_im")):
        C = inp[key][:L].reshape(L, 8, 4, 2, 16, 64)
        for blk in range(4):
            for gl2 in range(2):
                c0 = (2 * blk + gl2) * 16
                ctd[:, gl2 * 64:(gl2 + 1) * 64, :, blk, ri, c0:c0 + 16] = C[:, :, blk, gl2].transpose(0, 3, 1, 2)
    out["ctd"] = np.ascontiguousarray(ctd.reshape(L, 128, 8192))
    cst = np.zeros((128, 386), f)
    cst[:, 0:128] = np.triu(np.ones((128, 128), f))
    cst[:, 128:256] = 1.0
    cst[:, 256:385] = np.arange(129, dtype=f)[None, :]
    cst[:, 385] = 1e-6
    out["cst"] = cst
    out["epsd"] = np.full((128, 1), 1e-6, f)
    return out


_CACHE = {}


def run_model(inputs, L, S_core, ST, n_cores, dbg=False, skip=()):
    x = np.asarray(inputs["x"], np.float32)
    B, S, _ = x.shape
    assert S == S_core and B <= n_cores
    key = (L, S_core, ST, dbg, tuple(skip))
    if key not in _CACHE:
        _CACHE[key] = build_program(L, S_core, ST, dbg=dbg, skip=skip)
    nc = _CACHE[key]
    wts = prep_weights({k: np.asarray(v, np.float32) for k, v in inputs.items() if k != "x"}, L)
    in_maps = []
    for c in range(n_cores):
        b = c % B
        m = dict(wts)
        m["xT"] = np.ascontiguousarray(x[b].T)
        in_maps.append(m)
    res = run_bass_kernel_spmd(nc, in_maps, core_ids=list(range(n_cores)))
    out = np.stack([np.ascontiguousarray(res.results[b]["outT"].T) for b in range(B)], axis=0)
    return out.astype(np.float32), res


def kernel(**inputs):
    out, _ = run_model(inputs, 4, 4096, 2048, 4)
    return out
```

```python
import math
from contextlib import ExitStack
import numpy as np
import concourse.bass as bass
import concourse.mybir as mybir
from concourse.bass_utils import run_bass_kernel_spmd

F32 = mybir.dt.float32
BF16 = mybir.dt.bfloat16
AF = mybir.ActivationFunctionType
ALU = mybir.AluOpType

D = 2048
NK = 16
MIXW = 1024
FH = 5632
NJ = 44
TWO_PI = float(2 * math.pi)
MAGIC = 12582912.0


class Buf:
    __slots__ = ("name", "w", "r", "dsem")

    def __init__(self, name):
        self.name = name
        self.w = None
        self.r = {}
        self.dsem = None


class Eng:
    def __init__(self, name, sem):
        self.name = name
        self.sem = sem
        self.cnt = 0
        self.waited = {}
        self.prog = []


class KB:
    def __init__(self, nc):
        self.nc = nc
        self.stack = ExitStack()
        self.sems = {}
        self.semcnt = {}
        self.free_dsems = []
        self.dirty = {}
        self.eng = {}
        for name in ("pe", "act", "dve", "pool", "sp"):
            self.eng[name] = Eng(name, self.newsem("e_" + name))
        self.uid = 0

    def newsem(self, name):
        h = self.stack.enter_context(self.nc.semaphore(name))
        key = len(self.sems)
        self.sems[key] = h
        self.semcnt[key] = 0
        return key

    def buf(self, name="b"):
        self.uid += 1
        return Buf(f"{name}_{self.uid}")

    def _deps(self, reads, writes):
        deps = {}
        for b in reads:
            if b.w is not None and deps.get(b.w[0], 0) < b.w[1]:
                deps[b.w[0]] = b.w[1]
        for b in writes:
            if b.w is not None and deps.get(b.w[0], 0) < b.w[1]:
                deps[b.w[0]] = b.w[1]
            for k, v in b.r.items():
                if deps.get(k, 0) < v:
                    deps[k] = v
        return deps

    def _emit_waits(self, e, deps, skip=None):
        for k, v in deps.items():
            if k == skip:
                continue
            if e.waited.get(k, 0) < v:
                e.prog.append(("w", k, v))
                e.waited[k] = v

    def _update(self, tok, reads, writes):
        k, v = tok
        for b in reads:
            if b.r.get(k, 0) < v:
                b.r[k] = v
        for b in writes:
            b.w = tok
            b.r = {}

    def op(self, en, fns, reads=(), writes=()):
        e = self.eng[en]
        if callable(fns):
            fns = [fns]
        deps = self._deps(reads, writes)
        self._emit_waits(e, deps, skip=e.sem if en == "pe" else None)
        for f in fns[:-1]:
            e.prog.append(("i", f, None, 0))
        e.cnt += 1
        e.prog.append(("i", fns[-1], e.sem, 1))
        tok = (e.sem, e.cnt)
        self._update(tok, reads, writes)
        return tok

    def dma(self, en, fns, reads=(), writes=(), owner=None):
        e = self.eng[en]
        if callable(fns):
            fns = [fns]
        if owner.dsem is None:
            owner.dsem = self.free_dsems.pop() if self.free_dsems else self.newsem("d%d" % len(self.sems))
        k = owner.dsem
        deps = self._deps(reads, writes)
        self._emit_waits(e, deps)
        for f in fns:
            self.semcnt[k] += 16
            e.prog.append(("i", f, k, 16))
        tok = (k, self.semcnt[k])
        self.dirty[k] = self.semcnt[k]
        self._update(tok, reads, writes)
        return tok

    def barrier(self):
        toks = dict(self.dirty)
        for e in self.eng.values():
            if e.cnt > 0:
                toks[e.sem] = e.cnt
        for e in self.eng.values():
            self._emit_waits(e, toks)
        self.dirty = {}

    def release(self, bufs):
        for b in bufs:
            if b.dsem is not None:
                self.free_dsems.append(b.dsem)
                b.dsem = None

    def finalize(self):
        nc = self.nc
        self.barrier()
        engs, sems = self.eng, self.sems

        def replay(e, h):
            for it in e.prog:
                if it[0] == "w":
                    h.wait_ge(sems[it[1]], it[2])
                else:
                    inst = it[1](h)
                    if it[2] is not None:
                        inst.then_inc(sems[it[2]], it[3])

        with nc.Block() as block:
            @block.tensor
            def _(h):
                replay(engs["pe"], h)

            @block.scalar
            def _(h):
                replay(engs["act"], h)

            @block.vector
            def _(h):
                replay(engs["dve"], h)

            @block.gpsimd
            def _(h):
                replay(engs["pool"], h)

            @block.sync
            def _(h):
                replay(engs["sp"], h)
        self.stack.close()


class Phase:
    def __init__(self, kb):
        self.kb = kb
        self.st = ExitStack()
        self.bufs = []

    def tile(self, name, shape, dtype):
        kb = self.kb
        kb.uid += 1
        t = self.st.enter_context(kb.nc.sbuf_tensor(f"{name}_{kb.uid}", list(shape), dtype))
        b = kb.buf(name)
        self.bufs.append(b)
        return t, b

    def ring(self, name, n, shape, dtype):
        return Ring([self.tile(name, shape, dtype) for _ in range(n)])

    def close(self):
        self.kb.barrier()
        self.kb.release(self.bufs)
        self.st.close()


class Ring:
    def __init__(self, items):
        self.items = items
        self.i = 0

    def next(self):
        it = self.items[self.i % len(self.items)]
        self.i += 1
        return it


def col_layout(L):
    segs = [("gmix", L * 16), ("gffn", L * 16), ("gfin", 16), ("gbias", L * 48), ("convw", L * 32),
            ("convb", L * 8), ("ba", L * 8), ("bx", L * 8), ("lam", L * 8), ("ssmd", L * 8),
            ("acre", L * 32), ("acim", L * 32), ("lsc", L * 32), ("iota", 1)]
    off, o = {}, 0
    for n, w in segs:
        off[n] = o
        o += w
    return off, o


def build_program(L, S, ST, dbg=False, has_prev=False, skip=()):
    nc = bass.Bass("TRN2", target_bir_lowering=False)
    kb = KB(nc)
    gst = kb.stack
    nST = S // ST
    nTT = ST // 512
    COFF, NCOL = col_layout(L)
    okind = "ExternalOutput" if dbg else "Internal"

    def din(name, shape, dt=F32):
        return nc.dram_tensor(name, list(shape), dt, kind="ExternalInput").ap()

    def dscr(name, shape, dt):
        return nc.dram_tensor(name, list(shape), dt, kind=okind).ap()

    xT = din("xT", [D, S])
    w_in_t = din("w_in_t", [L, 96, 128, 16, 128])
    wbr_t = din("wbr_t", [L, 16, 128, 4, 8, 128])
    wout_t = din("wout_t", [L, 16, 128, 16, 128])
    wgu_t = din("wgu_t", [L, NJ, 128, 2, 16, 128])
    wdn_t = din("wdn_t", [L, 16, 128, NJ, 128])
    colp_d = din("colp", [128, NCOL])
    rowp_d = din("rowp", [L, 3, 4096])
    lruw_d = din("lruw", [L, 128, 8, 2, 128])
    abias_d = din("abias", [L, 8, 128, 640])
    bbd_d = din("bbd", [L, 128, 2, 4096])
    ct_d = din("ctd", [L, 128, 8192])
    cst_d = din("cst", [128, 386])
    eps_d = din("epsd", [128, 1])
    outT = nc.dram_tensor("outT", [D, S], F32, kind="ExternalOutput").ap()

    xr = dscr("xr", [D, S], F32)
    lxT = dscr("lxT", [MIXW, S], BF16)
    lgT = dscr("lgT", [MIXW, ST], BF16)
    qT = dscr("qT", [MIXW, ST], BF16)
    kT = dscr("kT", [MIXW, S], BF16)
    vS = dscr("vS", [S, MIXW], BF16)
    uT = dscr("uT", [MIXW, ST], BF16)
    gT = dscr("gT", [3 * D, ST], BF16)
    yaT = dscr("yaT", [MIXW, ST], BF16)
    ybT = dscr("ybT", [MIXW, ST], BF16)
    ycT = dscr("ycT", [MIXW, ST], BF16)
    mT = dscr("mT", [D, ST], BF16)
    actT = dscr("actT", [FH, ST], BF16)
    tabE = nc.dram_tensor("tabE", [2, 128, 4096], F32, kind="Internal").ap()
    tabF = nc.dram_tensor("tabF", [2, 128, 32 * 129], F32, kind="Internal").ap()
    tabB = nc.dram_tensor("tabB", [2, 128, 4096], BF16, kind="Internal").ap()

    def gtile(name, shape, dt):
        t = gst.enter_context(nc.sbuf_tensor(name, list(shape), dt))
        return t, kb.buf(name)

    colp, b_colp = gtile("colp_s", [128, NCOL], F32)
    nsp8, b_nsp8 = gtile("nsp8", [128, L * 8], F32)
    ones_f, b_onesf = gtile("ones_f", [128, 128], F32)
    ones_b, b_onesb = gtile("ones_b", [128, 128], BF16)
    tri_b, b_trib = gtile("tri_b", [128, 128], BF16)
    ntri_b, b_ntrib = gtile("ntri_b", [128, 128], BF16)
    iorow, b_iorow = gtile("iorow", [128, 129], F32)
    lru_h, b_lruh = gtile("lru_h", [128, 8], F32)
    s5ca, b_s5ca = gtile("s5ca", [128, 32], F32)
    s5cb, b_s5cb = gtile("s5cb", [128, 32], F32)
    epsc, b_epsc = gtile("epsc", [128, 1], F32)
    psum = gst.enter_context(nc.psum_tensor("psum", [128, 4096], F32))
    pb = [kb.buf(f"ps{i}") for i in range(8)]

    def bank(i):
        return psum[:, i * 512:(i + 1) * 512]

    def col(name, idx, n=1):
        o = COFF[name] + idx
        return colp[:, o:o + n]

    def I(name, **kw):
        return lambda h: getattr(h, name)(**kw)

    def MM(out, lhsT, rhs, start=True, stop=True):
        return I("matmul", out=out, lhsT=lhsT, rhs=rhs, start=start, stop=stop)

    def DMA(out, in_):
        return I("dma_start", out=out, in_=in_)

    def V(name, r, w, **kw):
        return kb.op("dve", I(name, **kw), reads=r, writes=w)

    def A(r, w, out, in_, func, **kw):
        return kb.op("act", I("activation", out=out, in_=in_, func=func, **kw), reads=r, writes=w)

    def TT(r, w, out, in0, in1, op):
        return V("tensor_tensor", r, w, out=out, in0=in0, in1=in1, op=op)

    def TS(r, w, out, in0, s1, s2, op0, op1=None):
        if op1 is None:
            return V("tensor_scalar", r, w, out=out, in0=in0, scalar1=s1, scalar2=None, op0=op0)
        return V("tensor_scalar", r, w, out=out, in0=in0, scalar1=s1, scalar2=s2, op0=op0, op1=op1)

    def STT(r, w, out, in0, scalar, in1, op0, op1):
        return V("scalar_tensor_tensor", r, w, out=out, in0=in0, scalar=scalar, in1=in1, op0=op0, op1=op1)

    kb.dma("sp", DMA(colp[:], colp_d), writes=[b_colp], owner=b_colp)
    kb.dma("sp", DMA(ones_f[:], cst_d[:, 128:256]), writes=[b_onesf], owner=b_onesf)
    kb.dma("sp", DMA(iorow[:], cst_d[:, 256:385]), writes=[b_iorow], owner=b_iorow)
    kb.dma("pool", DMA(tri_b[:], cst_d[:, 0:128]), writes=[b_trib], owner=b_trib)
    kb.dma("pool", DMA(ones_b[:], cst_d[:, 128:256]), writes=[b_onesb], owner=b_onesb)
    TS([b_trib], [b_ntrib], ntri_b[:], tri_b[:], -1.0, None, ALU.mult)
    kb.dma("sp", DMA(epsc[:], eps_d), writes=[b_epsc], owner=b_epsc)
    A([b_colp], [b_nsp8], nsp8[:], col("lam", 0, L * 8), AF.Exp, scale=-1.0)
    A([b_nsp8], [b_nsp8], nsp8[:], nsp8[:], AF.Ln, bias=1.0, scale=1.0)
    TS([b_nsp8], [b_nsp8], nsp8[:], nsp8[:], -8.0, None, ALU.mult)
    b_x0 = kb.buf("x0")
    kb.dma("sp", [DMA(xr[c * 128:(c + 1) * 128, :], xT[c * 128:(c + 1) * 128, :]) for c in range(16)], writes=[b_x0], owner=b_x0)
    kb.barrier()

    xr_v = xr.rearrange("(k p) t -> p k t", p=128)

    def rms_norm_tile(xin, b_xin, tg, sq_ring, rs_ring, pbi):
        kb.dma("sp", DMA(xin[:], xr_v[:, :, tg:tg + 512]), writes=[b_xin], owner=b_xin)
        for k in range(NK):
            sq, b_sq = sq_ring.next()
            A([b_xin], [b_sq], sq[:], xin[:, k, :], AF.Square)
            kb.op("pe", MM(bank(pbi), ones_f[:], sq[:], start=(k == 0), stop=(k == NK - 1)), reads=[b_sq, b_onesf], writes=[pb[pbi]])
        rs, b_rs = rs_ring.next()
        A([pb[pbi], b_epsc], [b_rs], rs[:], bank(pbi), AF.Sqrt, scale=1.0 / D, bias=epsc[:])
        V("reciprocal", [b_rs], [b_rs], out=rs[:], in_=rs[:])
        return rs, b_rs

    def rms_norm_to(hT, b_hT, gname, l, t0, xin, b_xin, sq_ring, rs_ring, pbi):
        for tt in range(nTT):
            rs, b_rs = rms_norm_tile(xin, b_xin, t0 + tt * 512, sq_ring, rs_ring, pbi)
            for k in range(NK):
                STT([b_xin, b_rs, b_colp], [b_hT], hT[:, k, tt * 512:(tt + 1) * 512], xin[:, k, :], col(gname, l * 16 + k), rs[:], ALU.mult, ALU.mult)

    evac_flip = [0]

    def evac_copy(out_ap, in_ap, reads, writes):
        evac_flip[0] ^= 1
        if evac_flip[0]:
            A(reads, writes, out_ap, in_ap, AF.Copy)
        else:
            V("tensor_copy", reads, writes, out=out_ap, in_=in_ap)

    def proj_residual(l, t0, src_dram, nk, w_dram_l, chunk_tt):
        srcv = src_dram.rearrange("(k p) t -> p k t", p=128)
        splits = [(a, min(a + 16, nk)) for a in range(0, nk, 16)]
        for c0 in range(0, nTT, chunk_tt):
            ph = Phase(kb)
            ncols = chunk_tt * 512
            sm, b_sm = ph.tile("sm", [128, nk, ncols], BF16)
            kb.dma("sp", [DMA(sm[:, a:b, :], srcv[:, a:b, c0 * 512:c0 * 512 + ncols]) for a, b in splits], writes=[b_sm], owner=b_sm)
            wo_ring = ph.ring("wo", 2, [128, nk, 128], BF16)
            xs_ring = ph.ring("xs", 3, [128, 512], F32)
            it = 0
            for f in range(16):
                wo, b_wo = wo_ring.next()
                kb.dma("pool", [DMA(wo[:, a:b, :], w_dram_l[f, :, a:b, :]) for a, b in splits], writes=[b_wo], owner=b_wo)
                for tt in range(chunk_tt):
                    bi = it % 6
                    it += 1
                    tg = t0 + (c0 + tt) * 512
                    xs, b_xs = xs_ring.next()
                    kb.dma("sp", DMA(xs[:], xr[f * 128:(f + 1) * 128, tg:tg + 512]), writes=[b_xs], owner=b_xs)
                    kb.op("pe", [MM(bank(bi), wo[:, k, :], sm[:, k, tt * 512:(tt + 1) * 512], start=(k == 0), stop=(k == nk - 1)) for k in range(nk)],
                          reads=[b_wo, b_sm], writes=[pb[bi]])
                    TT([pb[bi], b_xs], [b_xs], xs[:], bank(bi), xs[:], ALU.add)
                    kb.dma("sp", DMA(xr[f * 128:(f + 1) * 128, tg:tg + 512], xs[:]), reads=[b_xs], owner=b_xs)
            ph.close()

    def v3(ap):
        return ap.rearrange("p (b j) -> p b j", b=4)

    for l in range(L):
        for s in range(nST):
            t0 = s * ST
            first = (s == 0) and not has_prev
            if "p1" not in skip:
                ph = Phase(kb)
                hT, b_hT = ph.tile("hT", [128, NK, ST], BF16)
                xin, b_xin = ph.tile("xin", [128, NK, 512], F32)
                sq_ring = ph.ring("sq", 2, [128, 512], F32)
                rs_ring = ph.ring("rs", 2, [128, 512], F32)
                wt_ring = ph.ring("wt", 3, [128, NK, 128], BF16)
                stg_ring = ph.ring("stg", 3, [128, ST], BF16)
                wv_ring = ph.ring("wv", 1, [128, NK, 512], BF16)
                sv_ring = ph.ring("sv", 3, [128, 512], BF16)
                rms_norm_to(hT, b_hT, "gmix", l, t0, xin, b_xin, sq_ring, rs_ring, 7)
                pbr = 0
                for m in list(range(0, 32)) + list(range(40, 96)):
                    wt, b_wt = wt_ring.next()
                    kb.dma("pool", DMA(wt[:], w_in_t[l, m]), writes=[b_wt], owner=b_wt)
                    stg, b_stg = stg_ring.next()
                    for tt in range(nTT):
                        bi = pbr % 6
                        pbr += 1
                        kb.op("pe", [MM(bank(bi), wt[:, k, :], hT[:, k, tt * 512:(tt + 1) * 512], start=(k == 0), stop=(k == NK - 1)) for k in range(NK)],
                              reads=[b_wt, b_hT], writes=[pb[bi]])
                        o_ap = stg[:, tt * 512:(tt + 1) * 512]
                        if m < 48:
                            evac_copy(o_ap, bank(bi), [pb[bi]], [b_stg])
                        else:
                            A([pb[bi], b_colp], [b_stg], o_ap, bank(bi), AF.Sigmoid, bias=col("gbias", l * 48 + (m - 48)), scale=1.0)
                    if m < 8:
                        dst = lxT[m * 128:(m + 1) * 128, t0:t0 + ST]
                    elif m < 16:
                        dst = lgT[(m - 8) * 128:(m - 7) * 128, :]
                    elif m < 24:
                        dst = qT[(m - 16) * 128:(m - 15) * 128, :]
                    elif m < 32:
                        dst = kT[(m - 24) * 128:(m - 23) * 128, t0:t0 + ST]
                    elif m < 48:
                        dst = uT[(m - 40) * 128:(m - 39) * 128, :]
                    else:
                        dst = gT[(m - 48) * 128:(m - 47) * 128, :]
                    kb.dma("sp", DMA(dst, stg[:]), reads=[b_stg], owner=b_stg)
                for cb in range(2):
                    wv, b_wv = wv_ring.next()
                    kb.dma("pool", [DMA(wv[:, :, mm * 128:(mm + 1) * 128], w_in_t[l, 32 + 4 * cb + mm]) for mm in range(4)], writes=[b_wv], owner=b_wv)
                    for j in range(ST // 128):
                        bi = pbr % 6
                        pbr += 1
                        kb.op("pe", [MM(bank(bi), hT[:, k, j * 128:(j + 1) * 128], wv[:, k, :], start=(k == 0), stop=(k == NK - 1)) for k in range(NK)],
                              reads=[b_wv, b_hT], writes=[pb[bi]])
                        sv, b_sv = sv_ring.next()
                        evac_copy(sv[:], bank(bi), [pb[bi]], [b_sv])
                        kb.dma("sp", DMA(vS[t0 + j * 128:t0 + (j + 1) * 128, cb * 512:(cb + 1) * 512], sv[:]), reads=[b_sv], owner=b_sv)
                ph.close()

            if "p2a" not in skip:
                ph = Phase(kb)
                bd, b_bd = ph.tile("bd", [128, 8, 2, 128], BF16)
                kb.dma("pool", DMA(bd[:], lruw_d[l]), writes=[b_bd], owner=b_bd)
                lx_ring = ph.ring("lx", 3, [128, 515], BF16)
                lg_ring = ph.ring("lg", 3, [128, 512], BF16)
                ya_ring = ph.ring("ya", 2, [128, 512], BF16)
                hs_ring = ph.ring("hs", 2, [128, 512], F32)
                xc_ring = ph.ring("xc", 2, [128, 512], F32)
                xcb_ring = ph.ring("xcb", 2, [128, 512], BF16)
                rr, b_rr = ph.tile("rr", [128, 512], F32)
                ii, b_ii = ph.tile("ii", [128, 512], F32)
                aa, b_aa = ph.tile("aa", [128, 512], F32)
                a2, b_a2 = ph.tile("a2", [128, 512], F32)
                gg, b_gg = ph.tile("gg", [128, 512], F32)
                if first:
                    V("memset", [], [b_lruh], ap=lru_h[:], constant=0.0)
                lunits = [(ct, tt) for ct in range(8) for tt in range(nTT)]
                lst = {}

                def lru_load(n):
                    ct, tt = lunits[n]
                    r0, r1 = ct * 128, (ct + 1) * 128
                    tg = t0 + tt * 512
                    lx, b_lx = lx_ring.next()
                    lg, b_lg = lg_ring.next()
                    if tg == 0 and not has_prev:
                        V("memset", [], [b_lx], ap=lx[:, 0:3], constant=0.0)
                        kb.dma("sp", DMA(lx[:, 3:515], lxT[r0:r1, 0:512]), writes=[b_lx], owner=b_lx)
                    else:
                        kb.dma("sp", DMA(lx[:], lxT[r0:r1, tg - 3:tg + 512]), writes=[b_lx], owner=b_lx)
                    kb.dma("sp", DMA(lg[:], lgT[r0:r1, tt * 512:(tt + 1) * 512]), writes=[b_lg], owner=b_lg)
                    lst[n] = dict(lx=lx, b_lx=b_lx, lg=lg, b_lg=b_lg)

                def lru_front(n):
                    ct, tt = lunits[n]
                    d = lst[n]
                    lx, b_lx = d["lx"], d["b_lx"]
                    xc, b_xc = xc_ring.next()
                    xcb, b_xcb = xcb_ring.next()
                    cwi = (l * 8 + ct) * 4
                    TS([b_lx, b_colp], [b_xc], xc[:], lx[:, 3:515], col("convw", cwi + 3), col("convb", l * 8 + ct), ALU.mult, ALU.add)
                    for k in range(3):
                        STT([b_lx, b_xc, b_colp], [b_xc], xc[:], lx[:, k:k + 512], col("convw", cwi + k), xc[:], ALU.mult, ALU.add)
                    V("tensor_copy", [b_xc], [b_xcb], out=xcb[:], in_=xc[:])
                    pbase = (n % 2) * 2
                    kb.op("pe", MM(bank(pbase), bd[:, ct, 0, :], xcb[:]), reads=[b_bd, b_xcb], writes=[pb[pbase]])
                    kb.op("pe", MM(bank(pbase + 1), bd[:, ct, 1, :], xcb[:]), reads=[b_bd, b_xcb], writes=[pb[pbase + 1]])
                    d.update(xc=xc, b_xc=b_xc, pbase=pbase)

                lru_load(0)
                if len(lunits) > 1:
                    lru_load(1)
                lru_front(0)
                for n in range(len(lunits)):
                    ct, tt = lunits[n]
                    r0, r1 = ct * 128, (ct + 1) * 128
                    d = lst[n]
                    xc, b_xc, pbase = d["xc"], d["b_xc"], d["pbase"]
                    lg, b_lg = d["lg"], d["b_lg"]
                    if n + 2 < len(lunits):
                        lru_load(n + 2)
                    A([pb[pbase], b_colp], [b_rr], rr[:], bank(pbase), AF.Sigmoid, bias=col("ba", l * 8 + ct), scale=1.0)
                    A([pb[pbase + 1], b_colp], [b_ii], ii[:], bank(pbase + 1), AF.Sigmoid, bias=col("bx", l * 8 + ct), scale=1.0)
                    A([b_rr, b_nsp8], [b_aa], aa[:], rr[:], AF.Exp, scale=nsp8[:, l * 8 + ct:l * 8 + ct + 1])
                    TT([b_ii, b_xc], [b_ii], ii[:], ii[:], xc[:], ALU.mult)
                    TT([b_aa], [b_a2], a2[:], aa[:], aa[:], ALU.mult)
                    A([b_a2], [b_a2], a2[:], a2[:], AF.Sqrt, scale=-1.0, bias=1.0)
                    if n + 1 < len(lunits):
                        lru_front(n + 1)
                    A([b_lg], [b_gg], gg[:], lg[:], AF.Gelu_apprx_tanh)
                    TT([b_ii, b_a2], [b_ii], ii[:], ii[:], a2[:], ALU.mult)
                    hs, b_hs = hs_ring.next()
                    V("tensor_tensor_scan", [b_aa, b_ii, b_lruh], [b_hs], out=hs[:], data0=aa[:], data1=ii[:], initial=lru_h[:, ct:ct + 1], op0=ALU.mult, op1=ALU.add)
                    V("tensor_copy", [b_hs], [b_lruh], out=lru_h[:, ct:ct + 1], in_=hs[:, 511:512])
                    ya, b_ya = ya_ring.next()
                    TT([b_hs, b_gg], [b_ya], ya[:], hs[:], gg[:], ALU.mult)
                    kb.dma("sp", DMA(yaT[r0:r1, tt * 512:(tt + 1) * 512], ya[:]), reads=[b_ya], owner=b_ya)
                    del lst[n]
                ph.close()

            if "p2b" not in skip:
                ph = Phase(kb)
                NKT = (ST + 512) // 128
                qh_ring = ph.ring("qh", 2, [128, ST], BF16)
                kh_ring = ph.ring("kh", 2, [128, ST + 512], BF16)
                vh_ring = ph.ring("vh", 2, [128, NKT, 128], BF16)
                bh_ring = ph.ring("bh", 2, [128, 640], F32)
                yb_ring = ph.ring("yb", 2, [128, ST], BF16)
                sf_ring = ph.ring("sf", 2, [128, 640], F32)
                pt_ring = ph.ring("pt", 3, [128, 640], BF16)
                rc_ring = ph.ring("rc", 2, [128, 128], F32)
                NQ = ST // 128
                aunits = [(hd, m) for hd in range(8) for m in range(NQ)]
                NU = len(aunits)
                hst, ust = {}, {}

                def att_load(hd):
                    r0, r1 = hd * 128, (hd + 1) * 128
                    qh, b_qh = qh_ring.next()
                    kh, b_kh = kh_ring.next()
                    vh, b_vh = vh_ring.next()
                    bh, b_bh = bh_ring.next()
                    yb, b_yb = yb_ring.next()
                    kb.dma("sp", DMA(qh[:], qT[r0:r1, :]), writes=[b_qh], owner=b_qh)
                    kb.dma("sp", DMA(bh[:], abias_d[l, hd]), writes=[b_bh], owner=b_bh)
                    if first:
                        kb.dma("sp", DMA(kh[:, 512:], kT[r0:r1, 0:ST]), writes=[b_kh], owner=b_kh)
                        kb.dma("sp", DMA(vh[:, 4:, :], vS[0:ST, r0:r1].rearrange("(n p) d -> p n d", p=128)), writes=[b_vh], owner=b_vh)
                    else:
                        kb.dma("sp", DMA(kh[:], kT[r0:r1, t0 - 512:t0 + ST]), writes=[b_kh], owner=b_kh)
                        kb.dma("sp", DMA(vh[:], vS[t0 - 512:t0 + ST, r0:r1].rearrange("(n p) d -> p n d", p=128)), writes=[b_vh], owner=b_vh)
                    hst[hd] = dict(qh=qh, b_qh=b_qh, kh=kh, b_kh=b_kh, vh=vh, b_vh=b_vh, bh=bh, b_bh=b_bh, yb=yb, b_yb=b_yb)

                def att_scores(n):
                    hd, m = aunits[n]
                    if hd not in hst:
                        att_load(hd)
                    H = hst[hd]
                    kts = [kt for kt in range(5) if (not first) or (m - 4 + kt) >= 0]
                    sset = n % 3
                    sb = 2 * sset
                    sc = psum[:, sb * 512:sb * 512 + 640]
                    kb.op("pe", [MM(sc[:, kt * 128:(kt + 1) * 128], H["kh"][:, (m + kt) * 128:(m + kt + 1) * 128], H["qh"][:, m * 128:(m + 1) * 128]) for kt in kts],
                          reads=[H["b_kh"], H["b_qh"]], writes=[pb[sb], pb[sb + 1]])
                    ust[n] = dict(kts=kts, sb=sb, sc=sc, lo=kts[0] * 128)

                def att_sf_exp(n):
                    hd, m = aunits[n]
                    H, U = hst[hd], ust[n]
                    sf, b_sf = sf_ring.next()
                    pt, b_pt = pt_ring.next()
                    lo, hi, sb, sc = U["lo"], 640, U["sb"], U["sc"]
                    STT([pb[sb], pb[sb + 1], H["b_bh"]], [b_sf], sf[:, lo:hi], sc[:, lo:hi], float(128 ** -0.5), H["bh"][:, lo:hi], ALU.mult, ALU.add)
                    A([b_sf], [b_pt], pt[:, lo:hi], sf[:, lo:hi], AF.Exp)
                    U.update(pt=pt, b_pt=b_pt)

                def att_pv(n):
                    hd, m = aunits[n]
                    H, U = hst[hd], ust[n]
                    kts, pt = U["kts"], U["pt"]
                    pbk = 6 + (n % 2)
                    nk = len(kts)
                    fns = [MM(bank(pbk)[:, 0:128], H["vh"][:, m + kt, :], pt[:, kt * 128:(kt + 1) * 128], start=(i_ == 0), stop=(i_ == nk - 1)) for i_, kt in enumerate(kts)]
                    fns += [MM(bank(pbk)[:, 128:256], ones_b[:], pt[:, kt * 128:(kt + 1) * 128], start=(i_ == 0), stop=(i_ == nk - 1)) for i_, kt in enumerate(kts)]
                    kb.op("pe", fns, reads=[H["b_vh"], U["b_pt"], b_onesb], writes=[pb[pbk]])
                    U["pbk"] = pbk

                def att_fin(n):
                    hd, m = aunits[n]
                    H, U = hst[hd], ust[n]
                    pbk = U["pbk"]
                    rc, b_rc = rc_ring.next()
                    V("reciprocal", [pb[pbk]], [b_rc], out=rc[:], in_=bank(pbk)[:, 128:256])
                    TT([pb[pbk], b_rc], [H["b_yb"]], H["yb"][:, m * 128:(m + 1) * 128], bank(pbk)[:, 0:128], rc[:], ALU.mult)
                    if m == NQ - 1:
                        kb.dma("sp", DMA(ybT[hd * 128:(hd + 1) * 128, :], H["yb"][:]), reads=[H["b_yb"]], owner=H["b_yb"])
                    del ust[n]

                att_scores(0)
                if NU > 1:
                    att_scores(1)
                att_sf_exp(0)
                for n in range(NU):
                    if n + 2 < NU:
                        att_scores(n + 2)
                    att_pv(n)
                    if n + 1 < NU:
                        att_sf_exp(n + 1)
                    if n >= 1:
                        att_fin(n - 1)
                att_fin(NU - 1)
                ph.close()


            if "p2c" not in skip:
                ph = Phase(kb)
                W = 4096
                NF = 32 * 129
                E_re, b_Ere = ph.tile("E_re", [128, W], F32)
                E_im, b_Eim = ph.tile("E_im", [128, W], F32)
                F_re, b_Fre = ph.tile("F_re", [128, 32, 129], F32)
                F_im, b_Fim = ph.tile("F_im", [128, 32, 129], F32)
                bbr, b_bbr = ph.tile("bbr", [128, W], BF16)
                bbi, b_bbi = ph.tile("bbi", [128, W], BF16)
                Ffr = F_re[:].rearrange("p b j -> p (b j)")
                Ffi = F_im[:].rearrange("p b j -> p (b j)")
                MUL, ADD, SUB = ALU.mult, ALU.add, ALU.subtract
                if s == 0:
                    pp = Phase(kb)
                    ar, b_ar = pp.tile("ar", [128, W], F32)
                    ai, b_ai = pp.tile("ai", [128, W], F32)
                    sr, b_sr = pp.tile("sr", [128, W], F32)
                    t1f, b_t1 = pp.tile("t1", [128, NF], F32)
                    t2f, b_t2 = pp.tile("t2", [128, NF], F32)
                    t3f, b_t3 = pp.tile("t3", [128, NF], F32)
                    t1, t2, t3 = t1f[:, 0:W], t2f[:, 0:W], t3f[:, 0:W]
                    bdr, b_bdr = Ffr[:, 0:W], b_Fre
                    bdi, b_bdi = Ffi[:, 0:W], b_Fim
                    kb.dma("sp", DMA(ar[:], rowp_d[l, 0:1, :].to_broadcast([128, W])), writes=[b_ar], owner=b_ar)
                    kb.dma("sp", DMA(ai[:], rowp_d[l, 1:2, :].to_broadcast([128, W])), writes=[b_ai], owner=b_ai)
                    kb.dma("sp", DMA(sr[:], rowp_d[l, 2:3, :].to_broadcast([128, W])), writes=[b_sr], owner=b_sr)
                    kb.dma("sp", DMA(bdr, bbd_d[l, :, 0, :]), writes=[b_bdr], owner=b_bdr)
                    kb.dma("sp", DMA(bdi, bbd_d[l, :, 1, :]), writes=[b_bdi], owner=b_bdi)

                    def sincos(dst_sin, b_ds, dst_cos, b_dc, ycyc, b_y, tmp, b_tmp):
                        TS([b_y], [b_tmp], tmp, ycyc, MAGIC, MAGIC, ALU.add, ALU.subtract)
                        TT([b_y, b_tmp], [b_tmp], tmp, ycyc, tmp, ALU.subtract)
                        A([b_tmp], [b_ds], dst_sin, tmp, AF.Sin, scale=TWO_PI)
                        TS([b_y], [b_y], ycyc, ycyc, 0.25, None, ALU.add)
                        TS([b_y], [b_tmp], tmp, ycyc, MAGIC, MAGIC, ALU.add, ALU.subtract)
                        TT([b_y, b_tmp], [b_tmp], tmp, ycyc, tmp, ALU.subtract)
                        A([b_tmp], [b_dc], dst_cos, tmp, AF.Sin, scale=TWO_PI)

                    MUL, ADD, SUB = ALU.mult, ALU.add, ALU.subtract
                    A([b_sr], [b_sr], sr[:], sr[:], AF.Exp)
                    TT([b_ar, b_sr], [b_t1], t1, ar[:], sr[:], MUL)
                    A([b_t1], [b_t1], t1, t1, AF.Exp)
                    TT([b_ai, b_sr], [b_t2], t2, ai[:], sr[:], MUL)
                    TS([b_t2], [b_t2], t2, t2, 1.0 / TWO_PI, None, MUL)
                    sincos(E_im[:], b_Eim, E_re[:], b_Ere, t2, b_t2, t3, b_t3)
                    TT([b_Ere, b_t1], [b_Ere], E_re[:], E_re[:], t1, MUL)
                    TT([b_Eim, b_t1], [b_Eim], E_im[:], E_im[:], t1, MUL)
                    TS([b_Ere], [b_Ere], E_re[:], E_re[:], -1.0, None, ADD)
                    TT([b_ar], [b_t1], t1, ar[:], ar[:], MUL)
                    TT([b_ai], [b_t2], t2, ai[:], ai[:], MUL)
                    TT([b_t1, b_t2], [b_t1], t1, t1, t2, ADD)
                    V("reciprocal", [b_t1], [b_t1], out=t1, in_=t1)
                    TT([b_Ere, b_ar], [b_t2], t2, E_re[:], ar[:], MUL)
                    TT([b_Eim, b_ai], [b_t3], t3, E_im[:], ai[:], MUL)
                    TT([b_t2, b_t3], [b_t2], t2, t2, t3, ADD)
                    TT([b_t2, b_t1], [b_t2], t2, t2, t1, MUL)
                    TT([b_Eim, b_ar], [b_t3], t3, E_im[:], ar[:], MUL)
                    TT([b_Ere, b_ai], [b_Eim], E_im[:], E_re[:], ai[:], MUL)
                    TT([b_t3, b_Eim], [b_t3], t3, t3, E_im[:], SUB)
                    TT([b_t3, b_t1], [b_t3], t3, t3, t1, MUL)
                    TT([b_t2, b_bdr], [b_Ere], E_re[:], t2, bdr, MUL)
                    TT([b_t3, b_bdi], [b_Eim], E_im[:], t3, bdi, MUL)
                    TT([b_Ere, b_Eim], [b_bbr], bbr[:], E_re[:], E_im[:], SUB)
                    TT([b_t2, b_bdi], [b_Ere], E_re[:], t2, bdi, MUL)
                    TT([b_t3, b_bdr], [b_Eim], E_im[:], t3, bdr, MUL)
                    TT([b_Ere, b_Eim], [b_bbi], bbi[:], E_re[:], E_im[:], ADD)
                    iocol = col("iota", 0)
                    TT([b_ar, b_sr], [b_t1], t1, ar[:], sr[:], MUL)
                    TS([b_t1, b_colp], [b_t1], t1, t1, iocol, -1.0, MUL, MUL)
                    A([b_t1], [b_t1], t1, t1, AF.Exp)
                    TT([b_ai, b_sr], [b_t2], t2, ai[:], sr[:], MUL)
                    TS([b_t2, b_colp], [b_t2], t2, t2, iocol, -1.0 / TWO_PI, MUL, MUL)
                    sincos(E_im[:], b_Eim, E_re[:], b_Ere, t2, b_t2, t3, b_t3)
                    TT([b_Ere, b_t1], [b_Ere], E_re[:], E_re[:], t1, MUL)
                    TT([b_Eim, b_t1], [b_Eim], E_im[:], E_im[:], t1, MUL)
                    stc, b_stc = pp.tile("stc", [128, 32], F32)
                    alc, b_alc = pp.tile("alc", [128, 32], F32)
                    thc, b_thc = pp.tile("thc", [128, 32], F32)
                    A([b_colp], [b_stc], stc[:], col("lsc", l * 32, 32), AF.Exp)
                    TT([b_colp, b_stc], [b_alc], alc[:], col("acre", l * 32, 32), stc[:], MUL)
                    TT([b_colp, b_stc], [b_thc], thc[:], col("acim", l * 32, 32), stc[:], MUL)
                    TS([b_thc], [b_thc], thc[:], thc[:], 1.0 / TWO_PI, None, MUL)
                    g1 = t1f[:].rearrange("p (b j) -> p b j", b=32)
                    g2 = t2f[:].rearrange("p (b j) -> p b j", b=32)
                    io_b = iorow[:].unsqueeze(1).to_broadcast([128, 32, 129])
                    TT([b_iorow, b_alc], [b_t1], g1, io_b, alc[:].unsqueeze(2).to_broadcast([128, 32, 129]), MUL)
                    A([b_t1], [b_t1], t1f[:], t1f[:], AF.Exp)
                    TT([b_iorow, b_thc], [b_t2], g2, io_b, thc[:].unsqueeze(2).to_broadcast([128, 32, 129]), MUL)
                    sincos(Ffi, b_Fim, Ffr, b_Fre, t2f[:], b_t2, t3f[:], b_t3)
                    TT([b_Fre, b_t1], [b_Fre], Ffr, Ffr, t1f[:], MUL)
                    TT([b_Fim, b_t1], [b_Fim], Ffi, Ffi, t1f[:], MUL)
                    if nST > 1:
                        kb.dma("sp", DMA(tabE[0], E_re[:]), reads=[b_Ere], owner=b_Ere)
                        kb.dma("sp", DMA(tabE[1], E_im[:]), reads=[b_Eim], owner=b_Eim)
                        kb.dma("sp", DMA(tabF[0], Ffr), reads=[b_Fre], owner=b_Fre)
                        kb.dma("sp", DMA(tabF[1], Ffi), reads=[b_Fim], owner=b_Fim)
                        kb.dma("sp", DMA(tabB[0], bbr[:]), reads=[b_bbr], owner=b_bbr)
                        kb.dma("sp", DMA(tabB[1], bbi[:]), reads=[b_bbi], owner=b_bbi)
                    pp.close()
                else:
                    kb.dma("sp", DMA(E_re[:], tabE[0]), writes=[b_Ere], owner=b_Ere)
                    kb.dma("sp", DMA(E_im[:], tabE[1]), writes=[b_Eim], owner=b_Eim)
                    kb.dma("sp", DMA(Ffr, tabF[0]), writes=[b_Fre], owner=b_Fre)
                    kb.dma("sp", DMA(Ffi, tabF[1]), writes=[b_Fim], owner=b_Fim)
                    kb.dma("sp", DMA(bbr[:], tabB[0]), writes=[b_bbr], owner=b_bbr)
                    kb.dma("sp", DMA(bbi[:], tabB[1]), writes=[b_bbi], owner=b_bbi)
                cmat, b_cmat = ph.tile("cmat", [128, 8192], BF16)
                kb.dma("pool", [DMA(cmat[:, q * 2048:(q + 1) * 2048], ct_d[l, :, q * 2048:(q + 1) * 2048]) for q in range(4)], writes=[b_cmat], owner=b_cmat)
                ncmat, b_ncmat = ph.tile("ncmat", [128, 8192], BF16)
                TS([b_cmat], [b_ncmat], ncmat[:], cmat[:], -1.0, None, MUL)
                u_all, b_u = ph.tile("u_all", [128, 8, ST], BF16)
                kb.dma("sp", DMA(u_all[:], uT.rearrange("(c p) t -> p c t", p=128)), writes=[b_u], owner=b_u)
                p_ring = ph.ring("pp4", 2, [128, 4, 512], BF16)
                q_ring = ph.ring("qq4", 2, [128, 4, 512], BF16)
                cp_ring = ph.ring("cp", 2, [128, 2, 512], F32)
                yc_ring = ph.ring("yc", 4, [128, 128], BF16)
                ty_ring = ph.ring("ty", 2, [128, 128], F32)
                tn_ring = ph.ring("tn", 2, [128, 4, 4], F32)
                if first:
                    V("memset", [], [b_s5ca], ap=s5ca[:], constant=0.0)
                    V("memset", [], [b_s5cb], ap=s5cb[:], constant=0.0)
                units = [(j, ct) for j in range(ST // 128) for ct in range(8)]

                def bu_banks(ui):
                    base = (ui % 2) * 4
                    return base, base + 1, base + 2, base + 3

                def emit_bu(ui):
                    j, ct = units[ui]
                    br, bi_, _, _ = bu_banks(ui)
                    ul = u_all[:, ct, j * 128:(j + 1) * 128]
                    c0, c1 = ct * 512, (ct + 1) * 512
                    kb.op("pe", MM(bank(br), ul, bbr[:, c0:c1]), reads=[b_u, b_bbr], writes=[pb[br]])
                    kb.op("pe", MM(bank(bi_), ul, bbi[:, c0:c1]), reads=[b_u, b_bbi], writes=[pb[bi_]])

                def st_eprod(ui, stt):
                    j, ct = units[ui]
                    br, bi_, cr_, ci_ = bu_banks(ui)
                    c0, c1 = ct * 512, (ct + 1) * 512
                    pp4, b_p = p_ring.next()
                    stt["p"] = (pp4, b_p)
                    TT([pb[br], b_Ere], [b_p], pp4[:, 0, :], bank(br), E_re[:, c0:c1], MUL)
                    TT([pb[bi_], b_Eim], [b_p], pp4[:, 1, :], bank(bi_), E_im[:, c0:c1], MUL)
                    TT([pb[bi_], b_Ere], [b_p], pp4[:, 2, :], bank(bi_), E_re[:, c0:c1], MUL)
                    TT([pb[br], b_Eim], [b_p], pp4[:, 3, :], bank(br), E_im[:, c0:c1], MUL)
                    fns = []
                    for blk in range(4):
                        bs = slice(blk * 128, (blk + 1) * 128)
                        fns.append(MM(bank(cr_)[:, bs], pp4[:, 0, bs], tri_b[:], start=True, stop=False))
                        fns.append(MM(bank(cr_)[:, bs], pp4[:, 1, bs], ntri_b[:], start=False, stop=True))
                        fns.append(MM(bank(ci_)[:, bs], pp4[:, 2, bs], tri_b[:], start=True, stop=False))
                        fns.append(MM(bank(ci_)[:, bs], pp4[:, 3, bs], tri_b[:], start=False, stop=True))
                    kb.op("pe", fns, reads=[b_p, b_trib, b_ntrib], writes=[pb[cr_], pb[ci_]])

                def st_xprod(ui, stt):
                    j, ct = units[ui]
                    br, bi_, cr_, ci_ = bu_banks(ui)
                    cp, b_cp = cp_ring.next()
                    s0, s1 = ct * 4, ct * 4 + 4
                    TT([pb[cr_], b_s5ca], [b_cp], v3(cp[:, 0, :]), v3(bank(cr_)), s5ca[:, s0:s1].unsqueeze(2).to_broadcast([128, 4, 128]), ADD)
                    TT([pb[ci_], b_s5cb], [b_cp], v3(cp[:, 1, :]), v3(bank(ci_)), s5cb[:, s0:s1].unsqueeze(2).to_broadcast([128, 4, 128]), ADD)
                    Fr = F_re[:, s0:s1, 0:128]
                    Fi = F_im[:, s0:s1, 0:128]
                    qq4, b_q = q_ring.next()
                    TT([b_cp, b_Fre], [b_q], v3(qq4[:, 0, :]), v3(cp[:, 0, :]), Fr, MUL)
                    TT([b_cp, b_Fim], [b_q], v3(qq4[:, 1, :]), v3(cp[:, 1, :]), Fi, MUL)
                    TT([b_cp, b_Fre], [b_q], v3(qq4[:, 2, :]), v3(cp[:, 1, :]), Fr, MUL)
                    TT([b_cp, b_Fim], [b_q], v3(qq4[:, 3, :]), v3(cp[:, 0, :]), Fi, MUL)
                    tn, b_tn = tn_ring.next()
                    cr = v3(cp[:, 0, :])[:, :, 127]
                    ci = v3(cp[:, 1, :])[:, :, 127]
                    Gr = F_re[:, s0:s1, 128]
                    Gi = F_im[:, s0:s1, 128]
                    TT([b_cp, b_Fre], [b_tn], tn[:, 0, :], cr, Gr, MUL)
                    TT([b_cp, b_Fim], [b_tn], tn[:, 1, :], ci, Gi, MUL)
                    TT([b_cp, b_Fre], [b_tn], tn[:, 2, :], ci, Gr, MUL)
                    TT([b_cp, b_Fim], [b_tn], tn[:, 3, :], cr, Gi, MUL)
                    TT([b_tn], [b_s5ca], s5ca[:, s0:s1], tn[:, 0, :], tn[:, 1, :], SUB)
                    TT([b_tn], [b_s5cb], s5cb[:, s0:s1], tn[:, 2, :], tn[:, 3, :], ADD)
                    fns = []
                    for blk in range(4):
                        bs = slice(blk * 128, (blk + 1) * 128)
                        o_re = ((ct * 4 + blk) * 2 + 0) * 128
                        o_im = ((ct * 4 + blk) * 2 + 1) * 128
                        ops = ((cmat, o_re, 0), (ncmat, o_re, 1), (ncmat, o_im, 2), (ncmat, o_im, 3))
                        for oi, (cm, o, qi) in enumerate(ops):
                            fns.append(MM(bank(cr_)[:, 0:128], cm[:, o:o + 128], qq4[:, qi, bs], start=(blk == 0 and oi == 0), stop=(blk == 3 and oi == 3)))
                    kb.op("pe", fns, reads=[b_cmat, b_ncmat, b_q], writes=[pb[cr_]])

                def st_out(ui, stt):
                    j, ct = units[ui]
                    br, bi_, cr_, ci_ = bu_banks(ui)
                    ul = u_all[:, ct, j * 128:(j + 1) * 128]
                    ty, b_ty = ty_ring.next()
                    yc, b_yc = yc_ring.next()
                    STT([b_u, pb[cr_], b_colp], [b_ty], ty[:], ul, col("ssmd", l * 8 + ct), bank(cr_)[:, 0:128], MUL, ADD)
                    A([b_ty], [b_yc], yc[:], ty[:], AF.Gelu_apprx_tanh)
                    kb.dma("sp", DMA(ycT[ct * 128:(ct + 1) * 128, j * 128:(j + 1) * 128], yc[:]), reads=[b_yc], owner=b_yc)

                nU = len(units)
                emit_bu(0)
                emit_bu(1)
                for pi in range(0, nU, 2):
                    a, b = pi, pi + 1
                    sa, sb_ = {}, {}
                    st_eprod(a, sa)
                    st_eprod(b, sb_)
                    st_xprod(a, sa)
                    if a + 2 < nU:
                        emit_bu(a + 2)
                    st_xprod(b, sb_)
                    if b + 2 < nU:
                        emit_bu(b + 2)
                    st_out(a, sa)
                    st_out(b, sb_)
                ph.close()


            if "p3" not in skip:
                ph = Phase(kb)
                ys = []
                for nm, src in (("ya", yaT), ("yb", ybT), ("yc", ycT)):
                    t, b = ph.tile(nm, [128, 8, ST], BF16)
                    kb.dma("sp", DMA(t[:], src.rearrange("(c p) t -> p c t", p=128)), writes=[b], owner=b)
                    ys.append((t, b))
                ys.append(ys[2])
                wb_ring = ph.ring("wb", 2, [128, 4, 8, 128], BF16)
                g_ring = ph.ring("gg", 2, [128, 3, 512], BF16)
                ms_ring = ph.ring("ms", 2, [128, ST], BF16)
                tq_ring = ph.ring("tq", 2, [128, 4, 512], F32)
                gT_v = gT.rearrange("(b f p) t -> f p b t", b=3, f=16)
                it = 0
                for f in range(16):
                    wb, b_wb = wb_ring.next()
                    kb.dma("pool", [DMA(wb[:, q, :, :], wbr_t[l, f, :, q, :, :]) for q in range(4)], writes=[b_wb], owner=b_wb)
                    ms, b_ms = ms_ring.next()
                    for tt in range(nTT):
                        ts0, ts1 = tt * 512, (tt + 1) * 512
                        g3, b_g3 = g_ring.next()
                        kb.dma("sp", DMA(g3[:], gT_v[f][:, :, ts0:ts1]), writes=[b_g3], owner=b_g3)
                        b0 = (it % 2) * 4
                        it += 1
                        for q in range(4):
                            yt, b_yt = ys[q]
                            kb.op("pe", [MM(bank(b0 + q), wb[:, q, k, :], yt[:, k, ts0:ts1], start=(k == 0), stop=(k == 7)) for k in range(8)],
                                  reads=[b_wb, b_yt], writes=[pb[b0 + q]])
                        tq, b_tq = tq_ring.next()
                        A([pb[b0 + 3]], [b_tq], tq[:, 3, :], bank(b0 + 3), AF.Sigmoid)
                        TT([pb[b0], b_g3], [b_tq], tq[:, 0, :], bank(b0 + 0), g3[:, 0, :], ALU.mult)
                        TT([pb[b0 + 1], b_g3], [b_tq], tq[:, 1, :], bank(b0 + 1), g3[:, 1, :], ALU.mult)
                        TT([pb[b0 + 2], b_tq], [b_tq], tq[:, 2, :], bank(b0 + 2), tq[:, 3, :], ALU.mult)
                        TT([b_tq, b_g3], [b_tq], tq[:, 2, :], tq[:, 2, :], g3[:, 2, :], ALU.mult)
                        TT([b_tq], [b_tq], tq[:, 0, :], tq[:, 0, :], tq[:, 1, :], ALU.add)
                        TT([b_tq], [b_ms], ms[:, ts0:ts1], tq[:, 0, :], tq[:, 2, :], ALU.add)
                    kb.dma("sp", DMA(mT[f * 128:(f + 1) * 128, :], ms[:]), reads=[b_ms], owner=b_ms)
                ph.close()
                proj_residual(l, t0, mT, 16, wout_t[l], nTT)

            if "p4" not in skip:
                ph = Phase(kb)
                hT, b_hT = ph.tile("hT", [128, NK, ST], BF16)
                xin, b_xin = ph.tile("xin", [128, NK, 512], F32)
                sq_ring = ph.ring("sq", 2, [128, 512], F32)
                rs_ring = ph.ring("rs", 2, [128, 512], F32)
                rms_norm_to(hT, b_hT, "gffn", l, t0, xin, b_xin, sq_ring, rs_ring, 7)
                wg_ring = ph.ring("wg", 2, [128, 2, NK, 128], BF16)
                as_ring = ph.ring("as", 2, [128, ST], BF16)
                sl_ring = ph.ring("sl", 2, [128, 512], F32)
                it = 0
                for jj in range(NJ):
                    wg, b_wg = wg_ring.next()
                    kb.dma("pool", [DMA(wg[:, q, :, :], wgu_t[l, jj, :, q, :, :]) for q in range(2)], writes=[b_wg], owner=b_wg)
                    ast, b_ast = as_ring.next()
                    for tt in range(nTT):
                        ts0, ts1 = tt * 512, (tt + 1) * 512
                        bg = (it % 3) * 2
                        it += 1
                        for q in range(2):
                            kb.op("pe", [MM(bank(bg + q), wg[:, q, k, :], hT[:, k, ts0:ts1], start=(k == 0), stop=(k == NK - 1)) for k in range(NK)],
                                  reads=[b_wg, b_hT], writes=[pb[bg + q]])
                        sl, b_sl = sl_ring.next()
                        A([pb[bg]], [b_sl], sl[:], bank(bg), AF.Silu)
                        TT([pb[bg + 1], b_sl], [b_ast], ast[:, ts0:ts1], bank(bg + 1), sl[:], ALU.mult)
                    kb.dma("sp", DMA(actT[jj * 128:(jj + 1) * 128, :], ast[:]), reads=[b_ast], owner=b_ast)
                ph.close()
                proj_residual(l, t0, actT, NJ, wdn_t[l], min(2, nTT))

    ph = Phase(kb)
    xin_ring = ph.ring("xin", 2, [128, NK, 512], F32)
    sq_ring = ph.ring("sq", 2, [128, 512], F32)
    rs_ring = ph.ring("rs", 2, [128, 512], F32)
    outv = outT.rearrange("(k p) t -> p k t", p=128)
    for tt in range(S // 512):
        tg = tt * 512
        xin, b_xin = xin_ring.next()
        rs, b_rs = rms_norm_tile(xin, b_xin, tg, sq_ring, rs_ring, 7)
        for k in range(NK):
            STT([b_xin, b_rs, b_colp], [b_xin], xin[:, k, :], xin[:, k, :], col("gfin", k), rs[:], ALU.mult, ALU.mult)
        kb.dma("sp", DMA(outv[:, :, tg:tg + 512], xin[:]), reads=[b_xin], owner=b_xin)
    ph.close()
    kb.finalize()
    return nc


def prep_weights(inp, L):
    f = np.float32
    out = {}
    w_in = inp["w_in"][:L]
    out["w_in_t"] = np.ascontiguousarray(w_in.reshape(L, 16, 128, 96, 128).transpose(0, 3, 2, 1, 4))
    wb = np.concatenate([inp["w_branch"][:L], inp["ssm_w_glu"][:L][:, None]], axis=1)
    out["wbr_t"] = np.ascontiguousarray(wb.reshape(L, 4, 8, 128, 16, 128).transpose(0, 4, 3, 1, 2, 5))
    out["wout_t"] = np.ascontiguousarray(inp["w_out"][:L].reshape(L, 16, 128, 16, 128).transpose(0, 3, 2, 1, 4))
    wgu = np.stack([inp["w_ffn_gate"][:L], inp["w_ffn_up"][:L]], axis=1)
    out["wgu_t"] = np.ascontiguousarray(wgu.reshape(L, 2, 16, 128, NJ, 128).transpose(0, 4, 3, 1, 2, 5))
    out["wdn_t"] = np.ascontiguousarray(inp["w_ffn_down"][:L].reshape(L, NJ, 128, 16, 128).transpose(0, 3, 2, 1, 4))
    COFF, NCOL = col_layout(L)
    colp = np.zeros((128, NCOL), f)

    def put(name, arr):
        colp[:, COFF[name]:COFF[name] + arr.shape[1]] = arr
    put("gmix", inp["norm_mix_g"][:L].reshape(L, 16, 128).transpose(2, 0, 1).reshape(128, -1))
    put("gffn", inp["norm_ffn_g"][:L].reshape(L, 16, 128).transpose(2, 0, 1).reshape(128, -1))
    put("gfin", inp["norm_final_g"].reshape(16, 128).T)
    put("gbias", inp["gate_bias"][:L].reshape(L, 3, 16, 128).transpose(3, 0, 1, 2).reshape(128, -1))
    put("convw", inp["lru_conv_w"][:L].reshape(L, 4, 8, 128).transpose(3, 0, 2, 1).reshape(128, -1))
    for nm, key in (("convb", "lru_conv_b"), ("ba", "lru_ba"), ("bx", "lru_bx"), ("lam", "lru_lambda"), ("ssmd", "ssm_d")):
        put(nm, inp[key][:L].reshape(L, 8, 128).transpose(2, 0, 1).reshape(128, -1))
    for nm, arr in (("acre", inp["ssm_a_re"][:L]), ("acim", inp["ssm_a_im"][:L]),
                    ("lsc", np.repeat(inp["ssm_log_step"][:L][:, :, None], 64, axis=2))):
        put(nm, arr.reshape(L, 32, 2, 64).transpose(2, 3, 0, 1).reshape(128, -1))
    colp[:, COFF["iota"]] = np.arange(128, dtype=f)
    out["colp"] = colp
    rowp = np.stack([inp["ssm_a_re"][:L].reshape(L, 4096), inp["ssm_a_im"][:L].reshape(L, 4096),
                     np.repeat(inp["ssm_log_step"][:L][:, :, None], 64, axis=2).reshape(L, 4096)], axis=1)
    out["rowp"] = np.ascontiguousarray(rowp.astype(f))
    lruw = np.zeros((L, 128, 8, 2, 128), f)
    for wi, key in enumerate(("lru_wa", "lru_wx")):
        w = inp[key][:L].reshape(L, 8, 2, 64, 64)
        for nl in range(2):
            lruw[:, nl * 64:(nl + 1) * 64, :, wi, nl * 64:(nl + 1) * 64] = w[:, :, nl].transpose(0, 2, 1, 3)
    out["lruw"] = lruw
    kk = np.arange(128)[:, None, None]
    kt = np.arange(5)[None, :, None]
    qq = np.arange(128)[None, None, :]
    dist = (4 - kt) * 128 + qq - kk
    rel = np.clip(dist, -128, 128) + 128
    cdiff = 8 - 2 * kt + qq // 64 - kk // 64
    valid = (cdiff >= 0) & (cdiff <= 8)
    ab = inp["attn_rel_bias"][:L][:, :, rel]
    ab = np.where(valid[None, None], ab, f(-30000.0)).astype(f)
    out["abias"] = np.ascontiguousarray(ab.reshape(L, 8, 128, 640))
    bbd = np.zeros((L, 128, 2, 8, 8, 64), f)
    for ri, key in enumerate(("ssm_b_re", "ssm_b_im")):
        B = inp[key][:L].reshape(L, 8, 8, 64, 16)
        for gl in range(8):
            bbd[:, gl * 16:(gl + 1) * 16, ri, :, gl, :] = B[:, :, gl].transpose(0, 3, 1, 2)
    out["bbd"] = np.ascontiguousarray(bbd.reshape(L, 128, 2, 4096))
    ctd = np.zeros((L, 128, 8, 4, 2, 128), f)
    for ri, key in enumerate(("ssm_c_re", "ssm_c_im")):
        C = inp[key][:L].reshape(L, 8, 4, 2, 16, 64)
        for blk in range(4):
            for gl2 in range(2):
                c0 = (2 * blk + gl2) * 16
                ctd[:, gl2 * 64:(gl2 + 1) * 64, :, blk, ri, c0:c0 + 16] = C[:, :, blk, gl2].transpose(0, 3, 1, 2)
    out["ctd"] = np.ascontiguousarray(ctd.reshape(L, 128, 8192))
    cst = np.zeros((128, 386), f)
    cst[:, 0:128] = np.triu(np.ones((128, 128), f))
    cst[:, 128:256] = 1.0
    cst[:, 256:385] = np.arange(129, dtype=f)[None, :]
    cst[:, 385] = 1e-6
    out["cst"] = cst
    out["epsd"] = np.full((128, 1), 1e-6, f)
    return out


_CACHE = {}


def run_model(inputs, L, S_core, ST, n_cores, dbg=False, skip=()):
    x = np.asarray(inputs["x"], np.float32)
    B, S, _ = x.shape
    assert S == S_core and B <= n_cores
    key = (L, S_core, ST, dbg, tuple(skip))
    if key not in _CACHE:
        _CACHE[key] = build_program(L, S_core, ST, dbg=dbg, skip=skip)
    nc = _CACHE[key]
    wts = prep_weights({k: np.asarray(v, np.float32) for k, v in inputs.items() if k != "x"}, L)
    in_maps = []
    for c in range(n_cores):
        b = c % B
        m = dict(wts)
        m["xT"] = np.ascontiguousarray(x[b].T)
        in_maps.append(m)
    res = run_bass_kernel_spmd(nc, in_maps, core_ids=list(range(n_cores)))
    out = np.stack([np.ascontiguousarray(res.results[b]["outT"].T) for b in range(B)], axis=0)
    return out.astype(np.float32), res


def kernel(**inputs):
    out, _ = run_model(inputs, 4, 4096, 2048, 4)
    return out
```

```python
import math
from contextlib import ExitStack
import numpy as np
import concourse.bass as bass
import concourse.mybir as mybir
from concourse.bass_utils import run_bass_kernel_spmd

F32 = mybir.dt.float32
BF16 = mybir.dt.bfloat16
AF = mybir.ActivationFunctionType
ALU = mybir.AluOpType

D = 2048
NK = 16
MIXW = 1024
FH = 5632
NJ = 44
TWO_PI = float(2 * math.pi)
MAGIC = 12582912.0


class Buf:
    __slots__ = ("name", "w", "r", "dsem")

    def __init__(self, name):
        self.name = name
        self.w = None
        self.r = {}
        self.dsem = None


class Eng:
    def __init__(self, name, sem):
        self.name = name
        self.sem = sem
        self.cnt = 0
        self.waited = {}
        self.prog = []


class KB:
    def __init__(self, nc):
        self.nc = nc
        self.stack = ExitStack()
        self.sems = {}
        self.semcnt = {}
        self.free_dsems = []
        self.dirty = {}
        self.eng = {}
        for name in ("pe", "act", "dve", "pool", "sp"):
            self.eng[name] = Eng(name, self.newsem("e_" + name))
        self.uid = 0

    def newsem(self, name):
        h = self.stack.enter_context(self.nc.semaphore(name))
        key = len(self.sems)
        self.sems[key] = h
        self.semcnt[key] = 0
        return key

    def buf(self, name="b"):
        self.uid += 1
        return Buf(f"{name}_{self.uid}")

    def _deps(self, reads, writes):
        deps = {}
        for b in reads:
            if b.w is not None and deps.get(b.w[0], 0) < b.w[1]:
                deps[b.w[0]] = b.w[1]
        for b in writes:
            if b.w is not None and deps.get(b.w[0], 0) < b.w[1]:
                deps[b.w[0]] = b.w[1]
            for k, v in b.r.items():
                if deps.get(k, 0) < v:
                    deps[k] = v
        return deps

    def _emit_waits(self, e, deps, skip=None):
        for k, v in deps.items():
            if k == skip:
                continue
            if e.waited.get(k, 0) < v:
                e.prog.append(("w", k, v))
                e.waited[k] = v

    def _update(self, tok, reads, writes):
        k, v = tok
        for b in reads:
            if b.r.get(k, 0) < v:
                b.r[k] = v
        for b in writes:
            b.w = tok
            b.r = {}

    def op(self, en, fns, reads=(), writes=()):
        e = self.eng[en]
        if callable(fns):
            fns = [fns]
        deps = self._deps(reads, writes)
        self._emit_waits(e, deps, skip=e.sem if en == "pe" else None)
        for f in fns[:-1]:
            e.prog.append(("i", f, None, 0))
        e.cnt += 1
        e.prog.append(("i", fns[-1], e.sem, 1))
        tok = (e.sem, e.cnt)
        self._update(tok, reads, writes)
        return tok

    def dma(self, en, fns, reads=(), writes=(), owner=None):
        e = self.eng[en]
        if callable(fns):
            fns = [fns]
        if owner.dsem is None:
            owner.dsem = self.free_dsems.pop() if self.free_dsems else self.newsem("d%d" % len(self.sems))
        k = owner.dsem
        deps = self._deps(reads, writes)
        self._emit_waits(e, deps)
        for f in fns:
            self.semcnt[k] += 16
            e.prog.append(("i", f, k, 16))
        tok = (k, self.semcnt[k])
        self.dirty[k] = self.semcnt[k]
        self._update(tok, reads, writes)
        return tok

    def barrier(self):
        toks = dict(self.dirty)
        for e in self.eng.values():
            if e.cnt > 0:
                toks[e.sem] = e.cnt
        for e in self.eng.values():
            self._emit_waits(e, toks)
        self.dirty = {}

    def release(self, bufs):
        for b in bufs:
            if b.dsem is not None:
                self.free_dsems.append(b.dsem)
                b.dsem = None

    def finalize(self):
        nc = self.nc
        self.barrier()
        engs, sems = self.eng, self.sems

        def replay(e, h):
            for it in e.prog:
                if it[0] == "w":
                    h.wait_ge(sems[it[1]], it[2])
                else:
                    inst = it[1](h)
                    if it[2] is not None:
                        inst.then_inc(sems[it[2]], it[3])

        with nc.Block() as block:
            @block.tensor
            def _(h):
                replay(engs["pe"], h)

            @block.scalar
            def _(h):
                replay(engs["act"], h)

            @block.vector
            def _(h):
                replay(engs["dve"], h)

            @block.gpsimd
            def _(h):
                replay(engs["pool"], h)

            @block.sync
            def _(h):
                replay(engs["sp"], h)
        self.stack.close()


class Phase:
    def __init__(self, kb):
        self.kb = kb
        self.st = ExitStack()
        self.bufs = []

    def tile(self, name, shape, dtype):
        kb = self.kb
        kb.uid += 1
        t = self.st.enter_context(kb.nc.sbuf_tensor(f"{name}_{kb.uid}", list(shape), dtype))
        b = kb.buf(name)
        self.bufs.append(b)
        return t, b

    def ring(self, name, n, shape, dtype):
        return Ring([self.tile(name, shape, dtype) for _ in range(n)])

    def close(self):
        self.kb.barrier()
        self.kb.release(self.bufs)
        self.st.close()


class Ring:
    def __init__(self, items):
        self.items = items
        self.i = 0

    def next(self):
        it = self.items[self.i % len(self.items)]
        self.i += 1
        return it


def col_layout(L):
    segs = [("gmix", L * 16), ("gffn", L * 16), ("gfin", 16), ("gbias", L * 48), ("convw", L * 32),
            ("convb", L * 8), ("ba", L * 8), ("bx", L * 8), ("lam", L * 8), ("ssmd", L * 8),
            ("acre", L * 32), ("acim", L * 32), ("lsc", L * 32), ("iota", 1)]
    off, o = {}, 0
    for n, w in segs:
        off[n] = o
        o += w
    return off, o


def build_program(L, S, ST, dbg=False, has_prev=False, skip=()):
    nc = bass.Bass("TRN2", target_bir_lowering=False)
    kb = KB(nc)
    gst = kb.stack
    nST = S // ST
    nTT = ST // 512
    COFF, NCOL = col_layout(L)
    okind = "ExternalOutput" if dbg else "Internal"

    def din(name, shape, dt=F32):
        return nc.dram_tensor(name, list(shape), dt, kind="ExternalInput").ap()

    def dscr(name, shape, dt):
        return nc.dram_tensor(name, list(shape), dt, kind=okind).ap()

    xT = din("xT", [D, S])
    w_in_t = din("w_in_t", [L, 96, 128, 16, 128])
    wbr_t = din("wbr_t", [L, 16, 128, 4, 8, 128])
    wout_t = din("wout_t", [L, 16, 128, 16, 128])
    wgu_t = din("wgu_t", [L, NJ, 128, 2, 16, 128])
    wdn_t = din("wdn_t", [L, 16, 128, NJ, 128])
    colp_d = din("colp", [128, NCOL])
    rowp_d = din("rowp", [L, 3, 4096])
    lruw_d = din("lruw", [L, 128, 8, 2, 128])
    abias_d = din("abias", [L, 8, 128, 640])
    bbd_d = din("bbd", [L, 128, 2, 4096])
    ct_d = din("ctd", [L, 128, 8192])
    cst_d = din("cst", [128, 386])
    eps_d = din("epsd", [128, 1])
    outT = nc.dram_tensor("outT", [D, S], F32, kind="ExternalOutput").ap()

    xr = dscr("xr", [D, S], F32)
    lxT = dscr("lxT", [MIXW, S], BF16)
    lgT = dscr("lgT", [MIXW, ST], BF16)
    qT = dscr("qT", [MIXW, ST], BF16)
    kT = dscr("kT", [MIXW, S], BF16)
    vS = dscr("vS", [S, MIXW], BF16)
    uT = dscr("uT", [MIXW, ST], BF16)
    gT = dscr("gT", [3 * D, ST], BF16)
    yaT = dscr("yaT", [MIXW, ST], BF16)
    ybT = dscr("ybT", [MIXW, ST], BF16)
    ycT = dscr("ycT", [MIXW, ST], BF16)
    mT = dscr("mT", [D, ST], BF16)
    actT = dscr("actT", [FH, ST], BF16)
    tabE = nc.dram_tensor("tabE", [2, 128, 4096], F32, kind="Internal").ap()
    tabF = nc.dram_tensor("tabF", [2, 128, 32 * 129], F32, kind="Internal").ap()
    tabB = nc.dram_tensor("tabB", [2, 128, 4096], BF16, kind="Internal").ap()

    def gtile(name, shape, dt):
        t = gst.enter_context(nc.sbuf_tensor(name, list(shape), dt))
        return t, kb.buf(name)

    colp, b_colp = gtile("colp_s", [128, NCOL], F32)
    nsp8, b_nsp8 = gtile("nsp8", [128, L * 8], F32)
    ones_f, b_onesf = gtile("ones_f", [128, 128], F32)
    ones_b, b_onesb = gtile("ones_b", [128, 128], BF16)
    tri_b, b_trib = gtile("tri_b", [128, 128], BF16)
    ntri_b, b_ntrib = gtile("ntri_b", [128, 128], BF16)
    iorow, b_iorow = gtile("iorow", [128, 129], F32)
    lru_h, b_lruh = gtile("lru_h", [128, 8], F32)
    s5ca, b_s5ca = gtile("s5ca", [128, 32], F32)
    s5cb, b_s5cb = gtile("s5cb", [128, 32], F32)
    epsc, b_epsc = gtile("epsc", [128, 1], F32)
    psum = gst.enter_context(nc.psum_tensor("psum", [128, 4096], F32))
    pb = [kb.buf(f"ps{i}") for i in range(8)]

    def bank(i):
        return psum[:, i * 512:(i + 1) * 512]

    def col(name, idx, n=1):
        o = COFF[name] + idx
        return colp[:, o:o + n]

    def I(name, **kw):
        return lambda h: getattr(h, name)(**kw)

    def MM(out, lhsT, rhs, start=True, stop=True):
        return I("matmul", out=out, lhsT=lhsT, rhs=rhs, start=start, stop=stop)

    def DMA(out, in_):
        return I("dma_start", out=out, in_=in_)

    def V(name, r, w, **kw):
        return kb.op("dve", I(name, **kw), reads=r, writes=w)

    def A(r, w, out, in_, func, **kw):
        return kb.op("act", I("activation", out=out, in_=in_, func=func, **kw), reads=r, writes=w)

    def TT(r, w, out, in0, in1, op):
        return V("tensor_tensor", r, w, out=out, in0=in0, in1=in1, op=op)

    def TS(r, w, out, in0, s1, s2, op0, op1=None):
        if op1 is None:
            return V("tensor_scalar", r, w, out=out, in0=in0, scalar1=s1, scalar2=None, op0=op0)
        return V("tensor_scalar", r, w, out=out, in0=in0, scalar1=s1, scalar2=s2, op0=op0, op1=op1)

    def STT(r, w, out, in0, scalar, in1, op0, op1):
        return V("scalar_tensor_tensor", r, w, out=out, in0=in0, scalar=scalar, in1=in1, op0=op0, op1=op1)

    kb.dma("sp", DMA(colp[:], colp_d), writes=[b_colp], owner=b_colp)
    kb.dma("sp", DMA(ones_f[:], cst_d[:, 128:256]), writes=[b_onesf], owner=b_onesf)
    kb.dma("sp", DMA(iorow[:], cst_d[:, 256:385]), writes=[b_iorow], owner=b_iorow)
    kb.dma("pool", DMA(tri_b[:], cst_d[:, 0:128]), writes=[b_trib], owner=b_trib)
    kb.dma("pool", DMA(ones_b[:], cst_d[:, 128:256]), writes=[b_onesb], owner=b_onesb)
    TS([b_trib], [b_ntrib], ntri_b[:], tri_b[:], -1.0, None, ALU.mult)
    kb.dma("sp", DMA(epsc[:], eps_d), writes=[b_epsc], owner=b_epsc)
    A([b_colp], [b_nsp8], nsp8[:], col("lam", 0, L * 8), AF.Exp, scale=-1.0)
    A([b_nsp8], [b_nsp8], nsp8[:], nsp8[:], AF.Ln, bias=1.0, scale=1.0)
    TS([b_nsp8], [b_nsp8], nsp8[:], nsp8[:], -8.0, None, ALU.mult)
    b_x0 = kb.buf("x0")
    kb.dma("sp", [DMA(xr[c * 128:(c + 1) * 128, :], xT[c * 128:(c + 1) * 128, :]) for c in range(16)], writes=[b_x0], owner=b_x0)
    kb.barrier()

    xr_v = xr.rearrange("(k p) t -> p k t", p=128)

    def rms_norm_tile(xin, b_xin, tg, sq_ring, rs_ring, pbi):
        kb.dma("sp", [DMA(xin[:, 0:8, :], xr_v[:, 0:8, tg:tg + 512]), DMA(xin[:, 8:16, :], xr_v[:, 8:16, tg:tg + 512])], writes=[b_xin], owner=b_xin)
        for k in range(NK):
            sq, b_sq = sq_ring.next()
            A([b_xin], [b_sq], sq[:], xin[:, k, :], AF.Square)
            kb.op("pe", MM(bank(pbi), ones_b[:], sq[:], start=(k == 0), stop=(k == NK - 1)), reads=[b_sq, b_onesb], writes=[pb[pbi]])
        rs, b_rs = rs_ring.next()
        A([pb[pbi], b_epsc], [b_rs], rs[:], bank(pbi), AF.Sqrt, scale=1.0 / D, bias=epsc[:])
        V("reciprocal", [b_rs], [b_rs], out=rs[:], in_=rs[:])
        return rs, b_rs

    def rms_norm_to(hT, b_hT, gname, l, t0, xin_ring, sq_ring, rs_ring, pbis):
        for tt in range(nTT):
            xin, b_xin = xin_ring.next()
            rs, b_rs = rms_norm_tile(xin, b_xin, t0 + tt * 512, sq_ring, rs_ring, pbis[tt % len(pbis)])
            for k in range(NK):
                STT([b_xin, b_rs, b_colp], [b_hT], hT[:, k, tt * 512:(tt + 1) * 512], xin[:, k, :], col(gname, l * 16 + k), rs[:], ALU.mult, ALU.mult)

    evac_flip = [0]

    def evac_copy(out_ap, in_ap, reads, writes):
        evac_flip[0] ^= 1
        if evac_flip[0]:
            A(reads, writes, out_ap, in_ap, AF.Copy)
        else:
            V("tensor_copy", reads, writes, out=out_ap, in_=in_ap)

    def proj_residual(l, t0, src_dram, nk, w_dram_l, chunk_tt):
        srcv = src_dram.rearrange("(k p) t -> p k t", p=128)
        splits = [(a, min(a + 16, nk)) for a in range(0, nk, 16)]
        for c0 in range(0, nTT, chunk_tt):
            ph = Phase(kb)
            ncols = chunk_tt * 512
            sm, b_sm = ph.tile("sm", [128, nk, ncols], BF16)
            kb.dma("sp", [DMA(sm[:, a:b, :], srcv[:, a:b, c0 * 512:c0 * 512 + ncols]) for a, b in splits], writes=[b_sm], owner=b_sm)
            wo_ring = ph.ring("wo", 2, [128, nk, 128], BF16)
            xs_ring = ph.ring("xs", 3, [128, 512], F32)
            it = 0
            for f in range(16):
                wo, b_wo = wo_ring.next()
                kb.dma("pool", [DMA(wo[:, a:b, :], w_dram_l[f, :, a:b, :]) for a, b in splits], writes=[b_wo], owner=b_wo)
                for tt in range(chunk_tt):
                    bi = it % 6
                    it += 1
                    tg = t0 + (c0 + tt) * 512
                    xs, b_xs = xs_ring.next()
                    kb.dma("sp", DMA(xs[:], xr[f * 128:(f + 1) * 128, tg:tg + 512]), writes=[b_xs], owner=b_xs)
                    kb.op("pe", [MM(bank(bi), wo[:, k, :], sm[:, k, tt * 512:(tt + 1) * 512], start=(k == 0), stop=(k == nk - 1)) for k in range(nk)],
                          reads=[b_wo, b_sm], writes=[pb[bi]])
                    TT([pb[bi], b_xs], [b_xs], xs[:], bank(bi), xs[:], ALU.add)
                    kb.dma("sp", DMA(xr[f * 128:(f + 1) * 128, tg:tg + 512], xs[:]), reads=[b_xs], owner=b_xs)
            ph.close()

    def v3(ap):
        return ap.rearrange("p (b j) -> p b j", b=4)

    for l in range(L):
        for s in range(nST):
            t0 = s * ST
            first = (s == 0) and not has_prev
            if "p1" not in skip:
                ph = Phase(kb)
                hT, b_hT = ph.tile("hT", [128, NK, ST], BF16)
                xin_ring = ph.ring("xin", 2, [128, NK, 512], F32)
                sq_ring = ph.ring("sq", 3, [128, 512], BF16)
                rs_ring = ph.ring("rs", 2, [128, 512], F32)
                wt_ring = ph.ring("wt", 3, [128, NK, 128], BF16)
                stg_ring = ph.ring("stg", 3, [128, ST], BF16)
                wv_ring = ph.ring("wv", 1, [128, NK, 512], BF16)
                sv_ring = ph.ring("sv", 3, [128, 512], BF16)
                rms_norm_to(hT, b_hT, "gmix", l, t0, xin_ring, sq_ring, rs_ring, (6, 7))
                pbr = 0
                for m in list(range(0, 32)) + list(range(40, 96)):
                    wt, b_wt = wt_ring.next()
                    kb.dma("pool", DMA(wt[:], w_in_t[l, m]), writes=[b_wt], owner=b_wt)
                    stg, b_stg = stg_ring.next()
                    for tt in range(nTT):
                        bi = pbr % 6
                        pbr += 1
                        kb.op("pe", [MM(bank(bi), wt[:, k, :], hT[:, k, tt * 512:(tt + 1) * 512], start=(k == 0), stop=(k == NK - 1)) for k in range(NK)],
                              reads=[b_wt, b_hT], writes=[pb[bi]])
                        o_ap = stg[:, tt * 512:(tt + 1) * 512]
                        if m < 48:
                            evac_copy(o_ap, bank(bi), [pb[bi]], [b_stg])
                        else:
                            A([pb[bi], b_colp], [b_stg], o_ap, bank(bi), AF.Sigmoid, bias=col("gbias", l * 48 + (m - 48)), scale=1.0)
                    if m < 8:
                        dst = lxT[m * 128:(m + 1) * 128, t0:t0 + ST]
                    elif m < 16:
                        dst = lgT[(m - 8) * 128:(m - 7) * 128, :]
                    elif m < 24:
                        dst = qT[(m - 16) * 128:(m - 15) * 128, :]
                    elif m < 32:
                        dst = kT[(m - 24) * 128:(m - 23) * 128, t0:t0 + ST]
                    elif m < 48:
                        dst = uT[(m - 40) * 128:(m - 39) * 128, :]
                    else:
                        dst = gT[(m - 48) * 128:(m - 47) * 128, :]
                    kb.dma("sp", DMA(dst, stg[:]), reads=[b_stg], owner=b_stg)
                for cb in range(2):
                    wv, b_wv = wv_ring.next()
                    kb.dma("pool", [DMA(wv[:, :, mm * 128:(mm + 1) * 128], w_in_t[l, 32 + 4 * cb + mm]) for mm in range(4)], writes=[b_wv], owner=b_wv)
                    for j in range(ST // 128):
                        bi = pbr % 6
                        pbr += 1
                        kb.op("pe", [MM(bank(bi), hT[:, k, j * 128:(j + 1) * 128], wv[:, k, :], start=(k == 0), stop=(k == NK - 1)) for k in range(NK)],
                              reads=[b_wv, b_hT], writes=[pb[bi]])
                        sv, b_sv = sv_ring.next()
                        evac_copy(sv[:], bank(bi), [pb[bi]], [b_sv])
                        kb.dma("sp", DMA(vS[t0 + j * 128:t0 + (j + 1) * 128, cb * 512:(cb + 1) * 512], sv[:]), reads=[b_sv], owner=b_sv)
                ph.close()

            if "p2a" not in skip:
                ph = Phase(kb)
                bd, b_bd = ph.tile("bd", [128, 8, 2, 128], BF16)
                kb.dma("pool", DMA(bd[:], lruw_d[l]), writes=[b_bd], owner=b_bd)
                lx_ring = ph.ring("lx", 3, [128, 515], BF16)
                lg_ring = ph.ring("lg", 3, [128, 512], BF16)
                ya_ring = ph.ring("ya", 2, [128, 512], BF16)
                hs_ring = ph.ring("hs", 2, [128, 512], F32)
                xc_ring = ph.ring("xc", 2, [128, 512], F32)
                xcb_ring = ph.ring("xcb", 2, [128, 512], BF16)
                rr, b_rr = ph.tile("rr", [128, 512], F32)
                ii, b_ii = ph.tile("ii", [128, 512], F32)
                aa, b_aa = ph.tile("aa", [128, 512], F32)
                a2, b_a2 = ph.tile("a2", [128, 512], F32)
                gg, b_gg = ph.tile("gg", [128, 512], F32)
                if first:
                    V("memset", [], [b_lruh], ap=lru_h[:], constant=0.0)
                lunits = [(ct, tt) for ct in range(8) for tt in range(nTT)]
                lst = {}

                def lru_load(n):
                    ct, tt = lunits[n]
                    r0, r1 = ct * 128, (ct + 1) * 128
                    tg = t0 + tt * 512
                    lx, b_lx = lx_ring.next()
                    lg, b_lg = lg_ring.next()
                    if tg == 0 and not has_prev:
                        V("memset", [], [b_lx], ap=lx[:, 0:3], constant=0.0)
                        kb.dma("sp", DMA(lx[:, 3:515], lxT[r0:r1, 0:512]), writes=[b_lx], owner=b_lx)
                    else:
                        kb.dma("sp", DMA(lx[:], lxT[r0:r1, tg - 3:tg + 512]), writes=[b_lx], owner=b_lx)
                    kb.dma("sp", DMA(lg[:], lgT[r0:r1, tt * 512:(tt + 1) * 512]), writes=[b_lg], owner=b_lg)
                    lst[n] = dict(lx=lx, b_lx=b_lx, lg=lg, b_lg=b_lg)

                def lru_front(n):
                    ct, tt = lunits[n]
                    d = lst[n]
                    lx, b_lx = d["lx"], d["b_lx"]
                    xc, b_xc = xc_ring.next()
                    xcb, b_xcb = xcb_ring.next()
                    cwi = (l * 8 + ct) * 4
                    TS([b_lx, b_colp], [b_xc], xc[:], lx[:, 3:515], col("convw", cwi + 3), col("convb", l * 8 + ct), ALU.mult, ALU.add)
                    for k in range(3):
                        STT([b_lx, b_xc, b_colp], [b_xc], xc[:], lx[:, k:k + 512], col("convw", cwi + k), xc[:], ALU.mult, ALU.add)
                    V("tensor_copy", [b_xc], [b_xcb], out=xcb[:], in_=xc[:])
                    pbase = (n % 2) * 2
                    kb.op("pe", MM(bank(pbase), bd[:, ct, 0, :], xcb[:]), reads=[b_bd, b_xcb], writes=[pb[pbase]])
                    kb.op("pe", MM(bank(pbase + 1), bd[:, ct, 1, :], xcb[:]), reads=[b_bd, b_xcb], writes=[pb[pbase + 1]])
                    d.update(xc=xc, b_xc=b_xc, pbase=pbase)

                lru_load(0)
                if len(lunits) > 1:
                    lru_load(1)
                lru_front(0)
                for n in range(len(lunits)):
                    ct, tt = lunits[n]
                    r0, r1 = ct * 128, (ct + 1) * 128
                    d = lst[n]
                    xc, b_xc, pbase = d["xc"], d["b_xc"], d["pbase"]
                    lg, b_lg = d["lg"], d["b_lg"]
                    if n + 2 < len(lunits):
                        lru_load(n + 2)
                    A([pb[pbase], b_colp], [b_rr], rr[:], bank(pbase), AF.Sigmoid, bias=col("ba", l * 8 + ct), scale=1.0)
                    A([pb[pbase + 1], b_colp], [b_ii], ii[:], bank(pbase + 1), AF.Sigmoid, bias=col("bx", l * 8 + ct), scale=1.0)
                    A([b_rr, b_nsp8], [b_aa], aa[:], rr[:], AF.Exp, scale=nsp8[:, l * 8 + ct:l * 8 + ct + 1])
                    TT([b_ii, b_xc], [b_ii], ii[:], ii[:], xc[:], ALU.mult)
                    TT([b_aa], [b_a2], a2[:], aa[:], aa[:], ALU.mult)
                    A([b_a2], [b_a2], a2[:], a2[:], AF.Sqrt, scale=-1.0, bias=1.0)
                    if n + 1 < len(lunits):
                        lru_front(n + 1)
                    A([b_lg], [b_gg], gg[:], lg[:], AF.Gelu_apprx_tanh)
                    TT([b_ii, b_a2], [b_ii], ii[:], ii[:], a2[:], ALU.mult)
                    hs, b_hs = hs_ring.next()
                    V("tensor_tensor_scan", [b_aa, b_ii, b_lruh], [b_hs], out=hs[:], data0=aa[:], data1=ii[:], initial=lru_h[:, ct:ct + 1], op0=ALU.mult, op1=ALU.add)
                    V("tensor_copy", [b_hs], [b_lruh], out=lru_h[:, ct:ct + 1], in_=hs[:, 511:512])
                    ya, b_ya = ya_ring.next()
                    TT([b_hs, b_gg], [b_ya], ya[:], hs[:], gg[:], ALU.mult)
                    kb.dma("sp", DMA(yaT[r0:r1, tt * 512:(tt + 1) * 512], ya[:]), reads=[b_ya], owner=b_ya)
                    del lst[n]
                ph.close()

            if "p2b" not in skip:
                ph = Phase(kb)
                NKT = (ST + 512) // 128
                qh_ring = ph.ring("qh", 2, [128, ST], BF16)
                kh_ring = ph.ring("kh", 2, [128, ST + 512], BF16)
                vh_ring = ph.ring("vh", 2, [128, NKT, 128], BF16)
                bh_ring = ph.ring("bh", 2, [128, 640], F32)
                yb_ring = ph.ring("yb", 2, [128, ST], BF16)
                sf_ring = ph.ring("sf", 2, [128, 640], F32)
                pt_ring = ph.ring("pt", 3, [128, 640], BF16)
                rc_ring = ph.ring("rc", 2, [128, 128], F32)
                NQ = ST // 128
                aunits = [(hd, m) for hd in range(8) for m in range(NQ)]
                NU = len(aunits)
                hst, ust = {}, {}

                def att_load(hd):
                    r0, r1 = hd * 128, (hd + 1) * 128
                    qh, b_qh = qh_ring.next()
                    kh, b_kh = kh_ring.next()
                    vh, b_vh = vh_ring.next()
                    bh, b_bh = bh_ring.next()
                    yb, b_yb = yb_ring.next()
                    kb.dma("sp", DMA(qh[:], qT[r0:r1, :]), writes=[b_qh], owner=b_qh)
                    kb.dma("sp", DMA(bh[:], abias_d[l, hd]), writes=[b_bh], owner=b_bh)
                    if first:
                        kb.dma("sp", DMA(kh[:, 512:], kT[r0:r1, 0:ST]), writes=[b_kh], owner=b_kh)
                        kb.dma("sp", DMA(vh[:, 4:, :], vS[0:ST, r0:r1].rearrange("(n p) d -> p n d", p=128)), writes=[b_vh], owner=b_vh)
                    else:
                        kb.dma("sp", DMA(kh[:], kT[r0:r1, t0 - 512:t0 + ST]), writes=[b_kh], owner=b_kh)
                        kb.dma("sp", DMA(vh[:], vS[t0 - 512:t0 + ST, r0:r1].rearrange("(n p) d -> p n d", p=128)), writes=[b_vh], owner=b_vh)
                    hst[hd] = dict(qh=qh, b_qh=b_qh, kh=kh, b_kh=b_kh, vh=vh, b_vh=b_vh, bh=bh, b_bh=b_bh, yb=yb, b_yb=b_yb)

                def att_scores(n):
                    hd, m = aunits[n]
                    if hd not in hst:
                        att_load(hd)
                    H = hst[hd]
                    kts = [kt for kt in range(5) if (not first) or (m - 4 + kt) >= 0]
                    sset = n % 3
                    sb = 2 * sset
                    sc = psum[:, sb * 512:sb * 512 + 640]
                    kb.op("pe", [MM(sc[:, kt * 128:(kt + 1) * 128], H["kh"][:, (m + kt) * 128:(m + kt + 1) * 128], H["qh"][:, m * 128:(m + 1) * 128]) for kt in kts],
                          reads=[H["b_kh"], H["b_qh"]], writes=[pb[sb], pb[sb + 1]])
                    ust[n] = dict(kts=kts, sb=sb, sc=sc, lo=kts[0] * 128)

                def att_sf_exp(n):
                    hd, m = aunits[n]
                    H, U = hst[hd], ust[n]
                    sf, b_sf = sf_ring.next()
                    pt, b_pt = pt_ring.next()
                    lo, hi, sb, sc = U["lo"], 640, U["sb"], U["sc"]
                    STT([pb[sb], pb[sb + 1], H["b_bh"]], [b_sf], sf[:, lo:hi], sc[:, lo:hi], float(128 ** -0.5), H["bh"][:, lo:hi], ALU.mult, ALU.add)
                    A([b_sf], [b_pt], pt[:, lo:hi], sf[:, lo:hi], AF.Exp)
                    U.update(pt=pt, b_pt=b_pt)

                def att_pv(n):
                    hd, m = aunits[n]
                    H, U = hst[hd], ust[n]
                    kts, pt = U["kts"], U["pt"]
                    pbk = 6 + (n % 2)
                    nk = len(kts)
                    fns = [MM(bank(pbk)[:, 0:128], H["vh"][:, m + kt, :], pt[:, kt * 128:(kt + 1) * 128], start=(i_ == 0), stop=(i_ == nk - 1)) for i_, kt in enumerate(kts)]
                    fns += [MM(bank(pbk)[:, 128:256], ones_b[:], pt[:, kt * 128:(kt + 1) * 128], start=(i_ == 0), stop=(i_ == nk - 1)) for i_, kt in enumerate(kts)]
                    kb.op("pe", fns, reads=[H["b_vh"], U["b_pt"], b_onesb], writes=[pb[pbk]])
                    U["pbk"] = pbk

                def att_fin(n):
                    hd, m = aunits[n]
                    H, U = hst[hd], ust[n]
                    pbk = U["pbk"]
                    rc, b_rc = rc_ring.next()
                    V("reciprocal", [pb[pbk]], [b_rc], out=rc[:], in_=bank(pbk)[:, 128:256])
                    TT([pb[pbk], b_rc], [H["b_yb"]], H["yb"][:, m * 128:(m + 1) * 128], bank(pbk)[:, 0:128], rc[:], ALU.mult)
                    if m == NQ - 1:
                        kb.dma("sp", DMA(ybT[hd * 128:(hd + 1) * 128, :], H["yb"][:]), reads=[H["b_yb"]], owner=H["b_yb"])
                    del ust[n]

                att_scores(0)
                if NU > 1:
                    att_scores(1)
                att_sf_exp(0)
                for n in range(NU):
                    if n + 2 < NU:
                        att_scores(n + 2)
                    att_pv(n)
                    if n + 1 < NU:
                        att_sf_exp(n + 1)
                    if n >= 1:
                        att_fin(n - 1)
                att_fin(NU - 1)
                ph.close()


            if "p2c" not in skip:
                ph = Phase(kb)
                W = 4096
                NF = 32 * 129
                E_re, b_Ere = ph.tile("E_re", [128, W], F32)
                E_im, b_Eim = ph.tile("E_im", [128, W], F32)
                F_re, b_Fre = ph.tile("F_re", [128, 32, 129], F32)
                F_im, b_Fim = ph.tile("F_im", [128, 32, 129], F32)
                bbr, b_bbr = ph.tile("bbr", [128, W], BF16)
                bbi, b_bbi = ph.tile("bbi", [128, W], BF16)
                Ffr = F_re[:].rearrange("p b j -> p (b j)")
                Ffi = F_im[:].rearrange("p b j -> p (b j)")
                MUL, ADD, SUB = ALU.mult, ALU.add, ALU.subtract
                if s == 0:
                    pp = Phase(kb)
                    ar, b_ar = pp.tile("ar", [128, W], F32)
                    ai, b_ai = pp.tile("ai", [128, W], F32)
                    sr, b_sr = pp.tile("sr", [128, W], F32)
                    t1f, b_t1 = pp.tile("t1", [128, NF], F32)
                    t2f, b_t2 = pp.tile("t2", [128, NF], F32)
                    t3f, b_t3 = pp.tile("t3", [128, NF], F32)
                    t1, t2, t3 = t1f[:, 0:W], t2f[:, 0:W], t3f[:, 0:W]
                    bdr, b_bdr = Ffr[:, 0:W], b_Fre
                    bdi, b_bdi = Ffi[:, 0:W], b_Fim
                    kb.dma("sp", DMA(ar[:], rowp_d[l, 0:1, :].to_broadcast([128, W])), writes=[b_ar], owner=b_ar)
                    kb.dma("sp", DMA(ai[:], rowp_d[l, 1:2, :].to_broadcast([128, W])), writes=[b_ai], owner=b_ai)
                    kb.dma("sp", DMA(sr[:], rowp_d[l, 2:3, :].to_broadcast([128, W])), writes=[b_sr], owner=b_sr)
                    kb.dma("sp", DMA(bdr, bbd_d[l, :, 0, :]), writes=[b_bdr], owner=b_bdr)
                    kb.dma("sp", DMA(bdi, bbd_d[l, :, 1, :]), writes=[b_bdi], owner=b_bdi)

                    def sincos(dst_sin, b_ds, dst_cos, b_dc, ycyc, b_y, tmp, b_tmp):
                        TS([b_y], [b_tmp], tmp, ycyc, MAGIC, MAGIC, ALU.add, ALU.subtract)
                        TT([b_y, b_tmp], [b_tmp], tmp, ycyc, tmp, ALU.subtract)
                        A([b_tmp], [b_ds], dst_sin, tmp, AF.Sin, scale=TWO_PI)
                        TS([b_y], [b_y], ycyc, ycyc, 0.25, None, ALU.add)
                        TS([b_y], [b_tmp], tmp, ycyc, MAGIC, MAGIC, ALU.add, ALU.subtract)
                        TT([b_y, b_tmp], [b_tmp], tmp, ycyc, tmp, ALU.subtract)
                        A([b_tmp], [b_dc], dst_cos, tmp, AF.Sin, scale=TWO_PI)

                    MUL, ADD, SUB = ALU.mult, ALU.add, ALU.subtract
                    A([b_sr], [b_sr], sr[:], sr[:], AF.Exp)
                    TT([b_ar, b_sr], [b_t1], t1, ar[:], sr[:], MUL)
                    A([b_t1], [b_t1], t1, t1, AF.Exp)
                    TT([b_ai, b_sr], [b_t2], t2, ai[:], sr[:], MUL)
                    TS([b_t2], [b_t2], t2, t2, 1.0 / TWO_PI, None, MUL)
                    sincos(E_im[:], b_Eim, E_re[:], b_Ere, t2, b_t2, t3, b_t3)
                    TT([b_Ere, b_t1], [b_Ere], E_re[:], E_re[:], t1, MUL)
                    TT([b_Eim, b_t1], [b_Eim], E_im[:], E_im[:], t1, MUL)
                    TS([b_Ere], [b_Ere], E_re[:], E_re[:], -1.0, None, ADD)
                    TT([b_ar], [b_t1], t1, ar[:], ar[:], MUL)
                    TT([b_ai], [b_t2], t2, ai[:], ai[:], MUL)
                    TT([b_t1, b_t2], [b_t1], t1, t1, t2, ADD)
                    V("reciprocal", [b_t1], [b_t1], out=t1, in_=t1)
                    TT([b_Ere, b_ar], [b_t2], t2, E_re[:], ar[:], MUL)
                    TT([b_Eim, b_ai], [b_t3], t3, E_im[:], ai[:], MUL)
                    TT([b_t2, b_t3], [b_t2], t2, t2, t3, ADD)
                    TT([b_t2, b_t1], [b_t2], t2, t2, t1, MUL)
                    TT([b_Eim, b_ar], [b_t3], t3, E_im[:], ar[:], MUL)
                    TT([b_Ere, b_ai], [b_Eim], E_im[:], E_re[:], ai[:], MUL)
                    TT([b_t3, b_Eim], [b_t3], t3, t3, E_im[:], SUB)
                    TT([b_t3, b_t1], [b_t3], t3, t3, t1, MUL)
                    TT([b_t2, b_bdr], [b_Ere], E_re[:], t2, bdr, MUL)
                    TT([b_t3, b_bdi], [b_Eim], E_im[:], t3, bdi, MUL)
                    TT([b_Ere, b_Eim], [b_bbr], bbr[:], E_re[:], E_im[:], SUB)
                    TT([b_t2, b_bdi], [b_Ere], E_re[:], t2, bdi, MUL)
                    TT([b_t3, b_bdr], [b_Eim], E_im[:], t3, bdr, MUL)
                    TT([b_Ere, b_Eim], [b_bbi], bbi[:], E_re[:], E_im[:], ADD)
                    iocol = col("iota", 0)
                    TT([b_ar, b_sr], [b_t1], t1, ar[:], sr[:], MUL)
                    TS([b_t1, b_colp], [b_t1], t1, t1, iocol, -1.0, MUL, MUL)
                    A([b_t1], [b_t1], t1, t1, AF.Exp)
                    TT([b_ai, b_sr], [b_t2], t2, ai[:], sr[:], MUL)
                    TS([b_t2, b_colp], [b_t2], t2, t2, iocol, -1.0 / TWO_PI, MUL, MUL)
                    sincos(E_im[:], b_Eim, E_re[:], b_Ere, t2, b_t2, t3, b_t3)
                    TT([b_Ere, b_t1], [b_Ere], E_re[:], E_re[:], t1, MUL)
                    TT([b_Eim, b_t1], [b_Eim], E_im[:], E_im[:], t1, MUL)
                    stc, b_stc = pp.tile("stc", [128, 32], F32)
                    alc, b_alc = pp.tile("alc", [128, 32], F32)
                    thc, b_thc = pp.tile("thc", [128, 32], F32)
                    A([b_colp], [b_stc], stc[:], col("lsc", l * 32, 32), AF.Exp)
                    TT([b_colp, b_stc], [b_alc], alc[:], col("acre", l * 32, 32), stc[:], MUL)
                    TT([b_colp, b_stc], [b_thc], thc[:], col("acim", l * 32, 32), stc[:], MUL)
                    TS([b_thc], [b_thc], thc[:], thc[:], 1.0 / TWO_PI, None, MUL)
                    g1 = t1f[:].rearrange("p (b j) -> p b j", b=32)
                    g2 = t2f[:].rearrange("p (b j) -> p b j", b=32)
                    io_b = iorow[:].unsqueeze(1).to_broadcast([128, 32, 129])
                    TT([b_iorow, b_alc], [b_t1], g1, io_b, alc[:].unsqueeze(2).to_broadcast([128, 32, 129]), MUL)
                    A([b_t1], [b_t1], t1f[:], t1f[:], AF.Exp)
                    TT([b_iorow, b_thc], [b_t2], g2, io_b, thc[:].unsqueeze(2).to_broadcast([128, 32, 129]), MUL)
                    sincos(Ffi, b_Fim, Ffr, b_Fre, t2f[:], b_t2, t3f[:], b_t3)
                    TT([b_Fre, b_t1], [b_Fre], Ffr, Ffr, t1f[:], MUL)
                    TT([b_Fim, b_t1], [b_Fim], Ffi, Ffi, t1f[:], MUL)
                    if nST > 1:
                        kb.dma("sp", DMA(tabE[0], E_re[:]), reads=[b_Ere], owner=b_Ere)
                        kb.dma("sp", DMA(tabE[1], E_im[:]), reads=[b_Eim], owner=b_Eim)
                        kb.dma("sp", DMA(tabF[0], Ffr), reads=[b_Fre], owner=b_Fre)
                        kb.dma("sp", DMA(tabF[1], Ffi), reads=[b_Fim], owner=b_Fim)
                        kb.dma("sp", DMA(tabB[0], bbr[:]), reads=[b_bbr], owner=b_bbr)
                        kb.dma("sp", DMA(tabB[1], bbi[:]), reads=[b_bbi], owner=b_bbi)
                    pp.close()
                else:
                    kb.dma("sp", DMA(E_re[:], tabE[0]), writes=[b_Ere], owner=b_Ere)
                    kb.dma("sp", DMA(E_im[:], tabE[1]), writes=[b_Eim], owner=b_Eim)
                    kb.dma("sp", DMA(Ffr, tabF[0]), writes=[b_Fre], owner=b_Fre)
                    kb.dma("sp", DMA(Ffi, tabF[1]), writes=[b_Fim], owner=b_Fim)
                    kb.dma("sp", DMA(bbr[:], tabB[0]), writes=[b_bbr], owner=b_bbr)
                    kb.dma("sp", DMA(bbi[:], tabB[1]), writes=[b_bbi], owner=b_bbi)
                cmat, b_cmat = ph.tile("cmat", [128, 8192], BF16)
                kb.dma("pool", [DMA(cmat[:, q * 2048:(q + 1) * 2048], ct_d[l, :, q * 2048:(q + 1) * 2048]) for q in range(4)], writes=[b_cmat], owner=b_cmat)
                ncmat, b_ncmat = ph.tile("ncmat", [128, 8192], BF16)
                TS([b_cmat], [b_ncmat], ncmat[:], cmat[:], -1.0, None, MUL)
                u_all, b_u = ph.tile("u_all", [128, 8, ST], BF16)
                kb.dma("sp", DMA(u_all[:], uT.rearrange("(c p) t -> p c t", p=128)), writes=[b_u], owner=b_u)
                p_ring = ph.ring("pp4", 2, [128, 4, 512], BF16)
                q_ring = ph.ring("qq4", 2, [128, 4, 512], BF16)
                cp_ring = ph.ring("cp", 2, [128, 2, 512], F32)
                yc_ring = ph.ring("yc", 4, [128, 128], BF16)
                ty_ring = ph.ring("ty", 2, [128, 128], F32)
                tn_ring = ph.ring("tn", 2, [128, 4, 4], F32)
                if first:
                    V("memset", [], [b_s5ca], ap=s5ca[:], constant=0.0)
                    V("memset", [], [b_s5cb], ap=s5cb[:], constant=0.0)
                units = [(j, ct) for j in range(ST // 128) for ct in range(8)]

                def bu_banks(ui):
                    base = (ui % 2) * 4
                    return base, base + 1, base + 2, base + 3

                def emit_bu(ui):
                    j, ct = units[ui]
                    br, bi_, _, _ = bu_banks(ui)
                    ul = u_all[:, ct, j * 128:(j + 1) * 128]
                    c0, c1 = ct * 512, (ct + 1) * 512
                    kb.op("pe", MM(bank(br), ul, bbr[:, c0:c1]), reads=[b_u, b_bbr], writes=[pb[br]])
                    kb.op("pe", MM(bank(bi_), ul, bbi[:, c0:c1]), reads=[b_u, b_bbi], writes=[pb[bi_]])

                def st_eprod(ui, stt):
                    j, ct = units[ui]
                    br, bi_, cr_, ci_ = bu_banks(ui)
                    c0, c1 = ct * 512, (ct + 1) * 512
                    pp4, b_p = p_ring.next()
                    stt["p"] = (pp4, b_p)
                    TT([pb[br], b_Ere], [b_p], pp4[:, 0, :], bank(br), E_re[:, c0:c1], MUL)
                    TT([pb[bi_], b_Eim], [b_p], pp4[:, 1, :], bank(bi_), E_im[:, c0:c1], MUL)
                    TT([pb[bi_], b_Ere], [b_p], pp4[:, 2, :], bank(bi_), E_re[:, c0:c1], MUL)
                    TT([pb[br], b_Eim], [b_p], pp4[:, 3, :], bank(br), E_im[:, c0:c1], MUL)
                    fns = []
                    for blk in range(4):
                        bs = slice(blk * 128, (blk + 1) * 128)
                        fns.append(MM(bank(cr_)[:, bs], pp4[:, 0, bs], tri_b[:], start=True, stop=False))
                        fns.append(MM(bank(cr_)[:, bs], pp4[:, 1, bs], ntri_b[:], start=False, stop=True))
                        fns.append(MM(bank(ci_)[:, bs], pp4[:, 2, bs], tri_b[:], start=True, stop=False))
                        fns.append(MM(bank(ci_)[:, bs], pp4[:, 3, bs], tri_b[:], start=False, stop=True))
                    kb.op("pe", fns, reads=[b_p, b_trib, b_ntrib], writes=[pb[cr_], pb[ci_]])

                def st_xprod(ui, stt):
                    j, ct = units[ui]
                    br, bi_, cr_, ci_ = bu_banks(ui)
                    cp, b_cp = cp_ring.next()
                    s0, s1 = ct * 4, ct * 4 + 4
                    TT([pb[cr_], b_s5ca], [b_cp], v3(cp[:, 0, :]), v3(bank(cr_)), s5ca[:, s0:s1].unsqueeze(2).to_broadcast([128, 4, 128]), ADD)
                    TT([pb[ci_], b_s5cb], [b_cp], v3(cp[:, 1, :]), v3(bank(ci_)), s5cb[:, s0:s1].unsqueeze(2).to_broadcast([128, 4, 128]), ADD)
                    Fr = F_re[:, s0:s1, 0:128]
                    Fi = F_im[:, s0:s1, 0:128]
                    qq4, b_q = q_ring.next()
                    TT([b_cp, b_Fre], [b_q], v3(qq4[:, 0, :]), v3(cp[:, 0, :]), Fr, MUL)
                    TT([b_cp, b_Fim], [b_q], v3(qq4[:, 1, :]), v3(cp[:, 1, :]), Fi, MUL)
                    TT([b_cp, b_Fre], [b_q], v3(qq4[:, 2, :]), v3(cp[:, 1, :]), Fr, MUL)
                    TT([b_cp, b_Fim], [b_q], v3(qq4[:, 3, :]), v3(cp[:, 0, :]), Fi, MUL)
                    tn, b_tn = tn_ring.next()
                    cr = v3(cp[:, 0, :])[:, :, 127]
                    ci = v3(cp[:, 1, :])[:, :, 127]
                    Gr = F_re[:, s0:s1, 128]
                    Gi = F_im[:, s0:s1, 128]
                    TT([b_cp, b_Fre], [b_tn], tn[:, 0, :], cr, Gr, MUL)
                    TT([b_cp, b_Fim], [b_tn], tn[:, 1, :], ci, Gi, MUL)
                    TT([b_cp, b_Fre], [b_tn], tn[:, 2, :], ci, Gr, MUL)
                    TT([b_cp, b_Fim], [b_tn], tn[:, 3, :], cr, Gi, MUL)
                    TT([b_tn], [b_s5ca], s5ca[:, s0:s1], tn[:, 0, :], tn[:, 1, :], SUB)
                    TT([b_tn], [b_s5cb], s5cb[:, s0:s1], tn[:, 2, :], tn[:, 3, :], ADD)
                    fns = []
                    for blk in range(4):
                        bs = slice(blk * 128, (blk + 1) * 128)
                        o_re = ((ct * 4 + blk) * 2 + 0) * 128
                        o_im = ((ct * 4 + blk) * 2 + 1) * 128
                        ops = ((cmat, o_re, 0), (ncmat, o_re, 1), (ncmat, o_im, 2), (ncmat, o_im, 3))
                        for oi, (cm, o, qi) in enumerate(ops):
                            fns.append(MM(bank(cr_)[:, 0:128], cm[:, o:o + 128], qq4[:, qi, bs], start=(blk == 0 and oi == 0), stop=(blk == 3 and oi == 3)))
                    kb.op("pe", fns, reads=[b_cmat, b_ncmat, b_q], writes=[pb[cr_]])

                def st_out(ui, stt):
                    j, ct = units[ui]
                    br, bi_, cr_, ci_ = bu_banks(ui)
                    ul = u_all[:, ct, j * 128:(j + 1) * 128]
                    ty, b_ty = ty_ring.next()
                    yc, b_yc = yc_ring.next()
                    STT([b_u, pb[cr_], b_colp], [b_ty], ty[:], ul, col("ssmd", l * 8 + ct), bank(cr_)[:, 0:128], MUL, ADD)
                    A([b_ty], [b_yc], yc[:], ty[:], AF.Gelu_apprx_tanh)
                    kb.dma("sp", DMA(ycT[ct * 128:(ct + 1) * 128, j * 128:(j + 1) * 128], yc[:]), reads=[b_yc], owner=b_yc)

                nU = len(units)
                emit_bu(0)
                emit_bu(1)
                for pi in range(0, nU, 2):
                    a, b = pi, pi + 1
                    sa, sb_ = {}, {}
                    st_eprod(a, sa)
                    st_eprod(b, sb_)
                    st_xprod(a, sa)
                    if a + 2 < nU:
                        emit_bu(a + 2)
                    st_xprod(b, sb_)
                    if b + 2 < nU:
                        emit_bu(b + 2)
                    st_out(a, sa)
                    st_out(b, sb_)
                ph.close()


            if "p3" not in skip:
                ph = Phase(kb)
                ys = []
                for nm, src in (("ya", yaT), ("yb", ybT), ("yc", ycT)):
                    t, b = ph.tile(nm, [128, 8, ST], BF16)
                    kb.dma("sp", DMA(t[:], src.rearrange("(c p) t -> p c t", p=128)), writes=[b], owner=b)
                    ys.append((t, b))
                ys.append(ys[2])
                wb_ring = ph.ring("wb", 2, [128, 4, 8, 128], BF16)
                g_ring = ph.ring("gg", 2, [128, 3, 512], BF16)
                ms_ring = ph.ring("ms", 2, [128, ST], BF16)
                tq_ring = ph.ring("tq", 2, [128, 4, 512], F32)
                gT_v = gT.rearrange("(b f p) t -> f p b t", b=3, f=16)
                it = 0
                for f in range(16):
                    wb, b_wb = wb_ring.next()
                    kb.dma("pool", [DMA(wb[:, q, :, :], wbr_t[l, f, :, q, :, :]) for q in range(4)], writes=[b_wb], owner=b_wb)
                    ms, b_ms = ms_ring.next()
                    for tt in range(nTT):
                        ts0, ts1 = tt * 512, (tt + 1) * 512
                        g3, b_g3 = g_ring.next()
                        kb.dma("sp", DMA(g3[:], gT_v[f][:, :, ts0:ts1]), writes=[b_g3], owner=b_g3)
                        b0 = (it % 2) * 4
                        it += 1
                        for q in range(4):
                            yt, b_yt = ys[q]
                            kb.op("pe", [MM(bank(b0 + q), wb[:, q, k, :], yt[:, k, ts0:ts1], start=(k == 0), stop=(k == 7)) for k in range(8)],
                                  reads=[b_wb, b_yt], writes=[pb[b0 + q]])
                        tq, b_tq = tq_ring.next()
                        A([pb[b0 + 3]], [b_tq], tq[:, 3, :], bank(b0 + 3), AF.Sigmoid)
                        TT([pb[b0], b_g3], [b_tq], tq[:, 0, :], bank(b0 + 0), g3[:, 0, :], ALU.mult)
                        TT([pb[b0 + 1], b_g3], [b_tq], tq[:, 1, :], bank(b0 + 1), g3[:, 1, :], ALU.mult)
                        TT([pb[b0 + 2], b_tq], [b_tq], tq[:, 2, :], bank(b0 + 2), tq[:, 3, :], ALU.mult)
                        TT([b_tq, b_g3], [b_tq], tq[:, 2, :], tq[:, 2, :], g3[:, 2, :], ALU.mult)
                        TT([b_tq], [b_tq], tq[:, 0, :], tq[:, 0, :], tq[:, 1, :], ALU.add)
                        TT([b_tq], [b_ms], ms[:, ts0:ts1], tq[:, 0, :], tq[:, 2, :], ALU.add)
                    kb.dma("sp", DMA(mT[f * 128:(f + 1) * 128, :], ms[:]), reads=[b_ms], owner=b_ms)
                ph.close()
                proj_residual(l, t0, mT, 16, wout_t[l], nTT)

            if "p4" not in skip:
                ph = Phase(kb)
                hT, b_hT = ph.tile("hT", [128, NK, ST], BF16)
                xin_ring = ph.ring("xin", 2, [128, NK, 512], F32)
                sq_ring = ph.ring("sq", 3, [128, 512], BF16)
                rs_ring = ph.ring("rs", 2, [128, 512], F32)
                rms_norm_to(hT, b_hT, "gffn", l, t0, xin_ring, sq_ring, rs_ring, (6, 7))
                wg_ring = ph.ring("wg", 2, [128, 2, NK, 128], BF16)
                as_ring = ph.ring("as", 2, [128, ST], BF16)
                sl_ring = ph.ring("sl", 2, [128, 512], F32)
                it = 0
                for jj in range(NJ):
                    wg, b_wg = wg_ring.next()
                    kb.dma("pool", [DMA(wg[:, q, :, :], wgu_t[l, jj, :, q, :, :]) for q in range(2)], writes=[b_wg], owner=b_wg)
                    ast, b_ast = as_ring.next()
                    for tt in range(nTT):
                        ts0, ts1 = tt * 512, (tt + 1) * 512
                        bg = (it % 3) * 2
                        it += 1
                        for q in range(2):
                            kb.op("pe", [MM(bank(bg + q), wg[:, q, k, :], hT[:, k, ts0:ts1], start=(k == 0), stop=(k == NK - 1)) for k in range(NK)],
                                  reads=[b_wg, b_hT], writes=[pb[bg + q]])
                        sl, b_sl = sl_ring.next()
                        A([pb[bg]], [b_sl], sl[:], bank(bg), AF.Silu)
                        TT([pb[bg + 1], b_sl], [b_ast], ast[:, ts0:ts1], bank(bg + 1), sl[:], ALU.mult)
                    kb.dma("sp", DMA(actT[jj * 128:(jj + 1) * 128, :], ast[:]), reads=[b_ast], owner=b_ast)
                ph.close()
                proj_residual(l, t0, actT, NJ, wdn_t[l], min(2, nTT))

    ph = Phase(kb)
    xin_ring = ph.ring("xin", 2, [128, NK, 512], F32)
    sq_ring = ph.ring("sq", 3, [128, 512], BF16)
    rs_ring = ph.ring("rs", 2, [128, 512], F32)
    outv = outT.rearrange("(k p) t -> p k t", p=128)
    for tt in range(S // 512):
        tg = tt * 512
        xin, b_xin = xin_ring.next()
        rs, b_rs = rms_norm_tile(xin, b_xin, tg, sq_ring, rs_ring, 6 + (tt % 2))
        for k in range(NK):
            STT([b_xin, b_rs, b_colp], [b_xin], xin[:, k, :], xin[:, k, :], col("gfin", k), rs[:], ALU.mult, ALU.mult)
        kb.dma("sp", DMA(outv[:, :, tg:tg + 512], xin[:]), reads=[b_xin], owner=b_xin)
    ph.close()
    kb.finalize()
    return nc


def prep_weights(inp, L):
    f = np.float32
    out = {}
    w_in = inp["w_in"][:L]
    out["w_in_t"] = np.ascontiguousarray(w_in.reshape(L, 16, 128, 96, 128).transpose(0, 3, 2, 1, 4))
    wb = np.concatenate([inp["w_branch"][:L], inp["ssm_w_glu"][:L][:, None]], axis=1)
    out["wbr_t"] = np.ascontiguousarray(wb.reshape(L, 4, 8, 128, 16, 128).transpose(0, 4, 3, 1, 2, 5))
    out["wout_t"] = np.ascontiguousarray(inp["w_out"][:L].reshape(L, 16, 128, 16, 128).transpose(0, 3, 2, 1, 4))
    wgu = np.stack([inp["w_ffn_gate"][:L], inp["w_ffn_up"][:L]], axis=1)
    out["wgu_t"] = np.ascontiguousarray(wgu.reshape(L, 2, 16, 128, NJ, 128).transpose(0, 4, 3, 1, 2, 5))
    out["wdn_t"] = np.ascontiguousarray(inp["w_ffn_down"][:L].reshape(L, NJ, 128, 16, 128).transpose(0, 3, 2, 1, 4))
    COFF, NCOL = col_layout(L)
    colp = np.zeros((128, NCOL), f)

    def put(name, arr):
        colp[:, COFF[name]:COFF[name] + arr.shape[1]] = arr
    put("gmix", inp["norm_mix_g"][:L].reshape(L, 16, 128).transpose(2, 0, 1).reshape(128, -1))
    put("gffn", inp["norm_ffn_g"][:L].reshape(L, 16, 128).transpose(2, 0, 1).reshape(128, -1))
    put("gfin", inp["norm_final_g"].reshape(16, 128).T)
    put("gbias", inp["gate_bias"][:L].reshape(L, 3, 16, 128).transpose(3, 0, 1, 2).reshape(128, -1))
    put("convw", inp["lru_conv_w"][:L].reshape(L, 4, 8, 128).transpose(3, 0, 2, 1).reshape(128, -1))
    for nm, key in (("convb", "lru_conv_b"), ("ba", "lru_ba"), ("bx", "lru_bx"), ("lam", "lru_lambda"), ("ssmd", "ssm_d")):
        put(nm, inp[key][:L].reshape(L, 8, 128).transpose(2, 0, 1).reshape(128, -1))
    for nm, arr in (("acre", inp["ssm_a_re"][:L]), ("acim", inp["ssm_a_im"][:L]),
                    ("lsc", np.repeat(inp["ssm_log_step"][:L][:, :, None], 64, axis=2))):
        put(nm, arr.reshape(L, 32, 2, 64).transpose(2, 3, 0, 1).reshape(128, -1))
    colp[:, COFF["iota"]] = np.arange(128, dtype=f)
    out["colp"] = colp
    rowp = np.stack([inp["ssm_a_re"][:L].reshape(L, 4096), inp["ssm_a_im"][:L].reshape(L, 4096),
                     np.repeat(inp["ssm_log_step"][:L][:, :, None], 64, axis=2).reshape(L, 4096)], axis=1)
    out["rowp"] = np.ascontiguousarray(rowp.astype(f))
    lruw = np.zeros((L, 128, 8, 2, 128), f)
    for wi, key in enumerate(("lru_wa", "lru_wx")):
        w = inp[key][:L].reshape(L, 8, 2, 64, 64)
        for nl in range(2):
            lruw[:, nl * 64:(nl + 1) * 64, :, wi, nl * 64:(nl + 1) * 64] = w[:, :, nl].transpose(0, 2, 1, 3)
    out["lruw"] = lruw
    kk = np.arange(128)[:, None, None]
    kt = np.arange(5)[None, :, None]
    qq = np.arange(128)[None, None, :]
    dist = (4 - kt) * 128 + qq - kk
    rel = np.clip(dist, -128, 128) + 128
    cdiff = 8 - 2 * kt + qq // 64 - kk // 64
    valid = (cdiff >= 0) & (cdiff <= 8)
    ab = inp["attn_rel_bias"][:L][:, :, rel]
    ab = np.where(valid[None, None], ab, f(-30000.0)).astype(f)
    out["abias"] = np.ascontiguousarray(ab.reshape(L, 8, 128, 640))
    bbd = np.zeros((L, 128, 2, 8, 8, 64), f)
    for ri, key in enumerate(("ssm_b_re", "ssm_b_im")):
        B = inp[key][:L].reshape(L, 8, 8, 64, 16)
        for gl in range(8):
            bbd[:, gl * 16:(gl + 1) * 16, ri, :, gl, :] = B[:, :, gl].transpose(0, 3, 1, 2)
    out["bbd"] = np.ascontiguousarray(bbd.reshape(L, 128, 2, 4096))
    ctd = np.zeros((L, 128, 8, 4, 2, 128), f)
    for ri, key in enumerate(("ssm_c_re", "ssm_c_im")):
        C = inp[key][:L].reshape(L, 8, 4, 2, 16, 64)
        for blk in range(4):
            for gl2 in range(2):
                c0 = (2 * blk + gl2) * 16
                ctd[:, gl2 * 64:(gl2 + 1) * 64, :, blk, ri, c0:c0 + 16] = C[:, :, blk, gl2].transpose(0, 3, 1, 2)
    out["ctd"] = np.ascontiguousarray(ctd.reshape(L, 128, 8192))
    cst = np.zeros((128, 386), f)
    cst[:, 0:128] = np.triu(np.ones((128, 128), f))
    cst[:, 128:256] = 1.0
    cst[:, 256:385] = np.arange(129, dtype=f)[None, :]
    cst[:, 385] = 1e-6
    out["cst"] = cst
    out["epsd"] = np.full((128, 1), 1e-6, f)
    return out


_CACHE = {}


def run_model(inputs, L, S_core, ST, n_cores, dbg=False, skip=(), spread=False):
    x = np.asarray(inputs["x"], np.float32)
    B, S, _ = x.shape
    assert S == S_core and B <= n_cores
    key = (L, S_core, ST, dbg, tuple(skip))
    if key not in _CACHE:
        _CACHE[key] = build_program(L, S_core, ST, dbg=dbg, skip=skip)
    nc = _CACHE[key]
    wts = prep_weights({k: np.asarray(v, np.float32) for k, v in inputs.items() if k != "x"}, L)
    if spread:
        real = {0: 0, 1: 1, 4: 2, 5: 3}
        zw = dict(wts)
        for k in ("w_in_t", "wbr_t", "wout_t", "wgu_t", "wdn_t"):
            zw[k] = np.zeros_like(wts[k])
        zx = np.zeros((D, S), np.float32)
        in_maps = []
        for c in range(8):
            if c in real and real[c] < B:
                m = dict(wts)
                m["xT"] = np.ascontiguousarray(x[real[c]].T)
            else:
                m = dict(zw)
                m["xT"] = zx
            in_maps.append(m)
        res = run_bass_kernel_spmd(nc, in_maps, core_ids=list(range(8)))
        slots = [c for c in range(8) if c in real and real[c] < B]
        out = np.stack([np.ascontiguousarray(res.results[c]["outT"].T) for c in sorted(slots, key=lambda c: real[c])], axis=0)
        return out.astype(np.float32), res
    in_maps = []
    for c in range(n_cores):
        b = c % B
        m = dict(wts)
        m["xT"] = np.ascontiguousarray(x[b].T)
        in_maps.append(m)
    res = run_bass_kernel_spmd(nc, in_maps, core_ids=list(range(n_cores)))
    out = np.stack([np.ascontiguousarray(res.results[b]["outT"].T) for b in range(B)], axis=0)
    return out.astype(np.float32), res


def kernel(**inputs):
    out, _ = run_model(inputs, 4, 4096, 2048, 8, spread=True)
    return out
```

```python
import math
from contextlib import ExitStack
import numpy as np
import concourse.bass as bass
import concourse.mybir as mybir
from concourse.bass_utils import run_bass_kernel_spmd

F32 = mybir.dt.float32
BF16 = mybir.dt.bfloat16
AF = mybir.ActivationFunctionType
ALU = mybir.AluOpType

D = 2048
NK = 16
MIXW = 1024
FH = 5632
NJ = 44
TWO_PI = float(2 * math.pi)
MAGIC = 12582912.0


class Buf:
    __slots__ = ("name", "w", "r", "dsem")

    def __init__(self, name):
        self.name = name
        self.w = None
        self.r = {}
        self.dsem = None


class Eng:
    def __init__(self, name, sem):
        self.name = name
        self.sem = sem
        self.cnt = 0
        self.waited = {}
        self.prog = []


class KB:
    def __init__(self, nc):
        self.nc = nc
        self.stack = ExitStack()
        self.sems = {}
        self.semcnt = {}
        self.free_dsems = []
        self.dirty = {}
        self.eng = {}
        for name in ("pe", "act", "dve", "pool", "sp"):
            self.eng[name] = Eng(name, self.newsem("e_" + name))
        self.uid = 0

    def newsem(self, name):
        h = self.stack.enter_context(self.nc.semaphore(name))
        key = len(self.sems)
        self.sems[key] = h
        self.semcnt[key] = 0
        return key

    def buf(self, name="b"):
        self.uid += 1
        return Buf(f"{name}_{self.uid}")

    def _deps(self, reads, writes):
        deps = {}
        for b in reads:
            if b.w is not None and deps.get(b.w[0], 0) < b.w[1]:
                deps[b.w[0]] = b.w[1]
        for b in writes:
            if b.w is not None and deps.get(b.w[0], 0) < b.w[1]:
                deps[b.w[0]] = b.w[1]
            for k, v in b.r.items():
                if deps.get(k, 0) < v:
                    deps[k] = v
        return deps

    def _emit_waits(self, e, deps, skip=None):
        for k, v in deps.items():
            if k == skip:
                continue
            if e.waited.get(k, 0) < v:
                e.prog.append(("w", k, v))
                e.waited[k] = v

    def _update(self, tok, reads, writes):
        k, v = tok
        for b in reads:
            if b.r.get(k, 0) < v:
                b.r[k] = v
        for b in writes:
            b.w = tok
            b.r = {}

    def op(self, en, fns, reads=(), writes=()):
        e = self.eng[en]
        if callable(fns):
            fns = [fns]
        deps = self._deps(reads, writes)
        self._emit_waits(e, deps, skip=e.sem if en == "pe" else None)
        for f in fns[:-1]:
            e.prog.append(("i", f, None, 0))
        e.cnt += 1
        e.prog.append(("i", fns[-1], e.sem, 1))
        tok = (e.sem, e.cnt)
        self._update(tok, reads, writes)
        return tok

    def dma(self, en, fns, reads=(), writes=(), owner=None):
        e = self.eng[en]
        if callable(fns):
            fns = [fns]
        if owner.dsem is None:
            owner.dsem = self.free_dsems.pop() if self.free_dsems else self.newsem("d%d" % len(self.sems))
        k = owner.dsem
        deps = self._deps(reads, writes)
        self._emit_waits(e, deps)
        for f in fns:
            self.semcnt[k] += 16
            e.prog.append(("i", f, k, 16))
        tok = (k, self.semcnt[k])
        self.dirty[k] = self.semcnt[k]
        self._update(tok, reads, writes)
        return tok

    def barrier(self):
        toks = dict(self.dirty)
        for e in self.eng.values():
            if e.cnt > 0:
                toks[e.sem] = e.cnt
        for e in self.eng.values():
            self._emit_waits(e, toks)
        self.dirty = {}

    def release(self, bufs):
        for b in bufs:
            if b.dsem is not None:
                self.free_dsems.append(b.dsem)
                b.dsem = None

    def finalize(self):
        nc = self.nc
        self.barrier()
        engs, sems = self.eng, self.sems

        def replay(e, h):
            for it in e.prog:
                if it[0] == "w":
                    h.wait_ge(sems[it[1]], it[2])
                else:
                    inst = it[1](h)
                    if it[2] is not None:
                        inst.then_inc(sems[it[2]], it[3])

        with nc.Block() as block:
            @block.tensor
            def _(h):
                replay(engs["pe"], h)

            @block.scalar
            def _(h):
                replay(engs["act"], h)

            @block.vector
            def _(h):
                replay(engs["dve"], h)

            @block.gpsimd
            def _(h):
                replay(engs["pool"], h)

            @block.sync
            def _(h):
                replay(engs["sp"], h)
        self.stack.close()


class Phase:
    def __init__(self, kb):
        self.kb = kb
        self.st = ExitStack()
        self.bufs = []

    def tile(self, name, shape, dtype):
        kb = self.kb
        kb.uid += 1
        t = self.st.enter_context(kb.nc.sbuf_tensor(f"{name}_{kb.uid}", list(shape), dtype))
        b = kb.buf(name)
        self.bufs.append(b)
        return t, b

    def ring(self, name, n, shape, dtype):
        return Ring([self.tile(name, shape, dtype) for _ in range(n)])

    def close(self):
        self.kb.barrier()
        self.kb.release(self.bufs)
        self.st.close()


class Ring:
    def __init__(self, items):
        self.items = items
        self.i = 0

    def next(self):
        it = self.items[self.i % len(self.items)]
        self.i += 1
        return it


def col_layout(L):
    segs = [("gmix", L * 16), ("gffn", L * 16), ("gfin", 16), ("gbias", L * 48), ("convw", L * 32),
            ("convb", L * 8), ("ba", L * 8), ("bx", L * 8), ("lam", L * 8), ("ssmd", L * 8),
            ("acre", L * 32), ("acim", L * 32), ("lsc", L * 32), ("iota", 1)]
    off, o = {}, 0
    for n, w in segs:
        off[n] = o
        o += w
    return off, o


def build_program(L, S, ST, dbg=False, has_prev=False, skip=()):
    nc = bass.Bass("TRN2", target_bir_lowering=False)
    kb = KB(nc)
    gst = kb.stack
    nST = S // ST
    nTT = ST // 512
    COFF, NCOL = col_layout(L)
    okind = "ExternalOutput" if dbg else "Internal"

    def din(name, shape, dt=F32):
        return nc.dram_tensor(name, list(shape), dt, kind="ExternalInput").ap()

    def dscr(name, shape, dt):
        return nc.dram_tensor(name, list(shape), dt, kind=okind).ap()

    xT = din("xT", [D, S])
    w_in_t = din("w_in_t", [L, 96, 128, 16, 128])
    wbr_t = din("wbr_t", [L, 16, 128, 4, 8, 128])
    wout_t = din("wout_t", [L, 16, 128, 16, 128])
    wgu_t = din("wgu_t", [L, NJ, 128, 2, 16, 128])
    wdn_t = din("wdn_t", [L, 16, 128, NJ, 128])
    colp_d = din("colp", [128, NCOL])
    rowp_d = din("rowp", [L, 3, 4096])
    lruw_d = din("lruw", [L, 128, 8, 2, 128])
    abias_d = din("abias", [L, 8, 128, 640])
    bbd_d = din("bbd", [L, 128, 2, 4096])
    ct_d = din("ctd", [L, 128, 8192])
    cst_d = din("cst", [128, 386])
    eps_d = din("epsd", [128, 1])
    outT = nc.dram_tensor("outT", [D, S], F32, kind="ExternalOutput").ap()

    xr = dscr("xr", [D, S], F32)
    lxT = dscr("lxT", [MIXW, S], BF16)
    lgT = dscr("lgT", [MIXW, ST], BF16)
    qT = dscr("qT", [MIXW, ST], BF16)
    kT = dscr("kT", [MIXW, S], BF16)
    vS = dscr("vS", [S, MIXW], BF16)
    uT = dscr("uT", [MIXW, ST], BF16)
    gT = dscr("gT", [3 * D, ST], BF16)
    yaT = dscr("yaT", [MIXW, ST], BF16)
    ybT = dscr("ybT", [MIXW, ST], BF16)
    ycT = dscr("ycT", [MIXW, ST], BF16)
    mT = dscr("mT", [D, ST], BF16)
    actT = dscr("actT", [FH, ST], BF16)
    tabE = nc.dram_tensor("tabE", [2, 128, 4096], F32, kind="Internal").ap()
    tabF = nc.dram_tensor("tabF", [2, 128, 32 * 129], F32, kind="Internal").ap()
    tabB = nc.dram_tensor("tabB", [2, 128, 4096], BF16, kind="Internal").ap()

    def gtile(name, shape, dt):
        t = gst.enter_context(nc.sbuf_tensor(name, list(shape), dt))
        return t, kb.buf(name)

    colp, b_colp = gtile("colp_s", [128, NCOL], F32)
    nsp8, b_nsp8 = gtile("nsp8", [128, L * 8], F32)
    ones_f, b_onesf = gtile("ones_f", [128, 128], F32)
    ones_b, b_onesb = gtile("ones_b", [128, 128], BF16)
    tri_b, b_trib = gtile("tri_b", [128, 128], BF16)
    ntri_b, b_ntrib = gtile("ntri_b", [128, 128], BF16)
    iorow, b_iorow = gtile("iorow", [128, 129], F32)
    lru_h, b_lruh = gtile("lru_h", [128, 8], F32)
    s5ca, b_s5ca = gtile("s5ca", [128, 32], F32)
    s5cb, b_s5cb = gtile("s5cb", [128, 32], F32)
    epsc, b_epsc = gtile("epsc", [128, 1], F32)
    psum = gst.enter_context(nc.psum_tensor("psum", [128, 4096], F32))
    pb = [kb.buf(f"ps{i}") for i in range(8)]

    def bank(i):
        return psum[:, i * 512:(i + 1) * 512]

    def col(name, idx, n=1):
        o = COFF[name] + idx
        return colp[:, o:o + n]

    def I(name, **kw):
        return lambda h: getattr(h, name)(**kw)

    def MM(out, lhsT, rhs, start=True, stop=True):
        return I("matmul", out=out, lhsT=lhsT, rhs=rhs, start=start, stop=stop)

    def DMA(out, in_):
        return I("dma_start", out=out, in_=in_)

    def V(name, r, w, **kw):
        return kb.op("dve", I(name, **kw), reads=r, writes=w)

    def A(r, w, out, in_, func, **kw):
        return kb.op("act", I("activation", out=out, in_=in_, func=func, **kw), reads=r, writes=w)

    def TT(r, w, out, in0, in1, op):
        return V("tensor_tensor", r, w, out=out, in0=in0, in1=in1, op=op)

    def TS(r, w, out, in0, s1, s2, op0, op1=None):
        if op1 is None:
            return V("tensor_scalar", r, w, out=out, in0=in0, scalar1=s1, scalar2=None, op0=op0)
        return V("tensor_scalar", r, w, out=out, in0=in0, scalar1=s1, scalar2=s2, op0=op0, op1=op1)

    def STT(r, w, out, in0, scalar, in1, op0, op1):
        return V("scalar_tensor_tensor", r, w, out=out, in0=in0, scalar=scalar, in1=in1, op0=op0, op1=op1)

    kb.dma("sp", DMA(colp[:], colp_d), writes=[b_colp], owner=b_colp)
    kb.dma("sp", DMA(ones_f[:], cst_d[:, 128:256]), writes=[b_onesf], owner=b_onesf)
    kb.dma("sp", DMA(iorow[:], cst_d[:, 256:385]), writes=[b_iorow], owner=b_iorow)
    kb.dma("pool", DMA(tri_b[:], cst_d[:, 0:128]), writes=[b_trib], owner=b_trib)
    kb.dma("pool", DMA(ones_b[:], cst_d[:, 128:256]), writes=[b_onesb], owner=b_onesb)
    TS([b_trib], [b_ntrib], ntri_b[:], tri_b[:], -1.0, None, ALU.mult)
    kb.dma("sp", DMA(epsc[:], eps_d), writes=[b_epsc], owner=b_epsc)
    A([b_colp], [b_nsp8], nsp8[:], col("lam", 0, L * 8), AF.Exp, scale=-1.0)
    A([b_nsp8], [b_nsp8], nsp8[:], nsp8[:], AF.Ln, bias=1.0, scale=1.0)
    TS([b_nsp8], [b_nsp8], nsp8[:], nsp8[:], -8.0, None, ALU.mult)
    b_x0 = kb.buf("x0")
    kb.dma("sp", [DMA(xr[c * 128:(c + 1) * 128, :], xT[c * 128:(c + 1) * 128, :]) for c in range(16)], writes=[b_x0], owner=b_x0)
    kb.barrier()

    xr_v = xr.rearrange("(k p) t -> p k t", p=128)

    def rms_norm_tile(xin, b_xin, tg, sq_ring, rs_ring, pbi):
        kb.dma("sp", [DMA(xin[:, 0:8, :], xr_v[:, 0:8, tg:tg + 512]), DMA(xin[:, 8:16, :], xr_v[:, 8:16, tg:tg + 512])], writes=[b_xin], owner=b_xin)
        for k in range(NK):
            sq, b_sq = sq_ring.next()
            A([b_xin], [b_sq], sq[:], xin[:, k, :], AF.Square)
            kb.op("pe", MM(bank(pbi), ones_b[:], sq[:], start=(k == 0), stop=(k == NK - 1)), reads=[b_sq, b_onesb], writes=[pb[pbi]])
        rs, b_rs = rs_ring.next()
        A([pb[pbi], b_epsc], [b_rs], rs[:], bank(pbi), AF.Sqrt, scale=1.0 / D, bias=epsc[:])
        V("reciprocal", [b_rs], [b_rs], out=rs[:], in_=rs[:])
        return rs, b_rs

    def rms_norm_to(hT, b_hT, gname, l, t0, xin_ring, sq_ring, rs_ring, pbis):
        for tt in range(nTT):
            xin, b_xin = xin_ring.next()
            rs, b_rs = rms_norm_tile(xin, b_xin, t0 + tt * 512, sq_ring, rs_ring, pbis[tt % len(pbis)])
            for k in range(NK):
                STT([b_xin, b_rs, b_colp], [b_hT], hT[:, k, tt * 512:(tt + 1) * 512], xin[:, k, :], col(gname, l * 16 + k), rs[:], ALU.mult, ALU.mult)

    evac_flip = [0]

    def evac_copy(out_ap, in_ap, reads, writes):
        evac_flip[0] ^= 1
        if evac_flip[0]:
            A(reads, writes, out_ap, in_ap, AF.Copy)
        else:
            V("tensor_copy", reads, writes, out=out_ap, in_=in_ap)

    def proj_residual(l, t0, src_dram, nk, w_dram_l, chunk_tt):
        srcv = src_dram.rearrange("(k p) t -> p k t", p=128)
        splits = [(a, min(a + 16, nk)) for a in range(0, nk, 16)]
        for c0 in range(0, nTT, chunk_tt):
            ph = Phase(kb)
            ncols = chunk_tt * 512
            sm, b_sm = ph.tile("sm", [128, nk, ncols], BF16)
            kb.dma("sp", [DMA(sm[:, a:b, :], srcv[:, a:b, c0 * 512:c0 * 512 + ncols]) for a, b in splits], writes=[b_sm], owner=b_sm)
            wo_ring = ph.ring("wo", 2, [128, nk, 128], BF16)
            xs_ring = ph.ring("xs", 3, [128, 512], F32)
            it = 0
            for f in range(16):
                wo, b_wo = wo_ring.next()
                kb.dma("pool", [DMA(wo[:, a:b, :], w_dram_l[f, :, a:b, :]) for a, b in splits], writes=[b_wo], owner=b_wo)
                for tt in range(chunk_tt):
                    bi = it % 6
                    it += 1
                    tg = t0 + (c0 + tt) * 512
                    xs, b_xs = xs_ring.next()
                    kb.dma("sp", DMA(xs[:], xr[f * 128:(f + 1) * 128, tg:tg + 512]), writes=[b_xs], owner=b_xs)
                    kb.op("pe", [MM(bank(bi), wo[:, k, :], sm[:, k, tt * 512:(tt + 1) * 512], start=(k == 0), stop=(k == nk - 1)) for k in range(nk)],
                          reads=[b_wo, b_sm], writes=[pb[bi]])
                    TT([pb[bi], b_xs], [b_xs], xs[:], bank(bi), xs[:], ALU.add)
                    kb.dma("sp", DMA(xr[f * 128:(f + 1) * 128, tg:tg + 512], xs[:]), reads=[b_xs], owner=b_xs)
            ph.close()

    def v3(ap):
        return ap.rearrange("p (b j) -> p b j", b=4)

    for l in range(L):
        for s in range(nST):
            t0 = s * ST
            first = (s == 0) and not has_prev
            if "p1" not in skip:
                phO = Phase(kb)
                hT, b_hT = phO.tile("hT", [128, NK, ST], BF16)
                ph = Phase(kb)
                xin_ring = ph.ring("xin", 2, [128, NK, 512], F32)
                sq_ring = ph.ring("sq", 3, [128, 512], BF16)
                rs_ring = ph.ring("rs", 2, [128, 512], F32)
                wt_ring = ph.ring("wt", 3, [128, NK, 128], BF16)
                stg_ring = ph.ring("stg", 3, [128, ST], BF16)
                wv_ring = ph.ring("wv", 1, [128, NK, 512], BF16)
                sv_ring = ph.ring("sv", 3, [128, 512], BF16)
                rms_norm_to(hT, b_hT, "gmix", l, t0, xin_ring, sq_ring, rs_ring, (6, 7))
                pbr = 0
                for m in list(range(0, 32)) + list(range(40, 48)):
                    wt, b_wt = wt_ring.next()
                    kb.dma("pool", DMA(wt[:], w_in_t[l, m]), writes=[b_wt], owner=b_wt)
                    stg, b_stg = stg_ring.next()
                    for tt in range(nTT):
                        bi = pbr % 6
                        pbr += 1
                        kb.op("pe", [MM(bank(bi), wt[:, k, :], hT[:, k, tt * 512:(tt + 1) * 512], start=(k == 0), stop=(k == NK - 1)) for k in range(NK)],
                              reads=[b_wt, b_hT], writes=[pb[bi]])
                        o_ap = stg[:, tt * 512:(tt + 1) * 512]
                        if m < 48:
                            evac_copy(o_ap, bank(bi), [pb[bi]], [b_stg])
                        else:
                            A([pb[bi], b_colp], [b_stg], o_ap, bank(bi), AF.Sigmoid, bias=col("gbias", l * 48 + (m - 48)), scale=1.0)
                    if m < 8:
                        dst = lxT[m * 128:(m + 1) * 128, t0:t0 + ST]
                    elif m < 16:
                        dst = lgT[(m - 8) * 128:(m - 7) * 128, :]
                    elif m < 24:
                        dst = qT[(m - 16) * 128:(m - 15) * 128, :]
                    elif m < 32:
                        dst = kT[(m - 24) * 128:(m - 23) * 128, t0:t0 + ST]
                    elif m < 48:
                        dst = uT[(m - 40) * 128:(m - 39) * 128, :]
                    else:
                        dst = gT[(m - 48) * 128:(m - 47) * 128, :]
                    kb.dma("sp", DMA(dst, stg[:]), reads=[b_stg], owner=b_stg)
                for cb in range(2):
                    wv, b_wv = wv_ring.next()
                    kb.dma("pool", [DMA(wv[:, :, mm * 128:(mm + 1) * 128], w_in_t[l, 32 + 4 * cb + mm]) for mm in range(4)], writes=[b_wv], owner=b_wv)
                    for j in range(ST // 128):
                        bi = pbr % 6
                        pbr += 1
                        kb.op("pe", [MM(bank(bi), hT[:, k, j * 128:(j + 1) * 128], wv[:, k, :], start=(k == 0), stop=(k == NK - 1)) for k in range(NK)],
                              reads=[b_wv, b_hT], writes=[pb[bi]])
                        sv, b_sv = sv_ring.next()
                        evac_copy(sv[:], bank(bi), [pb[bi]], [b_sv])
                        kb.dma("sp", DMA(vS[t0 + j * 128:t0 + (j + 1) * 128, cb * 512:(cb + 1) * 512], sv[:]), reads=[b_sv], owner=b_sv)
                ph.close()

            if "p1" not in skip:
                ph = Phase(kb)
                gwt_ring = ph.ring("gwt", 3, [128, NK, 128], BF16)
                gst_ring = ph.ring("gst", 3, [128, ST], BF16)

                def gates_gen():
                    pend = None
                    g = 0
                    for m in range(48, 96):
                        wt, b_wt = gwt_ring.next()
                        kb.dma("pool", DMA(wt[:], w_in_t[l, m]), writes=[b_wt], owner=b_wt)
                        stg, b_stg = gst_ring.next()
                        for tt in range(nTT):
                            bi = 6 + (g % 2)
                            g += 1
                            kb.op("pe", [MM(bank(bi), wt[:, k, :], hT[:, k, tt * 512:(tt + 1) * 512], start=(k == 0), stop=(k == NK - 1)) for k in range(NK)],
                                  reads=[b_wt, b_hT], writes=[pb[bi]])
                            if pend is not None:
                                pend()
                            def fin(bi=bi, m=m, tt=tt, stg=stg, b_stg=b_stg):
                                A([pb[bi], b_colp], [b_stg], stg[:, tt * 512:(tt + 1) * 512], bank(bi), AF.Sigmoid, bias=col("gbias", l * 48 + (m - 48)), scale=1.0)
                                if tt == nTT - 1:
                                    kb.dma("sp", DMA(gT[(m - 48) * 128:(m - 47) * 128, :], stg[:]), reads=[b_stg], owner=b_stg)
                            pend = fin
                            yield
                    pend()

                ggen = gates_gen()

                def gstep(k):
                    for _ in range(k):
                        if next(ggen, "done") == "done":
                            return False
                    return True
            else:
                ph = Phase(kb)

                def gstep(k):
                    return False

            if "p2a" not in skip:
                bd, b_bd = ph.tile("bd", [128, 8, 2, 128], BF16)
                kb.dma("pool", DMA(bd[:], lruw_d[l]), writes=[b_bd], owner=b_bd)
                lx_ring = ph.ring("lx", 3, [128, 515], BF16)
                lg_ring = ph.ring("lg", 3, [128, 512], BF16)
                ya_ring = ph.ring("ya", 2, [128, 512], BF16)
                hs_ring = ph.ring("hs", 2, [128, 512], F32)
                xc_ring = ph.ring("xc", 2, [128, 512], F32)
                xcb_ring = ph.ring("xcb", 2, [128, 512], BF16)
                rr, b_rr = ph.tile("rr", [128, 512], F32)
                ii, b_ii = ph.tile("ii", [128, 512], F32)
                aa, b_aa = ph.tile("aa", [128, 512], F32)
                a2, b_a2 = ph.tile("a2", [128, 512], F32)
                gg, b_gg = ph.tile("gg", [128, 512], F32)
                if first:
                    V("memset", [], [b_lruh], ap=lru_h[:], constant=0.0)
                lunits = [(ct, tt) for ct in range(8) for tt in range(nTT)]
                lst = {}

                def lru_load(n):
                    ct, tt = lunits[n]
                    r0, r1 = ct * 128, (ct + 1) * 128
                    tg = t0 + tt * 512
                    lx, b_lx = lx_ring.next()
                    lg, b_lg = lg_ring.next()
                    if tg == 0 and not has_prev:
                        V("memset", [], [b_lx], ap=lx[:, 0:3], constant=0.0)
                        kb.dma("sp", DMA(lx[:, 3:515], lxT[r0:r1, 0:512]), writes=[b_lx], owner=b_lx)
                    else:
                        kb.dma("sp", DMA(lx[:], lxT[r0:r1, tg - 3:tg + 512]), writes=[b_lx], owner=b_lx)
                    kb.dma("sp", DMA(lg[:], lgT[r0:r1, tt * 512:(tt + 1) * 512]), writes=[b_lg], owner=b_lg)
                    lst[n] = dict(lx=lx, b_lx=b_lx, lg=lg, b_lg=b_lg)

                def lru_front(n):
                    ct, tt = lunits[n]
                    d = lst[n]
                    lx, b_lx = d["lx"], d["b_lx"]
                    xc, b_xc = xc_ring.next()
                    xcb, b_xcb = xcb_ring.next()
                    cwi = (l * 8 + ct) * 4
                    TS([b_lx, b_colp], [b_xc], xc[:], lx[:, 3:515], col("convw", cwi + 3), col("convb", l * 8 + ct), ALU.mult, ALU.add)
                    for k in range(3):
                        STT([b_lx, b_xc, b_colp], [b_xc], xc[:], lx[:, k:k + 512], col("convw", cwi + k), xc[:], ALU.mult, ALU.add)
                    V("tensor_copy", [b_xc], [b_xcb], out=xcb[:], in_=xc[:])
                    pbase = (n % 2) * 2
                    kb.op("pe", MM(bank(pbase), bd[:, ct, 0, :], xcb[:]), reads=[b_bd, b_xcb], writes=[pb[pbase]])
                    kb.op("pe", MM(bank(pbase + 1), bd[:, ct, 1, :], xcb[:]), reads=[b_bd, b_xcb], writes=[pb[pbase + 1]])
                    d.update(xc=xc, b_xc=b_xc, pbase=pbase)

                lru_load(0)
                if len(lunits) > 1:
                    lru_load(1)
                lru_front(0)
                for n in range(len(lunits)):
                    ct, tt = lunits[n]
                    r0, r1 = ct * 128, (ct + 1) * 128
                    d = lst[n]
                    xc, b_xc, pbase = d["xc"], d["b_xc"], d["pbase"]
                    lg, b_lg = d["lg"], d["b_lg"]
                    if n + 2 < len(lunits):
                        lru_load(n + 2)
                    A([pb[pbase], b_colp], [b_rr], rr[:], bank(pbase), AF.Sigmoid, bias=col("ba", l * 8 + ct), scale=1.0)
                    A([pb[pbase + 1], b_colp], [b_ii], ii[:], bank(pbase + 1), AF.Sigmoid, bias=col("bx", l * 8 + ct), scale=1.0)
                    A([b_rr, b_nsp8], [b_aa], aa[:], rr[:], AF.Exp, scale=nsp8[:, l * 8 + ct:l * 8 + ct + 1])
                    TT([b_ii, b_xc], [b_ii], ii[:], ii[:], xc[:], ALU.mult)
                    TT([b_aa], [b_a2], a2[:], aa[:], aa[:], ALU.mult)
                    A([b_a2], [b_a2], a2[:], a2[:], AF.Sqrt, scale=-1.0, bias=1.0)
                    gstep(1)
                    if n + 1 < len(lunits):
                        lru_front(n + 1)
                    gstep(1)
                    A([b_lg], [b_gg], gg[:], lg[:], AF.Gelu_apprx_tanh)
                    TT([b_ii, b_a2], [b_ii], ii[:], ii[:], a2[:], ALU.mult)
                    hs, b_hs = hs_ring.next()
                    V("tensor_tensor_scan", [b_aa, b_ii, b_lruh], [b_hs], out=hs[:], data0=aa[:], data1=ii[:], initial=lru_h[:, ct:ct + 1], op0=ALU.mult, op1=ALU.add)
                    V("tensor_copy", [b_hs], [b_lruh], out=lru_h[:, ct:ct + 1], in_=hs[:, 511:512])
                    ya, b_ya = ya_ring.next()
                    TT([b_hs, b_gg], [b_ya], ya[:], hs[:], gg[:], ALU.mult)
                    kb.dma("sp", DMA(yaT[r0:r1, tt * 512:(tt + 1) * 512], ya[:]), reads=[b_ya], owner=b_ya)
                    del lst[n]
                    gstep(1)

            if "p2b" not in skip:
                NKT = (ST + 512) // 128
                qh_ring = ph.ring("qh", 2, [128, ST], BF16)
                kh_ring = ph.ring("kh", 2, [128, ST + 512], BF16)
                vh_ring = ph.ring("vh", 2, [128, NKT, 128], BF16)
                bh_ring = ph.ring("bh", 2, [128, 640], F32)
                yb_ring = ph.ring("yb", 2, [128, ST], BF16)
                sf_ring = ph.ring("sf", 2, [128, 640], F32)
                pt_ring = ph.ring("pt", 3, [128, 640], BF16)
                rc_ring = ph.ring("rc", 2, [128, 128], F32)
                NQ = ST // 128
                aunits = [(hd, m) for hd in range(8) for m in range(NQ)]
                NU = len(aunits)
                hst, ust = {}, {}

                def att_load(hd):
                    r0, r1 = hd * 128, (hd + 1) * 128
                    qh, b_qh = qh_ring.next()
                    kh, b_kh = kh_ring.next()
                    vh, b_vh = vh_ring.next()
                    bh, b_bh = bh_ring.next()
                    yb, b_yb = yb_ring.next()
                    kb.dma("sp", DMA(qh[:], qT[r0:r1, :]), writes=[b_qh], owner=b_qh)
                    kb.dma("sp", DMA(bh[:], abias_d[l, hd]), writes=[b_bh], owner=b_bh)
                    if first:
                        kb.dma("sp", DMA(kh[:, 512:], kT[r0:r1, 0:ST]), writes=[b_kh], owner=b_kh)
                        kb.dma("sp", DMA(vh[:, 4:, :], vS[0:ST, r0:r1].rearrange("(n p) d -> p n d", p=128)), writes=[b_vh], owner=b_vh)
                    else:
                        kb.dma("sp", DMA(kh[:], kT[r0:r1, t0 - 512:t0 + ST]), writes=[b_kh], owner=b_kh)
                        kb.dma("sp", DMA(vh[:], vS[t0 - 512:t0 + ST, r0:r1].rearrange("(n p) d -> p n d", p=128)), writes=[b_vh], owner=b_vh)
                    hst[hd] = dict(qh=qh, b_qh=b_qh, kh=kh, b_kh=b_kh, vh=vh, b_vh=b_vh, bh=bh, b_bh=b_bh, yb=yb, b_yb=b_yb)

                def att_scores(n):
                    hd, m = aunits[n]
                    if hd not in hst:
                        att_load(hd)
                    H = hst[hd]
                    kts = [kt for kt in range(5) if (not first) or (m - 4 + kt) >= 0]
                    sset = n % 2
                    sb = 2 * sset
                    sc = psum[:, sb * 512:sb * 512 + 640]
                    kb.op("pe", [MM(sc[:, kt * 128:(kt + 1) * 128], H["kh"][:, (m + kt) * 128:(m + kt + 1) * 128], H["qh"][:, m * 128:(m + 1) * 128]) for kt in kts],
                          reads=[H["b_kh"], H["b_qh"]], writes=[pb[sb], pb[sb + 1]])
                    ust[n] = dict(kts=kts, sb=sb, sc=sc, lo=kts[0] * 128)

                def att_sf_exp(n):
                    hd, m = aunits[n]
                    H, U = hst[hd], ust[n]
                    sf, b_sf = sf_ring.next()
                    pt, b_pt = pt_ring.next()
                    lo, hi, sb, sc = U["lo"], 640, U["sb"], U["sc"]
                    STT([pb[sb], pb[sb + 1], H["b_bh"]], [b_sf], sf[:, lo:hi], sc[:, lo:hi], float(128 ** -0.5), H["bh"][:, lo:hi], ALU.mult, ALU.add)
                    A([b_sf], [b_pt], pt[:, lo:hi], sf[:, lo:hi], AF.Exp)
                    U.update(pt=pt, b_pt=b_pt)

                def att_pv(n):
                    hd, m = aunits[n]
                    H, U = hst[hd], ust[n]
                    kts, pt = U["kts"], U["pt"]
                    pbk = 4 + (n % 2)
                    nk = len(kts)
                    fns = [MM(bank(pbk)[:, 0:128], H["vh"][:, m + kt, :], pt[:, kt * 128:(kt + 1) * 128], start=(i_ == 0), stop=(i_ == nk - 1)) for i_, kt in enumerate(kts)]
                    fns += [MM(bank(pbk)[:, 128:256], ones_b[:], pt[:, kt * 128:(kt + 1) * 128], start=(i_ == 0), stop=(i_ == nk - 1)) for i_, kt in enumerate(kts)]
                    kb.op("pe", fns, reads=[H["b_vh"], U["b_pt"], b_onesb], writes=[pb[pbk]])
                    U["pbk"] = pbk

                def att_fin(n):
                    hd, m = aunits[n]
                    H, U = hst[hd], ust[n]
                    pbk = U["pbk"]
                    rc, b_rc = rc_ring.next()
                    V("reciprocal", [pb[pbk]], [b_rc], out=rc[:], in_=bank(pbk)[:, 128:256])
                    TT([pb[pbk], b_rc], [H["b_yb"]], H["yb"][:, m * 128:(m + 1) * 128], bank(pbk)[:, 0:128], rc[:], ALU.mult)
                    if m == NQ - 1:
                        kb.dma("sp", DMA(ybT[hd * 128:(hd + 1) * 128, :], H["yb"][:]), reads=[H["b_yb"]], owner=H["b_yb"])
                    del ust[n]

                att_scores(0)
                if NU > 1:
                    att_scores(1)
                att_sf_exp(0)
                for n in range(NU):
                    if n + 2 < NU:
                        att_scores(n + 2)
                    att_pv(n)
                    if n + 1 < NU:
                        att_sf_exp(n + 1)
                    if n >= 1:
                        att_fin(n - 1)
                    if n % 4 != 3:
                        gstep(1)
                att_fin(NU - 1)
            while gstep(8):
                pass
            ph.close()
            if "p1" not in skip:
                phO.close()


            if "p2c" not in skip:
                ph = Phase(kb)
                W = 4096
                NF = 32 * 129
                E_re, b_Ere = ph.tile("E_re", [128, W], F32)
                E_im, b_Eim = ph.tile("E_im", [128, W], F32)
                F_re, b_Fre = ph.tile("F_re", [128, 32, 129], F32)
                F_im, b_Fim = ph.tile("F_im", [128, 32, 129], F32)
                bbr, b_bbr = ph.tile("bbr", [128, W], BF16)
                bbi, b_bbi = ph.tile("bbi", [128, W], BF16)
                Ffr = F_re[:].rearrange("p b j -> p (b j)")
                Ffi = F_im[:].rearrange("p b j -> p (b j)")
                MUL, ADD, SUB = ALU.mult, ALU.add, ALU.subtract
                if s == 0:
                    pp = Phase(kb)
                    ar, b_ar = pp.tile("ar", [128, W], F32)
                    ai, b_ai = pp.tile("ai", [128, W], F32)
                    sr, b_sr = pp.tile("sr", [128, W], F32)
                    t1f, b_t1 = pp.tile("t1", [128, NF], F32)
                    t2f, b_t2 = pp.tile("t2", [128, NF], F32)
                    t3f, b_t3 = pp.tile("t3", [128, NF], F32)
                    t1, t2, t3 = t1f[:, 0:W], t2f[:, 0:W], t3f[:, 0:W]
                    bdr, b_bdr = Ffr[:, 0:W], b_Fre
                    bdi, b_bdi = Ffi[:, 0:W], b_Fim
                    kb.dma("sp", DMA(ar[:], rowp_d[l, 0:1, :].to_broadcast([128, W])), writes=[b_ar], owner=b_ar)
                    kb.dma("sp", DMA(ai[:], rowp_d[l, 1:2, :].to_broadcast([128, W])), writes=[b_ai], owner=b_ai)
                    kb.dma("sp", DMA(sr[:], rowp_d[l, 2:3, :].to_broadcast([128, W])), writes=[b_sr], owner=b_sr)
                    kb.dma("sp", DMA(bdr, bbd_d[l, :, 0, :]), writes=[b_bdr], owner=b_bdr)
                    kb.dma("sp", DMA(bdi, bbd_d[l, :, 1, :]), writes=[b_bdi], owner=b_bdi)

                    def sincos(dst_sin, b_ds, dst_cos, b_dc, ycyc, b_y, tmp, b_tmp):
                        TS([b_y], [b_tmp], tmp, ycyc, MAGIC, MAGIC, ALU.add, ALU.subtract)
                        TT([b_y, b_tmp], [b_tmp], tmp, ycyc, tmp, ALU.subtract)
                        A([b_tmp], [b_ds], dst_sin, tmp, AF.Sin, scale=TWO_PI)
                        TS([b_y], [b_y], ycyc, ycyc, 0.25, None, ALU.add)
                        TS([b_y], [b_tmp], tmp, ycyc, MAGIC, MAGIC, ALU.add, ALU.subtract)
                        TT([b_y, b_tmp], [b_tmp], tmp, ycyc, tmp, ALU.subtract)
                        A([b_tmp], [b_dc], dst_cos, tmp, AF.Sin, scale=TWO_PI)

                    MUL, ADD, SUB = ALU.mult, ALU.add, ALU.subtract
                    A([b_sr], [b_sr], sr[:], sr[:], AF.Exp)
                    TT([b_ar, b_sr], [b_t1], t1, ar[:], sr[:], MUL)
                    A([b_t1], [b_t1], t1, t1, AF.Exp)
                    TT([b_ai, b_sr], [b_t2], t2, ai[:], sr[:], MUL)
                    TS([b_t2], [b_t2], t2, t2, 1.0 / TWO_PI, None, MUL)
                    sincos(E_im[:], b_Eim, E_re[:], b_Ere, t2, b_t2, t3, b_t3)
                    TT([b_Ere, b_t1], [b_Ere], E_re[:], E_re[:], t1, MUL)
                    TT([b_Eim, b_t1], [b_Eim], E_im[:], E_im[:], t1, MUL)
                    TS([b_Ere], [b_Ere], E_re[:], E_re[:], -1.0, None, ADD)
                    TT([b_ar], [b_t1], t1, ar[:], ar[:], MUL)
                    TT([b_ai], [b_t2], t2, ai[:], ai[:], MUL)
                    TT([b_t1, b_t2], [b_t1], t1, t1, t2, ADD)
                    V("reciprocal", [b_t1], [b_t1], out=t1, in_=t1)
                    TT([b_Ere, b_ar], [b_t2], t2, E_re[:], ar[:], MUL)
                    TT([b_Eim, b_ai], [b_t3], t3, E_im[:], ai[:], MUL)
                    TT([b_t2, b_t3], [b_t2], t2, t2, t3, ADD)
                    TT([b_t2, b_t1], [b_t2], t2, t2, t1, MUL)
                    TT([b_Eim, b_ar], [b_t3], t3, E_im[:], ar[:], MUL)
                    TT([b_Ere, b_ai], [b_Eim], E_im[:], E_re[:], ai[:], MUL)
                    TT([b_t3, b_Eim], [b_t3], t3, t3, E_im[:], SUB)
                    TT([b_t3, b_t1], [b_t3], t3, t3, t1, MUL)
                    TT([b_t2, b_bdr], [b_Ere], E_re[:], t2, bdr, MUL)
                    TT([b_t3, b_bdi], [b_Eim], E_im[:], t3, bdi, MUL)
                    TT([b_Ere, b_Eim], [b_bbr], bbr[:], E_re[:], E_im[:], SUB)
                    TT([b_t2, b_bdi], [b_Ere], E_re[:], t2, bdi, MUL)
                    TT([b_t3, b_bdr], [b_Eim], E_im[:], t3, bdr, MUL)
                    TT([b_Ere, b_Eim], [b_bbi], bbi[:], E_re[:], E_im[:], ADD)
                    iocol = col("iota", 0)
                    TT([b_ar, b_sr], [b_t1], t1, ar[:], sr[:], MUL)
                    TS([b_t1, b_colp], [b_t1], t1, t1, iocol, -1.0, MUL, MUL)
                    A([b_t1], [b_t1], t1, t1, AF.Exp)
                    TT([b_ai, b_sr], [b_t2], t2, ai[:], sr[:], MUL)
                    TS([b_t2, b_colp], [b_t2], t2, t2, iocol, -1.0 / TWO_PI, MUL, MUL)
                    sincos(E_im[:], b_Eim, E_re[:], b_Ere, t2, b_t2, t3, b_t3)
                    TT([b_Ere, b_t1], [b_Ere], E_re[:], E_re[:], t1, MUL)
                    TT([b_Eim, b_t1], [b_Eim], E_im[:], E_im[:], t1, MUL)
                    stc, b_stc = pp.tile("stc", [128, 32], F32)
                    alc, b_alc = pp.tile("alc", [128, 32], F32)
                    thc, b_thc = pp.tile("thc", [128, 32], F32)
                    A([b_colp], [b_stc], stc[:], col("lsc", l * 32, 32), AF.Exp)
                    TT([b_colp, b_stc], [b_alc], alc[:], col("acre", l * 32, 32), stc[:], MUL)
                    TT([b_colp, b_stc], [b_thc], thc[:], col("acim", l * 32, 32), stc[:], MUL)
                    TS([b_thc], [b_thc], thc[:], thc[:], 1.0 / TWO_PI, None, MUL)
                    g1 = t1f[:].rearrange("p (b j) -> p b j", b=32)
                    g2 = t2f[:].rearrange("p (b j) -> p b j", b=32)
                    io_b = iorow[:].unsqueeze(1).to_broadcast([128, 32, 129])
                    TT([b_iorow, b_alc], [b_t1], g1, io_b, alc[:].unsqueeze(2).to_broadcast([128, 32, 129]), MUL)
                    A([b_t1], [b_t1], t1f[:], t1f[:], AF.Exp)
                    TT([b_iorow, b_thc], [b_t2], g2, io_b, thc[:].unsqueeze(2).to_broadcast([128, 32, 129]), MUL)
                    sincos(Ffi, b_Fim, Ffr, b_Fre, t2f[:], b_t2, t3f[:], b_t3)
                    TT([b_Fre, b_t1], [b_Fre], Ffr, Ffr, t1f[:], MUL)
                    TT([b_Fim, b_t1], [b_Fim], Ffi, Ffi, t1f[:], MUL)
                    if nST > 1:
                        kb.dma("sp", DMA(tabE[0], E_re[:]), reads=[b_Ere], owner=b_Ere)
                        kb.dma("sp", DMA(tabE[1], E_im[:]), reads=[b_Eim], owner=b_Eim)
                        kb.dma("sp", DMA(tabF[0], Ffr), reads=[b_Fre], owner=b_Fre)
                        kb.dma("sp", DMA(tabF[1], Ffi), reads=[b_Fim], owner=b_Fim)
                        kb.dma("sp", DMA(tabB[0], bbr[:]), reads=[b_bbr], owner=b_bbr)
                        kb.dma("sp", DMA(tabB[1], bbi[:]), reads=[b_bbi], owner=b_bbi)
                    pp.close()
                else:
                    kb.dma("sp", DMA(E_re[:], tabE[0]), writes=[b_Ere], owner=b_Ere)
                    kb.dma("sp", DMA(E_im[:], tabE[1]), writes=[b_Eim], owner=b_Eim)
                    kb.dma("sp", DMA(Ffr, tabF[0]), writes=[b_Fre], owner=b_Fre)
                    kb.dma("sp", DMA(Ffi, tabF[1]), writes=[b_Fim], owner=b_Fim)
                    kb.dma("sp", DMA(bbr[:], tabB[0]), writes=[b_bbr], owner=b_bbr)
                    kb.dma("sp", DMA(bbi[:], tabB[1]), writes=[b_bbi], owner=b_bbi)
                cmat, b_cmat = ph.tile("cmat", [128, 8192], BF16)
                kb.dma("pool", [DMA(cmat[:, q * 2048:(q + 1) * 2048], ct_d[l, :, q * 2048:(q + 1) * 2048]) for q in range(4)], writes=[b_cmat], owner=b_cmat)
                ncmat, b_ncmat = ph.tile("ncmat", [128, 8192], BF16)
                TS([b_cmat], [b_ncmat], ncmat[:], cmat[:], -1.0, None, MUL)
                u_all, b_u = ph.tile("u_all", [128, 8, ST], BF16)
                kb.dma("sp", DMA(u_all[:], uT.rearrange("(c p) t -> p c t", p=128)), writes=[b_u], owner=b_u)
                p_ring = ph.ring("pp4", 2, [128, 4, 512], BF16)
                q_ring = ph.ring("qq4", 2, [128, 4, 512], BF16)
                cp_ring = ph.ring("cp", 2, [128, 2, 512], F32)
                yc_ring = ph.ring("yc", 4, [128, 128], BF16)
                ty_ring = ph.ring("ty", 2, [128, 128], F32)
                tn_ring = ph.ring("tn", 2, [128, 4, 4], F32)
                if first:
                    V("memset", [], [b_s5ca], ap=s5ca[:], constant=0.0)
                    V("memset", [], [b_s5cb], ap=s5cb[:], constant=0.0)
                units = [(j, ct) for j in range(ST // 128) for ct in range(8)]

                def bu_banks(ui):
                    base = (ui % 2) * 4
                    return base, base + 1, base + 2, base + 3

                def emit_bu(ui):
                    j, ct = units[ui]
                    br, bi_, _, _ = bu_banks(ui)
                    ul = u_all[:, ct, j * 128:(j + 1) * 128]
                    c0, c1 = ct * 512, (ct + 1) * 512
                    kb.op("pe", MM(bank(br), ul, bbr[:, c0:c1]), reads=[b_u, b_bbr], writes=[pb[br]])
                    kb.op("pe", MM(bank(bi_), ul, bbi[:, c0:c1]), reads=[b_u, b_bbi], writes=[pb[bi_]])

                def st_eprod(ui, stt):
                    j, ct = units[ui]
                    br, bi_, cr_, ci_ = bu_banks(ui)
                    c0, c1 = ct * 512, (ct + 1) * 512
                    pp4, b_p = p_ring.next()
                    stt["p"] = (pp4, b_p)
                    TT([pb[br], b_Ere], [b_p], pp4[:, 0, :], bank(br), E_re[:, c0:c1], MUL)
                    TT([pb[bi_], b_Eim], [b_p], pp4[:, 1, :], bank(bi_), E_im[:, c0:c1], MUL)
                    TT([pb[bi_], b_Ere], [b_p], pp4[:, 2, :], bank(bi_), E_re[:, c0:c1], MUL)
                    TT([pb[br], b_Eim], [b_p], pp4[:, 3, :], bank(br), E_im[:, c0:c1], MUL)
                    fns = []
                    for blk in range(4):
                        bs = slice(blk * 128, (blk + 1) * 128)
                        fns.append(MM(bank(cr_)[:, bs], pp4[:, 0, bs], tri_b[:], start=True, stop=False))
                        fns.append(MM(bank(cr_)[:, bs], pp4[:, 1, bs], ntri_b[:], start=False, stop=True))
                        fns.append(MM(bank(ci_)[:, bs], pp4[:, 2, bs], tri_b[:], start=True, stop=False))
                        fns.append(MM(bank(ci_)[:, bs], pp4[:, 3, bs], tri_b[:], start=False, stop=True))
                    kb.op("pe", fns, reads=[b_p, b_trib, b_ntrib], writes=[pb[cr_], pb[ci_]])

                def st_xprod(ui, stt):
                    j, ct = units[ui]
                    br, bi_, cr_, ci_ = bu_banks(ui)
                    cp, b_cp = cp_ring.next()
                    s0, s1 = ct * 4, ct * 4 + 4
                    TT([pb[cr_], b_s5ca], [b_cp], v3(cp[:, 0, :]), v3(bank(cr_)), s5ca[:, s0:s1].unsqueeze(2).to_broadcast([128, 4, 128]), ADD)
                    TT([pb[ci_], b_s5cb], [b_cp], v3(cp[:, 1, :]), v3(bank(ci_)), s5cb[:, s0:s1].unsqueeze(2).to_broadcast([128, 4, 128]), ADD)
                    Fr = F_re[:, s0:s1, 0:128]
                    Fi = F_im[:, s0:s1, 0:128]
                    qq4, b_q = q_ring.next()
                    TT([b_cp, b_Fre], [b_q], v3(qq4[:, 0, :]), v3(cp[:, 0, :]), Fr, MUL)
                    TT([b_cp, b_Fim], [b_q], v3(qq4[:, 1, :]), v3(cp[:, 1, :]), Fi, MUL)
                    TT([b_cp, b_Fre], [b_q], v3(qq4[:, 2, :]), v3(cp[:, 1, :]), Fr, MUL)
                    TT([b_cp, b_Fim], [b_q], v3(qq4[:, 3, :]), v3(cp[:, 0, :]), Fi, MUL)
                    tn, b_tn = tn_ring.next()
                    cr = v3(cp[:, 0, :])[:, :, 127]
                    ci = v3(cp[:, 1, :])[:, :, 127]
                    Gr = F_re[:, s0:s1, 128]
                    Gi = F_im[:, s0:s1, 128]
                    TT([b_cp, b_Fre], [b_tn], tn[:, 0, :], cr, Gr, MUL)
                    TT([b_cp, b_Fim], [b_tn], tn[:, 1, :], ci, Gi, MUL)
                    TT([b_cp, b_Fre], [b_tn], tn[:, 2, :], ci, Gr, MUL)
                    TT([b_cp, b_Fim], [b_tn], tn[:, 3, :], cr, Gi, MUL)
                    TT([b_tn], [b_s5ca], s5ca[:, s0:s1], tn[:, 0, :], tn[:, 1, :], SUB)
                    TT([b_tn], [b_s5cb], s5cb[:, s0:s1], tn[:, 2, :], tn[:, 3, :], ADD)
                    fns = []
                    for blk in range(4):
                        bs = slice(blk * 128, (blk + 1) * 128)
                        o_re = ((ct * 4 + blk) * 2 + 0) * 128
                        o_im = ((ct * 4 + blk) * 2 + 1) * 128
                        ops = ((cmat, o_re, 0), (ncmat, o_re, 1), (ncmat, o_im, 2), (ncmat, o_im, 3))
                        for oi, (cm, o, qi) in enumerate(ops):
                            fns.append(MM(bank(cr_)[:, 0:128], cm[:, o:o + 128], qq4[:, qi, bs], start=(blk == 0 and oi == 0), stop=(blk == 3 and oi == 3)))
                    kb.op("pe", fns, reads=[b_cmat, b_ncmat, b_q], writes=[pb[cr_]])

                def st_out(ui, stt):
                    j, ct = units[ui]
                    br, bi_, cr_, ci_ = bu_banks(ui)
                    ul = u_all[:, ct, j * 128:(j + 1) * 128]
                    ty, b_ty = ty_ring.next()
                    yc, b_yc = yc_ring.next()
                    STT([b_u, pb[cr_], b_colp], [b_ty], ty[:], ul, col("ssmd", l * 8 + ct), bank(cr_)[:, 0:128], MUL, ADD)
                    A([b_ty], [b_yc], yc[:], ty[:], AF.Gelu_apprx_tanh)
                    kb.dma("sp", DMA(ycT[ct * 128:(ct + 1) * 128, j * 128:(j + 1) * 128], yc[:]), reads=[b_yc], owner=b_yc)

                nU = len(units)
                emit_bu(0)
                emit_bu(1)
                for pi in range(0, nU, 2):
                    a, b = pi, pi + 1
                    sa, sb_ = {}, {}
                    st_eprod(a, sa)
                    st_eprod(b, sb_)
                    st_xprod(a, sa)
                    if a + 2 < nU:
                        emit_bu(a + 2)
                    st_xprod(b, sb_)
                    if b + 2 < nU:
                        emit_bu(b + 2)
                    st_out(a, sa)
                    st_out(b, sb_)
                ph.close()


            if "p3" not in skip:
                ph = Phase(kb)
                ys = []
                for nm, src in (("ya", yaT), ("yb", ybT), ("yc", ycT)):
                    t, b = ph.tile(nm, [128, 8, ST], BF16)
                    kb.dma("sp", DMA(t[:], src.rearrange("(c p) t -> p c t", p=128)), writes=[b], owner=b)
                    ys.append((t, b))
                ys.append(ys[2])
                wb_ring = ph.ring("wb", 2, [128, 4, 8, 128], BF16)
                g_ring = ph.ring("gg", 2, [128, 3, 512], BF16)
                ms_ring = ph.ring("ms", 2, [128, ST], BF16)
                tq_ring = ph.ring("tq", 2, [128, 4, 512], F32)
                gT_v = gT.rearrange("(b f p) t -> f p b t", b=3, f=16)
                it = 0
                for f in range(16):
                    wb, b_wb = wb_ring.next()
                    kb.dma("pool", [DMA(wb[:, q, :, :], wbr_t[l, f, :, q, :, :]) for q in range(4)], writes=[b_wb], owner=b_wb)
                    ms, b_ms = ms_ring.next()
                    for tt in range(nTT):
                        ts0, ts1 = tt * 512, (tt + 1) * 512
                        g3, b_g3 = g_ring.next()
                        kb.dma("sp", DMA(g3[:], gT_v[f][:, :, ts0:ts1]), writes=[b_g3], owner=b_g3)
                        b0 = (it % 2) * 4
                        it += 1
                        for q in range(4):
                            yt, b_yt = ys[q]
                            kb.op("pe", [MM(bank(b0 + q), wb[:, q, k, :], yt[:, k, ts0:ts1], start=(k == 0), stop=(k == 7)) for k in range(8)],
                                  reads=[b_wb, b_yt], writes=[pb[b0 + q]])
                        tq, b_tq = tq_ring.next()
                        A([pb[b0 + 3]], [b_tq], tq[:, 3, :], bank(b0 + 3), AF.Sigmoid)
                        TT([pb[b0], b_g3], [b_tq], tq[:, 0, :], bank(b0 + 0), g3[:, 0, :], ALU.mult)
                        TT([pb[b0 + 1], b_g3], [b_tq], tq[:, 1, :], bank(b0 + 1), g3[:, 1, :], ALU.mult)
                        TT([pb[b0 + 2], b_tq], [b_tq], tq[:, 2, :], bank(b0 + 2), tq[:, 3, :], ALU.mult)
                        TT([b_tq, b_g3], [b_tq], tq[:, 2, :], tq[:, 2, :], g3[:, 2, :], ALU.mult)
                        TT([b_tq], [b_tq], tq[:, 0, :], tq[:, 0, :], tq[:, 1, :], ALU.add)
                        TT([b_tq], [b_ms], ms[:, ts0:ts1], tq[:, 0, :], tq[:, 2, :], ALU.add)
                    kb.dma("sp", DMA(mT[f * 128:(f + 1) * 128, :], ms[:]), reads=[b_ms], owner=b_ms)
                ph.close()
                proj_residual(l, t0, mT, 16, wout_t[l], nTT)

            if "p4" not in skip:
                ph = Phase(kb)
                hT, b_hT = ph.tile("hT", [128, NK, ST], BF16)
                xin_ring = ph.ring("xin", 2, [128, NK, 512], F32)
                sq_ring = ph.ring("sq", 3, [128, 512], BF16)
                rs_ring = ph.ring("rs", 2, [128, 512], F32)
                rms_norm_to(hT, b_hT, "gffn", l, t0, xin_ring, sq_ring, rs_ring, (6, 7))
                wg_ring = ph.ring("wg", 2, [128, 2, NK, 128], BF16)
                as_ring = ph.ring("as", 2, [128, ST], BF16)
                sl_ring = ph.ring("sl", 2, [128, 512], F32)
                it = 0
                for jj in range(NJ):
                    wg, b_wg = wg_ring.next()
                    kb.dma("pool", [DMA(wg[:, q, :, :], wgu_t[l, jj, :, q, :, :]) for q in range(2)], writes=[b_wg], owner=b_wg)
                    ast, b_ast = as_ring.next()
                    for tt in range(nTT):
                        ts0, ts1 = tt * 512, (tt + 1) * 512
                        bg = (it % 3) * 2
                        it += 1
                        for q in range(2):
                            kb.op("pe", [MM(bank(bg + q), wg[:, q, k, :], hT[:, k, ts0:ts1], start=(k == 0), stop=(k == NK - 1)) for k in range(NK)],
                                  reads=[b_wg, b_hT], writes=[pb[bg + q]])
                        sl, b_sl = sl_ring.next()
                        A([pb[bg]], [b_sl], sl[:], bank(bg), AF.Silu)
                        TT([pb[bg + 1], b_sl], [b_ast], ast[:, ts0:ts1], bank(bg + 1), sl[:], ALU.mult)
                    kb.dma("sp", DMA(actT[jj * 128:(jj + 1) * 128, :], ast[:]), reads=[b_ast], owner=b_ast)
                ph.close()
                proj_residual(l, t0, actT, NJ, wdn_t[l], min(2, nTT))

    ph = Phase(kb)
    xin_ring = ph.ring("xin", 2, [128, NK, 512], F32)
    sq_ring = ph.ring("sq", 3, [128, 512], BF16)
    rs_ring = ph.ring("rs", 2, [128, 512], F32)
    outv = outT.rearrange("(k p) t -> p k t", p=128)
    for tt in range(S // 512):
        tg = tt * 512
        xin, b_xin = xin_ring.next()
        rs, b_rs = rms_norm_tile(xin, b_xin, tg, sq_ring, rs_ring, 6 + (tt % 2))
        for k in range(NK):
            STT([b_xin, b_rs, b_colp], [b_xin], xin[:, k, :], xin[:, k, :], col("gfin", k), rs[:], ALU.mult, ALU.mult)
        kb.dma("sp", DMA(outv[:, :, tg:tg + 512], xin[:]), reads=[b_xin], owner=b_xin)
    ph.close()
    kb.finalize()
    return nc


def prep_weights(inp, L):
    f = np.float32
    out = {}
    w_in = inp["w_in"][:L]
    out["w_in_t"] = np.ascontiguousarray(w_in.reshape(L, 16, 128, 96, 128).transpose(0, 3, 2, 1, 4))
    wb = np.concatenate([inp["w_branch"][:L], inp["ssm_w_glu"][:L][:, None]], axis=1)
    out["wbr_t"] = np.ascontiguousarray(wb.reshape(L, 4, 8, 128, 16, 128).transpose(0, 4, 3, 1, 2, 5))
    out["wout_t"] = np.ascontiguousarray(inp["w_out"][:L].reshape(L, 16, 128, 16, 128).transpose(0, 3, 2, 1, 4))
    wgu = np.stack([inp["w_ffn_gate"][:L], inp["w_ffn_up"][:L]], axis=1)
    out["wgu_t"] = np.ascontiguousarray(wgu.reshape(L, 2, 16, 128, NJ, 128).transpose(0, 4, 3, 1, 2, 5))
    out["wdn_t"] = np.ascontiguousarray(inp["w_ffn_down"][:L].reshape(L, NJ, 128, 16, 128).transpose(0, 3, 2, 1, 4))
    COFF, NCOL = col_layout(L)
    colp = np.zeros((128, NCOL), f)

    def put(name, arr):
        colp[:, COFF[name]:COFF[name] + arr.shape[1]] = arr
    put("gmix", inp["norm_mix_g"][:L].reshape(L, 16, 128).transpose(2, 0, 1).reshape(128, -1))
    put("gffn", inp["norm_ffn_g"][:L].reshape(L, 16, 128).transpose(2, 0, 1).reshape(128, -1))
    put("gfin", inp["norm_final_g"].reshape(16, 128).T)
    put("gbias", inp["gate_bias"][:L].reshape(L, 3, 16, 128).transpose(3, 0, 1, 2).reshape(128, -1))
    put("convw", inp["lru_conv_w"][:L].reshape(L, 4, 8, 128).transpose(3, 0, 2, 1).reshape(128, -1))
    for nm, key in (("convb", "lru_conv_b"), ("ba", "lru_ba"), ("bx", "lru_bx"), ("lam", "lru_lambda"), ("ssmd", "ssm_d")):
        put(nm, inp[key][:L].reshape(L, 8, 128).transpose(2, 0, 1).reshape(128, -1))
    for nm, arr in (("acre", inp["ssm_a_re"][:L]), ("acim", inp["ssm_a_im"][:L]),
                    ("lsc", np.repeat(inp["ssm_log_step"][:L][:, :, None], 64, axis=2))):
        put(nm, arr.reshape(L, 32, 2, 64).transpose(2, 3, 0, 1).reshape(128, -1))
    colp[:, COFF["iota"]] = np.arange(128, dtype=f)
    out["colp"] = colp
    rowp = np.stack([inp["ssm_a_re"][:L].reshape(L, 4096), inp["ssm_a_im"][:L].reshape(L, 4096),
                     np.repeat(inp["ssm_log_step"][:L][:, :, None], 64, axis=2).reshape(L, 4096)], axis=1)
    out["rowp"] = np.ascontiguousarray(rowp.astype(f))
    lruw = np.zeros((L, 128, 8, 2, 128), f)
    for wi, key in enumerate(("lru_wa", "lru_wx")):
        w = inp[key][:L].reshape(L, 8, 2, 64, 64)
        for nl in range(2):
            lruw[:, nl * 64:(nl + 1) * 64, :, wi, nl * 64:(nl + 1) * 64] = w[:, :, nl].transpose(0, 2, 1, 3)
    out["lruw"] = lruw
    kk = np.arange(128)[:, None, None]
    kt = np.arange(5)[None, :, None]
    qq = np.arange(128)[None, None, :]
    dist = (4 - kt) * 128 + qq - kk
    rel = np.clip(dist, -128, 128) + 128
    cdiff = 8 - 2 * kt + qq // 64 - kk // 64
    valid = (cdiff >= 0) & (cdiff <= 8)
    ab = inp["attn_rel_bias"][:L][:, :, rel]
    ab = np.where(valid[None, None], ab, f(-30000.0)).astype(f)
    out["abias"] = np.ascontiguousarray(ab.reshape(L, 8, 128, 640))
    bbd = np.zeros((L, 128, 2, 8, 8, 64), f)
    for ri, key in enumerate(("ssm_b_re", "ssm_b_im")):
        B = inp[key][:L].reshape(L, 8, 8, 64, 16)
        for gl in range(8):
            bbd[:, gl * 16:(gl + 1) * 16, ri, :, gl, :] = B[:, :, gl].transpose(0, 3, 1, 2)
    out["bbd"] = np.ascontiguousarray(bbd.reshape(L, 128, 2, 4096))
    ctd = np.zeros((L, 128, 8, 4, 2, 128), f)
    for ri, key in enumerate(("ssm_c_re", "ssm_c_im")):
        C = inp[key][:L].reshape(L, 8, 4, 2, 16, 64)
        for blk in range(4):
            for gl2 in range(2):
                c0 = (2 * blk + gl2) * 16
                ctd[:, gl2 * 64:(gl2 + 1) * 64, :, blk, ri, c0:c0 + 16] = C[:, :, blk, gl2].transpose(0, 3, 1, 2)
    out["ctd"] = np.ascontiguousarray(ctd.reshape(L, 128, 8192))
    cst = np.zeros((128, 386), f)
    cst[:, 0:128] = np.triu(np.ones((128, 128), f))
    cst[:, 128:256] = 1.0
    cst[:, 256:385] = np.arange(129, dtype=f)[None, :]
    cst[:, 385] = 1e-6
    out["cst"] = cst
    out["epsd"] = np.full((128, 1), 1e-6, f)
    return out


_CACHE = {}


def run_model(inputs, L, S_core, ST, n_cores, dbg=False, skip=(), spread=False):
    x = np.asarray(inputs["x"], np.float32)
    B, S, _ = x.shape
    assert S == S_core and B <= n_cores
    key = (L, S_core, ST, dbg, tuple(skip))
    if key not in _CACHE:
        _CACHE[key] = build_program(L, S_core, ST, dbg=dbg, skip=skip)
    nc = _CACHE[key]
    wts = prep_weights({k: np.asarray(v, np.float32) for k, v in inputs.items() if k != "x"}, L)
    if spread:
        real = {0: 0, 1: 1, 4: 2, 5: 3}
        zw = dict(wts)
        for k in ("w_in_t", "wbr_t", "wout_t", "wgu_t", "wdn_t"):
            zw[k] = np.zeros_like(wts[k])
        zx = np.zeros((D, S), np.float32)
        in_maps = []
        for c in range(8):
            if c in real and real[c] < B:
                m = dict(wts)
                m["xT"] = np.ascontiguousarray(x[real[c]].T)
            else:
                m = dict(zw)
                m["xT"] = zx
            in_maps.append(m)
        res = run_bass_kernel_spmd(nc, in_maps, core_ids=list(range(8)))
        slots = [c for c in range(8) if c in real and real[c] < B]
        out = np.stack([np.ascontiguousarray(res.results[c]["outT"].T) for c in sorted(slots, key=lambda c: real[c])], axis=0)
        return out.astype(np.float32), res
    in_maps = []
    for c in range(n_cores):
        b = c % B
        m = dict(wts)
        m["xT"] = np.ascontiguousarray(x[b].T)
        in_maps.append(m)
    res = run_bass_kernel_spmd(nc, in_maps, core_ids=list(range(n_cores)))
    out = np.stack([np.ascontiguousarray(res.results[b]["outT"].T) for b in range(B)], axis=0)
    return out.astype(np.float32), res


def kernel(**inputs):
    out, _ = run_model(inputs, 4, 4096, 2048, 8, spread=True)
    return out
```

```python
import math
from contextlib import ExitStack
import numpy as np
import concourse.bass as bass
import concourse.mybir as mybir
from concourse.bass_utils import run_bass_kernel_spmd

F32 = mybir.dt.float32
BF16 = mybir.dt.bfloat16
AF = mybir.ActivationFunctionType
ALU = mybir.AluOpType

D = 2048
NK = 16
MIXW = 1024
FH = 5632
NJ = 44
TWO_PI = float(2 * math.pi)
MAGIC = 12582912.0


class Buf:
    __slots__ = ("name", "w", "r", "dsem")

    def __init__(self, name):
        self.name = name
        self.w = None
        self.r = {}
        self.dsem = None


class Eng:
    def __init__(self, name, sem):
        self.name = name
        self.sem = sem
        self.cnt = 0
        self.waited = {}
        self.prog = []


class KB:
    def __init__(self, nc):
        self.nc = nc
        self.stack = ExitStack()
        self.sems = {}
        self.semcnt = {}
        self.free_dsems = []
        self.dirty = {}
        self.eng = {}
        for name in ("pe", "act", "dve", "pool", "sp"):
            self.eng[name] = Eng(name, self.newsem("e_" + name))
        self.uid = 0

    def newsem(self, name):
        h = self.stack.enter_context(self.nc.semaphore(name))
        key = len(self.sems)
        self.sems[key] = h
        self.semcnt[key] = 0
        return key

    def buf(self, name="b"):
        self.uid += 1
        return Buf(f"{name}_{self.uid}")

    def _deps(self, reads, writes):
        deps = {}
        for b in reads:
            if b.w is not None and deps.get(b.w[0], 0) < b.w[1]:
                deps[b.w[0]] = b.w[1]
        for b in writes:
            if b.w is not None and deps.get(b.w[0], 0) < b.w[1]:
                deps[b.w[0]] = b.w[1]
            for k, v in b.r.items():
                if deps.get(k, 0) < v:
                    deps[k] = v
        return deps

    def _emit_waits(self, e, deps, skip=None):
        for k, v in deps.items():
            if k == skip:
                continue
            if e.waited.get(k, 0) < v:
                e.prog.append(("w", k, v))
                e.waited[k] = v

    def _update(self, tok, reads, writes):
        k, v = tok
        for b in reads:
            if b.r.get(k, 0) < v:
                b.r[k] = v
        for b in writes:
            b.w = tok
            b.r = {}

    def op(self, en, fns, reads=(), writes=()):
        e = self.eng[en]
        if callable(fns):
            fns = [fns]
        deps = self._deps(reads, writes)
        self._emit_waits(e, deps, skip=e.sem if en == "pe" else None)
        for f in fns[:-1]:
            e.prog.append(("i", f, None, 0))
        e.cnt += 1
        e.prog.append(("i", fns[-1], e.sem, 1))
        tok = (e.sem, e.cnt)
        self._update(tok, reads, writes)
        return tok

    def dma(self, en, fns, reads=(), writes=(), owner=None):
        e = self.eng[en]
        if callable(fns):
            fns = [fns]
        if owner.dsem is None:
            owner.dsem = self.free_dsems.pop() if self.free_dsems else self.newsem("d%d" % len(self.sems))
        k = owner.dsem
        deps = self._deps(reads, writes)
        self._emit_waits(e, deps)
        for f in fns:
            self.semcnt[k] += 16
            e.prog.append(("i", f, k, 16))
        tok = (k, self.semcnt[k])
        self.dirty[k] = self.semcnt[k]
        self._update(tok, reads, writes)
        return tok

    def barrier(self):
        toks = dict(self.dirty)
        for e in self.eng.values():
            if e.cnt > 0:
                toks[e.sem] = e.cnt
        for e in self.eng.values():
            self._emit_waits(e, toks)
        self.dirty = {}

    def release(self, bufs):
        for b in bufs:
            if b.dsem is not None:
                self.free_dsems.append(b.dsem)
                b.dsem = None

    def finalize(self):
        nc = self.nc
        self.barrier()
        engs, sems = self.eng, self.sems

        def replay(e, h):
            for it in e.prog:
                if it[0] == "w":
                    h.wait_ge(sems[it[1]], it[2])
                else:
                    inst = it[1](h)
                    if it[2] is not None:
                        inst.then_inc(sems[it[2]], it[3])

        with nc.Block() as block:
            @block.tensor
            def _(h):
                replay(engs["pe"], h)

            @block.scalar
            def _(h):
                replay(engs["act"], h)

            @block.vector
            def _(h):
                replay(engs["dve"], h)

            @block.gpsimd
            def _(h):
                replay(engs["pool"], h)

            @block.sync
            def _(h):
                replay(engs["sp"], h)
        self.stack.close()


class Phase:
    def __init__(self, kb):
        self.kb = kb
        self.st = ExitStack()
        self.bufs = []

    def tile(self, name, shape, dtype):
        kb = self.kb
        kb.uid += 1
        t = self.st.enter_context(kb.nc.sbuf_tensor(f"{name}_{kb.uid}", list(shape), dtype))
        b = kb.buf(name)
        self.bufs.append(b)
        return t, b

    def ring(self, name, n, shape, dtype):
        return Ring([self.tile(name, shape, dtype) for _ in range(n)])

    def close(self):
        self.kb.barrier()
        self.kb.release(self.bufs)
        self.st.close()


class Ring:
    def __init__(self, items):
        self.items = items
        self.i = 0

    def next(self):
        it = self.items[self.i % len(self.items)]
        self.i += 1
        return it


def col_layout(L):
    segs = [("gmix", L * 16), ("gffn", L * 16), ("gfin", 16), ("gbias", L * 48), ("convw", L * 32),
            ("convb", L * 8), ("ba", L * 8), ("bx", L * 8), ("lam", L * 8), ("ssmd", L * 8),
            ("acre", L * 32), ("acim", L * 32), ("lsc", L * 32), ("iota", 1)]
    off, o = {}, 0
    for n, w in segs:
        off[n] = o
        o += w
    return off, o


def build_program(L, S, ST, dbg=False, has_prev=False, skip=()):
    nc = bass.Bass("TRN2", target_bir_lowering=False)
    kb = KB(nc)
    gst = kb.stack
    nST = S // ST
    nTT = ST // 512
    COFF, NCOL = col_layout(L)
    okind = "ExternalOutput" if dbg else "Internal"

    def din(name, shape, dt=F32):
        return nc.dram_tensor(name, list(shape), dt, kind="ExternalInput").ap()

    def dscr(name, shape, dt):
        return nc.dram_tensor(name, list(shape), dt, kind=okind).ap()

    xT = din("xT", [D, S])
    w_in_t = din("w_in_t", [L, 96, 128, 16, 128])
    wbr_t = din("wbr_t", [L, 16, 128, 4, 8, 128])
    wout_t = din("wout_t", [L, 16, 128, 16, 128])
    wgu_t = din("wgu_t", [L, NJ, 128, 2, 16, 128])
    wdn_t = din("wdn_t", [L, 16, 128, NJ, 128])
    colp_d = din("colp", [128, NCOL])
    rowp_d = din("rowp", [L, 3, 4096])
    lruw_d = din("lruw", [L, 128, 8, 2, 128])
    abias_d = din("abias", [L, 8, 128, 640])
    bbd_d = din("bbd", [L, 128, 2, 4096])
    ct_d = din("ctd", [L, 128, 8192])
    cst_d = din("cst", [128, 386])
    eps_d = din("epsd", [128, 1])
    outT = nc.dram_tensor("outT", [D, S], F32, kind="ExternalOutput").ap()

    xr = dscr("xr", [D, S], F32)
    lxT = dscr("lxT", [MIXW, S], BF16)
    lgT = dscr("lgT", [MIXW, ST], BF16)
    qT = dscr("qT", [MIXW, ST], BF16)
    kT = dscr("kT", [MIXW, S], BF16)
    vS = dscr("vS", [S, MIXW], BF16)
    uT = dscr("uT", [MIXW, ST], BF16)
    gT = dscr("gT", [3 * D, ST], BF16)
    yaT = dscr("yaT", [MIXW, ST], BF16)
    ybT = dscr("ybT", [MIXW, ST], BF16)
    ycT = dscr("ycT", [MIXW, ST], BF16)
    mT = dscr("mT", [D, ST], BF16)
    actT = dscr("actT", [FH, ST], BF16)
    tabE = nc.dram_tensor("tabE", [2, 128, 4096], F32, kind="Internal").ap()
    tabF = nc.dram_tensor("tabF", [2, 128, 32 * 129], F32, kind="Internal").ap()
    tabB = nc.dram_tensor("tabB", [2, 128, 4096], BF16, kind="Internal").ap()

    def gtile(name, shape, dt):
        t = gst.enter_context(nc.sbuf_tensor(name, list(shape), dt))
        return t, kb.buf(name)

    colp, b_colp = gtile("colp_s", [128, NCOL], F32)
    nsp8, b_nsp8 = gtile("nsp8", [128, L * 8], F32)
    ones_f, b_onesf = gtile("ones_f", [128, 128], F32)
    ones_b, b_onesb = gtile("ones_b", [128, 128], BF16)
    tri_b, b_trib = gtile("tri_b", [128, 128], BF16)
    ntri_b, b_ntrib = gtile("ntri_b", [128, 128], BF16)
    iorow, b_iorow = gtile("iorow", [128, 129], F32)
    lru_h, b_lruh = gtile("lru_h", [128, 8], F32)
    s5c, b_s5c = gtile("s5c", [128, 2, 32], F32)
    s5ca, s5cb = s5c[:, 0, :], s5c[:, 1, :]
    b_s5ca = b_s5cb = b_s5c
    epsc, b_epsc = gtile("epsc", [128, 1], F32)
    psum = gst.enter_context(nc.psum_tensor("psum", [128, 4096], F32))
    pb = [kb.buf(f"ps{i}") for i in range(8)]

    def bank(i):
        return psum[:, i * 512:(i + 1) * 512]

    def col(name, idx, n=1):
        o = COFF[name] + idx
        return colp[:, o:o + n]

    def I(name, **kw):
        return lambda h: getattr(h, name)(**kw)

    def MM(out, lhsT, rhs, start=True, stop=True):
        return I("matmul", out=out, lhsT=lhsT, rhs=rhs, start=start, stop=stop)

    def DMA(out, in_):
        return I("dma_start", out=out, in_=in_)

    def V(name, r, w, **kw):
        return kb.op("dve", I(name, **kw), reads=r, writes=w)

    def A(r, w, out, in_, func, **kw):
        return kb.op("act", I("activation", out=out, in_=in_, func=func, **kw), reads=r, writes=w)

    def TT(r, w, out, in0, in1, op):
        return V("tensor_tensor", r, w, out=out, in0=in0, in1=in1, op=op)

    def TS(r, w, out, in0, s1, s2, op0, op1=None):
        if op1 is None:
            return V("tensor_scalar", r, w, out=out, in0=in0, scalar1=s1, scalar2=None, op0=op0)
        return V("tensor_scalar", r, w, out=out, in0=in0, scalar1=s1, scalar2=s2, op0=op0, op1=op1)

    def STT(r, w, out, in0, scalar, in1, op0, op1):
        return V("scalar_tensor_tensor", r, w, out=out, in0=in0, scalar=scalar, in1=in1, op0=op0, op1=op1)

    kb.dma("sp", DMA(colp[:], colp_d), writes=[b_colp], owner=b_colp)
    kb.dma("sp", DMA(ones_f[:], cst_d[:, 128:256]), writes=[b_onesf], owner=b_onesf)
    kb.dma("sp", DMA(iorow[:], cst_d[:, 256:385]), writes=[b_iorow], owner=b_iorow)
    kb.dma("pool", DMA(tri_b[:], cst_d[:, 0:128]), writes=[b_trib], owner=b_trib)
    kb.dma("pool", DMA(ones_b[:], cst_d[:, 128:256]), writes=[b_onesb], owner=b_onesb)
    TS([b_trib], [b_ntrib], ntri_b[:], tri_b[:], -1.0, None, ALU.mult)
    kb.dma("sp", DMA(epsc[:], eps_d), writes=[b_epsc], owner=b_epsc)
    A([b_colp], [b_nsp8], nsp8[:], col("lam", 0, L * 8), AF.Exp, scale=-1.0)
    A([b_nsp8], [b_nsp8], nsp8[:], nsp8[:], AF.Ln, bias=1.0, scale=1.0)
    TS([b_nsp8], [b_nsp8], nsp8[:], nsp8[:], -8.0, None, ALU.mult)
    b_x0 = kb.buf("x0")
    kb.dma("sp", [DMA(xr[c * 128:(c + 1) * 128, :], xT[c * 128:(c + 1) * 128, :]) for c in range(16)], writes=[b_x0], owner=b_x0)
    kb.barrier()

    xr_v = xr.rearrange("(k p) t -> p k t", p=128)

    def rms_norm_tile(xin, b_xin, tg, sq_ring, rs_ring, pbi):
        kb.dma("sp", [DMA(xin[:, 0:8, :], xr_v[:, 0:8, tg:tg + 512]), DMA(xin[:, 8:16, :], xr_v[:, 8:16, tg:tg + 512])], writes=[b_xin], owner=b_xin)
        for k in range(NK):
            sq, b_sq = sq_ring.next()
            A([b_xin], [b_sq], sq[:], xin[:, k, :], AF.Square)
            kb.op("pe", MM(bank(pbi), ones_b[:], sq[:], start=(k == 0), stop=(k == NK - 1)), reads=[b_sq, b_onesb], writes=[pb[pbi]])
        rs, b_rs = rs_ring.next()
        A([pb[pbi], b_epsc], [b_rs], rs[:], bank(pbi), AF.Sqrt, scale=1.0 / D, bias=epsc[:])
        V("reciprocal", [b_rs], [b_rs], out=rs[:], in_=rs[:])
        return rs, b_rs

    def rms_norm_to(hT, b_hT, gname, l, t0, xin_ring, sq_ring, rs_ring, pbis):
        for tt in range(nTT):
            xin, b_xin = xin_ring.next()
            rs, b_rs = rms_norm_tile(xin, b_xin, t0 + tt * 512, sq_ring, rs_ring, pbis[tt % len(pbis)])
            for k in range(NK):
                STT([b_xin, b_rs, b_colp], [b_hT], hT[:, k, tt * 512:(tt + 1) * 512], xin[:, k, :], col(gname, l * 16 + k), rs[:], ALU.mult, ALU.mult)

    evac_flip = [0]

    def evac_copy(out_ap, in_ap, reads, writes):
        evac_flip[0] ^= 1
        if evac_flip[0]:
            A(reads, writes, out_ap, in_ap, AF.Copy)
        else:
            V("tensor_copy", reads, writes, out=out_ap, in_=in_ap)

    def proj_residual(l, t0, src_dram, nk, w_dram_l, chunk_tt):
        srcv = src_dram.rearrange("(k p) t -> p k t", p=128)
        splits = [(a, min(a + 16, nk)) for a in range(0, nk, 16)]
        for c0 in range(0, nTT, chunk_tt):
            ph = Phase(kb)
            sms = [ph.tile("sm", [128, nk, 512], BF16) for _ in range(chunk_tt)]
            for tt in range(chunk_tt):
                sm, b_sm = sms[tt]
                cs0 = (c0 + tt) * 512
                kb.dma("sp", [DMA(sm[:, a:b, :], srcv[:, a:b, cs0:cs0 + 512]) for a, b in splits], writes=[b_sm], owner=b_sm)
            wo_ring = ph.ring("wo", 3, [128, nk, 128], BF16)
            xs_ring = ph.ring("xs", 3, [128, 512], F32)
            it = 0
            for f in range(16):
                wo, b_wo = wo_ring.next()
                kb.dma("pool", [DMA(wo[:, a:b, :], w_dram_l[f, :, a:b, :]) for a, b in splits], writes=[b_wo], owner=b_wo)
                for tt in range(chunk_tt):
                    sm, b_sm = sms[tt]
                    bi = it % 6
                    it += 1
                    tg = t0 + (c0 + tt) * 512
                    xs, b_xs = xs_ring.next()
                    kb.dma("sp", DMA(xs[:], xr[f * 128:(f + 1) * 128, tg:tg + 512]), writes=[b_xs], owner=b_xs)
                    kb.op("pe", [MM(bank(bi), wo[:, k, :], sm[:, k, :], start=(k == 0), stop=(k == nk - 1)) for k in range(nk)],
                          reads=[b_wo, b_sm], writes=[pb[bi]])
                    TT([pb[bi], b_xs], [b_xs], xs[:], bank(bi), xs[:], ALU.add)
                    kb.dma("sp", DMA(xr[f * 128:(f + 1) * 128, tg:tg + 512], xs[:]), reads=[b_xs], owner=b_xs)
            ph.close()

    def v3(ap):
        return ap.rearrange("p (b j) -> p b j", b=4)

    for l in range(L):
        for s in range(nST):
            t0 = s * ST
            first = (s == 0) and not has_prev
            if "p1" not in skip:
                phO = Phase(kb)
                hT, b_hT = phO.tile("hT", [128, NK, ST], BF16)
                ph = Phase(kb)
                xin_ring = ph.ring("xin", 2, [128, NK, 512], F32)
                sq_ring = ph.ring("sq", 3, [128, 512], BF16)
                rs_ring = ph.ring("rs", 2, [128, 512], F32)
                wt_ring = ph.ring("wt", 3, [128, NK, 128], BF16)
                stg_ring = ph.ring("stg", 3, [128, ST], BF16)
                wv_ring = ph.ring("wv", 1, [128, NK, 512], BF16)
                sv_ring = ph.ring("sv", 3, [128, 512], BF16)
                rms_norm_to(hT, b_hT, "gmix", l, t0, xin_ring, sq_ring, rs_ring, (6, 7))
                pbr = 0
                for m in list(range(0, 32)) + list(range(40, 48)):
                    wt, b_wt = wt_ring.next()
                    kb.dma("pool", DMA(wt[:], w_in_t[l, m]), writes=[b_wt], owner=b_wt)
                    stg, b_stg = stg_ring.next()
                    for tt in range(nTT):
                        bi = pbr % 6
                        pbr += 1
                        kb.op("pe", [MM(bank(bi), wt[:, k, :], hT[:, k, tt * 512:(tt + 1) * 512], start=(k == 0), stop=(k == NK - 1)) for k in range(NK)],
                              reads=[b_wt, b_hT], writes=[pb[bi]])
                        o_ap = stg[:, tt * 512:(tt + 1) * 512]
                        if m < 48:
                            evac_copy(o_ap, bank(bi), [pb[bi]], [b_stg])
                        else:
                            A([pb[bi], b_colp], [b_stg], o_ap, bank(bi), AF.Sigmoid, bias=col("gbias", l * 48 + (m - 48)), scale=1.0)
                    if m < 8:
                        dst = lxT[m * 128:(m + 1) * 128, t0:t0 + ST]
                    elif m < 16:
                        dst = lgT[(m - 8) * 128:(m - 7) * 128, :]
                    elif m < 24:
                        dst = qT[(m - 16) * 128:(m - 15) * 128, :]
                    elif m < 32:
                        dst = kT[(m - 24) * 128:(m - 23) * 128, t0:t0 + ST]
                    elif m < 48:
                        dst = uT[(m - 40) * 128:(m - 39) * 128, :]
                    else:
                        dst = gT[(m - 48) * 128:(m - 47) * 128, :]
                    kb.dma("sp", DMA(dst, stg[:]), reads=[b_stg], owner=b_stg)
                for cb in range(2):
                    wv, b_wv = wv_ring.next()
                    kb.dma("pool", [DMA(wv[:, :, mm * 128:(mm + 1) * 128], w_in_t[l, 32 + 4 * cb + mm]) for mm in range(4)], writes=[b_wv], owner=b_wv)
                    for j in range(ST // 128):
                        bi = pbr % 6
                        pbr += 1
                        kb.op("pe", [MM(bank(bi), hT[:, k, j * 128:(j + 1) * 128], wv[:, k, :], start=(k == 0), stop=(k == NK - 1)) for k in range(NK)],
                              reads=[b_wv, b_hT], writes=[pb[bi]])
                        sv, b_sv = sv_ring.next()
                        evac_copy(sv[:], bank(bi), [pb[bi]], [b_sv])
                        kb.dma("sp", DMA(vS[t0 + j * 128:t0 + (j + 1) * 128, cb * 512:(cb + 1) * 512], sv[:]), reads=[b_sv], owner=b_sv)
                ph.close()

            if "p1" not in skip:
                ph = Phase(kb)
                gwt_ring = ph.ring("gwt", 3, [128, NK, 128], BF16)
                gst_ring = ph.ring("gst", 3, [128, ST], BF16)

                def gates_gen():
                    pend = None
                    g = 0
                    for m in range(48, 96):
                        wt, b_wt = gwt_ring.next()
                        kb.dma("pool", DMA(wt[:], w_in_t[l, m]), writes=[b_wt], owner=b_wt)
                        stg, b_stg = gst_ring.next()
                        for tt in range(nTT):
                            bi = 6 + (g % 2)
                            g += 1
                            kb.op("pe", [MM(bank(bi), wt[:, k, :], hT[:, k, tt * 512:(tt + 1) * 512], start=(k == 0), stop=(k == NK - 1)) for k in range(NK)],
                                  reads=[b_wt, b_hT], writes=[pb[bi]])
                            if pend is not None:
                                pend()
                            def fin(bi=bi, m=m, tt=tt, stg=stg, b_stg=b_stg):
                                A([pb[bi], b_colp], [b_stg], stg[:, tt * 512:(tt + 1) * 512], bank(bi), AF.Sigmoid, bias=col("gbias", l * 48 + (m - 48)), scale=1.0)
                                if tt == nTT - 1:
                                    kb.dma("sp", DMA(gT[(m - 48) * 128:(m - 47) * 128, :], stg[:]), reads=[b_stg], owner=b_stg)
                            pend = fin
                            yield
                    pend()

                ggen = gates_gen()

                def gstep(k):
                    for _ in range(k):
                        if next(ggen, "done") == "done":
                            return False
                    return True
            else:
                ph = Phase(kb)

                def gstep(k):
                    return False

            if "p2a" not in skip:
                bd, b_bd = ph.tile("bd", [128, 8, 2, 128], BF16)
                kb.dma("pool", DMA(bd[:], lruw_d[l]), writes=[b_bd], owner=b_bd)
                lx_ring = ph.ring("lx", 3, [128, 515], BF16)
                lg_ring = ph.ring("lg", 3, [128, 512], BF16)
                ya_ring = ph.ring("ya", 2, [128, 512], BF16)
                hs_ring = ph.ring("hs", 2, [128, 512], F32)
                xc_ring = ph.ring("xc", 2, [128, 512], F32)
                xcb_ring = ph.ring("xcb", 2, [128, 512], BF16)
                rr, b_rr = ph.tile("rr", [128, 512], F32)
                ii, b_ii = ph.tile("ii", [128, 512], F32)
                aa, b_aa = ph.tile("aa", [128, 512], F32)
                a2, b_a2 = ph.tile("a2", [128, 512], F32)
                gg, b_gg = ph.tile("gg", [128, 512], F32)
                if first:
                    V("memset", [], [b_lruh], ap=lru_h[:], constant=0.0)
                lunits = [(ct, tt) for ct in range(8) for tt in range(nTT)]
                lst = {}

                def lru_load(n):
                    ct, tt = lunits[n]
                    r0, r1 = ct * 128, (ct + 1) * 128
                    tg = t0 + tt * 512
                    lx, b_lx = lx_ring.next()
                    lg, b_lg = lg_ring.next()
                    if tg == 0 and not has_prev:
                        V("memset", [], [b_lx], ap=lx[:, 0:3], constant=0.0)
                        kb.dma("sp", DMA(lx[:, 3:515], lxT[r0:r1, 0:512]), writes=[b_lx], owner=b_lx)
                    else:
                        kb.dma("sp", DMA(lx[:], lxT[r0:r1, tg - 3:tg + 512]), writes=[b_lx], owner=b_lx)
                    kb.dma("sp", DMA(lg[:], lgT[r0:r1, tt * 512:(tt + 1) * 512]), writes=[b_lg], owner=b_lg)
                    lst[n] = dict(lx=lx, b_lx=b_lx, lg=lg, b_lg=b_lg)

                def lru_front(n):
                    ct, tt = lunits[n]
                    d = lst[n]
                    lx, b_lx = d["lx"], d["b_lx"]
                    xc, b_xc = xc_ring.next()
                    xcb, b_xcb = xcb_ring.next()
                    cwi = (l * 8 + ct) * 4
                    TS([b_lx, b_colp], [b_xc], xc[:], lx[:, 3:515], col("convw", cwi + 3), col("convb", l * 8 + ct), ALU.mult, ALU.add)
                    for k in range(3):
                        STT([b_lx, b_xc, b_colp], [b_xc], xc[:], lx[:, k:k + 512], col("convw", cwi + k), xc[:], ALU.mult, ALU.add)
                    V("tensor_copy", [b_xc], [b_xcb], out=xcb[:], in_=xc[:])
                    pbase = (n % 2) * 2
                    kb.op("pe", MM(bank(pbase), bd[:, ct, 0, :], xcb[:]), reads=[b_bd, b_xcb], writes=[pb[pbase]])
                    kb.op("pe", MM(bank(pbase + 1), bd[:, ct, 1, :], xcb[:]), reads=[b_bd, b_xcb], writes=[pb[pbase + 1]])
                    d.update(xc=xc, b_xc=b_xc, pbase=pbase)

                lru_load(0)
                if len(lunits) > 1:
                    lru_load(1)
                lru_front(0)
                for n in range(len(lunits)):
                    ct, tt = lunits[n]
                    r0, r1 = ct * 128, (ct + 1) * 128
                    d = lst[n]
                    xc, b_xc, pbase = d["xc"], d["b_xc"], d["pbase"]
                    lg, b_lg = d["lg"], d["b_lg"]
                    if n + 2 < len(lunits):
                        lru_load(n + 2)
                    A([pb[pbase], b_colp], [b_rr], rr[:], bank(pbase), AF.Sigmoid, bias=col("ba", l * 8 + ct), scale=1.0)
                    A([pb[pbase + 1], b_colp], [b_ii], ii[:], bank(pbase + 1), AF.Sigmoid, bias=col("bx", l * 8 + ct), scale=1.0)
                    A([b_rr, b_nsp8], [b_aa], aa[:], rr[:], AF.Exp, scale=nsp8[:, l * 8 + ct:l * 8 + ct + 1])
                    TT([b_ii, b_xc], [b_ii], ii[:], ii[:], xc[:], ALU.mult)
                    TT([b_aa], [b_a2], a2[:], aa[:], aa[:], ALU.mult)
                    A([b_a2], [b_a2], a2[:], a2[:], AF.Sqrt, scale=-1.0, bias=1.0)
                    gstep(1)
                    if n + 1 < len(lunits):
                        lru_front(n + 1)
                    gstep(1)
                    A([b_lg], [b_gg], gg[:], lg[:], AF.Gelu_apprx_tanh)
                    TT([b_ii, b_a2], [b_ii], ii[:], ii[:], a2[:], ALU.mult)
                    hs, b_hs = hs_ring.next()
                    V("tensor_tensor_scan", [b_aa, b_ii, b_lruh], [b_hs], out=hs[:], data0=aa[:], data1=ii[:], initial=lru_h[:, ct:ct + 1], op0=ALU.mult, op1=ALU.add)
                    V("tensor_copy", [b_hs], [b_lruh], out=lru_h[:, ct:ct + 1], in_=hs[:, 511:512])
                    ya, b_ya = ya_ring.next()
                    TT([b_hs, b_gg], [b_ya], ya[:], hs[:], gg[:], ALU.mult)
                    kb.dma("sp", DMA(yaT[r0:r1, tt * 512:(tt + 1) * 512], ya[:]), reads=[b_ya], owner=b_ya)
                    del lst[n]
                    gstep(1)

            if "p2b" not in skip:
                NKT = (ST + 512) // 128
                qh_ring = ph.ring("qh", 2, [128, ST], BF16)
                kh_ring = ph.ring("kh", 2, [128, ST + 512], BF16)
                vh_ring = ph.ring("vh", 2, [128, NKT, 128], BF16)
                bh_ring = ph.ring("bh", 2, [128, 640], F32)
                yb_ring = ph.ring("yb", 2, [128, ST], BF16)
                sf_ring = ph.ring("sf", 2, [128, 640], F32)
                pt_ring = ph.ring("pt", 3, [128, 640], BF16)
                rc_ring = ph.ring("rc", 2, [128, 128], F32)
                NQ = ST // 128
                aunits = [(hd, m) for hd in range(8) for m in range(NQ)]
                NU = len(aunits)
                hst, ust = {}, {}

                def att_load(hd):
                    r0, r1 = hd * 128, (hd + 1) * 128
                    qh, b_qh = qh_ring.next()
                    kh, b_kh = kh_ring.next()
                    vh, b_vh = vh_ring.next()
                    bh, b_bh = bh_ring.next()
                    yb, b_yb = yb_ring.next()
                    kb.dma("sp", DMA(qh[:], qT[r0:r1, :]), writes=[b_qh], owner=b_qh)
                    kb.dma("sp", DMA(bh[:], abias_d[l, hd]), writes=[b_bh], owner=b_bh)
                    if first:
                        kb.dma("sp", DMA(kh[:, 512:], kT[r0:r1, 0:ST]), writes=[b_kh], owner=b_kh)
                        kb.dma("sp", DMA(vh[:, 4:, :], vS[0:ST, r0:r1].rearrange("(n p) d -> p n d", p=128)), writes=[b_vh], owner=b_vh)
                    else:
                        kb.dma("sp", DMA(kh[:], kT[r0:r1, t0 - 512:t0 + ST]), writes=[b_kh], owner=b_kh)
                        kb.dma("sp", DMA(vh[:], vS[t0 - 512:t0 + ST, r0:r1].rearrange("(n p) d -> p n d", p=128)), writes=[b_vh], owner=b_vh)
                    hst[hd] = dict(qh=qh, b_qh=b_qh, kh=kh, b_kh=b_kh, vh=vh, b_vh=b_vh, bh=bh, b_bh=b_bh, yb=yb, b_yb=b_yb)

                def att_scores(n):
                    hd, m = aunits[n]
                    if hd not in hst:
                        att_load(hd)
                    H = hst[hd]
                    kts = [kt for kt in range(5) if (not first) or (m - 4 + kt) >= 0]
                    sset = n % 2
                    sb = 2 * sset
                    sc = psum[:, sb * 512:sb * 512 + 640]
                    kb.op("pe", [MM(sc[:, kt * 128:(kt + 1) * 128], H["kh"][:, (m + kt) * 128:(m + kt + 1) * 128], H["qh"][:, m * 128:(m + 1) * 128]) for kt in kts],
                          reads=[H["b_kh"], H["b_qh"]], writes=[pb[sb], pb[sb + 1]])
                    ust[n] = dict(kts=kts, sb=sb, sc=sc, lo=kts[0] * 128)

                def att_sf_exp(n):
                    hd, m = aunits[n]
                    H, U = hst[hd], ust[n]
                    sf, b_sf = sf_ring.next()
                    pt, b_pt = pt_ring.next()
                    lo, hi, sb, sc = U["lo"], 640, U["sb"], U["sc"]
                    STT([pb[sb], pb[sb + 1], H["b_bh"]], [b_sf], sf[:, lo:hi], sc[:, lo:hi], float(128 ** -0.5), H["bh"][:, lo:hi], ALU.mult, ALU.add)
                    A([b_sf], [b_pt], pt[:, lo:hi], sf[:, lo:hi], AF.Exp)
                    U.update(pt=pt, b_pt=b_pt)

                def att_pv(n):
                    hd, m = aunits[n]
                    H, U = hst[hd], ust[n]
                    kts, pt = U["kts"], U["pt"]
                    pbk = 4 + (n % 2)
                    nk = len(kts)
                    fns = [MM(bank(pbk)[:, 0:128], H["vh"][:, m + kt, :], pt[:, kt * 128:(kt + 1) * 128], start=(i_ == 0), stop=(i_ == nk - 1)) for i_, kt in enumerate(kts)]
                    fns += [MM(bank(pbk)[:, 128:256], ones_b[:], pt[:, kt * 128:(kt + 1) * 128], start=(i_ == 0), stop=(i_ == nk - 1)) for i_, kt in enumerate(kts)]
                    kb.op("pe", fns, reads=[H["b_vh"], U["b_pt"], b_onesb], writes=[pb[pbk]])
                    U["pbk"] = pbk

                def att_fin(n):
                    hd, m = aunits[n]
                    H, U = hst[hd], ust[n]
                    pbk = U["pbk"]
                    rc, b_rc = rc_ring.next()
                    V("reciprocal", [pb[pbk]], [b_rc], out=rc[:], in_=bank(pbk)[:, 128:256])
                    TT([pb[pbk], b_rc], [H["b_yb"]], H["yb"][:, m * 128:(m + 1) * 128], bank(pbk)[:, 0:128], rc[:], ALU.mult)
                    if m == NQ - 1:
                        kb.dma("sp", DMA(ybT[hd * 128:(hd + 1) * 128, :], H["yb"][:]), reads=[H["b_yb"]], owner=H["b_yb"])
                    del ust[n]

                att_scores(0)
                if NU > 1:
                    att_scores(1)
                att_sf_exp(0)
                for n in range(NU):
                    if n + 2 < NU:
                        att_scores(n + 2)
                    att_pv(n)
                    if n + 1 < NU:
                        att_sf_exp(n + 1)
                    if n >= 1:
                        att_fin(n - 1)
                    if n % 4 != 3:
                        gstep(1)
                att_fin(NU - 1)
            while gstep(8):
                pass
            ph.close()
            if "p1" not in skip:
                phO.close()


            if "p2c" not in skip:
                ph = Phase(kb)
                W = 4096
                NF = 32 * 129
                E2, b_Ere = ph.tile("E2", [128, 2, W], F32)
                b_Eim = b_Ere
                E_re, E_im = E2[:, 0, :], E2[:, 1, :]
                F2, b_Fre = ph.tile("F2", [128, 2, 32, 129], F32)
                b_Fim = b_Fre
                F_re, F_im = F2[:, 0, :, :], F2[:, 1, :, :]
                bbr, b_bbr = ph.tile("bbr", [128, W], BF16)
                bbi, b_bbi = ph.tile("bbi", [128, W], BF16)
                Ffr = F_re[:].rearrange("p b j -> p (b j)")
                Ffi = F_im[:].rearrange("p b j -> p (b j)")
                MUL, ADD, SUB = ALU.mult, ALU.add, ALU.subtract
                if s == 0:
                    pp = Phase(kb)
                    ar, b_ar = pp.tile("ar", [128, W], F32)
                    ai, b_ai = pp.tile("ai", [128, W], F32)
                    sr, b_sr = pp.tile("sr", [128, W], F32)
                    t1f, b_t1 = pp.tile("t1", [128, NF], F32)
                    t2f, b_t2 = pp.tile("t2", [128, NF], F32)
                    t3f, b_t3 = pp.tile("t3", [128, NF], F32)
                    t1, t2, t3 = t1f[:, 0:W], t2f[:, 0:W], t3f[:, 0:W]
                    bdr, b_bdr = Ffr[:, 0:W], b_Fre
                    bdi, b_bdi = Ffi[:, 0:W], b_Fim
                    kb.dma("sp", DMA(ar[:], rowp_d[l, 0:1, :].to_broadcast([128, W])), writes=[b_ar], owner=b_ar)
                    kb.dma("sp", DMA(ai[:], rowp_d[l, 1:2, :].to_broadcast([128, W])), writes=[b_ai], owner=b_ai)
                    kb.dma("sp", DMA(sr[:], rowp_d[l, 2:3, :].to_broadcast([128, W])), writes=[b_sr], owner=b_sr)
                    kb.dma("sp", DMA(bdr, bbd_d[l, :, 0, :]), writes=[b_bdr], owner=b_bdr)
                    kb.dma("sp", DMA(bdi, bbd_d[l, :, 1, :]), writes=[b_bdi], owner=b_bdi)

                    def sincos(dst_sin, b_ds, dst_cos, b_dc, ycyc, b_y, tmp, b_tmp):
                        TS([b_y], [b_tmp], tmp, ycyc, MAGIC, MAGIC, ALU.add, ALU.subtract)
                        TT([b_y, b_tmp], [b_tmp], tmp, ycyc, tmp, ALU.subtract)
                        A([b_tmp], [b_ds], dst_sin, tmp, AF.Sin, scale=TWO_PI)
                        TS([b_y], [b_y], ycyc, ycyc, 0.25, None, ALU.add)
                        TS([b_y], [b_tmp], tmp, ycyc, MAGIC, MAGIC, ALU.add, ALU.subtract)
                        TT([b_y, b_tmp], [b_tmp], tmp, ycyc, tmp, ALU.subtract)
                        A([b_tmp], [b_dc], dst_cos, tmp, AF.Sin, scale=TWO_PI)

                    MUL, ADD, SUB = ALU.mult, ALU.add, ALU.subtract
                    A([b_sr], [b_sr], sr[:], sr[:], AF.Exp)
                    TT([b_ar, b_sr], [b_t1], t1, ar[:], sr[:], MUL)
                    A([b_t1], [b_t1], t1, t1, AF.Exp)
                    TT([b_ai, b_sr], [b_t2], t2, ai[:], sr[:], MUL)
                    TS([b_t2], [b_t2], t2, t2, 1.0 / TWO_PI, None, MUL)
                    sincos(E_im[:], b_Eim, E_re[:], b_Ere, t2, b_t2, t3, b_t3)
                    TT([b_Ere, b_t1], [b_Ere], E_re[:], E_re[:], t1, MUL)
                    TT([b_Eim, b_t1], [b_Eim], E_im[:], E_im[:], t1, MUL)
                    TS([b_Ere], [b_Ere], E_re[:], E_re[:], -1.0, None, ADD)
                    TT([b_ar], [b_t1], t1, ar[:], ar[:], MUL)
                    TT([b_ai], [b_t2], t2, ai[:], ai[:], MUL)
                    TT([b_t1, b_t2], [b_t1], t1, t1, t2, ADD)
                    V("reciprocal", [b_t1], [b_t1], out=t1, in_=t1)
                    TT([b_Ere, b_ar], [b_t2], t2, E_re[:], ar[:], MUL)
                    TT([b_Eim, b_ai], [b_t3], t3, E_im[:], ai[:], MUL)
                    TT([b_t2, b_t3], [b_t2], t2, t2, t3, ADD)
                    TT([b_t2, b_t1], [b_t2], t2, t2, t1, MUL)
                    TT([b_Eim, b_ar], [b_t3], t3, E_im[:], ar[:], MUL)
                    TT([b_Ere, b_ai], [b_Eim], E_im[:], E_re[:], ai[:], MUL)
                    TT([b_t3, b_Eim], [b_t3], t3, t3, E_im[:], SUB)
                    TT([b_t3, b_t1], [b_t3], t3, t3, t1, MUL)
                    TT([b_t2, b_bdr], [b_Ere], E_re[:], t2, bdr, MUL)
                    TT([b_t3, b_bdi], [b_Eim], E_im[:], t3, bdi, MUL)
                    TT([b_Ere, b_Eim], [b_bbr], bbr[:], E_re[:], E_im[:], SUB)
                    TT([b_t2, b_bdi], [b_Ere], E_re[:], t2, bdi, MUL)
                    TT([b_t3, b_bdr], [b_Eim], E_im[:], t3, bdr, MUL)
                    TT([b_Ere, b_Eim], [b_bbi], bbi[:], E_re[:], E_im[:], ADD)
                    iocol = col("iota", 0)
                    TT([b_ar, b_sr], [b_t1], t1, ar[:], sr[:], MUL)
                    TS([b_t1, b_colp], [b_t1], t1, t1, iocol, -1.0, MUL, MUL)
                    A([b_t1], [b_t1], t1, t1, AF.Exp)
                    TT([b_ai, b_sr], [b_t2], t2, ai[:], sr[:], MUL)
                    TS([b_t2, b_colp], [b_t2], t2, t2, iocol, -1.0 / TWO_PI, MUL, MUL)
                    sincos(E_im[:], b_Eim, E_re[:], b_Ere, t2, b_t2, t3, b_t3)
                    TT([b_Ere, b_t1], [b_Ere], E_re[:], E_re[:], t1, MUL)
                    TT([b_Eim, b_t1], [b_Eim], E_im[:], E_im[:], t1, MUL)
                    stc, b_stc = pp.tile("stc", [128, 32], F32)
                    alc, b_alc = pp.tile("alc", [128, 32], F32)
                    thc, b_thc = pp.tile("thc", [128, 32], F32)
                    A([b_colp], [b_stc], stc[:], col("lsc", l * 32, 32), AF.Exp)
                    TT([b_colp, b_stc], [b_alc], alc[:], col("acre", l * 32, 32), stc[:], MUL)
                    TT([b_colp, b_stc], [b_thc], thc[:], col("acim", l * 32, 32), stc[:], MUL)
                    TS([b_thc], [b_thc], thc[:], thc[:], 1.0 / TWO_PI, None, MUL)
                    g1 = t1f[:].rearrange("p (b j) -> p b j", b=32)
                    g2 = t2f[:].rearrange("p (b j) -> p b j", b=32)
                    io_b = iorow[:].unsqueeze(1).to_broadcast([128, 32, 129])
                    TT([b_iorow, b_alc], [b_t1], g1, io_b, alc[:].unsqueeze(2).to_broadcast([128, 32, 129]), MUL)
                    A([b_t1], [b_t1], t1f[:], t1f[:], AF.Exp)
                    TT([b_iorow, b_thc], [b_t2], g2, io_b, thc[:].unsqueeze(2).to_broadcast([128, 32, 129]), MUL)
                    sincos(Ffi, b_Fim, Ffr, b_Fre, t2f[:], b_t2, t3f[:], b_t3)
                    TT([b_Fre, b_t1], [b_Fre], Ffr, Ffr, t1f[:], MUL)
                    TT([b_Fim, b_t1], [b_Fim], Ffi, Ffi, t1f[:], MUL)
                    if nST > 1:
                        kb.dma("sp", DMA(tabE[0], E_re[:]), reads=[b_Ere], owner=b_Ere)
                        kb.dma("sp", DMA(tabE[1], E_im[:]), reads=[b_Eim], owner=b_Eim)
                        kb.dma("sp", DMA(tabF[0], Ffr), reads=[b_Fre], owner=b_Fre)
                        kb.dma("sp", DMA(tabF[1], Ffi), reads=[b_Fim], owner=b_Fim)
                        kb.dma("sp", DMA(tabB[0], bbr[:]), reads=[b_bbr], owner=b_bbr)
                        kb.dma("sp", DMA(tabB[1], bbi[:]), reads=[b_bbi], owner=b_bbi)
                    pp.close()
                else:
                    kb.dma("sp", DMA(E_re[:], tabE[0]), writes=[b_Ere], owner=b_Ere)
                    kb.dma("sp", DMA(E_im[:], tabE[1]), writes=[b_Eim], owner=b_Eim)
                    kb.dma("sp", DMA(Ffr, tabF[0]), writes=[b_Fre], owner=b_Fre)
                    kb.dma("sp", DMA(Ffi, tabF[1]), writes=[b_Fim], owner=b_Fim)
                    kb.dma("sp", DMA(bbr[:], tabB[0]), writes=[b_bbr], owner=b_bbr)
                    kb.dma("sp", DMA(bbi[:], tabB[1]), writes=[b_bbi], owner=b_bbi)
                cmat, b_cmat = ph.tile("cmat", [128, 8192], BF16)
                kb.dma("pool", [DMA(cmat[:, q * 2048:(q + 1) * 2048], ct_d[l, :, q * 2048:(q + 1) * 2048]) for q in range(4)], writes=[b_cmat], owner=b_cmat)
                ncmat, b_ncmat = ph.tile("ncmat", [128, 8192], BF16)
                TS([b_cmat], [b_ncmat], ncmat[:], cmat[:], -1.0, None, MUL)
                u_all, b_u = ph.tile("u_all", [128, 8, ST], BF16)
                kb.dma("sp", DMA(u_all[:], uT.rearrange("(c p) t -> p c t", p=128)), writes=[b_u], owner=b_u)
                p_ring = ph.ring("pp4", 2, [128, 4, 512], BF16)
                q_ring = ph.ring("qq4", 2, [128, 4, 512], BF16)
                cp_ring = ph.ring("cp", 2, [128, 2, 512], F32)
                yc_ring = ph.ring("yc", 4, [128, 128], BF16)
                ty_ring = ph.ring("ty", 2, [128, 128], F32)
                tn_ring = ph.ring("tn", 2, [128, 4, 4], F32)
                if first:
                    V("memset", [], [b_s5c], ap=s5c[:], constant=0.0)
                units = [(j, ct) for j in range(ST // 128) for ct in range(8)]

                def bu_banks(ui):
                    base = (ui % 2) * 4
                    return base, base + 1, base + 2, base + 3

                def emit_bu(ui):
                    j, ct = units[ui]
                    br, bi_, _, _ = bu_banks(ui)
                    ul = u_all[:, ct, j * 128:(j + 1) * 128]
                    c0, c1 = ct * 512, (ct + 1) * 512
                    kb.op("pe", MM(bank(br), ul, bbr[:, c0:c1]), reads=[b_u, b_bbr], writes=[pb[br]])
                    kb.op("pe", MM(bank(bi_), ul, bbi[:, c0:c1]), reads=[b_u, b_bbi], writes=[pb[bi_]])

                def st_eprod(ui, stt):
                    j, ct = units[ui]
                    br, bi_, cr_, ci_ = bu_banks(ui)
                    c0, c1 = ct * 512, (ct + 1) * 512
                    pp4, b_p = p_ring.next()
                    stt["p"] = (pp4, b_p)
                    E2ct = E2[:, :, c0:c1]
                    TT([pb[br], b_Ere], [b_p], pp4[:, 0:2, :], bank(br).unsqueeze(1).to_broadcast([128, 2, 512]), E2ct, MUL)
                    TT([pb[bi_], b_Ere], [b_p], pp4[:, 2:4, :], bank(bi_).unsqueeze(1).to_broadcast([128, 2, 512]), E2ct, MUL)
                    fns = []
                    for blk in range(4):
                        bs = slice(blk * 128, (blk + 1) * 128)
                        fns.append(MM(bank(cr_)[:, bs], pp4[:, 0, bs], tri_b[:], start=True, stop=False))
                        fns.append(MM(bank(cr_)[:, bs], pp4[:, 3, bs], ntri_b[:], start=False, stop=True))
                        fns.append(MM(bank(ci_)[:, bs], pp4[:, 2, bs], tri_b[:], start=True, stop=False))
                        fns.append(MM(bank(ci_)[:, bs], pp4[:, 1, bs], tri_b[:], start=False, stop=True))
                    kb.op("pe", fns, reads=[b_p, b_trib, b_ntrib], writes=[pb[cr_], pb[ci_]])

                def st_xprod(ui, stt):
                    j, ct = units[ui]
                    br, bi_, cr_, ci_ = bu_banks(ui)
                    cp, b_cp = cp_ring.next()
                    s0, s1 = ct * 4, ct * 4 + 4
                    v4 = lambda ap: ap.rearrange("p r (b j) -> p r b j", b=4)
                    cs2 = psum[:, cr_ * 512:(cr_ + 2) * 512].rearrange("p (r b j) -> p r b j", r=2, b=4)
                    TT([pb[cr_], pb[ci_], b_s5c], [b_cp], v4(cp[:]), cs2, s5c[:, :, s0:s1].unsqueeze(3).to_broadcast([128, 2, 4, 128]), ADD)
                    F2ct = F2[:, :, s0:s1, 0:128]
                    qq4, b_q = q_ring.next()
                    TT([b_cp, b_Fre], [b_q], v4(qq4[:, 0:2, :]), v3(cp[:, 0, :]).unsqueeze(1).to_broadcast([128, 2, 4, 128]), F2ct, MUL)
                    TT([b_cp, b_Fre], [b_q], v4(qq4[:, 2:4, :]), v3(cp[:, 1, :]).unsqueeze(1).to_broadcast([128, 2, 4, 128]), F2ct, MUL)
                    tn, b_tn = tn_ring.next()
                    cr = v3(cp[:, 0, :])[:, :, 127]
                    ci = v3(cp[:, 1, :])[:, :, 127]
                    Gr = F_re[:, s0:s1, 128]
                    Gi = F_im[:, s0:s1, 128]
                    TT([b_cp, b_Fre], [b_tn], tn[:, 0, :], cr, Gr, MUL)
                    TT([b_cp, b_Fim], [b_tn], tn[:, 1, :], ci, Gi, MUL)
                    TT([b_cp, b_Fre], [b_tn], tn[:, 2, :], ci, Gr, MUL)
                    TT([b_cp, b_Fim], [b_tn], tn[:, 3, :], cr, Gi, MUL)
                    TT([b_tn], [b_s5ca], s5ca[:, s0:s1], tn[:, 0, :], tn[:, 1, :], SUB)
                    TT([b_tn], [b_s5cb], s5cb[:, s0:s1], tn[:, 2, :], tn[:, 3, :], ADD)
                    fns = []
                    for blk in range(4):
                        bs = slice(blk * 128, (blk + 1) * 128)
                        o_re = ((ct * 4 + blk) * 2 + 0) * 128
                        o_im = ((ct * 4 + blk) * 2 + 1) * 128
                        ops = ((cmat, o_re, 0), (ncmat, o_re, 3), (ncmat, o_im, 2), (ncmat, o_im, 1))
                        for oi, (cm, o, qi) in enumerate(ops):
                            fns.append(MM(bank(cr_)[:, 0:128], cm[:, o:o + 128], qq4[:, qi, bs], start=(blk == 0 and oi == 0), stop=(blk == 3 and oi == 3)))
                    kb.op("pe", fns, reads=[b_cmat, b_ncmat, b_q], writes=[pb[cr_]])

                def st_out(ui, stt):
                    j, ct = units[ui]
                    br, bi_, cr_, ci_ = bu_banks(ui)
                    ul = u_all[:, ct, j * 128:(j + 1) * 128]
                    ty, b_ty = ty_ring.next()
                    yc, b_yc = yc_ring.next()
                    STT([b_u, pb[cr_], b_colp], [b_ty], ty[:], ul, col("ssmd", l * 8 + ct), bank(cr_)[:, 0:128], MUL, ADD)
                    A([b_ty], [b_yc], yc[:], ty[:], AF.Gelu_apprx_tanh)
                    kb.dma("sp", DMA(ycT[ct * 128:(ct + 1) * 128, j * 128:(j + 1) * 128], yc[:]), reads=[b_yc], owner=b_yc)

                nU = len(units)
                emit_bu(0)
                emit_bu(1)
                for pi in range(0, nU, 2):
                    a, b = pi, pi + 1
                    sa, sb_ = {}, {}
                    st_eprod(a, sa)
                    st_eprod(b, sb_)
                    st_xprod(a, sa)
                    if a + 2 < nU:
                        emit_bu(a + 2)
                    st_xprod(b, sb_)
                    if b + 2 < nU:
                        emit_bu(b + 2)
                    st_out(a, sa)
                    st_out(b, sb_)
                ph.close()


            if "p3" not in skip:
                ph = Phase(kb)
                ys = []
                for nm, src in (("ya", yaT), ("yb", ybT), ("yc", ycT)):
                    t = None
                    parts = []
                    srcv_ = src.rearrange("(c p) t -> p c t", p=128)
                    for tt in range(nTT):
                        tl, b = ph.tile(nm, [128, 8, 512], BF16)
                        kb.dma("sp", DMA(tl[:], srcv_[:, :, tt * 512:(tt + 1) * 512]), writes=[b], owner=b)
                        parts.append((tl, b))
                    ys.append(parts)
                ys.append(ys[2])
                wb_ring = ph.ring("wb", 3, [128, 4, 8, 128], BF16)
                g_ring = ph.ring("gg", 2, [128, 3, 512], BF16)
                ms_ring = ph.ring("ms", 2, [128, ST], BF16)
                tq_ring = ph.ring("tq", 2, [128, 4, 512], F32)
                gT_v = gT.rearrange("(b f p) t -> f p b t", b=3, f=16)
                it = 0
                for f in range(16):
                    wb, b_wb = wb_ring.next()
                    kb.dma("pool", [DMA(wb[:, q, :, :], wbr_t[l, f, :, q, :, :]) for q in range(4)], writes=[b_wb], owner=b_wb)
                    ms, b_ms = ms_ring.next()
                    for tt in range(nTT):
                        ts0, ts1 = tt * 512, (tt + 1) * 512
                        g3, b_g3 = g_ring.next()
                        kb.dma("sp", DMA(g3[:], gT_v[f][:, :, ts0:ts1]), writes=[b_g3], owner=b_g3)
                        b0 = (it % 2) * 4
                        it += 1
                        for q in range(4):
                            yt, b_yt = ys[q][tt]
                            kb.op("pe", [MM(bank(b0 + q), wb[:, q, k, :], yt[:, k, :], start=(k == 0), stop=(k == 7)) for k in range(8)],
                                  reads=[b_wb, b_yt], writes=[pb[b0 + q]])
                        tq, b_tq = tq_ring.next()
                        A([pb[b0 + 3]], [b_tq], tq[:, 3, :], bank(b0 + 3), AF.Sigmoid)
                        TT([pb[b0], b_g3], [b_tq], tq[:, 0, :], bank(b0 + 0), g3[:, 0, :], ALU.mult)
                        TT([pb[b0 + 1], b_g3], [b_tq], tq[:, 1, :], bank(b0 + 1), g3[:, 1, :], ALU.mult)
                        TT([pb[b0 + 2], b_tq], [b_tq], tq[:, 2, :], bank(b0 + 2), tq[:, 3, :], ALU.mult)
                        TT([b_tq, b_g3], [b_tq], tq[:, 2, :], tq[:, 2, :], g3[:, 2, :], ALU.mult)
                        TT([b_tq], [b_tq], tq[:, 0, :], tq[:, 0, :], tq[:, 1, :], ALU.add)
                        TT([b_tq], [b_ms], ms[:, ts0:ts1], tq[:, 0, :], tq[:, 2, :], ALU.add)
                    kb.dma("sp", DMA(mT[f * 128:(f + 1) * 128, :], ms[:]), reads=[b_ms], owner=b_ms)
                ph.close()
                proj_residual(l, t0, mT, 16, wout_t[l], nTT)

            if "p4" not in skip:
                ph = Phase(kb)
                hT, b_hT = ph.tile("hT", [128, NK, ST], BF16)
                xin_ring = ph.ring("xin", 2, [128, NK, 512], F32)
                sq_ring = ph.ring("sq", 3, [128, 512], BF16)
                rs_ring = ph.ring("rs", 2, [128, 512], F32)
                rms_norm_to(hT, b_hT, "gffn", l, t0, xin_ring, sq_ring, rs_ring, (6, 7))
                wg_ring = ph.ring("wg", 3, [128, 2, NK, 128], BF16)
                as_ring = ph.ring("as", 2, [128, ST], BF16)
                sl_ring = ph.ring("sl", 2, [128, 512], F32)
                it = 0
                for jj in range(NJ):
                    wg, b_wg = wg_ring.next()
                    kb.dma("pool", [DMA(wg[:, q, :, :], wgu_t[l, jj, :, q, :, :]) for q in range(2)], writes=[b_wg], owner=b_wg)
                    ast, b_ast = as_ring.next()
                    for tt in range(nTT):
                        ts0, ts1 = tt * 512, (tt + 1) * 512
                        bg = (it % 3) * 2
                        it += 1
                        for q in range(2):
                            kb.op("pe", [MM(bank(bg + q), wg[:, q, k, :], hT[:, k, ts0:ts1], start=(k == 0), stop=(k == NK - 1)) for k in range(NK)],
                                  reads=[b_wg, b_hT], writes=[pb[bg + q]])
                        sl, b_sl = sl_ring.next()
                        A([pb[bg]], [b_sl], sl[:], bank(bg), AF.Silu)
                        TT([pb[bg + 1], b_sl], [b_ast], ast[:, ts0:ts1], bank(bg + 1), sl[:], ALU.mult)
                    kb.dma("sp", DMA(actT[jj * 128:(jj + 1) * 128, :], ast[:]), reads=[b_ast], owner=b_ast)
                ph.close()
                proj_residual(l, t0, actT, NJ, wdn_t[l], min(2, nTT))

    ph = Phase(kb)
    xin_ring = ph.ring("xin", 2, [128, NK, 512], F32)
    sq_ring = ph.ring("sq", 3, [128, 512], BF16)
    rs_ring = ph.ring("rs", 2, [128, 512], F32)
    outv = outT.rearrange("(k p) t -> p k t", p=128)
    for tt in range(S // 512):
        tg = tt * 512
        xin, b_xin = xin_ring.next()
        rs, b_rs = rms_norm_tile(xin, b_xin, tg, sq_ring, rs_ring, 6 + (tt % 2))
        for k in range(NK):
            STT([b_xin, b_rs, b_colp], [b_xin], xin[:, k, :], xin[:, k, :], col("gfin", k), rs[:], ALU.mult, ALU.mult)
        kb.dma("sp", DMA(outv[:, :, tg:tg + 512], xin[:]), reads=[b_xin], owner=b_xin)
    ph.close()
    kb.finalize()
    return nc


def prep_weights(inp, L):
    f = np.float32
    out = {}
    w_in = inp["w_in"][:L]
    out["w_in_t"] = np.ascontiguousarray(w_in.reshape(L, 16, 128, 96, 128).transpose(0, 3, 2, 1, 4))
    wb = np.concatenate([inp["w_branch"][:L], inp["ssm_w_glu"][:L][:, None]], axis=1)
    out["wbr_t"] = np.ascontiguousarray(wb.reshape(L, 4, 8, 128, 16, 128).transpose(0, 4, 3, 1, 2, 5))
    out["wout_t"] = np.ascontiguousarray(inp["w_out"][:L].reshape(L, 16, 128, 16, 128).transpose(0, 3, 2, 1, 4))
    wgu = np.stack([inp["w_ffn_gate"][:L], inp["w_ffn_up"][:L]], axis=1)
    out["wgu_t"] = np.ascontiguousarray(wgu.reshape(L, 2, 16, 128, NJ, 128).transpose(0, 4, 3, 1, 2, 5))
    out["wdn_t"] = np.ascontiguousarray(inp["w_ffn_down"][:L].reshape(L, NJ, 128, 16, 128).transpose(0, 3, 2, 1, 4))
    COFF, NCOL = col_layout(L)
    colp = np.zeros((128, NCOL), f)

    def put(name, arr):
        colp[:, COFF[name]:COFF[name] + arr.shape[1]] = arr
    put("gmix", inp["norm_mix_g"][:L].reshape(L, 16, 128).transpose(2, 0, 1).reshape(128, -1))
    put("gffn", inp["norm_ffn_g"][:L].reshape(L, 16, 128).transpose(2, 0, 1).reshape(128, -1))
    put("gfin", inp["norm_final_g"].reshape(16, 128).T)
    put("gbias", inp["gate_bias"][:L].reshape(L, 3, 16, 128).transpose(3, 0, 1, 2).reshape(128, -1))
    put("convw", inp["lru_conv_w"][:L].reshape(L, 4, 8, 128).transpose(3, 0, 2, 1).reshape(128, -1))
    for nm, key in (("convb", "lru_conv_b"), ("ba", "lru_ba"), ("bx", "lru_bx"), ("lam", "lru_lambda"), ("ssmd", "ssm_d")):
        put(nm, inp[key][:L].reshape(L, 8, 128).transpose(2, 0, 1).reshape(128, -1))
    for nm, arr in (("acre", inp["ssm_a_re"][:L]), ("acim", inp["ssm_a_im"][:L]),
                    ("lsc", np.repeat(inp["ssm_log_step"][:L][:, :, None], 64, axis=2))):
        put(nm, arr.reshape(L, 32, 2, 64).transpose(2, 3, 0, 1).reshape(128, -1))
    colp[:, COFF["iota"]] = np.arange(128, dtype=f)
    out["colp"] = colp
    rowp = np.stack([inp["ssm_a_re"][:L].reshape(L, 4096), inp["ssm_a_im"][:L].reshape(L, 4096),
                     np.repeat(inp["ssm_log_step"][:L][:, :, None], 64, axis=2).reshape(L, 4096)], axis=1)
    out["rowp"] = np.ascontiguousarray(rowp.astype(f))
    lruw = np.zeros((L, 128, 8, 2, 128), f)
    for wi, key in enumerate(("lru_wa", "lru_wx")):
        w = inp[key][:L].reshape(L, 8, 2, 64, 64)
        for nl in range(2):
            lruw[:, nl * 64:(nl + 1) * 64, :, wi, nl * 64:(nl + 1) * 64] = w[:, :, nl].transpose(0, 2, 1, 3)
    out["lruw"] = lruw
    kk = np.arange(128)[:, None, None]
    kt = np.arange(5)[None, :, None]
    qq = np.arange(128)[None, None, :]
    dist = (4 - kt) * 128 + qq - kk
    rel = np.clip(dist, -128, 128) + 128
    cdiff = 8 - 2 * kt + qq // 64 - kk // 64
    valid = (cdiff >= 0) & (cdiff <= 8)
    ab = inp["attn_rel_bias"][:L][:, :, rel]
    ab = np.where(valid[None, None], ab, f(-30000.0)).astype(f)
    out["abias"] = np.ascontiguousarray(ab.reshape(L, 8, 128, 640))
    bbd = np.zeros((L, 128, 2, 8, 8, 64), f)
    for ri, key in enumerate(("ssm_b_re", "ssm_b_im")):
        B = inp[key][:L].reshape(L, 8, 8, 64, 16)
        for gl in range(8):
            bbd[:, gl * 16:(gl + 1) * 16, ri, :, gl, :] = B[:, :, gl].transpose(0, 3, 1, 2)
    out["bbd"] = np.ascontiguousarray(bbd.reshape(L, 128, 2, 4096))
    ctd = np.zeros((L, 128, 8, 4, 2, 128), f)
    for ri, key in enumerate(("ssm_c_re", "ssm_c_im")):
        C = inp[key][:L].reshape(L, 8, 4, 2, 16, 64)
        for blk in range(4):
            for gl2 in range(2):
                c0 = (2 * blk + gl2) * 16
                ctd[:, gl2 * 64:(gl2 + 1) * 64, :, blk, ri, c0:c0 + 16] = C[:, :, blk, gl2].transpose(0, 3, 1, 2)
    out["ctd"] = np.ascontiguousarray(ctd.reshape(L, 128, 8192))
    cst = np.zeros((128, 386), f)
    cst[:, 0:128] = np.triu(np.ones((128, 128), f))
    cst[:, 128:256] = 1.0
    cst[:, 256:385] = np.arange(129, dtype=f)[None, :]
    cst[:, 385] = 1e-6
    out["cst"] = cst
    out["epsd"] = np.full((128, 1), 1e-6, f)
    return out


_CACHE = {}


def run_model(inputs, L, S_core, ST, n_cores, dbg=False, skip=(), spread=False):
    x = np.asarray(inputs["x"], np.float32)
    B, S, _ = x.shape
    assert S == S_core and B <= n_cores
    key = (L, S_core, ST, dbg, tuple(skip))
    if key not in _CACHE:
        _CACHE[key] = build_program(L, S_core, ST, dbg=dbg, skip=skip)
    nc = _CACHE[key]
    wts = prep_weights({k: np.asarray(v, np.float32) for k, v in inputs.items() if k != "x"}, L)
    if spread:
        real = {0: 0, 1: 1, 4: 2, 5: 3}
        zw = dict(wts)
        for k in ("w_in_t", "wbr_t", "wout_t", "wgu_t", "wdn_t"):
            zw[k] = np.zeros_like(wts[k])
        zx = np.zeros((D, S), np.float32)
        in_maps = []
        for c in range(8):
            if c in real and real[c] < B:
                m = dict(wts)
                m["xT"] = np.ascontiguousarray(x[real[c]].T)
            else:
                m = dict(zw)
                m["xT"] = zx
            in_maps.append(m)
        res = run_bass_kernel_spmd(nc, in_maps, core_ids=list(range(8)))
        slots = [c for c in range(8) if c in real and real[c] < B]
        out = np.stack([np.ascontiguousarray(res.results[c]["outT"].T) for c in sorted(slots, key=lambda c: real[c])], axis=0)
        return out.astype(np.float32), res
    in_maps = []
    for c in range(n_cores):
        b = c % B
        m = dict(wts)
        m["xT"] = np.ascontiguousarray(x[b].T)
        in_maps.append(m)
    res = run_bass_kernel_spmd(nc, in_maps, core_ids=list(range(n_cores)))
    out = np.stack([np.ascontiguousarray(res.results[b]["outT"].T) for b in range(B)], axis=0)
    return out.astype(np.float32), res


def kernel(**inputs):
    out, _ = run_model(inputs, 4, 4096, 2048, 8, spread=True)
    return out
```

```python
import math
from contextlib import ExitStack
import numpy as np
import concourse.bass as bass
import concourse.mybir as mybir
from concourse.bass_utils import run_bass_kernel_spmd

F32 = mybir.dt.float32
BF16 = mybir.dt.bfloat16
AF = mybir.ActivationFunctionType
ALU = mybir.AluOpType

D = 2048
NK = 16
MIXW = 1024
FH = 5632
NJ = 44
TWO_PI = float(2 * math.pi)
MAGIC = 12582912.0


class Buf:
    __slots__ = ("name", "w", "r", "dsem")

    def __init__(self, name):
        self.name = name
        self.w = None
        self.r = {}
        self.dsem = None


class Eng:
    def __init__(self, name, sem):
        self.name = name
        self.sem = sem
        self.cnt = 0
        self.waited = {}
        self.prog = []


class KB:
    def __init__(self, nc):
        self.nc = nc
        self.stack = ExitStack()
        self.sems = {}
        self.semcnt = {}
        self.free_dsems = []
        self.dirty = {}
        self.eng = {}
        for name in ("pe", "act", "dve", "pool", "sp"):
            self.eng[name] = Eng(name, self.newsem("e_" + name))
        self.uid = 0

    def newsem(self, name):
        h = self.stack.enter_context(self.nc.semaphore(name))
        key = len(self.sems)
        self.sems[key] = h
        self.semcnt[key] = 0
        return key

    def buf(self, name="b"):
        self.uid += 1
        return Buf(f"{name}_{self.uid}")

    def _deps(self, reads, writes):
        deps = {}
        for b in reads:
            if b.w is not None and deps.get(b.w[0], 0) < b.w[1]:
                deps[b.w[0]] = b.w[1]
        for b in writes:
            if b.w is not None and deps.get(b.w[0], 0) < b.w[1]:
                deps[b.w[0]] = b.w[1]
            for k, v in b.r.items():
                if deps.get(k, 0) < v:
                    deps[k] = v
        return deps

    def _emit_waits(self, e, deps, skip=None):
        for k, v in deps.items():
            if k == skip:
                continue
            if e.waited.get(k, 0) < v:
                e.prog.append(("w", k, v))
                e.waited[k] = v

    def _update(self, tok, reads, writes):
        k, v = tok
        for b in reads:
            if b.r.get(k, 0) < v:
                b.r[k] = v
        for b in writes:
            b.w = tok
            b.r = {}

    def op(self, en, fns, reads=(), writes=()):
        e = self.eng[en]
        if callable(fns):
            fns = [fns]
        deps = self._deps(reads, writes)
        self._emit_waits(e, deps, skip=e.sem if en == "pe" else None)
        for f in fns[:-1]:
            e.prog.append(("i", f, None, 0))
        e.cnt += 1
        e.prog.append(("i", fns[-1], e.sem, 1))
        tok = (e.sem, e.cnt)
        self._update(tok, reads, writes)
        return tok

    def dma(self, en, fns, reads=(), writes=(), owner=None):
        e = self.eng[en]
        if callable(fns):
            fns = [fns]
        if owner.dsem is None:
            owner.dsem = self.free_dsems.pop() if self.free_dsems else self.newsem("d%d" % len(self.sems))
        k = owner.dsem
        deps = self._deps(reads, writes)
        self._emit_waits(e, deps)
        for f in fns:
            self.semcnt[k] += 16
            e.prog.append(("i", f, k, 16))
        tok = (k, self.semcnt[k])
        self.dirty[k] = self.semcnt[k]
        self._update(tok, reads, writes)
        return tok

    def barrier(self):
        toks = dict(self.dirty)
        for e in self.eng.values():
            if e.cnt > 0:
                toks[e.sem] = e.cnt
        for e in self.eng.values():
            self._emit_waits(e, toks)
        self.dirty = {}

    def release(self, bufs):
        for b in bufs:
            if b.dsem is not None:
                self.free_dsems.append(b.dsem)
                b.dsem = None

    def finalize(self):
        nc = self.nc
        self.barrier()
        engs, sems = self.eng, self.sems

        def replay(e, h):
            for it in e.prog:
                if it[0] == "w":
                    h.wait_ge(sems[it[1]], it[2])
                else:
                    inst = it[1](h)
                    if it[2] is not None:
                        inst.then_inc(sems[it[2]], it[3])

        with nc.Block() as block:
            @block.tensor
            def _(h):
                replay(engs["pe"], h)

            @block.scalar
            def _(h):
                replay(engs["act"], h)

            @block.vector
            def _(h):
                replay(engs["dve"], h)

            @block.gpsimd
            def _(h):
                replay(engs["pool"], h)

            @block.sync
            def _(h):
                replay(engs["sp"], h)
        self.stack.close()


class Phase:
    def __init__(self, kb):
        self.kb = kb
        self.st = ExitStack()
        self.bufs = []

    def tile(self, name, shape, dtype):
        kb = self.kb
        kb.uid += 1
        t = self.st.enter_context(kb.nc.sbuf_tensor(f"{name}_{kb.uid}", list(shape), dtype))
        b = kb.buf(name)
        self.bufs.append(b)
        return t, b

    def ring(self, name, n, shape, dtype):
        return Ring([self.tile(name, shape, dtype) for _ in range(n)])

    def close(self):
        self.kb.barrier()
        self.kb.release(self.bufs)
        self.st.close()


class Ring:
    def __init__(self, items):
        self.items = items
        self.i = 0

    def next(self):
        it = self.items[self.i % len(self.items)]
        self.i += 1
        return it


def col_layout(L):
    segs = [("gmix", L * 16), ("gffn", L * 16), ("gfin", 16), ("gbias", L * 48), ("convw", L * 32),
            ("convb", L * 8), ("ba", L * 8), ("bx", L * 8), ("lam", L * 8), ("ssmd", L * 8),
            ("acre", L * 32), ("acim", L * 32), ("lsc", L * 32), ("iota", 1)]
    off, o = {}, 0
    for n, w in segs:
        off[n] = o
        o += w
    return off, o


def build_program(L, S, ST, dbg=False, has_prev=False, skip=()):
    nc = bass.Bass("TRN2", target_bir_lowering=False)
    kb = KB(nc)
    gst = kb.stack
    nST = S // ST
    nTT = ST // 512
    COFF, NCOL = col_layout(L)
    okind = "ExternalOutput" if dbg else "Internal"

    def din(name, shape, dt=F32):
        return nc.dram_tensor(name, list(shape), dt, kind="ExternalInput").ap()

    def dscr(name, shape, dt):
        return nc.dram_tensor(name, list(shape), dt, kind=okind).ap()

    xT = din("xT", [D, S])
    w_in_t = din("w_in_t", [L, 96, 128, 16, 128])
    wbr_t = din("wbr_t", [L, 16, 128, 4, 8, 128])
    wout_t = din("wout_t", [L, 16, 128, 16, 128])
    wgu_t = din("wgu_t", [L, NJ, 128, 2, 16, 128])
    wdn_t = din("wdn_t", [L, 16, 128, NJ, 128])
    colp_d = din("colp", [128, NCOL])
    rowp_d = din("rowp", [L, 3, 4096])
    lruw_d = din("lruw", [L, 128, 8, 2, 128])
    abias_d = din("abias", [L, 8, 128, 640])
    bbd_d = din("bbd", [L, 128, 2, 4096])
    ct_d = din("ctd", [L, 128, 8192])
    cst_d = din("cst", [128, 386])
    eps_d = din("epsd", [128, 1])
    outT = nc.dram_tensor("outT", [D, S], F32, kind="ExternalOutput").ap()

    xr = dscr("xr", [D, S], F32)
    lxT = dscr("lxT", [MIXW, S], BF16)
    lgT = dscr("lgT", [MIXW, ST], BF16)
    qT = dscr("qT", [MIXW, ST], BF16)
    kT = dscr("kT", [MIXW, S], BF16)
    vS = dscr("vS", [S, MIXW], BF16)
    uT = dscr("uT", [MIXW, ST], BF16)
    gT = dscr("gT", [3 * D, ST], BF16)
    yaT = dscr("yaT", [MIXW, ST], BF16)
    ybT = dscr("ybT", [MIXW, ST], BF16)
    ycT = dscr("ycT", [MIXW, ST], BF16)
    mT = dscr("mT", [D, ST], BF16)
    actT = dscr("actT", [FH, ST], BF16)
    tabE = nc.dram_tensor("tabE", [2, 128, 4096], F32, kind="Internal").ap()
    tabF = nc.dram_tensor("tabF", [2, 128, 32 * 129], F32, kind="Internal").ap()
    tabB = nc.dram_tensor("tabB", [2, 128, 4096], BF16, kind="Internal").ap()

    def gtile(name, shape, dt):
        t = gst.enter_context(nc.sbuf_tensor(name, list(shape), dt))
        return t, kb.buf(name)

    colp, b_colp = gtile("colp_s", [128, NCOL], F32)
    nsp8, b_nsp8 = gtile("nsp8", [128, L * 8], F32)
    ones_f, b_onesf = gtile("ones_f", [128, 128], F32)
    ones_b, b_onesb = gtile("ones_b", [128, 128], BF16)
    tri_b, b_trib = gtile("tri_b", [128, 128], BF16)
    ntri_b, b_ntrib = gtile("ntri_b", [128, 128], BF16)
    iorow, b_iorow = gtile("iorow", [128, 129], F32)
    lru_h, b_lruh = gtile("lru_h", [128, 8], F32)
    s5c, b_s5c = gtile("s5c", [128, 2, 32], F32)
    s5ca, s5cb = s5c[:, 0, :], s5c[:, 1, :]
    b_s5ca = b_s5cb = b_s5c
    epsc, b_epsc = gtile("epsc", [128, 1], F32)
    psum = gst.enter_context(nc.psum_tensor("psum", [128, 4096], F32))
    pb = [kb.buf(f"ps{i}") for i in range(8)]

    def bank(i):
        return psum[:, i * 512:(i + 1) * 512]

    def col(name, idx, n=1):
        o = COFF[name] + idx
        return colp[:, o:o + n]

    def I(name, **kw):
        return lambda h: getattr(h, name)(**kw)

    def MM(out, lhsT, rhs, start=True, stop=True):
        return I("matmul", out=out, lhsT=lhsT, rhs=rhs, start=start, stop=stop)

    def DMA(out, in_):
        return I("dma_start", out=out, in_=in_)

    def V(name, r, w, **kw):
        return kb.op("dve", I(name, **kw), reads=r, writes=w)

    def A(r, w, out, in_, func, **kw):
        return kb.op("act", I("activation", out=out, in_=in_, func=func, **kw), reads=r, writes=w)

    def TT(r, w, out, in0, in1, op):
        return V("tensor_tensor", r, w, out=out, in0=in0, in1=in1, op=op)

    def TS(r, w, out, in0, s1, s2, op0, op1=None):
        if op1 is None:
            return V("tensor_scalar", r, w, out=out, in0=in0, scalar1=s1, scalar2=None, op0=op0)
        return V("tensor_scalar", r, w, out=out, in0=in0, scalar1=s1, scalar2=s2, op0=op0, op1=op1)

    def STT(r, w, out, in0, scalar, in1, op0, op1):
        return V("scalar_tensor_tensor", r, w, out=out, in0=in0, scalar=scalar, in1=in1, op0=op0, op1=op1)

    kb.dma("sp", DMA(colp[:], colp_d), writes=[b_colp], owner=b_colp)
    kb.dma("sp", DMA(ones_f[:], cst_d[:, 128:256]), writes=[b_onesf], owner=b_onesf)
    kb.dma("sp", DMA(iorow[:], cst_d[:, 256:385]), writes=[b_iorow], owner=b_iorow)
    kb.dma("pool", DMA(tri_b[:], cst_d[:, 0:128]), writes=[b_trib], owner=b_trib)
    kb.dma("pool", DMA(ones_b[:], cst_d[:, 128:256]), writes=[b_onesb], owner=b_onesb)
    TS([b_trib], [b_ntrib], ntri_b[:], tri_b[:], -1.0, None, ALU.mult)
    kb.dma("sp", DMA(epsc[:], eps_d), writes=[b_epsc], owner=b_epsc)
    A([b_colp], [b_nsp8], nsp8[:], col("lam", 0, L * 8), AF.Exp, scale=-1.0)
    A([b_nsp8], [b_nsp8], nsp8[:], nsp8[:], AF.Ln, bias=1.0, scale=1.0)
    TS([b_nsp8], [b_nsp8], nsp8[:], nsp8[:], -8.0, None, ALU.mult)
    b_x0 = kb.buf("x0")
    kb.dma("sp", [DMA(xr[c * 128:(c + 1) * 128, :], xT[c * 128:(c + 1) * 128, :]) for c in range(16)], writes=[b_x0], owner=b_x0)
    kb.barrier()

    xr_v = xr.rearrange("(k p) t -> p k t", p=128)

    def rms_norm_tile(xin, b_xin, tg, sq_ring, rs_ring, pbi):
        kb.dma("sp", [DMA(xin[:, 0:8, :], xr_v[:, 0:8, tg:tg + 512]), DMA(xin[:, 8:16, :], xr_v[:, 8:16, tg:tg + 512])], writes=[b_xin], owner=b_xin)
        for k in range(NK):
            sq, b_sq = sq_ring.next()
            A([b_xin], [b_sq], sq[:], xin[:, k, :], AF.Square)
            kb.op("pe", MM(bank(pbi), ones_b[:], sq[:], start=(k == 0), stop=(k == NK - 1)), reads=[b_sq, b_onesb], writes=[pb[pbi]])
        rs, b_rs = rs_ring.next()
        A([pb[pbi], b_epsc], [b_rs], rs[:], bank(pbi), AF.Sqrt, scale=1.0 / D, bias=epsc[:])
        V("reciprocal", [b_rs], [b_rs], out=rs[:], in_=rs[:])
        return rs, b_rs

    def rms_norm_to(hT, b_hT, gname, l, t0, xin_ring, sq_ring, rs_ring, pbis):
        for tt in range(nTT):
            xin, b_xin = xin_ring.next()
            rs, b_rs = rms_norm_tile(xin, b_xin, t0 + tt * 512, sq_ring, rs_ring, pbis[tt % len(pbis)])
            for k in range(NK):
                STT([b_xin, b_rs, b_colp], [b_hT], hT[:, k, tt * 512:(tt + 1) * 512], xin[:, k, :], col(gname, l * 16 + k), rs[:], ALU.mult, ALU.mult)

    evac_flip = [0]

    def evac_copy(out_ap, in_ap, reads, writes):
        evac_flip[0] ^= 1
        if evac_flip[0]:
            A(reads, writes, out_ap, in_ap, AF.Copy)
        else:
            V("tensor_copy", reads, writes, out=out_ap, in_=in_ap)

    def proj_residual(l, t0, src_dram, nk, w_dram_l, chunk_tt):
        srcv = src_dram.rearrange("(k p) t -> p k t", p=128)
        splits = [(a, min(a + 16, nk)) for a in range(0, nk, 16)]
        for c0 in range(0, nTT, chunk_tt):
            ph = Phase(kb)
            sms = [ph.tile("sm", [128, nk, 512], BF16) for _ in range(chunk_tt)]
            for tt in range(chunk_tt):
                sm, b_sm = sms[tt]
                cs0 = (c0 + tt) * 512
                kb.dma("sp", [DMA(sm[:, a:b, :], srcv[:, a:b, cs0:cs0 + 512]) for a, b in splits], writes=[b_sm], owner=b_sm)
            wo_ring = ph.ring("wo", 3, [128, nk, 128], BF16)
            xs_ring = ph.ring("xs", 3, [128, 512], F32)
            it = 0
            for f in range(16):
                wo, b_wo = wo_ring.next()
                kb.dma("pool", [DMA(wo[:, a:b, :], w_dram_l[f, :, a:b, :]) for a, b in splits], writes=[b_wo], owner=b_wo)
                for tt in range(chunk_tt):
                    sm, b_sm = sms[tt]
                    bi = it % 6
                    it += 1
                    tg = t0 + (c0 + tt) * 512
                    xs, b_xs = xs_ring.next()
                    kb.dma("sp", DMA(xs[:], xr[f * 128:(f + 1) * 128, tg:tg + 512]), writes=[b_xs], owner=b_xs)
                    kb.op("pe", [MM(bank(bi), wo[:, k, :], sm[:, k, :], start=(k == 0), stop=(k == nk - 1)) for k in range(nk)],
                          reads=[b_wo, b_sm], writes=[pb[bi]])
                    TT([pb[bi], b_xs], [b_xs], xs[:], bank(bi), xs[:], ALU.add)
                    kb.dma("sp", DMA(xr[f * 128:(f + 1) * 128, tg:tg + 512], xs[:]), reads=[b_xs], owner=b_xs)
            ph.close()

    def v3(ap):
        return ap.rearrange("p (b j) -> p b j", b=4)

    for l in range(L):
        for s in range(nST):
            t0 = s * ST
            first = (s == 0) and not has_prev
            if "p1" not in skip:
                phO = Phase(kb)
                hT, b_hT = phO.tile("hT", [128, NK, ST], BF16)
                ph = Phase(kb)
                xin_ring = ph.ring("xin", 2, [128, NK, 512], F32)
                sq_ring = ph.ring("sq", 3, [128, 512], BF16)
                rs_ring = ph.ring("rs", 2, [128, 512], F32)
                wt_ring = ph.ring("wt", 3, [128, NK, 128], BF16)
                stg_ring = ph.ring("stg", 3, [128, ST], BF16)
                wv_ring = ph.ring("wv", 1, [128, NK, 512], BF16)
                sv_ring = ph.ring("sv", 3, [128, 512], BF16)
                rms_norm_to(hT, b_hT, "gmix", l, t0, xin_ring, sq_ring, rs_ring, (6, 7))
                pbr = 0
                for m in list(range(0, 32)) + list(range(40, 48)):
                    wt, b_wt = wt_ring.next()
                    kb.dma("pool", DMA(wt[:], w_in_t[l, m]), writes=[b_wt], owner=b_wt)
                    stg, b_stg = stg_ring.next()
                    for tt in range(nTT):
                        bi = pbr % 6
                        pbr += 1
                        kb.op("pe", [MM(bank(bi), wt[:, k, :], hT[:, k, tt * 512:(tt + 1) * 512], start=(k == 0), stop=(k == NK - 1)) for k in range(NK)],
                              reads=[b_wt, b_hT], writes=[pb[bi]])
                        o_ap = stg[:, tt * 512:(tt + 1) * 512]
                        if m < 48:
                            evac_copy(o_ap, bank(bi), [pb[bi]], [b_stg])
                        else:
                            A([pb[bi], b_colp], [b_stg], o_ap, bank(bi), AF.Sigmoid, bias=col("gbias", l * 48 + (m - 48)), scale=1.0)
                    if m < 8:
                        dst = lxT[m * 128:(m + 1) * 128, t0:t0 + ST]
                    elif m < 16:
                        dst = lgT[(m - 8) * 128:(m - 7) * 128, :]
                    elif m < 24:
                        dst = qT[(m - 16) * 128:(m - 15) * 128, :]
                    elif m < 32:
                        dst = kT[(m - 24) * 128:(m - 23) * 128, t0:t0 + ST]
                    elif m < 48:
                        dst = uT[(m - 40) * 128:(m - 39) * 128, :]
                    else:
                        dst = gT[(m - 48) * 128:(m - 47) * 128, :]
                    kb.dma("sp", DMA(dst, stg[:]), reads=[b_stg], owner=b_stg)
                for cb in range(2):
                    wv, b_wv = wv_ring.next()
                    kb.dma("pool", [DMA(wv[:, :, mm * 128:(mm + 1) * 128], w_in_t[l, 32 + 4 * cb + mm]) for mm in range(4)], writes=[b_wv], owner=b_wv)
                    for j in range(ST // 128):
                        bi = pbr % 6
                        pbr += 1
                        kb.op("pe", [MM(bank(bi), hT[:, k, j * 128:(j + 1) * 128], wv[:, k, :], start=(k == 0), stop=(k == NK - 1)) for k in range(NK)],
                              reads=[b_wv, b_hT], writes=[pb[bi]])
                        sv, b_sv = sv_ring.next()
                        evac_copy(sv[:], bank(bi), [pb[bi]], [b_sv])
                        kb.dma("sp", DMA(vS[t0 + j * 128:t0 + (j + 1) * 128, cb * 512:(cb + 1) * 512], sv[:]), reads=[b_sv], owner=b_sv)
                ph.close()

            if "p1" not in skip:
                ph = Phase(kb)
                gwt_ring = ph.ring("gwt", 3, [128, NK, 128], BF16)
                gst_ring = ph.ring("gst", 3, [128, ST], BF16)

                def gates_gen():
                    pend = None
                    g = 0
                    for m in range(48, 96):
                        wt, b_wt = gwt_ring.next()
                        kb.dma("pool", DMA(wt[:], w_in_t[l, m]), writes=[b_wt], owner=b_wt)
                        stg, b_stg = gst_ring.next()
                        for tt in range(nTT):
                            bi = 6 + (g % 2)
                            g += 1
                            kb.op("pe", [MM(bank(bi), wt[:, k, :], hT[:, k, tt * 512:(tt + 1) * 512], start=(k == 0), stop=(k == NK - 1)) for k in range(NK)],
                                  reads=[b_wt, b_hT], writes=[pb[bi]])
                            if pend is not None:
                                pend()
                            def fin(bi=bi, m=m, tt=tt, stg=stg, b_stg=b_stg):
                                A([pb[bi], b_colp], [b_stg], stg[:, tt * 512:(tt + 1) * 512], bank(bi), AF.Sigmoid, bias=col("gbias", l * 48 + (m - 48)), scale=1.0)
                                if tt == nTT - 1:
                                    kb.dma("sp", DMA(gT[(m - 48) * 128:(m - 47) * 128, :], stg[:]), reads=[b_stg], owner=b_stg)
                            pend = fin
                            yield
                    pend()

                ggen = gates_gen()

                def gstep(k):
                    for _ in range(k):
                        if next(ggen, "done") == "done":
                            return False
                    return True
            else:
                ph = Phase(kb)

                def gstep(k):
                    return False

            if "p2a" not in skip:
                bd, b_bd = ph.tile("bd", [128, 8, 2, 128], BF16)
                kb.dma("pool", DMA(bd[:], lruw_d[l]), writes=[b_bd], owner=b_bd)
                lx_ring = ph.ring("lx", 3, [128, 515], BF16)
                lg_ring = ph.ring("lg", 3, [128, 512], BF16)
                ya_ring = ph.ring("ya", 2, [128, 512], BF16)
                hs_ring = ph.ring("hs", 2, [128, 512], F32)
                xc_ring = ph.ring("xc", 2, [128, 512], F32)
                xcb_ring = ph.ring("xcb", 2, [128, 512], BF16)
                rr, b_rr = ph.tile("rr", [128, 512], F32)
                ii, b_ii = ph.tile("ii", [128, 512], F32)
                aa, b_aa = ph.tile("aa", [128, 512], F32)
                a2, b_a2 = ph.tile("a2", [128, 512], F32)
                gg, b_gg = ph.tile("gg", [128, 512], F32)
                if first:
                    V("memset", [], [b_lruh], ap=lru_h[:], constant=0.0)
                lunits = [(ct, tt) for ct in range(8) for tt in range(nTT)]
                lst = {}

                def lru_load(n):
                    ct, tt = lunits[n]
                    r0, r1 = ct * 128, (ct + 1) * 128
                    tg = t0 + tt * 512
                    lx, b_lx = lx_ring.next()
                    lg, b_lg = lg_ring.next()
                    if tg == 0 and not has_prev:
                        V("memset", [], [b_lx], ap=lx[:, 0:3], constant=0.0)
                        kb.dma("sp", DMA(lx[:, 3:515], lxT[r0:r1, 0:512]), writes=[b_lx], owner=b_lx)
                    else:
                        kb.dma("sp", DMA(lx[:], lxT[r0:r1, tg - 3:tg + 512]), writes=[b_lx], owner=b_lx)
                    kb.dma("sp", DMA(lg[:], lgT[r0:r1, tt * 512:(tt + 1) * 512]), writes=[b_lg], owner=b_lg)
                    lst[n] = dict(lx=lx, b_lx=b_lx, lg=lg, b_lg=b_lg)

                def lru_front(n):
                    ct, tt = lunits[n]
                    d = lst[n]
                    lx, b_lx = d["lx"], d["b_lx"]
                    xc, b_xc = xc_ring.next()
                    xcb, b_xcb = xcb_ring.next()
                    cwi = (l * 8 + ct) * 4
                    TS([b_lx, b_colp], [b_xc], xc[:], lx[:, 3:515], col("convw", cwi + 3), col("convb", l * 8 + ct), ALU.mult, ALU.add)
                    for k in range(3):
                        STT([b_lx, b_xc, b_colp], [b_xc], xc[:], lx[:, k:k + 512], col("convw", cwi + k), xc[:], ALU.mult, ALU.add)
                    V("tensor_copy", [b_xc], [b_xcb], out=xcb[:], in_=xc[:])
                    pbase = (n % 2) * 2
                    kb.op("pe", MM(bank(pbase), bd[:, ct, 0, :], xcb[:]), reads=[b_bd, b_xcb], writes=[pb[pbase]])
                    kb.op("pe", MM(bank(pbase + 1), bd[:, ct, 1, :], xcb[:]), reads=[b_bd, b_xcb], writes=[pb[pbase + 1]])
                    d.update(xc=xc, b_xc=b_xc, pbase=pbase)

                lru_load(0)
                if len(lunits) > 1:
                    lru_load(1)
                lru_front(0)
                for n in range(len(lunits)):
                    ct, tt = lunits[n]
                    r0, r1 = ct * 128, (ct + 1) * 128
                    d = lst[n]
                    xc, b_xc, pbase = d["xc"], d["b_xc"], d["pbase"]
                    lg, b_lg = d["lg"], d["b_lg"]
                    if n + 2 < len(lunits):
                        lru_load(n + 2)
                    A([pb[pbase], b_colp], [b_rr], rr[:], bank(pbase), AF.Sigmoid, bias=col("ba", l * 8 + ct), scale=1.0)
                    A([pb[pbase + 1], b_colp], [b_ii], ii[:], bank(pbase + 1), AF.Sigmoid, bias=col("bx", l * 8 + ct), scale=1.0)
                    A([b_rr, b_nsp8], [b_aa], aa[:], rr[:], AF.Exp, scale=nsp8[:, l * 8 + ct:l * 8 + ct + 1])
                    TT([b_ii, b_xc], [b_ii], ii[:], ii[:], xc[:], ALU.mult)
                    TT([b_aa], [b_a2], a2[:], aa[:], aa[:], ALU.mult)
                    A([b_a2], [b_a2], a2[:], a2[:], AF.Sqrt, scale=-1.0, bias=1.0)
                    gstep(1)
                    if n + 1 < len(lunits):
                        lru_front(n + 1)
                    gstep(1)
                    A([b_lg], [b_gg], gg[:], lg[:], AF.Gelu_apprx_tanh)
                    TT([b_ii, b_a2], [b_ii], ii[:], ii[:], a2[:], ALU.mult)
                    hs, b_hs = hs_ring.next()
                    V("tensor_tensor_scan", [b_aa, b_ii, b_lruh], [b_hs], out=hs[:], data0=aa[:], data1=ii[:], initial=lru_h[:, ct:ct + 1], op0=ALU.mult, op1=ALU.add)
                    V("tensor_copy", [b_hs], [b_lruh], out=lru_h[:, ct:ct + 1], in_=hs[:, 511:512])
                    ya, b_ya = ya_ring.next()
                    TT([b_hs, b_gg], [b_ya], ya[:], hs[:], gg[:], ALU.mult)
                    kb.dma("sp", DMA(yaT[r0:r1, tt * 512:(tt + 1) * 512], ya[:]), reads=[b_ya], owner=b_ya)
                    del lst[n]
                    gstep(2)

            if "p2b" not in skip:
                NKT = (ST + 512) // 128
                qh_ring = ph.ring("qh", 2, [128, ST], BF16)
                kh_ring = ph.ring("kh", 2, [128, ST + 512], BF16)
                vh_ring = ph.ring("vh", 2, [128, NKT, 128], BF16)
                bh_ring = ph.ring("bh", 2, [128, 640], F32)
                yb_ring = ph.ring("yb", 2, [128, ST], BF16)
                sf_ring = ph.ring("sf", 2, [128, 640], F32)
                pt_ring = ph.ring("pt", 3, [128, 640], BF16)
                rc_ring = ph.ring("rc", 2, [128, 128], F32)
                NQ = ST // 128
                aunits = [(hd, m) for hd in range(8) for m in range(NQ)]
                NU = len(aunits)
                hst, ust = {}, {}

                def att_load(hd):
                    r0, r1 = hd * 128, (hd + 1) * 128
                    qh, b_qh = qh_ring.next()
                    kh, b_kh = kh_ring.next()
                    vh, b_vh = vh_ring.next()
                    bh, b_bh = bh_ring.next()
                    yb, b_yb = yb_ring.next()
                    kb.dma("sp", DMA(qh[:], qT[r0:r1, :]), writes=[b_qh], owner=b_qh)
                    kb.dma("sp", DMA(bh[:], abias_d[l, hd]), writes=[b_bh], owner=b_bh)
                    if first:
                        kb.dma("sp", DMA(kh[:, 512:], kT[r0:r1, 0:ST]), writes=[b_kh], owner=b_kh)
                        kb.dma("sp", DMA(vh[:, 4:, :], vS[0:ST, r0:r1].rearrange("(n p) d -> p n d", p=128)), writes=[b_vh], owner=b_vh)
                    else:
                        kb.dma("sp", DMA(kh[:], kT[r0:r1, t0 - 512:t0 + ST]), writes=[b_kh], owner=b_kh)
                        kb.dma("sp", DMA(vh[:], vS[t0 - 512:t0 + ST, r0:r1].rearrange("(n p) d -> p n d", p=128)), writes=[b_vh], owner=b_vh)
                    hst[hd] = dict(qh=qh, b_qh=b_qh, kh=kh, b_kh=b_kh, vh=vh, b_vh=b_vh, bh=bh, b_bh=b_bh, yb=yb, b_yb=b_yb)

                def att_scores(n):
                    hd, m = aunits[n]
                    if hd not in hst:
                        att_load(hd)
                    H = hst[hd]
                    kts = [kt for kt in range(5) if (not first) or (m - 4 + kt) >= 0]
                    sset = n % 2
                    sb = 2 * sset
                    sc = psum[:, sb * 512:sb * 512 + 640]
                    kb.op("pe", [MM(sc[:, kt * 128:(kt + 1) * 128], H["kh"][:, (m + kt) * 128:(m + kt + 1) * 128], H["qh"][:, m * 128:(m + 1) * 128]) for kt in kts],
                          reads=[H["b_kh"], H["b_qh"]], writes=[pb[sb], pb[sb + 1]])
                    ust[n] = dict(kts=kts, sb=sb, sc=sc, lo=kts[0] * 128)

                def att_sf_exp(n):
                    hd, m = aunits[n]
                    H, U = hst[hd], ust[n]
                    sf, b_sf = sf_ring.next()
                    pt, b_pt = pt_ring.next()
                    lo, hi, sb, sc = U["lo"], 640, U["sb"], U["sc"]
                    STT([pb[sb], pb[sb + 1], H["b_bh"]], [b_sf], sf[:, lo:hi], sc[:, lo:hi], float(128 ** -0.5), H["bh"][:, lo:hi], ALU.mult, ALU.add)
                    A([b_sf], [b_pt], pt[:, lo:hi], sf[:, lo:hi], AF.Exp)
                    U.update(pt=pt, b_pt=b_pt)

                def att_pv(n):
                    hd, m = aunits[n]
                    H, U = hst[hd], ust[n]
                    kts, pt = U["kts"], U["pt"]
                    pbk = 4 + (n % 2)
                    nk = len(kts)
                    fns = [MM(bank(pbk)[:, 0:128], H["vh"][:, m + kt, :], pt[:, kt * 128:(kt + 1) * 128], start=(i_ == 0), stop=(i_ == nk - 1)) for i_, kt in enumerate(kts)]
                    fns += [MM(bank(pbk)[:, 128:256], ones_b[:], pt[:, kt * 128:(kt + 1) * 128], start=(i_ == 0), stop=(i_ == nk - 1)) for i_, kt in enumerate(kts)]
                    kb.op("pe", fns, reads=[H["b_vh"], U["b_pt"], b_onesb], writes=[pb[pbk]])
                    U["pbk"] = pbk

                def att_fin(n):
                    hd, m = aunits[n]
                    H, U = hst[hd], ust[n]
                    pbk = U["pbk"]
                    rc, b_rc = rc_ring.next()
                    V("reciprocal", [pb[pbk]], [b_rc], out=rc[:], in_=bank(pbk)[:, 128:256])
                    TT([pb[pbk], b_rc], [H["b_yb"]], H["yb"][:, m * 128:(m + 1) * 128], bank(pbk)[:, 0:128], rc[:], ALU.mult)
                    if m == NQ - 1:
                        kb.dma("sp", DMA(ybT[hd * 128:(hd + 1) * 128, :], H["yb"][:]), reads=[H["b_yb"]], owner=H["b_yb"])
                    del ust[n]

                att_scores(0)
                if NU > 1:
                    att_scores(1)
                att_sf_exp(0)
                for n in range(NU):
                    if n + 2 < NU:
                        att_scores(n + 2)
                    att_pv(n)
                    if n + 1 < NU:
                        att_sf_exp(n + 1)
                    if n >= 1:
                        att_fin(n - 1)
                    if n % 2 == 0:
                        gstep(1)
                att_fin(NU - 1)
            while gstep(8):
                pass
            ph.close()
            if "p1" not in skip:
                phO.close()


            if "p2c" not in skip:
                ph = Phase(kb)
                W = 4096
                NF = 32 * 129
                E2, b_Ere = ph.tile("E2", [128, 2, W], F32)
                b_Eim = b_Ere
                E_re, E_im = E2[:, 0, :], E2[:, 1, :]
                F2, b_Fre = ph.tile("F2", [128, 2, 32, 129], F32)
                b_Fim = b_Fre
                F_re, F_im = F2[:, 0, :, :], F2[:, 1, :, :]
                bbr, b_bbr = ph.tile("bbr", [128, W], BF16)
                bbi, b_bbi = ph.tile("bbi", [128, W], BF16)
                Ffr = F_re[:].rearrange("p b j -> p (b j)")
                Ffi = F_im[:].rearrange("p b j -> p (b j)")
                MUL, ADD, SUB = ALU.mult, ALU.add, ALU.subtract
                if s == 0:
                    pp = Phase(kb)
                    ar, b_ar = pp.tile("ar", [128, W], F32)
                    ai, b_ai = pp.tile("ai", [128, W], F32)
                    sr, b_sr = pp.tile("sr", [128, W], F32)
                    t1f, b_t1 = pp.tile("t1", [128, NF], F32)
                    t2f, b_t2 = pp.tile("t2", [128, NF], F32)
                    t3f, b_t3 = pp.tile("t3", [128, NF], F32)
                    t1, t2, t3 = t1f[:, 0:W], t2f[:, 0:W], t3f[:, 0:W]
                    bdr, b_bdr = Ffr[:, 0:W], b_Fre
                    bdi, b_bdi = Ffi[:, 0:W], b_Fim
                    kb.dma("sp", DMA(ar[:], rowp_d[l, 0:1, :].to_broadcast([128, W])), writes=[b_ar], owner=b_ar)
                    kb.dma("sp", DMA(ai[:], rowp_d[l, 1:2, :].to_broadcast([128, W])), writes=[b_ai], owner=b_ai)
                    kb.dma("sp", DMA(sr[:], rowp_d[l, 2:3, :].to_broadcast([128, W])), writes=[b_sr], owner=b_sr)
                    kb.dma("sp", DMA(bdr, bbd_d[l, :, 0, :]), writes=[b_bdr], owner=b_bdr)
                    kb.dma("sp", DMA(bdi, bbd_d[l, :, 1, :]), writes=[b_bdi], owner=b_bdi)

                    def sincos(dst_sin, b_ds, dst_cos, b_dc, ycyc, b_y, tmp, b_tmp):
                        TS([b_y], [b_tmp], tmp, ycyc, MAGIC, MAGIC, ALU.add, ALU.subtract)
                        TT([b_y, b_tmp], [b_tmp], tmp, ycyc, tmp, ALU.subtract)
                        A([b_tmp], [b_ds], dst_sin, tmp, AF.Sin, scale=TWO_PI)
                        TS([b_y], [b_y], ycyc, ycyc, 0.25, None, ALU.add)
                        TS([b_y], [b_tmp], tmp, ycyc, MAGIC, MAGIC, ALU.add, ALU.subtract)
                        TT([b_y, b_tmp], [b_tmp], tmp, ycyc, tmp, ALU.subtract)
                        A([b_tmp], [b_dc], dst_cos, tmp, AF.Sin, scale=TWO_PI)

                    MUL, ADD, SUB = ALU.mult, ALU.add, ALU.subtract
                    A([b_sr], [b_sr], sr[:], sr[:], AF.Exp)
                    TT([b_ar, b_sr], [b_t1], t1, ar[:], sr[:], MUL)
                    A([b_t1], [b_t1], t1, t1, AF.Exp)
                    TT([b_ai, b_sr], [b_t2], t2, ai[:], sr[:], MUL)
                    TS([b_t2], [b_t2], t2, t2, 1.0 / TWO_PI, None, MUL)
                    sincos(E_im[:], b_Eim, E_re[:], b_Ere, t2, b_t2, t3, b_t3)
                    TT([b_Ere, b_t1], [b_Ere], E_re[:], E_re[:], t1, MUL)
                    TT([b_Eim, b_t1], [b_Eim], E_im[:], E_im[:], t1, MUL)
                    TS([b_Ere], [b_Ere], E_re[:], E_re[:], -1.0, None, ADD)
                    TT([b_ar], [b_t1], t1, ar[:], ar[:], MUL)
                    TT([b_ai], [b_t2], t2, ai[:], ai[:], MUL)
                    TT([b_t1, b_t2], [b_t1], t1, t1, t2, ADD)
                    V("reciprocal", [b_t1], [b_t1], out=t1, in_=t1)
                    TT([b_Ere, b_ar], [b_t2], t2, E_re[:], ar[:], MUL)
                    TT([b_Eim, b_ai], [b_t3], t3, E_im[:], ai[:], MUL)
                    TT([b_t2, b_t3], [b_t2], t2, t2, t3, ADD)
                    TT([b_t2, b_t1], [b_t2], t2, t2, t1, MUL)
                    TT([b_Eim, b_ar], [b_t3], t3, E_im[:], ar[:], MUL)
                    TT([b_Ere, b_ai], [b_Eim], E_im[:], E_re[:], ai[:], MUL)
                    TT([b_t3, b_Eim], [b_t3], t3, t3, E_im[:], SUB)
                    TT([b_t3, b_t1], [b_t3], t3, t3, t1, MUL)
                    TT([b_t2, b_bdr], [b_Ere], E_re[:], t2, bdr, MUL)
                    TT([b_t3, b_bdi], [b_Eim], E_im[:], t3, bdi, MUL)
                    TT([b_Ere, b_Eim], [b_bbr], bbr[:], E_re[:], E_im[:], SUB)
                    TT([b_t2, b_bdi], [b_Ere], E_re[:], t2, bdi, MUL)
                    TT([b_t3, b_bdr], [b_Eim], E_im[:], t3, bdr, MUL)
                    TT([b_Ere, b_Eim], [b_bbi], bbi[:], E_re[:], E_im[:], ADD)
                    iocol = col("iota", 0)
                    TT([b_ar, b_sr], [b_t1], t1, ar[:], sr[:], MUL)
                    TS([b_t1, b_colp], [b_t1], t1, t1, iocol, -1.0, MUL, MUL)
                    A([b_t1], [b_t1], t1, t1, AF.Exp)
                    TT([b_ai, b_sr], [b_t2], t2, ai[:], sr[:], MUL)
                    TS([b_t2, b_colp], [b_t2], t2, t2, iocol, -1.0 / TWO_PI, MUL, MUL)
                    sincos(E_im[:], b_Eim, E_re[:], b_Ere, t2, b_t2, t3, b_t3)
                    TT([b_Ere, b_t1], [b_Ere], E_re[:], E_re[:], t1, MUL)
                    TT([b_Eim, b_t1], [b_Eim], E_im[:], E_im[:], t1, MUL)
                    stc, b_stc = pp.tile("stc", [128, 32], F32)
                    alc, b_alc = pp.tile("alc", [128, 32], F32)
                    thc, b_thc = pp.tile("thc", [128, 32], F32)
                    A([b_colp], [b_stc], stc[:], col("lsc", l * 32, 32), AF.Exp)
                    TT([b_colp, b_stc], [b_alc], alc[:], col("acre", l * 32, 32), stc[:], MUL)
                    TT([b_colp, b_stc], [b_thc], thc[:], col("acim", l * 32, 32), stc[:], MUL)
                    TS([b_thc], [b_thc], thc[:], thc[:], 1.0 / TWO_PI, None, MUL)
                    g1 = t1f[:].rearrange("p (b j) -> p b j", b=32)
                    g2 = t2f[:].rearrange("p (b j) -> p b j", b=32)
                    io_b = iorow[:].unsqueeze(1).to_broadcast([128, 32, 129])
                    TT([b_iorow, b_alc], [b_t1], g1, io_b, alc[:].unsqueeze(2).to_broadcast([128, 32, 129]), MUL)
                    A([b_t1], [b_t1], t1f[:], t1f[:], AF.Exp)
                    TT([b_iorow, b_thc], [b_t2], g2, io_b, thc[:].unsqueeze(2).to_broadcast([128, 32, 129]), MUL)
                    sincos(Ffi, b_Fim, Ffr, b_Fre, t2f[:], b_t2, t3f[:], b_t3)
                    TT([b_Fre, b_t1], [b_Fre], Ffr, Ffr, t1f[:], MUL)
                    TT([b_Fim, b_t1], [b_Fim], Ffi, Ffi, t1f[:], MUL)
                    if nST > 1:
                        kb.dma("sp", DMA(tabE[0], E_re[:]), reads=[b_Ere], owner=b_Ere)
                        kb.dma("sp", DMA(tabE[1], E_im[:]), reads=[b_Eim], owner=b_Eim)
                        kb.dma("sp", DMA(tabF[0], Ffr), reads=[b_Fre], owner=b_Fre)
                        kb.dma("sp", DMA(tabF[1], Ffi), reads=[b_Fim], owner=b_Fim)
                        kb.dma("sp", DMA(tabB[0], bbr[:]), reads=[b_bbr], owner=b_bbr)
                        kb.dma("sp", DMA(tabB[1], bbi[:]), reads=[b_bbi], owner=b_bbi)
                    pp.close()
                else:
                    kb.dma("sp", DMA(E_re[:], tabE[0]), writes=[b_Ere], owner=b_Ere)
                    kb.dma("sp", DMA(E_im[:], tabE[1]), writes=[b_Eim], owner=b_Eim)
                    kb.dma("sp", DMA(Ffr, tabF[0]), writes=[b_Fre], owner=b_Fre)
                    kb.dma("sp", DMA(Ffi, tabF[1]), writes=[b_Fim], owner=b_Fim)
                    kb.dma("sp", DMA(bbr[:], tabB[0]), writes=[b_bbr], owner=b_bbr)
                    kb.dma("sp", DMA(bbi[:], tabB[1]), writes=[b_bbi], owner=b_bbi)
                cmat, b_cmat = ph.tile("cmat", [128, 8192], BF16)
                kb.dma("pool", [DMA(cmat[:, q * 2048:(q + 1) * 2048], ct_d[l, :, q * 2048:(q + 1) * 2048]) for q in range(4)], writes=[b_cmat], owner=b_cmat)
                ncmat, b_ncmat = ph.tile("ncmat", [128, 8192], BF16)
                TS([b_cmat], [b_ncmat], ncmat[:], cmat[:], -1.0, None, MUL)
                u_all, b_u = ph.tile("u_all", [128, 8, ST], BF16)
                kb.dma("sp", DMA(u_all[:], uT.rearrange("(c p) t -> p c t", p=128)), writes=[b_u], owner=b_u)
                p_ring = ph.ring("pp4", 2, [128, 4, 512], BF16)
                q_ring = ph.ring("qq4", 2, [128, 4, 512], BF16)
                cp_ring = ph.ring("cp", 2, [128, 2, 512], F32)
                yc_ring = ph.ring("yc", 4, [128, 128], BF16)
                ty_ring = ph.ring("ty", 2, [128, 128], F32)
                tn_ring = ph.ring("tn", 2, [128, 4, 4], F32)
                if first:
                    V("memset", [], [b_s5c], ap=s5c[:], constant=0.0)
                units = [(j, ct) for j in range(ST // 128) for ct in range(8)]

                def bu_banks(ui):
                    base = (ui % 2) * 4
                    return base, base + 1, base + 2, base + 3

                def emit_bu(ui):
                    j, ct = units[ui]
                    br, bi_, _, _ = bu_banks(ui)
                    ul = u_all[:, ct, j * 128:(j + 1) * 128]
                    c0, c1 = ct * 512, (ct + 1) * 512
                    kb.op("pe", MM(bank(br), ul, bbr[:, c0:c1]), reads=[b_u, b_bbr], writes=[pb[br]])
                    kb.op("pe", MM(bank(bi_), ul, bbi[:, c0:c1]), reads=[b_u, b_bbi], writes=[pb[bi_]])

                def st_eprod(ui, stt):
                    j, ct = units[ui]
                    br, bi_, cr_, ci_ = bu_banks(ui)
                    c0, c1 = ct * 512, (ct + 1) * 512
                    pp4, b_p = p_ring.next()
                    stt["p"] = (pp4, b_p)
                    E2ct = E2[:, :, c0:c1]
                    TT([pb[br], b_Ere], [b_p], pp4[:, 0:2, :], bank(br).unsqueeze(1).to_broadcast([128, 2, 512]), E2ct, MUL)
                    TT([pb[bi_], b_Ere], [b_p], pp4[:, 2:4, :], bank(bi_).unsqueeze(1).to_broadcast([128, 2, 512]), E2ct, MUL)
                    fns = []
                    for blk in range(4):
                        bs = slice(blk * 128, (blk + 1) * 128)
                        fns.append(MM(bank(cr_)[:, bs], pp4[:, 0, bs], tri_b[:], start=True, stop=False))
                        fns.append(MM(bank(cr_)[:, bs], pp4[:, 3, bs], ntri_b[:], start=False, stop=True))
                        fns.append(MM(bank(ci_)[:, bs], pp4[:, 2, bs], tri_b[:], start=True, stop=False))
                        fns.append(MM(bank(ci_)[:, bs], pp4[:, 1, bs], tri_b[:], start=False, stop=True))
                    kb.op("pe", fns, reads=[b_p, b_trib, b_ntrib], writes=[pb[cr_], pb[ci_]])

                def st_xprod(ui, stt):
                    j, ct = units[ui]
                    br, bi_, cr_, ci_ = bu_banks(ui)
                    cp, b_cp = cp_ring.next()
                    s0, s1 = ct * 4, ct * 4 + 4
                    v4 = lambda ap: ap.rearrange("p r (b j) -> p r b j", b=4)
                    cs2 = psum[:, cr_ * 512:(cr_ + 2) * 512].rearrange("p (r b j) -> p r b j", r=2, b=4)
                    TT([pb[cr_], pb[ci_], b_s5c], [b_cp], v4(cp[:]), cs2, s5c[:, :, s0:s1].unsqueeze(3).to_broadcast([128, 2, 4, 128]), ADD)
                    F2ct = F2[:, :, s0:s1, 0:128]
                    qq4, b_q = q_ring.next()
                    TT([b_cp, b_Fre], [b_q], v4(qq4[:, 0:2, :]), v3(cp[:, 0, :]).unsqueeze(1).to_broadcast([128, 2, 4, 128]), F2ct, MUL)
                    TT([b_cp, b_Fre], [b_q], v4(qq4[:, 2:4, :]), v3(cp[:, 1, :]).unsqueeze(1).to_broadcast([128, 2, 4, 128]), F2ct, MUL)
                    tn, b_tn = tn_ring.next()
                    cr = v3(cp[:, 0, :])[:, :, 127]
                    ci = v3(cp[:, 1, :])[:, :, 127]
                    Gr = F_re[:, s0:s1, 128]
                    Gi = F_im[:, s0:s1, 128]
                    TT([b_cp, b_Fre], [b_tn], tn[:, 0, :], cr, Gr, MUL)
                    TT([b_cp, b_Fim], [b_tn], tn[:, 1, :], ci, Gi, MUL)
                    TT([b_cp, b_Fre], [b_tn], tn[:, 2, :], ci, Gr, MUL)
                    TT([b_cp, b_Fim], [b_tn], tn[:, 3, :], cr, Gi, MUL)
                    TT([b_tn], [b_s5ca], s5ca[:, s0:s1], tn[:, 0, :], tn[:, 1, :], SUB)
                    TT([b_tn], [b_s5cb], s5cb[:, s0:s1], tn[:, 2, :], tn[:, 3, :], ADD)
                    fns = []
                    for blk in range(4):
                        bs = slice(blk * 128, (blk + 1) * 128)
                        o_re = ((ct * 4 + blk) * 2 + 0) * 128
                        o_im = ((ct * 4 + blk) * 2 + 1) * 128
                        ops = ((cmat, o_re, 0), (ncmat, o_re, 3), (ncmat, o_im, 2), (ncmat, o_im, 1))
                        for oi, (cm, o, qi) in enumerate(ops):
                            fns.append(MM(bank(cr_)[:, 0:128], cm[:, o:o + 128], qq4[:, qi, bs], start=(blk == 0 and oi == 0), stop=(blk == 3 and oi == 3)))
                    kb.op("pe", fns, reads=[b_cmat, b_ncmat, b_q], writes=[pb[cr_]])

                def st_out(ui, stt):
                    j, ct = units[ui]
                    br, bi_, cr_, ci_ = bu_banks(ui)
                    ul = u_all[:, ct, j * 128:(j + 1) * 128]
                    ty, b_ty = ty_ring.next()
                    yc, b_yc = yc_ring.next()
                    STT([b_u, pb[cr_], b_colp], [b_ty], ty[:], ul, col("ssmd", l * 8 + ct), bank(cr_)[:, 0:128], MUL, ADD)
                    A([b_ty], [b_yc], yc[:], ty[:], AF.Gelu_apprx_tanh)
                    kb.dma("sp", DMA(ycT[ct * 128:(ct + 1) * 128, j * 128:(j + 1) * 128], yc[:]), reads=[b_yc], owner=b_yc)

                nU = len(units)
                emit_bu(0)
                emit_bu(1)
                for pi in range(0, nU, 2):
                    a, b = pi, pi + 1
                    sa, sb_ = {}, {}
                    st_eprod(a, sa)
                    st_eprod(b, sb_)
                    st_xprod(a, sa)
                    if a + 2 < nU:
                        emit_bu(a + 2)
                    st_xprod(b, sb_)
                    if b + 2 < nU:
                        emit_bu(b + 2)
                    st_out(a, sa)
                    st_out(b, sb_)
                ph.close()


            if "p3" not in skip:
                ph = Phase(kb)
                ys = []
                for nm, src in (("ya", yaT), ("yb", ybT), ("yc", ycT)):
                    t = None
                    parts = []
                    srcv_ = src.rearrange("(c p) t -> p c t", p=128)
                    for tt in range(nTT):
                        tl, b = ph.tile(nm, [128, 8, 512], BF16)
                        kb.dma("sp", DMA(tl[:], srcv_[:, :, tt * 512:(tt + 1) * 512]), writes=[b], owner=b)
                        parts.append((tl, b))
                    ys.append(parts)
                ys.append(ys[2])
                wb_ring = ph.ring("wb", 3, [128, 4, 8, 128], BF16)
                g_ring = ph.ring("gg", 2, [128, 3, 512], BF16)
                ms_ring = ph.ring("ms", 2, [128, ST], BF16)
                tq_ring = ph.ring("tq", 2, [128, 4, 512], F32)
                gT_v = gT.rearrange("(b f p) t -> f p b t", b=3, f=16)
                it = 0
                for f in range(16):
                    wb, b_wb = wb_ring.next()
                    kb.dma("pool", [DMA(wb[:, q, :, :], wbr_t[l, f, :, q, :, :]) for q in range(4)], writes=[b_wb], owner=b_wb)
                    ms, b_ms = ms_ring.next()
                    for tt in range(nTT):
                        ts0, ts1 = tt * 512, (tt + 1) * 512
                        g3, b_g3 = g_ring.next()
                        kb.dma("sp", DMA(g3[:], gT_v[f][:, :, ts0:ts1]), writes=[b_g3], owner=b_g3)
                        b0 = (it % 2) * 4
                        it += 1
                        for q in range(4):
                            yt, b_yt = ys[q][tt]
                            kb.op("pe", [MM(bank(b0 + q), wb[:, q, k, :], yt[:, k, :], start=(k == 0), stop=(k == 7)) for k in range(8)],
                                  reads=[b_wb, b_yt], writes=[pb[b0 + q]])
                        tq, b_tq = tq_ring.next()
                        A([pb[b0 + 3]], [b_tq], tq[:, 3, :], bank(b0 + 3), AF.Sigmoid)
                        TT([pb[b0], b_g3], [b_tq], tq[:, 0, :], bank(b0 + 0), g3[:, 0, :], ALU.mult)
                        TT([pb[b0 + 1], b_g3], [b_tq], tq[:, 1, :], bank(b0 + 1), g3[:, 1, :], ALU.mult)
                        TT([pb[b0 + 2], b_tq], [b_tq], tq[:, 2, :], bank(b0 + 2), tq[:, 3, :], ALU.mult)
                        TT([b_tq, b_g3], [b_tq], tq[:, 2, :], tq[:, 2, :], g3[:, 2, :], ALU.mult)
                        TT([b_tq], [b_tq], tq[:, 0, :], tq[:, 0, :], tq[:, 1, :], ALU.add)
                        TT([b_tq], [b_ms], ms[:, ts0:ts1], tq[:, 0, :], tq[:, 2, :], ALU.add)
                    kb.dma("sp", DMA(mT[f * 128:(f + 1) * 128, :], ms[:]), reads=[b_ms], owner=b_ms)
                ph.close()
                proj_residual(l, t0, mT, 16, wout_t[l], nTT)

            if "p4" not in skip:
                ph = Phase(kb)
                hT, b_hT = ph.tile("hT", [128, NK, ST], BF16)
                xin_ring = ph.ring("xin", 2, [128, NK, 512], F32)
                sq_ring = ph.ring("sq", 3, [128, 512], BF16)
                rs_ring = ph.ring("rs", 2, [128, 512], F32)
                rms_norm_to(hT, b_hT, "gffn", l, t0, xin_ring, sq_ring, rs_ring, (6, 7))
                wg_ring = ph.ring("wg", 3, [128, 2, NK, 128], BF16)
                as_ring = ph.ring("as", 2, [128, ST], BF16)
                sl_ring = ph.ring("sl", 2, [128, 512], F32)
                it = 0
                for jj in range(NJ):
                    wg, b_wg = wg_ring.next()
                    kb.dma("pool", [DMA(wg[:, q, :, :], wgu_t[l, jj, :, q, :, :]) for q in range(2)], writes=[b_wg], owner=b_wg)
                    ast, b_ast = as_ring.next()
                    for tt in range(nTT):
                        ts0, ts1 = tt * 512, (tt + 1) * 512
                        bg = (it % 3) * 2
                        it += 1
                        for q in range(2):
                            kb.op("pe", [MM(bank(bg + q), wg[:, q, k, :], hT[:, k, ts0:ts1], start=(k == 0), stop=(k == NK - 1)) for k in range(NK)],
                                  reads=[b_wg, b_hT], writes=[pb[bg + q]])
                        sl, b_sl = sl_ring.next()
                        A([pb[bg]], [b_sl], sl[:], bank(bg), AF.Silu)
                        TT([pb[bg + 1], b_sl], [b_ast], ast[:, ts0:ts1], bank(bg + 1), sl[:], ALU.mult)
                    kb.dma("sp", DMA(actT[jj * 128:(jj + 1) * 128, :], ast[:]), reads=[b_ast], owner=b_ast)
                ph.close()
                proj_residual(l, t0, actT, NJ, wdn_t[l], min(2, nTT))

    ph = Phase(kb)
    xin_ring = ph.ring("xin", 2, [128, NK, 512], F32)
    sq_ring = ph.ring("sq", 3, [128, 512], BF16)
    rs_ring = ph.ring("rs", 2, [128, 512], F32)
    outv = outT.rearrange("(k p) t -> p k t", p=128)
    for tt in range(S // 512):
        tg = tt * 512
        xin, b_xin = xin_ring.next()
        rs, b_rs = rms_norm_tile(xin, b_xin, tg, sq_ring, rs_ring, 6 + (tt % 2))
        for k in range(NK):
            STT([b_xin, b_rs, b_colp], [b_xin], xin[:, k, :], xin[:, k, :], col("gfin", k), rs[:], ALU.mult, ALU.mult)
        kb.dma("sp", DMA(outv[:, :, tg:tg + 512], xin[:]), reads=[b_xin], owner=b_xin)
    ph.close()
    kb.finalize()
    return nc


def prep_weights(inp, L):
    f = np.float32
    out = {}
    w_in = inp["w_in"][:L]
    out["w_in_t"] = np.ascontiguousarray(w_in.reshape(L, 16, 128, 96, 128).transpose(0, 3, 2, 1, 4))
    wb = np.concatenate([inp["w_branch"][:L], inp["ssm_w_glu"][:L][:, None]], axis=1)
    out["wbr_t"] = np.ascontiguousarray(wb.reshape(L, 4, 8, 128, 16, 128).transpose(0, 4, 3, 1, 2, 5))
    out["wout_t"] = np.ascontiguousarray(inp["w_out"][:L].reshape(L, 16, 128, 16, 128).transpose(0, 3, 2, 1, 4))
    wgu = np.stack([inp["w_ffn_gate"][:L], inp["w_ffn_up"][:L]], axis=1)
    out["wgu_t"] = np.ascontiguousarray(wgu.reshape(L, 2, 16, 128, NJ, 128).transpose(0, 4, 3, 1, 2, 5))
    out["wdn_t"] = np.ascontiguousarray(inp["w_ffn_down"][:L].reshape(L, NJ, 128, 16, 128).transpose(0, 3, 2, 1, 4))
    COFF, NCOL = col_layout(L)
    colp = np.zeros((128, NCOL), f)

    def put(name, arr):
        colp[:, COFF[name]:COFF[name] + arr.shape[1]] = arr
    put("gmix", inp["norm_mix_g"][:L].reshape(L, 16, 128).transpose(2, 0, 1).reshape(128, -1))
    put("gffn", inp["norm_ffn_g"][:L].reshape(L, 16, 128).transpose(2, 0, 1).reshape(128, -1))
    put("gfin", inp["norm_final_g"].reshape(16, 128).T)
    put("gbias", inp["gate_bias"][:L].reshape(L, 3, 16, 128).transpose(3, 0, 1, 2).reshape(128, -1))
    put("convw", inp["lru_conv_w"][:L].reshape(L, 4, 8, 128).transpose(3, 0, 2, 1).reshape(128, -1))
    for nm, key in (("convb", "lru_conv_b"), ("ba", "lru_ba"), ("bx", "lru_bx"), ("lam", "lru_lambda"), ("ssmd", "ssm_d")):
        put(nm, inp[key][:L].reshape(L, 8, 128).transpose(2, 0, 1).reshape(128, -1))
    for nm, arr in (("acre", inp["ssm_a_re"][:L]), ("acim", inp["ssm_a_im"][:L]),
                    ("lsc", np.repeat(inp["ssm_log_step"][:L][:, :, None], 64, axis=2))):
        put(nm, arr.reshape(L, 32, 2, 64).transpose(2, 3, 0, 1).reshape(128, -1))
    colp[:, COFF["iota"]] = np.arange(128, dtype=f)
    out["colp"] = colp
    rowp = np.stack([inp["ssm_a_re"][:L].reshape(L, 4096), inp["ssm_a_im"][:L].reshape(L, 4096),
                     np.repeat(inp["ssm_log_step"][:L][:, :, None], 64, axis=2).reshape(L, 4096)], axis=1)
    out["rowp"] = np.ascontiguousarray(rowp.astype(f))
    lruw = np.zeros((L, 128, 8, 2, 128), f)
    for wi, key in enumerate(("lru_wa", "lru_wx")):
        w = inp[key][:L].reshape(L, 8, 2, 64, 64)
        for nl in range(2):
            lruw[:, nl * 64:(nl + 1) * 64, :, wi, nl * 64:(nl + 1) * 64] = w[:, :, nl].transpose(0, 2, 1, 3)
    out["lruw"] = lruw
    kk = np.arange(128)[:, None, None]
    kt = np.arange(5)[None, :, None]
    qq = np.arange(128)[None, None, :]
    dist = (4 - kt) * 128 + qq - kk
    rel = np.clip(dist, -128, 128) + 128
    cdiff = 8 - 2 * kt + qq // 64 - kk // 64
    valid = (cdiff >= 0) & (cdiff <= 8)
    ab = inp["attn_rel_bias"][:L][:, :, rel]
    ab = np.where(valid[None, None], ab, f(-30000.0)).astype(f)
    out["abias"] = np.ascontiguousarray(ab.reshape(L, 8, 128, 640))
    bbd = np.zeros((L, 128, 2, 8, 8, 64), f)
    for ri, key in enumerate(("ssm_b_re", "ssm_b_im")):
        B = inp[key][:L].reshape(L, 8, 8, 64, 16)
        for gl in range(8):
            bbd[:, gl * 16:(gl + 1) * 16, ri, :, gl, :] = B[:, :, gl].transpose(0, 3, 1, 2)
    out["bbd"] = np.ascontiguousarray(bbd.reshape(L, 128, 2, 4096))
    ctd = np.zeros((L, 128, 8, 4, 2, 128), f)
    for ri, key in enumerate(("ssm_c_re", "ssm_c_im")):
        C = inp[key][:L].reshape(L, 8, 4, 2, 16, 64)
        for blk in range(4):
            for gl2 in range(2):
                c0 = (2 * blk + gl2) * 16
                ctd[:, gl2 * 64:(gl2 + 1) * 64, :, blk, ri, c0:c0 + 16] = C[:, :, blk, gl2].transpose(0, 3, 1, 2)
    out["ctd"] = np.ascontiguousarray(ctd.reshape(L, 128, 8192))
    cst = np.zeros((128, 386), f)
    cst[:, 0:128] = np.triu(np.ones((128, 128), f))
    cst[:, 128:256] = 1.0
    cst[:, 256:385] = np.arange(129, dtype=f)[None, :]
    cst[:, 385] = 1e-6
    out["cst"] = cst
    out["epsd"] = np.full((128, 1), 1e-6, f)
    return out


_CACHE = {}


def run_model(inputs, L, S_core, ST, n_cores, dbg=False, skip=(), spread=False):
    x = np.asarray(inputs["x"], np.float32)
    B, S, _ = x.shape
    assert S == S_core and B <= n_cores
    key = (L, S_core, ST, dbg, tuple(skip))
    if key not in _CACHE:
        _CACHE[key] = build_program(L, S_core, ST, dbg=dbg, skip=skip)
    nc = _CACHE[key]
    wts = prep_weights({k: np.asarray(v, np.float32) for k, v in inputs.items() if k != "x"}, L)
    if spread:
        real = {0: 0, 1: 1, 4: 2, 5: 3}
        zw = dict(wts)
        for k in ("w_in_t", "wbr_t", "wout_t", "wgu_t", "wdn_t"):
            zw[k] = np.zeros_like(wts[k])
        zx = np.zeros((D, S), np.float32)
        in_maps = []
        for c in range(8):
            if c in real and real[c] < B:
                m = dict(wts)
                m["xT"] = np.ascontiguousarray(x[real[c]].T)
            else:
                m = dict(zw)
                m["xT"] = zx
            in_maps.append(m)
        res = run_bass_kernel_spmd(nc, in_maps, core_ids=list(range(8)))
        slots = [c for c in range(8) if c in real and real[c] < B]
        out = np.stack([np.ascontiguousarray(res.results[c]["outT"].T) for c in sorted(slots, key=lambda c: real[c])], axis=0)
        return out.astype(np.float32), res
    in_maps = []
    for c in range(n_cores):
        b = c % B
        m = dict(wts)
        m["xT"] = np.ascontiguousarray(x[b].T)
        in_maps.append(m)
    res = run_bass_kernel_spmd(nc, in_maps, core_ids=list(range(n_cores)))
    out = np.stack([np.ascontiguousarray(res.results[b]["outT"].T) for b in range(B)], axis=0)
    return out.astype(np.float32), res


def kernel(**inputs):
    out, _ = run_model(inputs, 4, 4096, 2048, 8, spread=True)
    return out
```

```python
import math
from contextlib import ExitStack
import numpy as np
import concourse.bass as bass
import concourse.mybir as mybir
from concourse.bass_utils import run_bass_kernel_spmd

F32 = mybir.dt.float32
BF16 = mybir.dt.bfloat16
AF = mybir.ActivationFunctionType
ALU = mybir.AluOpType

D = 2048
NK = 16
MIXW = 1024
FH = 5632
NJ = 44
TWO_PI = float(2 * math.pi)
MAGIC = 12582912.0


class Buf:
    __slots__ = ("name", "w", "r", "dsem")

    def __init__(self, name):
        self.name = name
        self.w = None
        self.r = {}
        self.dsem = None


class Eng:
    def __init__(self, name, sem):
        self.name = name
        self.sem = sem
        self.cnt = 0
        self.waited = {}
        self.prog = []


class KB:
    def __init__(self, nc):
        self.nc = nc
        self.stack = ExitStack()
        self.sems = {}
        self.semcnt = {}
        self.free_dsems = []
        self.dirty = {}
        self.eng = {}
        for name in ("pe", "act", "dve", "pool", "sp"):
            self.eng[name] = Eng(name, self.newsem("e_" + name))
        self.uid = 0

    def newsem(self, name):
        h = self.stack.enter_context(self.nc.semaphore(name))
        key = len(self.sems)
        self.sems[key] = h
        self.semcnt[key] = 0
        return key

    def buf(self, name="b"):
        self.uid += 1
        return Buf(f"{name}_{self.uid}")

    def _deps(self, reads, writes):
        deps = {}
        for b in reads:
            if b.w is not None and deps.get(b.w[0], 0) < b.w[1]:
                deps[b.w[0]] = b.w[1]
        for b in writes:
            if b.w is not None and deps.get(b.w[0], 0) < b.w[1]:
                deps[b.w[0]] = b.w[1]
            for k, v in b.r.items():
                if deps.get(k, 0) < v:
                    deps[k] = v
        return deps

    def _emit_waits(self, e, deps, skip=None):
        for k, v in deps.items():
            if k == skip:
                continue
            if e.waited.get(k, 0) < v:
                e.prog.append(("w", k, v))
                e.waited[k] = v

    def _update(self, tok, reads, writes):
        k, v = tok
        for b in reads:
            if b.r.get(k, 0) < v:
                b.r[k] = v
        for b in writes:
            b.w = tok
            b.r = {}

    def op(self, en, fns, reads=(), writes=()):
        e = self.eng[en]
        if callable(fns):
            fns = [fns]
        deps = self._deps(reads, writes)
        self._emit_waits(e, deps, skip=e.sem if en == "pe" else None)
        for f in fns[:-1]:
            e.prog.append(("i", f, None, 0))
        e.cnt += 1
        e.prog.append(("i", fns[-1], e.sem, 1))
        tok = (e.sem, e.cnt)
        self._update(tok, reads, writes)
        return tok

    def dma(self, en, fns, reads=(), writes=(), owner=None):
        e = self.eng[en]
        if callable(fns):
            fns = [fns]
        if owner.dsem is None:
            owner.dsem = self.free_dsems.pop() if self.free_dsems else self.newsem("d%d" % len(self.sems))
        k = owner.dsem
        deps = self._deps(reads, writes)
        self._emit_waits(e, deps)
        for f in fns:
            self.semcnt[k] += 16
            e.prog.append(("i", f, k, 16))
        tok = (k, self.semcnt[k])
        self.dirty[k] = self.semcnt[k]
        self._update(tok, reads, writes)
        return tok

    def barrier(self):
        toks = dict(self.dirty)
        for e in self.eng.values():
            if e.cnt > 0:
                toks[e.sem] = e.cnt
        for e in self.eng.values():
            self._emit_waits(e, toks)
        self.dirty = {}

    def release(self, bufs):
        for b in bufs:
            if b.dsem is not None:
                self.free_dsems.append(b.dsem)
                b.dsem = None

    def finalize(self):
        nc = self.nc
        self.barrier()
        engs, sems = self.eng, self.sems

        def replay(e, h):
            for it in e.prog:
                if it[0] == "w":
                    h.wait_ge(sems[it[1]], it[2])
                else:
                    inst = it[1](h)
                    if it[2] is not None:
                        inst.then_inc(sems[it[2]], it[3])

        with nc.Block() as block:
            @block.tensor
            def _(h):
                replay(engs["pe"], h)

            @block.scalar
            def _(h):
                replay(engs["act"], h)

            @block.vector
            def _(h):
                replay(engs["dve"], h)

            @block.gpsimd
            def _(h):
                replay(engs["pool"], h)

            @block.sync
            def _(h):
                replay(engs["sp"], h)
        self.stack.close()


class Phase:
    def __init__(self, kb):
        self.kb = kb
        self.st = ExitStack()
        self.bufs = []

    def tile(self, name, shape, dtype):
        kb = self.kb
        kb.uid += 1
        t = self.st.enter_context(kb.nc.sbuf_tensor(f"{name}_{kb.uid}", list(shape), dtype))
        b = kb.buf(name)
        self.bufs.append(b)
        return t, b

    def ring(self, name, n, shape, dtype):
        return Ring([self.tile(name, shape, dtype) for _ in range(n)])

    def close(self):
        self.kb.barrier()
        self.kb.release(self.bufs)
        self.st.close()


class Ring:
    def __init__(self, items):
        self.items = items
        self.i = 0

    def next(self):
        it = self.items[self.i % len(self.items)]
        self.i += 1
        return it


def col_layout(L):
    segs = [("gmix", L * 16), ("gffn", L * 16), ("gfin", 16), ("gbias", L * 48), ("convw", L * 32),
            ("convb", L * 8), ("ba", L * 8), ("bx", L * 8), ("lam", L * 8), ("ssmd", L * 8),
            ("acre", L * 32), ("acim", L * 32), ("lsc", L * 32), ("iota", 1)]
    off, o = {}, 0
    for n, w in segs:
        off[n] = o
        o += w
    return off, o


def build_program(L, S, ST, dbg=False, has_prev=False, skip=()):
    nc = bass.Bass("TRN2", target_bir_lowering=False)
    kb = KB(nc)
    gst = kb.stack
    nST = S // ST
    nTT = ST // 512
    COFF, NCOL = col_layout(L)
    okind = "ExternalOutput" if dbg else "Internal"

    def din(name, shape, dt=F32):
        return nc.dram_tensor(name, list(shape), dt, kind="ExternalInput").ap()

    def dscr(name, shape, dt):
        return nc.dram_tensor(name, list(shape), dt, kind=okind).ap()

    xT = din("xT", [D, S])
    w_in_t = din("w_in_t", [L, 96, 128, 16, 128])
    wbr_t = din("wbr_t", [L, 16, 128, 4, 8, 128])
    wout_t = din("wout_t", [L, 16, 128, 16, 128])
    wgu_t = din("wgu_t", [L, NJ, 128, 2, 16, 128])
    wdn_t = din("wdn_t", [L, 16, 128, NJ, 128])
    colp_d = din("colp", [128, NCOL])
    rowp_d = din("rowp", [L, 3, 4096])
    lruw_d = din("lruw", [L, 128, 8, 2, 128])
    abias_d = din("abias", [L, 8, 128, 640])
    bbd_d = din("bbd", [L, 128, 2, 4096])
    ct_d = din("ctd", [L, 128, 8192])
    cst_d = din("cst", [128, 386])
    eps_d = din("epsd", [128, 1])
    outT = nc.dram_tensor("outT", [D, S], F32, kind="ExternalOutput").ap()

    xr = dscr("xr", [D, S], F32)
    lxT = dscr("lxT", [MIXW, S], BF16)
    lgT = dscr("lgT", [MIXW, ST], BF16)
    qT = dscr("qT", [MIXW, ST], BF16)
    kT = dscr("kT", [MIXW, S], BF16)
    vS = dscr("vS", [S, MIXW], BF16)
    uT = dscr("uT", [MIXW, ST], BF16)
    gT = dscr("gT", [3 * D, ST], BF16)
    yaT = dscr("yaT", [MIXW, ST], BF16)
    ybT = dscr("ybT", [MIXW, ST], BF16)
    ycT = dscr("ycT", [MIXW, ST], BF16)
    mT = dscr("mT", [D, ST], BF16)
    actT = dscr("actT", [FH, ST], BF16)
    tabE = nc.dram_tensor("tabE", [2, 128, 4096], F32, kind="Internal").ap()
    tabF = nc.dram_tensor("tabF", [2, 128, 32 * 129], F32, kind="Internal").ap()
    tabB = nc.dram_tensor("tabB", [2, 128, 4096], BF16, kind="Internal").ap()

    def gtile(name, shape, dt):
        t = gst.enter_context(nc.sbuf_tensor(name, list(shape), dt))
        return t, kb.buf(name)

    colp, b_colp = gtile("colp_s", [128, NCOL], F32)
    nsp8, b_nsp8 = gtile("nsp8", [128, L * 8], F32)
    ones_f, b_onesf = gtile("ones_f", [128, 128], F32)
    ones_b, b_onesb = gtile("ones_b", [128, 128], BF16)
    tri_b, b_trib = gtile("tri_b", [128, 128], BF16)
    ntri_b, b_ntrib = gtile("ntri_b", [128, 128], BF16)
    iorow, b_iorow = gtile("iorow", [128, 129], F32)
    lru_h, b_lruh = gtile("lru_h", [128, 8], F32)
    s5c, b_s5c = gtile("s5c", [128, 2, 32], F32)
    s5ca, s5cb = s5c[:, 0, :], s5c[:, 1, :]
    b_s5ca = b_s5cb = b_s5c
    epsc, b_epsc = gtile("epsc", [128, 1], F32)
    psum = gst.enter_context(nc.psum_tensor("psum", [128, 4096], F32))
    pb = [kb.buf(f"ps{i}") for i in range(8)]

    def bank(i):
        return psum[:, i * 512:(i + 1) * 512]

    def col(name, idx, n=1):
        o = COFF[name] + idx
        return colp[:, o:o + n]

    def I(name, **kw):
        return lambda h: getattr(h, name)(**kw)

    def MM(out, lhsT, rhs, start=True, stop=True):
        return I("matmul", out=out, lhsT=lhsT, rhs=rhs, start=start, stop=stop)

    def DMA(out, in_):
        return I("dma_start", out=out, in_=in_)

    def V(name, r, w, **kw):
        return kb.op("dve", I(name, **kw), reads=r, writes=w)

    def A(r, w, out, in_, func, **kw):
        return kb.op("act", I("activation", out=out, in_=in_, func=func, **kw), reads=r, writes=w)

    def TT(r, w, out, in0, in1, op):
        return V("tensor_tensor", r, w, out=out, in0=in0, in1=in1, op=op)

    def TS(r, w, out, in0, s1, s2, op0, op1=None):
        if op1 is None:
            return V("tensor_scalar", r, w, out=out, in0=in0, scalar1=s1, scalar2=None, op0=op0)
        return V("tensor_scalar", r, w, out=out, in0=in0, scalar1=s1, scalar2=s2, op0=op0, op1=op1)

    def STT(r, w, out, in0, scalar, in1, op0, op1):
        return V("scalar_tensor_tensor", r, w, out=out, in0=in0, scalar=scalar, in1=in1, op0=op0, op1=op1)

    kb.dma("sp", DMA(colp[:], colp_d), writes=[b_colp], owner=b_colp)
    kb.dma("sp", DMA(ones_f[:], cst_d[:, 128:256]), writes=[b_onesf], owner=b_onesf)
    kb.dma("sp", DMA(iorow[:], cst_d[:, 256:385]), writes=[b_iorow], owner=b_iorow)
    kb.dma("pool", DMA(tri_b[:], cst_d[:, 0:128]), writes=[b_trib], owner=b_trib)
    kb.dma("pool", DMA(ones_b[:], cst_d[:, 128:256]), writes=[b_onesb], owner=b_onesb)
    TS([b_trib], [b_ntrib], ntri_b[:], tri_b[:], -1.0, None, ALU.mult)
    kb.dma("sp", DMA(epsc[:], eps_d), writes=[b_epsc], owner=b_epsc)
    A([b_colp], [b_nsp8], nsp8[:], col("lam", 0, L * 8), AF.Exp, scale=-1.0)
    A([b_nsp8], [b_nsp8], nsp8[:], nsp8[:], AF.Ln, bias=1.0, scale=1.0)
    TS([b_nsp8], [b_nsp8], nsp8[:], nsp8[:], -8.0, None, ALU.mult)
    b_x0 = kb.buf("x0")
    kb.dma("sp", [DMA(xr[c * 128:(c + 1) * 128, :], xT[c * 128:(c + 1) * 128, :]) for c in range(16)], writes=[b_x0], owner=b_x0)
    kb.barrier()

    xr_v = xr.rearrange("(k p) t -> p k t", p=128)

    def rms_norm_tile(xin, b_xin, tg, sq_ring, rs_ring, pbi):
        kb.dma("sp", [DMA(xin[:, 0:8, :], xr_v[:, 0:8, tg:tg + 512]), DMA(xin[:, 8:16, :], xr_v[:, 8:16, tg:tg + 512])], writes=[b_xin], owner=b_xin)
        for k in range(NK):
            sq, b_sq = sq_ring.next()
            A([b_xin], [b_sq], sq[:], xin[:, k, :], AF.Square)
            kb.op("pe", MM(bank(pbi), ones_b[:], sq[:], start=(k == 0), stop=(k == NK - 1)), reads=[b_sq, b_onesb], writes=[pb[pbi]])
        rs, b_rs = rs_ring.next()
        A([pb[pbi], b_epsc], [b_rs], rs[:], bank(pbi), AF.Sqrt, scale=1.0 / D, bias=epsc[:])
        V("reciprocal", [b_rs], [b_rs], out=rs[:], in_=rs[:])
        return rs, b_rs

    def rms_norm_to(hT, b_hT, gname, l, t0, xin_ring, sq_ring, rs_ring, pbis):
        for tt in range(nTT):
            xin, b_xin = xin_ring.next()
            rs, b_rs = rms_norm_tile(xin, b_xin, t0 + tt * 512, sq_ring, rs_ring, pbis[tt % len(pbis)])
            for k in range(NK):
                STT([b_xin, b_rs, b_colp], [b_hT], hT[:, k, tt * 512:(tt + 1) * 512], xin[:, k, :], col(gname, l * 16 + k), rs[:], ALU.mult, ALU.mult)

    evac_flip = [0]

    def evac_copy(out_ap, in_ap, reads, writes):
        evac_flip[0] ^= 1
        if evac_flip[0]:
            A(reads, writes, out_ap, in_ap, AF.Copy)
        else:
            V("tensor_copy", reads, writes, out=out_ap, in_=in_ap)

    def proj_residual(l, t0, src_dram, nk, w_dram_l, chunk_tt):
        srcv = src_dram.rearrange("(k p) t -> p k t", p=128)
        splits = [(a, min(a + 16, nk)) for a in range(0, nk, 16)]
        for c0 in range(0, nTT, chunk_tt):
            ph = Phase(kb)
            sms = [ph.tile("sm", [128, nk, 512], BF16) for _ in range(chunk_tt)]
            for tt in range(chunk_tt):
                sm, b_sm = sms[tt]
                cs0 = (c0 + tt) * 512
                kb.dma("sp", [DMA(sm[:, a:b, :], srcv[:, a:b, cs0:cs0 + 512]) for a, b in splits], writes=[b_sm], owner=b_sm)
            wo_ring = ph.ring("wo", 3, [128, nk, 128], BF16)
            xs_ring = ph.ring("xs", 3, [128, 512], F32)
            it = 0
            for f in range(16):
                wo, b_wo = wo_ring.next()
                kb.dma("pool", [DMA(wo[:, a:b, :], w_dram_l[f, :, a:b, :]) for a, b in splits], writes=[b_wo], owner=b_wo)
                for tt in range(chunk_tt):
                    sm, b_sm = sms[tt]
                    bi = it % 8
                    it += 1
                    tg = t0 + (c0 + tt) * 512
                    xs, b_xs = xs_ring.next()
                    kb.dma("sp", DMA(xs[:], xr[f * 128:(f + 1) * 128, tg:tg + 512]), writes=[b_xs], owner=b_xs)
                    kb.op("pe", [MM(bank(bi), wo[:, k, :], sm[:, k, :], start=(k == 0), stop=(k == nk - 1)) for k in range(nk)],
                          reads=[b_wo, b_sm], writes=[pb[bi]])
                    TT([pb[bi], b_xs], [b_xs], xs[:], bank(bi), xs[:], ALU.add)
                    kb.dma("sp", DMA(xr[f * 128:(f + 1) * 128, tg:tg + 512], xs[:]), reads=[b_xs], owner=b_xs)
            ph.close()

    def v3(ap):
        return ap.rearrange("p (b j) -> p b j", b=4)

    for l in range(L):
        for s in range(nST):
            t0 = s * ST
            first = (s == 0) and not has_prev
            if "p1" not in skip:
                phO = Phase(kb)
                hT, b_hT = phO.tile("hT", [128, NK, ST], BF16)
                ph = Phase(kb)
                xin_ring = ph.ring("xin", 2, [128, NK, 512], F32)
                sq_ring = ph.ring("sq", 3, [128, 512], BF16)
                rs_ring = ph.ring("rs", 2, [128, 512], F32)
                wt_ring = ph.ring("wt", 3, [128, NK, 128], BF16)
                stg_ring = ph.ring("stg", 3, [128, ST], BF16)
                wv_ring = ph.ring("wv", 1, [128, NK, 512], BF16)
                sv_ring = ph.ring("sv", 3, [128, 512], BF16)
                rms_norm_to(hT, b_hT, "gmix", l, t0, xin_ring, sq_ring, rs_ring, (6, 7))
                pbr = 0
                for m in list(range(0, 32)) + list(range(40, 48)):
                    wt, b_wt = wt_ring.next()
                    kb.dma("pool", DMA(wt[:], w_in_t[l, m]), writes=[b_wt], owner=b_wt)
                    stg, b_stg = stg_ring.next()
                    for tt in range(nTT):
                        bi = pbr % 8
                        pbr += 1
                        kb.op("pe", [MM(bank(bi), wt[:, k, :], hT[:, k, tt * 512:(tt + 1) * 512], start=(k == 0), stop=(k == NK - 1)) for k in range(NK)],
                              reads=[b_wt, b_hT], writes=[pb[bi]])
                        o_ap = stg[:, tt * 512:(tt + 1) * 512]
                        if m < 48:
                            evac_copy(o_ap, bank(bi), [pb[bi]], [b_stg])
                        else:
                            A([pb[bi], b_colp], [b_stg], o_ap, bank(bi), AF.Sigmoid, bias=col("gbias", l * 48 + (m - 48)), scale=1.0)
                    if m < 8:
                        dst = lxT[m * 128:(m + 1) * 128, t0:t0 + ST]
                    elif m < 16:
                        dst = lgT[(m - 8) * 128:(m - 7) * 128, :]
                    elif m < 24:
                        dst = qT[(m - 16) * 128:(m - 15) * 128, :]
                    elif m < 32:
                        dst = kT[(m - 24) * 128:(m - 23) * 128, t0:t0 + ST]
                    elif m < 48:
                        dst = uT[(m - 40) * 128:(m - 39) * 128, :]
                    else:
                        dst = gT[(m - 48) * 128:(m - 47) * 128, :]
                    kb.dma("sp", DMA(dst, stg[:]), reads=[b_stg], owner=b_stg)
                for cb in range(2):
                    wv, b_wv = wv_ring.next()
                    kb.dma("pool", [DMA(wv[:, :, mm * 128:(mm + 1) * 128], w_in_t[l, 32 + 4 * cb + mm]) for mm in range(4)], writes=[b_wv], owner=b_wv)
                    for j in range(ST // 128):
                        bi = pbr % 8
                        pbr += 1
                        kb.op("pe", [MM(bank(bi), hT[:, k, j * 128:(j + 1) * 128], wv[:, k, :], start=(k == 0), stop=(k == NK - 1)) for k in range(NK)],
                              reads=[b_wv, b_hT], writes=[pb[bi]])
                        sv, b_sv = sv_ring.next()
                        evac_copy(sv[:], bank(bi), [pb[bi]], [b_sv])
                        kb.dma("sp", DMA(vS[t0 + j * 128:t0 + (j + 1) * 128, cb * 512:(cb + 1) * 512], sv[:]), reads=[b_sv], owner=b_sv)
                ph.close()

            if "p1" not in skip:
                ph = Phase(kb)
                gwt_ring = ph.ring("gwt", 3, [128, NK, 128], BF16)
                gst_ring = ph.ring("gst", 3, [128, ST], BF16)

                def gates_gen():
                    pend = None
                    g = 0
                    for m in range(48, 96):
                        wt, b_wt = gwt_ring.next()
                        kb.dma("pool", DMA(wt[:], w_in_t[l, m]), writes=[b_wt], owner=b_wt)
                        stg, b_stg = gst_ring.next()
                        for tt in range(nTT):
                            bi = 6 + (g % 2)
                            g += 1
                            kb.op("pe", [MM(bank(bi), wt[:, k, :], hT[:, k, tt * 512:(tt + 1) * 512], start=(k == 0), stop=(k == NK - 1)) for k in range(NK)],
                                  reads=[b_wt, b_hT], writes=[pb[bi]])
                            if pend is not None:
                                pend()
                            def fin(bi=bi, m=m, tt=tt, stg=stg, b_stg=b_stg):
                                A([pb[bi], b_colp], [b_stg], stg[:, tt * 512:(tt + 1) * 512], bank(bi), AF.Sigmoid, bias=col("gbias", l * 48 + (m - 48)), scale=1.0)
                                if tt == nTT - 1:
                                    kb.dma("sp", DMA(gT[(m - 48) * 128:(m - 47) * 128, :], stg[:]), reads=[b_stg], owner=b_stg)
                            pend = fin
                            yield
                    pend()

                ggen = gates_gen()

                def gstep(k):
                    for _ in range(k):
                        if next(ggen, "done") == "done":
                            return False
                    return True
            else:
                ph = Phase(kb)

                def gstep(k):
                    return False

            if "p2a" not in skip:
                bd, b_bd = ph.tile("bd", [128, 8, 2, 128], BF16)
                kb.dma("pool", DMA(bd[:], lruw_d[l]), writes=[b_bd], owner=b_bd)
                lx_ring = ph.ring("lx", 3, [128, 515], BF16)
                lg_ring = ph.ring("lg", 3, [128, 512], BF16)
                ya_ring = ph.ring("ya", 2, [128, 512], BF16)
                hs_ring = ph.ring("hs", 2, [128, 512], F32)
                xc_ring = ph.ring("xc", 2, [128, 512], F32)
                xcb_ring = ph.ring("xcb", 2, [128, 512], BF16)
                rr, b_rr = ph.tile("rr", [128, 512], F32)
                ii, b_ii = ph.tile("ii", [128, 512], F32)
                aa, b_aa = ph.tile("aa", [128, 512], F32)
                a2, b_a2 = ph.tile("a2", [128, 512], F32)
                gg, b_gg = ph.tile("gg", [128, 512], F32)
                if first:
                    V("memset", [], [b_lruh], ap=lru_h[:], constant=0.0)
                lunits = [(ct, tt) for ct in range(8) for tt in range(nTT)]
                lst = {}

                def lru_load(n):
                    ct, tt = lunits[n]
                    r0, r1 = ct * 128, (ct + 1) * 128
                    tg = t0 + tt * 512
                    lx, b_lx = lx_ring.next()
                    lg, b_lg = lg_ring.next()
                    if tg == 0 and not has_prev:
                        V("memset", [], [b_lx], ap=lx[:, 0:3], constant=0.0)
                        kb.dma("sp", DMA(lx[:, 3:515], lxT[r0:r1, 0:512]), writes=[b_lx], owner=b_lx)
                    else:
                        kb.dma("sp", DMA(lx[:], lxT[r0:r1, tg - 3:tg + 512]), writes=[b_lx], owner=b_lx)
                    kb.dma("sp", DMA(lg[:], lgT[r0:r1, tt * 512:(tt + 1) * 512]), writes=[b_lg], owner=b_lg)
                    lst[n] = dict(lx=lx, b_lx=b_lx, lg=lg, b_lg=b_lg)

                def lru_front(n):
                    ct, tt = lunits[n]
                    d = lst[n]
                    lx, b_lx = d["lx"], d["b_lx"]
                    xc, b_xc = xc_ring.next()
                    xcb, b_xcb = xcb_ring.next()
                    cwi = (l * 8 + ct) * 4
                    TS([b_lx, b_colp], [b_xc], xc[:], lx[:, 3:515], col("convw", cwi + 3), col("convb", l * 8 + ct), ALU.mult, ALU.add)
                    for k in range(3):
                        STT([b_lx, b_xc, b_colp], [b_xc], xc[:], lx[:, k:k + 512], col("convw", cwi + k), xc[:], ALU.mult, ALU.add)
                    V("tensor_copy", [b_xc], [b_xcb], out=xcb[:], in_=xc[:])
                    pbase = (n % 2) * 2
                    kb.op("pe", MM(bank(pbase), bd[:, ct, 0, :], xcb[:]), reads=[b_bd, b_xcb], writes=[pb[pbase]])
                    kb.op("pe", MM(bank(pbase + 1), bd[:, ct, 1, :], xcb[:]), reads=[b_bd, b_xcb], writes=[pb[pbase + 1]])
                    d.update(xc=xc, b_xc=b_xc, pbase=pbase)

                lru_load(0)
                if len(lunits) > 1:
                    lru_load(1)
                lru_front(0)
                for n in range(len(lunits)):
                    ct, tt = lunits[n]
                    r0, r1 = ct * 128, (ct + 1) * 128
                    d = lst[n]
                    xc, b_xc, pbase = d["xc"], d["b_xc"], d["pbase"]
                    lg, b_lg = d["lg"], d["b_lg"]
                    if n + 2 < len(lunits):
                        lru_load(n + 2)
                    A([pb[pbase], b_colp], [b_rr], rr[:], bank(pbase), AF.Sigmoid, bias=col("ba", l * 8 + ct), scale=1.0)
                    A([pb[pbase + 1], b_colp], [b_ii], ii[:], bank(pbase + 1), AF.Sigmoid, bias=col("bx", l * 8 + ct), scale=1.0)
                    A([b_rr, b_nsp8], [b_aa], aa[:], rr[:], AF.Exp, scale=nsp8[:, l * 8 + ct:l * 8 + ct + 1])
                    TT([b_ii, b_xc], [b_ii], ii[:], ii[:], xc[:], ALU.mult)
                    TT([b_aa], [b_a2], a2[:], aa[:], aa[:], ALU.mult)
                    A([b_a2], [b_a2], a2[:], a2[:], AF.Sqrt, scale=-1.0, bias=1.0)
                    gstep(1)
                    if n + 1 < len(lunits):
                        lru_front(n + 1)
                    gstep(1)
                    A([b_lg], [b_gg], gg[:], lg[:], AF.Gelu_apprx_tanh)
                    TT([b_ii, b_a2], [b_ii], ii[:], ii[:], a2[:], ALU.mult)
                    hs, b_hs = hs_ring.next()
                    V("tensor_tensor_scan", [b_aa, b_ii, b_lruh], [b_hs], out=hs[:], data0=aa[:], data1=ii[:], initial=lru_h[:, ct:ct + 1], op0=ALU.mult, op1=ALU.add)
                    V("tensor_copy", [b_hs], [b_lruh], out=lru_h[:, ct:ct + 1], in_=hs[:, 511:512])
                    ya, b_ya = ya_ring.next()
                    TT([b_hs, b_gg], [b_ya], ya[:], hs[:], gg[:], ALU.mult)
                    kb.dma("sp", DMA(yaT[r0:r1, tt * 512:(tt + 1) * 512], ya[:]), reads=[b_ya], owner=b_ya)
                    del lst[n]
                    gstep(2)

            if "p2b" not in skip:
                NKT = (ST + 512) // 128
                qh_ring = ph.ring("qh", 2, [128, ST], BF16)
                kh_ring = ph.ring("kh", 2, [128, ST + 512], BF16)
                vh_ring = ph.ring("vh", 2, [128, NKT, 128], BF16)
                bh_ring = ph.ring("bh", 2, [128, 640], F32)
                yb_ring = ph.ring("yb", 2, [128, ST], BF16)
                sf_ring = ph.ring("sf", 2, [128, 640], F32)
                pt_ring = ph.ring("pt", 3, [128, 640], BF16)
                rc_ring = ph.ring("rc", 2, [128, 128], F32)
                NQ = ST // 128
                aunits = [(hd, m) for hd in range(8) for m in range(NQ)]
                NU = len(aunits)
                hst, ust = {}, {}

                def att_load(hd):
                    r0, r1 = hd * 128, (hd + 1) * 128
                    qh, b_qh = qh_ring.next()
                    kh, b_kh = kh_ring.next()
                    vh, b_vh = vh_ring.next()
                    bh, b_bh = bh_ring.next()
                    yb, b_yb = yb_ring.next()
                    kb.dma("sp", DMA(qh[:], qT[r0:r1, :]), writes=[b_qh], owner=b_qh)
                    kb.dma("sp", DMA(bh[:], abias_d[l, hd]), writes=[b_bh], owner=b_bh)
                    if first:
                        kb.dma("sp", DMA(kh[:, 512:], kT[r0:r1, 0:ST]), writes=[b_kh], owner=b_kh)
                        kb.dma("sp", DMA(vh[:, 4:, :], vS[0:ST, r0:r1].rearrange("(n p) d -> p n d", p=128)), writes=[b_vh], owner=b_vh)
                    else:
                        kb.dma("sp", DMA(kh[:], kT[r0:r1, t0 - 512:t0 + ST]), writes=[b_kh], owner=b_kh)
                        kb.dma("sp", DMA(vh[:], vS[t0 - 512:t0 + ST, r0:r1].rearrange("(n p) d -> p n d", p=128)), writes=[b_vh], owner=b_vh)
                    hst[hd] = dict(qh=qh, b_qh=b_qh, kh=kh, b_kh=b_kh, vh=vh, b_vh=b_vh, bh=bh, b_bh=b_bh, yb=yb, b_yb=b_yb)

                def att_scores(n):
                    hd, m = aunits[n]
                    if hd not in hst:
                        att_load(hd)
                    H = hst[hd]
                    kts = [kt for kt in range(5) if (not first) or (m - 4 + kt) >= 0]
                    sset = n % 2
                    sb = 2 * sset
                    sc = psum[:, sb * 512:sb * 512 + 640]
                    kb.op("pe", [MM(sc[:, kt * 128:(kt + 1) * 128], H["kh"][:, (m + kt) * 128:(m + kt + 1) * 128], H["qh"][:, m * 128:(m + 1) * 128]) for kt in kts],
                          reads=[H["b_kh"], H["b_qh"]], writes=[pb[sb], pb[sb + 1]])
                    ust[n] = dict(kts=kts, sb=sb, sc=sc, lo=kts[0] * 128)

                def att_sf_exp(n):
                    hd, m = aunits[n]
                    H, U = hst[hd], ust[n]
                    sf, b_sf = sf_ring.next()
                    pt, b_pt = pt_ring.next()
                    lo, hi, sb, sc = U["lo"], 640, U["sb"], U["sc"]
                    STT([pb[sb], pb[sb + 1], H["b_bh"]], [b_sf], sf[:, lo:hi], sc[:, lo:hi], float(128 ** -0.5), H["bh"][:, lo:hi], ALU.mult, ALU.add)
                    A([b_sf], [b_pt], pt[:, lo:hi], sf[:, lo:hi], AF.Exp)
                    U.update(pt=pt, b_pt=b_pt)

                def att_pv(n):
                    hd, m = aunits[n]
                    H, U = hst[hd], ust[n]
                    kts, pt = U["kts"], U["pt"]
                    pbk = 4 + (n % 2)
                    nk = len(kts)
                    fns = [MM(bank(pbk)[:, 0:128], H["vh"][:, m + kt, :], pt[:, kt * 128:(kt + 1) * 128], start=(i_ == 0), stop=(i_ == nk - 1)) for i_, kt in enumerate(kts)]
                    fns += [MM(bank(pbk)[:, 128:256], ones_b[:], pt[:, kt * 128:(kt + 1) * 128], start=(i_ == 0), stop=(i_ == nk - 1)) for i_, kt in enumerate(kts)]
                    kb.op("pe", fns, reads=[H["b_vh"], U["b_pt"], b_onesb], writes=[pb[pbk]])
                    U["pbk"] = pbk

                def att_fin(n):
                    hd, m = aunits[n]
                    H, U = hst[hd], ust[n]
                    pbk = U["pbk"]
                    rc, b_rc = rc_ring.next()
                    V("reciprocal", [pb[pbk]], [b_rc], out=rc[:], in_=bank(pbk)[:, 128:256])
                    TT([pb[pbk], b_rc], [H["b_yb"]], H["yb"][:, m * 128:(m + 1) * 128], bank(pbk)[:, 0:128], rc[:], ALU.mult)
                    if m == NQ - 1:
                        kb.dma("sp", DMA(ybT[hd * 128:(hd + 1) * 128, :], H["yb"][:]), reads=[H["b_yb"]], owner=H["b_yb"])
                    del ust[n]

                att_scores(0)
                if NU > 1:
                    att_scores(1)
                att_sf_exp(0)
                for n in range(NU):
                    if n + 2 < NU:
                        att_scores(n + 2)
                    att_pv(n)
                    if n + 1 < NU:
                        att_sf_exp(n + 1)
                    if n >= 1:
                        att_fin(n - 1)
                    if n % 2 == 0:
                        gstep(1)
                att_fin(NU - 1)
            while gstep(8):
                pass
            ph.close()
            if "p1" not in skip:
                phO.close()


            if "p2c" not in skip:
                ph = Phase(kb)
                W = 4096
                NF = 32 * 129
                E2, b_Ere = ph.tile("E2", [128, 2, W], F32)
                b_Eim = b_Ere
                E_re, E_im = E2[:, 0, :], E2[:, 1, :]
                F2, b_Fre = ph.tile("F2", [128, 2, 32, 129], F32)
                b_Fim = b_Fre
                F_re, F_im = F2[:, 0, :, :], F2[:, 1, :, :]
                bbr, b_bbr = ph.tile("bbr", [128, W], BF16)
                bbi, b_bbi = ph.tile("bbi", [128, W], BF16)
                Ffr = F_re[:].rearrange("p b j -> p (b j)")
                Ffi = F_im[:].rearrange("p b j -> p (b j)")
                MUL, ADD, SUB = ALU.mult, ALU.add, ALU.subtract
                if s == 0:
                    pp = Phase(kb)
                    ar, b_ar = pp.tile("ar", [128, W], F32)
                    ai, b_ai = pp.tile("ai", [128, W], F32)
                    sr, b_sr = pp.tile("sr", [128, W], F32)
                    t1f, b_t1 = pp.tile("t1", [128, NF], F32)
                    t2f, b_t2 = pp.tile("t2", [128, NF], F32)
                    t3f, b_t3 = pp.tile("t3", [128, NF], F32)
                    t1, t2, t3 = t1f[:, 0:W], t2f[:, 0:W], t3f[:, 0:W]
                    bdr, b_bdr = Ffr[:, 0:W], b_Fre
                    bdi, b_bdi = Ffi[:, 0:W], b_Fim
                    kb.dma("sp", DMA(ar[:], rowp_d[l, 0:1, :].to_broadcast([128, W])), writes=[b_ar], owner=b_ar)
                    kb.dma("sp", DMA(ai[:], rowp_d[l, 1:2, :].to_broadcast([128, W])), writes=[b_ai], owner=b_ai)
                    kb.dma("sp", DMA(sr[:], rowp_d[l, 2:3, :].to_broadcast([128, W])), writes=[b_sr], owner=b_sr)
                    kb.dma("sp", DMA(bdr, bbd_d[l, :, 0, :]), writes=[b_bdr], owner=b_bdr)
                    kb.dma("sp", DMA(bdi, bbd_d[l, :, 1, :]), writes=[b_bdi], owner=b_bdi)

                    def sincos(dst_sin, b_ds, dst_cos, b_dc, ycyc, b_y, tmp, b_tmp):
                        TS([b_y], [b_tmp], tmp, ycyc, MAGIC, MAGIC, ALU.add, ALU.subtract)
                        TT([b_y, b_tmp], [b_tmp], tmp, ycyc, tmp, ALU.subtract)
                        A([b_tmp], [b_ds], dst_sin, tmp, AF.Sin, scale=TWO_PI)
                        TS([b_y], [b_y], ycyc, ycyc, 0.25, None, ALU.add)
                        TS([b_y], [b_tmp], tmp, ycyc, MAGIC, MAGIC, ALU.add, ALU.subtract)
                        TT([b_y, b_tmp], [b_tmp], tmp, ycyc, tmp, ALU.subtract)
                        A([b_tmp], [b_dc], dst_cos, tmp, AF.Sin, scale=TWO_PI)

                    MUL, ADD, SUB = ALU.mult, ALU.add, ALU.subtract
                    A([b_sr], [b_sr], sr[:], sr[:], AF.Exp)
                    TT([b_ar, b_sr], [b_t1], t1, ar[:], sr[:], MUL)
                    A([b_t1], [b_t1], t1, t1, AF.Exp)
                    TT([b_ai, b_sr], [b_t2], t2, ai[:], sr[:], MUL)
                    TS([b_t2], [b_t2], t2, t2, 1.0 / TWO_PI, None, MUL)
                    sincos(E_im[:], b_Eim, E_re[:], b_Ere, t2, b_t2, t3, b_t3)
                    TT([b_Ere, b_t1], [b_Ere], E_re[:], E_re[:], t1, MUL)
                    TT([b_Eim, b_t1], [b_Eim], E_im[:], E_im[:], t1, MUL)
                    TS([b_Ere], [b_Ere], E_re[:], E_re[:], -1.0, None, ADD)
                    TT([b_ar], [b_t1], t1, ar[:], ar[:], MUL)
                    TT([b_ai], [b_t2], t2, ai[:], ai[:], MUL)
                    TT([b_t1, b_t2], [b_t1], t1, t1, t2, ADD)
                    V("reciprocal", [b_t1], [b_t1], out=t1, in_=t1)
                    TT([b_Ere, b_ar], [b_t2], t2, E_re[:], ar[:], MUL)
                    TT([b_Eim, b_ai], [b_t3], t3, E_im[:], ai[:], MUL)
                    TT([b_t2, b_t3], [b_t2], t2, t2, t3, ADD)
                    TT([b_t2, b_t1], [b_t2], t2, t2, t1, MUL)
                    TT([b_Eim, b_ar], [b_t3], t3, E_im[:], ar[:], MUL)
                    TT([b_Ere, b_ai], [b_Eim], E_im[:], E_re[:], ai[:], MUL)
                    TT([b_t3, b_Eim], [b_t3], t3, t3, E_im[:], SUB)
                    TT([b_t3, b_t1], [b_t3], t3, t3, t1, MUL)
                    TT([b_t2, b_bdr], [b_Ere], E_re[:], t2, bdr, MUL)
                    TT([b_t3, b_bdi], [b_Eim], E_im[:], t3, bdi, MUL)
                    TT([b_Ere, b_Eim], [b_bbr], bbr[:], E_re[:], E_im[:], SUB)
                    TT([b_t2, b_bdi], [b_Ere], E_re[:], t2, bdi, MUL)
                    TT([b_t3, b_bdr], [b_Eim], E_im[:], t3, bdr, MUL)
                    TT([b_Ere, b_Eim], [b_bbi], bbi[:], E_re[:], E_im[:], ADD)
                    iocol = col("iota", 0)
                    TT([b_ar, b_sr], [b_t1], t1, ar[:], sr[:], MUL)
                    TS([b_t1, b_colp], [b_t1], t1, t1, iocol, -1.0, MUL, MUL)
                    A([b_t1], [b_t1], t1, t1, AF.Exp)
                    TT([b_ai, b_sr], [b_t2], t2, ai[:], sr[:], MUL)
                    TS([b_t2, b_colp], [b_t2], t2, t2, iocol, -1.0 / TWO_PI, MUL, MUL)
                    sincos(E_im[:], b_Eim, E_re[:], b_Ere, t2, b_t2, t3, b_t3)
                    TT([b_Ere, b_t1], [b_Ere], E_re[:], E_re[:], t1, MUL)
                    TT([b_Eim, b_t1], [b_Eim], E_im[:], E_im[:], t1, MUL)
                    stc, b_stc = pp.tile("stc", [128, 32], F32)
                    alc, b_alc = pp.tile("alc", [128, 32], F32)
                    thc, b_thc = pp.tile("thc", [128, 32], F32)
                    A([b_colp], [b_stc], stc[:], col("lsc", l * 32, 32), AF.Exp)
                    TT([b_colp, b_stc], [b_alc], alc[:], col("acre", l * 32, 32), stc[:], MUL)
                    TT([b_colp, b_stc], [b_thc], thc[:], col("acim", l * 32, 32), stc[:], MUL)
                    TS([b_thc], [b_thc], thc[:], thc[:], 1.0 / TWO_PI, None, MUL)
                    g1 = t1f[:].rearrange("p (b j) -> p b j", b=32)
                    g2 = t2f[:].rearrange("p (b j) -> p b j", b=32)
                    io_b = iorow[:].unsqueeze(1).to_broadcast([128, 32, 129])
                    TT([b_iorow, b_alc], [b_t1], g1, io_b, alc[:].unsqueeze(2).to_broadcast([128, 32, 129]), MUL)
                    A([b_t1], [b_t1], t1f[:], t1f[:], AF.Exp)
                    TT([b_iorow, b_thc], [b_t2], g2, io_b, thc[:].unsqueeze(2).to_broadcast([128, 32, 129]), MUL)
                    sincos(Ffi, b_Fim, Ffr, b_Fre, t2f[:], b_t2, t3f[:], b_t3)
                    TT([b_Fre, b_t1], [b_Fre], Ffr, Ffr, t1f[:], MUL)
                    TT([b_Fim, b_t1], [b_Fim], Ffi, Ffi, t1f[:], MUL)
                    if nST > 1:
                        kb.dma("sp", DMA(tabE[0], E_re[:]), reads=[b_Ere], owner=b_Ere)
                        kb.dma("sp", DMA(tabE[1], E_im[:]), reads=[b_Eim], owner=b_Eim)
                        kb.dma("sp", DMA(tabF[0], Ffr), reads=[b_Fre], owner=b_Fre)
                        kb.dma("sp", DMA(tabF[1], Ffi), reads=[b_Fim], owner=b_Fim)
                        kb.dma("sp", DMA(tabB[0], bbr[:]), reads=[b_bbr], owner=b_bbr)
                        kb.dma("sp", DMA(tabB[1], bbi[:]), reads=[b_bbi], owner=b_bbi)
                    pp.close()
                else:
                    kb.dma("sp", DMA(E_re[:], tabE[0]), writes=[b_Ere], owner=b_Ere)
                    kb.dma("sp", DMA(E_im[:], tabE[1]), writes=[b_Eim], owner=b_Eim)
                    kb.dma("sp", DMA(Ffr, tabF[0]), writes=[b_Fre], owner=b_Fre)
                    kb.dma("sp", DMA(Ffi, tabF[1]), writes=[b_Fim], owner=b_Fim)
                    kb.dma("sp", DMA(bbr[:], tabB[0]), writes=[b_bbr], owner=b_bbr)
                    kb.dma("sp", DMA(bbi[:], tabB[1]), writes=[b_bbi], owner=b_bbi)
                cmat, b_cmat = ph.tile("cmat", [128, 8192], BF16)
                kb.dma("pool", [DMA(cmat[:, q * 2048:(q + 1) * 2048], ct_d[l, :, q * 2048:(q + 1) * 2048]) for q in range(4)], writes=[b_cmat], owner=b_cmat)
                ncmat, b_ncmat = ph.tile("ncmat", [128, 8192], BF16)
                TS([b_cmat], [b_ncmat], ncmat[:], cmat[:], -1.0, None, MUL)
                u_all, b_u = ph.tile("u_all", [128, 8, ST], BF16)
                kb.dma("sp", DMA(u_all[:], uT.rearrange("(c p) t -> p c t", p=128)), writes=[b_u], owner=b_u)
                p_ring = ph.ring("pp4", 2, [128, 4, 512], BF16)
                q_ring = ph.ring("qq4", 2, [128, 4, 512], BF16)
                cp_ring = ph.ring("cp", 2, [128, 2, 512], F32)
                yc_ring = ph.ring("yc", 4, [128, 128], BF16)
                ty_ring = ph.ring("ty", 2, [128, 128], F32)
                tn_ring = ph.ring("tn", 2, [128, 4, 4], F32)
                if first:
                    V("memset", [], [b_s5c], ap=s5c[:], constant=0.0)
                units = [(j, ct) for j in range(ST // 128) for ct in range(8)]

                def bu_banks(ui):
                    base = (ui % 2) * 4
                    return base, base + 1, base + 2, base + 3

                def emit_bu(ui):
                    j, ct = units[ui]
                    br, bi_, _, _ = bu_banks(ui)
                    ul = u_all[:, ct, j * 128:(j + 1) * 128]
                    c0, c1 = ct * 512, (ct + 1) * 512
                    kb.op("pe", MM(bank(br), ul, bbr[:, c0:c1]), reads=[b_u, b_bbr], writes=[pb[br]])
                    kb.op("pe", MM(bank(bi_), ul, bbi[:, c0:c1]), reads=[b_u, b_bbi], writes=[pb[bi_]])

                def st_eprod(ui, stt):
                    j, ct = units[ui]
                    br, bi_, cr_, ci_ = bu_banks(ui)
                    c0, c1 = ct * 512, (ct + 1) * 512
                    pp4, b_p = p_ring.next()
                    stt["p"] = (pp4, b_p)
                    E2ct = E2[:, :, c0:c1]
                    TT([pb[br], b_Ere], [b_p], pp4[:, 0:2, :], bank(br).unsqueeze(1).to_broadcast([128, 2, 512]), E2ct, MUL)
                    TT([pb[bi_], b_Ere], [b_p], pp4[:, 2:4, :], bank(bi_).unsqueeze(1).to_broadcast([128, 2, 512]), E2ct, MUL)
                    fns = []
                    for blk in range(4):
                        bs = slice(blk * 128, (blk + 1) * 128)
                        fns.append(MM(bank(cr_)[:, bs], pp4[:, 0, bs], tri_b[:], start=True, stop=False))
                        fns.append(MM(bank(cr_)[:, bs], pp4[:, 3, bs], ntri_b[:], start=False, stop=True))
                        fns.append(MM(bank(ci_)[:, bs], pp4[:, 2, bs], tri_b[:], start=True, stop=False))
                        fns.append(MM(bank(ci_)[:, bs], pp4[:, 1, bs], tri_b[:], start=False, stop=True))
                    kb.op("pe", fns, reads=[b_p, b_trib, b_ntrib], writes=[pb[cr_], pb[ci_]])

                def st_xprod(ui, stt):
                    j, ct = units[ui]
                    br, bi_, cr_, ci_ = bu_banks(ui)
                    cp, b_cp = cp_ring.next()
                    s0, s1 = ct * 4, ct * 4 + 4
                    v4 = lambda ap: ap.rearrange("p r (b j) -> p r b j", b=4)
                    cs2 = psum[:, cr_ * 512:(cr_ + 2) * 512].rearrange("p (r b j) -> p r b j", r=2, b=4)
                    TT([pb[cr_], pb[ci_], b_s5c], [b_cp], v4(cp[:]), cs2, s5c[:, :, s0:s1].unsqueeze(3).to_broadcast([128, 2, 4, 128]), ADD)
                    F2ct = F2[:, :, s0:s1, 0:128]
                    qq4, b_q = q_ring.next()
                    TT([b_cp, b_Fre], [b_q], v4(qq4[:, 0:2, :]), v3(cp[:, 0, :]).unsqueeze(1).to_broadcast([128, 2, 4, 128]), F2ct, MUL)
                    TT([b_cp, b_Fre], [b_q], v4(qq4[:, 2:4, :]), v3(cp[:, 1, :]).unsqueeze(1).to_broadcast([128, 2, 4, 128]), F2ct, MUL)
                    tn, b_tn = tn_ring.next()
                    cr = v3(cp[:, 0, :])[:, :, 127]
                    ci = v3(cp[:, 1, :])[:, :, 127]
                    Gr = F_re[:, s0:s1, 128]
                    Gi = F_im[:, s0:s1, 128]
                    TT([b_cp, b_Fre], [b_tn], tn[:, 0, :], cr, Gr, MUL)
                    TT([b_cp, b_Fim], [b_tn], tn[:, 1, :], ci, Gi, MUL)
                    TT([b_cp, b_Fre], [b_tn], tn[:, 2, :], ci, Gr, MUL)
                    TT([b_cp, b_Fim], [b_tn], tn[:, 3, :], cr, Gi, MUL)
                    TT([b_tn], [b_s5ca], s5ca[:, s0:s1], tn[:, 0, :], tn[:, 1, :], SUB)
                    TT([b_tn], [b_s5cb], s5cb[:, s0:s1], tn[:, 2, :], tn[:, 3, :], ADD)
                    fns = []
                    for blk in range(4):
                        bs = slice(blk * 128, (blk + 1) * 128)
                        o_re = ((ct * 4 + blk) * 2 + 0) * 128
                        o_im = ((ct * 4 + blk) * 2 + 1) * 128
                        ops = ((cmat, o_re, 0), (ncmat, o_re, 3), (ncmat, o_im, 2), (ncmat, o_im, 1))
                        for oi, (cm, o, qi) in enumerate(ops):
                            fns.append(MM(bank(cr_)[:, 0:128], cm[:, o:o + 128], qq4[:, qi, bs], start=(blk == 0 and oi == 0), stop=(blk == 3 and oi == 3)))
                    kb.op("pe", fns, reads=[b_cmat, b_ncmat, b_q], writes=[pb[cr_]])

                def st_out(ui, stt):
                    j, ct = units[ui]
                    br, bi_, cr_, ci_ = bu_banks(ui)
                    ul = u_all[:, ct, j * 128:(j + 1) * 128]
                    ty, b_ty = ty_ring.next()
                    yc, b_yc = yc_ring.next()
                    STT([b_u, pb[cr_], b_colp], [b_ty], ty[:], ul, col("ssmd", l * 8 + ct), bank(cr_)[:, 0:128], MUL, ADD)
                    A([b_ty], [b_yc], yc[:], ty[:], AF.Gelu_apprx_tanh)
                    kb.dma("sp", DMA(ycT[ct * 128:(ct + 1) * 128, j * 128:(j + 1) * 128], yc[:]), reads=[b_yc], owner=b_yc)

                nU = len(units)
                emit_bu(0)
                emit_bu(1)
                for pi in range(0, nU, 2):
                    a, b = pi, pi + 1
                    sa, sb_ = {}, {}
                    st_eprod(a, sa)
                    st_eprod(b, sb_)
                    st_xprod(a, sa)
                    if a + 2 < nU:
                        emit_bu(a + 2)
                    st_xprod(b, sb_)
                    if b + 2 < nU:
                        emit_bu(b + 2)
                    st_out(a, sa)
                    st_out(b, sb_)
                ph.close()


            if "p3" not in skip:
                ph = Phase(kb)
                ys = []
                for nm, src in (("ya", yaT), ("yb", ybT), ("yc", ycT)):
                    t = None
                    parts = []
                    srcv_ = src.rearrange("(c p) t -> p c t", p=128)
                    for tt in range(nTT):
                        tl, b = ph.tile(nm, [128, 8, 512], BF16)
                        kb.dma("sp", DMA(tl[:], srcv_[:, :, tt * 512:(tt + 1) * 512]), writes=[b], owner=b)
                        parts.append((tl, b))
                    ys.append(parts)
                ys.append(ys[2])
                wb_ring = ph.ring("wb", 3, [128, 4, 8, 128], BF16)
                g_ring = ph.ring("gg", 2, [128, 3, 512], BF16)
                ms_ring = ph.ring("ms", 2, [128, ST], BF16)
                tq_ring = ph.ring("tq", 2, [128, 4, 512], F32)
                gT_v = gT.rearrange("(b f p) t -> f p b t", b=3, f=16)
                it = 0
                for f in range(16):
                    wb, b_wb = wb_ring.next()
                    kb.dma("pool", [DMA(wb[:, q, :, :], wbr_t[l, f, :, q, :, :]) for q in range(4)], writes=[b_wb], owner=b_wb)
                    ms, b_ms = ms_ring.next()
                    for tt in range(nTT):
                        ts0, ts1 = tt * 512, (tt + 1) * 512
                        g3, b_g3 = g_ring.next()
                        kb.dma("sp", DMA(g3[:], gT_v[f][:, :, ts0:ts1]), writes=[b_g3], owner=b_g3)
                        b0 = (it % 2) * 4
                        it += 1
                        for q in range(4):
                            yt, b_yt = ys[q][tt]
                            kb.op("pe", [MM(bank(b0 + q), wb[:, q, k, :], yt[:, k, :], start=(k == 0), stop=(k == 7)) for k in range(8)],
                                  reads=[b_wb, b_yt], writes=[pb[b0 + q]])
                        tq, b_tq = tq_ring.next()
                        A([pb[b0 + 3]], [b_tq], tq[:, 3, :], bank(b0 + 3), AF.Sigmoid)
                        TT([pb[b0], b_g3], [b_tq], tq[:, 0, :], bank(b0 + 0), g3[:, 0, :], ALU.mult)
                        TT([pb[b0 + 1], b_g3], [b_tq], tq[:, 1, :], bank(b0 + 1), g3[:, 1, :], ALU.mult)
                        TT([pb[b0 + 2], b_tq], [b_tq], tq[:, 2, :], bank(b0 + 2), tq[:, 3, :], ALU.mult)
                        TT([b_tq, b_g3], [b_tq], tq[:, 2, :], tq[:, 2, :], g3[:, 2, :], ALU.mult)
                        TT([b_tq], [b_tq], tq[:, 0, :], tq[:, 0, :], tq[:, 1, :], ALU.add)
                        TT([b_tq], [b_ms], ms[:, ts0:ts1], tq[:, 0, :], tq[:, 2, :], ALU.add)
                    kb.dma("sp", DMA(mT[f * 128:(f + 1) * 128, :], ms[:]), reads=[b_ms], owner=b_ms)
                ph.close()
                proj_residual(l, t0, mT, 16, wout_t[l], nTT)

            if "p4" not in skip:
                ph = Phase(kb)
                hT, b_hT = ph.tile("hT", [128, NK, ST], BF16)
                xin_ring = ph.ring("xin", 2, [128, NK, 512], F32)
                sq_ring = ph.ring("sq", 3, [128, 512], BF16)
                rs_ring = ph.ring("rs", 2, [128, 512], F32)
                rms_norm_to(hT, b_hT, "gffn", l, t0, xin_ring, sq_ring, rs_ring, (6, 7))
                wg_ring = ph.ring("wg", 3, [128, 2, NK, 128], BF16)
                as_ring = ph.ring("as", 2, [128, ST], BF16)
                sl_ring = ph.ring("sl", 2, [128, 512], F32)
                it = 0
                for jj in range(NJ):
                    wg, b_wg = wg_ring.next()
                    kb.dma("pool", [DMA(wg[:, q, :, :], wgu_t[l, jj, :, q, :, :]) for q in range(2)], writes=[b_wg], owner=b_wg)
                    ast, b_ast = as_ring.next()
                    for tt in range(nTT):
                        ts0, ts1 = tt * 512, (tt + 1) * 512
                        bg = (it % 4) * 2
                        it += 1
                        for q in range(2):
                            kb.op("pe", [MM(bank(bg + q), wg[:, q, k, :], hT[:, k, ts0:ts1], start=(k == 0), stop=(k == NK - 1)) for k in range(NK)],
                                  reads=[b_wg, b_hT], writes=[pb[bg + q]])
                        sl, b_sl = sl_ring.next()
                        A([pb[bg]], [b_sl], sl[:], bank(bg), AF.Silu)
                        TT([pb[bg + 1], b_sl], [b_ast], ast[:, ts0:ts1], bank(bg + 1), sl[:], ALU.mult)
                    kb.dma("sp", DMA(actT[jj * 128:(jj + 1) * 128, :], ast[:]), reads=[b_ast], owner=b_ast)
                ph.close()
                proj_residual(l, t0, actT, NJ, wdn_t[l], min(2, nTT))

    ph = Phase(kb)
    xin_ring = ph.ring("xin", 2, [128, NK, 512], F32)
    sq_ring = ph.ring("sq", 3, [128, 512], BF16)
    rs_ring = ph.ring("rs", 2, [128, 512], F32)
    outv = outT.rearrange("(k p) t -> p k t", p=128)
    for tt in range(S // 512):
        tg = tt * 512
        xin, b_xin = xin_ring.next()
        rs, b_rs = rms_norm_tile(xin, b_xin, tg, sq_ring, rs_ring, 6 + (tt % 2))
        for k in range(NK):
            STT([b_xin, b_rs, b_colp], [b_xin], xin[:, k, :], xin[:, k, :], col("gfin", k), rs[:], ALU.mult, ALU.mult)
        kb.dma("sp", DMA(outv[:, :, tg:tg + 512], xin[:]), reads=[b_xin], owner=b_xin)
    ph.close()
    kb.finalize()
    return nc


def prep_weights(inp, L):
    f = np.float32
    out = {}
    w_in = inp["w_in"][:L]
    out["w_in_t"] = np.ascontiguousarray(w_in.reshape(L, 16, 128, 96, 128).transpose(0, 3, 2, 1, 4))
    wb = np.concatenate([inp["w_branch"][:L], inp["ssm_w_glu"][:L][:, None]], axis=1)
    out["wbr_t"] = np.ascontiguousarray(wb.reshape(L, 4, 8, 128, 16, 128).transpose(0, 4, 3, 1, 2, 5))
    out["wout_t"] = np.ascontiguousarray(inp["w_out"][:L].reshape(L, 16, 128, 16, 128).transpose(0, 3, 2, 1, 4))
    wgu = np.stack([inp["w_ffn_gate"][:L], inp["w_ffn_up"][:L]], axis=1)
    out["wgu_t"] = np.ascontiguousarray(wgu.reshape(L, 2, 16, 128, NJ, 128).transpose(0, 4, 3, 1, 2, 5))
    out["wdn_t"] = np.ascontiguousarray(inp["w_ffn_down"][:L].reshape(L, NJ, 128, 16, 128).transpose(0, 3, 2, 1, 4))
    COFF, NCOL = col_layout(L)
    colp = np.zeros((128, NCOL), f)

    def put(name, arr):
        colp[:, COFF[name]:COFF[name] + arr.shape[1]] = arr
    put("gmix", inp["norm_mix_g"][:L].reshape(L, 16, 128).transpose(2, 0, 1).reshape(128, -1))
    put("gffn", inp["norm_ffn_g"][:L].reshape(L, 16, 128).transpose(2, 0, 1).reshape(128, -1))
    put("gfin", inp["norm_final_g"].reshape(16, 128).T)
    put("gbias", inp["gate_bias"][:L].reshape(L, 3, 16, 128).transpose(3, 0, 1, 2).reshape(128, -1))
    put("convw", inp["lru_conv_w"][:L].reshape(L, 4, 8, 128).transpose(3, 0, 2, 1).reshape(128, -1))
    for nm, key in (("convb", "lru_conv_b"), ("ba", "lru_ba"), ("bx", "lru_bx"), ("lam", "lru_lambda"), ("ssmd", "ssm_d")):
        put(nm, inp[key][:L].reshape(L, 8, 128).transpose(2, 0, 1).reshape(128, -1))
    for nm, arr in (("acre", inp["ssm_a_re"][:L]), ("acim", inp["ssm_a_im"][:L]),
                    ("lsc", np.repeat(inp["ssm_log_step"][:L][:, :, None], 64, axis=2))):
        put(nm, arr.reshape(L, 32, 2, 64).transpose(2, 3, 0, 1).reshape(128, -1))
    colp[:, COFF["iota"]] = np.arange(128, dtype=f)
    out["colp"] = colp
    rowp = np.stack([inp["ssm_a_re"][:L].reshape(L, 4096), inp["ssm_a_im"][:L].reshape(L, 4096),
                     np.repeat(inp["ssm_log_step"][:L][:, :, None], 64, axis=2).reshape(L, 4096)], axis=1)
    out["rowp"] = np.ascontiguousarray(rowp.astype(f))
    lruw = np.zeros((L, 128, 8, 2, 128), f)
    for wi, key in enumerate(("lru_wa", "lru_wx")):
        w = inp[key][:L].reshape(L, 8, 2, 64, 64)
        for nl in range(2):
            lruw[:, nl * 64:(nl + 1) * 64, :, wi, nl * 64:(nl + 1) * 64] = w[:, :, nl].transpose(0, 2, 1, 3)
    out["lruw"] = lruw
    kk = np.arange(128)[:, None, None]
    kt = np.arange(5)[None, :, None]
    qq = np.arange(128)[None, None, :]
    dist = (4 - kt) * 128 + qq - kk
    rel = np.clip(dist, -128, 128) + 128
    cdiff = 8 - 2 * kt + qq // 64 - kk // 64
    valid = (cdiff >= 0) & (cdiff <= 8)
    ab = inp["attn_rel_bias"][:L][:, :, rel]
    ab = np.where(valid[None, None], ab, f(-30000.0)).astype(f)
    out["abias"] = np.ascontiguousarray(ab.reshape(L, 8, 128, 640))
    bbd = np.zeros((L, 128, 2, 8, 8, 64), f)
    for ri, key in enumerate(("ssm_b_re", "ssm_b_im")):
        B = inp[key][:L].reshape(L, 8, 8, 64, 16)
        for gl in range(8):
            bbd[:, gl * 16:(gl + 1) * 16, ri, :, gl, :] = B[:, :, gl].transpose(0, 3, 1, 2)
    out["bbd"] = np.ascontiguousarray(bbd.reshape(L, 128, 2, 4096))
    ctd = np.zeros((L, 128, 8, 4, 2, 128), f)
    for ri, key in enumerate(("ssm_c_re", "ssm_c_im")):
        C = inp[key][:L].reshape(L, 8, 4, 2, 16, 64)
        for blk in range(4):
            for gl2 in range(2):
                c0 = (2 * blk + gl2) * 16
                ctd[:, gl2 * 64:(gl2 + 1) * 64, :, blk, ri, c0:c0 + 16] = C[:, :, blk, gl2].transpose(0, 3, 1, 2)
    out["ctd"] = np.ascontiguousarray(ctd.reshape(L, 128, 8192))
    cst = np.zeros((128, 386), f)
    cst[:, 0:128] = np.triu(np.ones((128, 128), f))
    cst[:, 128:256] = 1.0
    cst[:, 256:385] = np.arange(129, dtype=f)[None, :]
    cst[:, 385] = 1e-6
    out["cst"] = cst
    out["epsd"] = np.full((128, 1), 1e-6, f)
    return out


_CACHE = {}


def run_model(inputs, L, S_core, ST, n_cores, dbg=False, skip=(), spread=False):
    x = np.asarray(inputs["x"], np.float32)
    B, S, _ = x.shape
    assert S == S_core and B <= n_cores
    key = (L, S_core, ST, dbg, tuple(skip))
    if key not in _CACHE:
        _CACHE[key] = build_program(L, S_core, ST, dbg=dbg, skip=skip)
    nc = _CACHE[key]
    wts = prep_weights({k: np.asarray(v, np.float32) for k, v in inputs.items() if k != "x"}, L)
    if spread:
        real = {0: 0, 1: 1, 4: 2, 5: 3}
        zw = dict(wts)
        for k in ("w_in_t", "wbr_t", "wout_t", "wgu_t", "wdn_t"):
            zw[k] = np.zeros_like(wts[k])
        zx = np.zeros((D, S), np.float32)
        in_maps = []
        for c in range(8):
            if c in real and real[c] < B:
                m = dict(wts)
                m["xT"] = np.ascontiguousarray(x[real[c]].T)
            else:
                m = dict(zw)
                m["xT"] = zx
            in_maps.append(m)
        res = run_bass_kernel_spmd(nc, in_maps, core_ids=list(range(8)))
        slots = [c for c in range(8) if c in real and real[c] < B]
        out = np.stack([np.ascontiguousarray(res.results[c]["outT"].T) for c in sorted(slots, key=lambda c: real[c])], axis=0)
        return out.astype(np.float32), res
    in_maps = []
    for c in range(n_cores):
        b = c % B
        m = dict(wts)
        m["xT"] = np.ascontiguousarray(x[b].T)
        in_maps.append(m)
    res = run_bass_kernel_spmd(nc, in_maps, core_ids=list(range(n_cores)))
    out = np.stack([np.ascontiguousarray(res.results[b]["outT"].T) for b in range(B)], axis=0)
    return out.astype(np.float32), res


def kernel(**inputs):
    out, _ = run_model(inputs, 4, 4096, 2048, 8, spread=True)
    return out
```
